# Optimizing a Trainium2 kernel written in Bass

```python
import math
import jax, jax.numpy as jnp
from jax import lax
import numpy as np

D_MODEL = 2048
BATCH = 4
SEQ = 4096
DEPTH = 1

D_MIX = D_MODEL
D_POOL = D_MIX // 2
POOL_WINDOWS = (2, 4, 8, 16)
N_POOL_GROUPS = len(POOL_WINDOWS)
POOL_GROUP = D_POOL // N_POOL_GROUPS
D_ATTN = D_MIX - D_POOL
N_HEADS = 8
HEAD_DIM = D_ATTN // N_HEADS
IDX_HEADS = 16
IDX_DIM = 64
INDEXER_SCALE = (IDX_HEADS * IDX_DIM) ** -0.5
MAX_TOPK = 256
Q_BLOCK = 128
N_BUCKETS = 32
MAX_DISTANCE = 128
PEER_HEADS = 8
PEER_KEYS = 128
N_EXPERTS = PEER_KEYS * PEER_KEYS
PEER_QDIM = 256
PEER_HALF = PEER_QDIM // 2
PEER_TOPK = 16
PEER_CHUNK = 128
ALPHA = (2 * DEPTH) ** 0.25
BETA = (8 * DEPTH) ** -0.25
LN_EPS = 1e-5
NEG_INF = -1e30

SPLIT_SIZES = (D_POOL, D_ATTN, D_ATTN, D_ATTN, IDX_HEADS * IDX_DIM, IDX_DIM, IDX_HEADS)
D_IN = sum(SPLIT_SIZES)

kernel_name = "hymba_pool_dsa_peer_deepnorm"


def layer_norm(x, g, b):
    xf = x.astype(jnp.float32)
    mu = jnp.mean(xf, axis=-1, keepdims=True)
    var = jnp.mean(jnp.square(xf - mu), axis=-1, keepdims=True)
    y = (xf - mu) * lax.rsqrt(var + LN_EPS)
    return (y * g.astype(jnp.float32) + b.astype(jnp.float32)).astype(x.dtype)


def split_columns(proj):
    parts, start = [], 0
    for size in SPLIT_SIZES:
        parts.append(proj[..., start:start + size])
        start += size
    return parts


def pool_mixer(v, pool_w, pool_scale):
    B, S, _ = v.shape
    vg = v.reshape(B, S, N_POOL_GROUPS, POOL_GROUP)
    c = jnp.cumsum(vg.astype(jnp.float32), axis=1)
    c = jnp.pad(c, ((0, 0), (1, 0), (0, 0), (0, 0)))
    pos = jnp.arange(S)
    means = []
    for g, w in enumerate(POOL_WINDOWS):
        cg = c[:, :, g]
        hi = cg[:, 1:]
        lo = cg[:, jnp.maximum(pos + 1 - w, 0)]
        cnt = jnp.minimum(pos + 1, w).astype(jnp.float32)[None, :, None]
        means.append((hi - lo) / cnt)
    pooled = jnp.stack(means, axis=2).astype(v.dtype) - vg
    mixed = jnp.einsum('bsgc,gcd->bsgd', pooled, pool_w)
    return mixed.reshape(B, S, D_POOL) * pool_scale


def t5_bucket(dist):
    max_exact = N_BUCKETS // 2
    d = jnp.maximum(dist, 1).astype(jnp.float32)
    large = max_exact + (jnp.log(d / max_exact) / math.log(MAX_DISTANCE / max_exact)
                         * (N_BUCKETS - max_exact)).astype(jnp.int32)
    large = jnp.minimum(large, N_BUCKETS - 1)
    return jnp.where(dist < max_exact, dist, large)


def sparse_attention(q, k, v, q_idx, k_idx, w_idx, rel_bias):
    B, S = q.shape[:2]
    top_k = min(MAX_TOPK, S // 4)
    n_blocks = S // Q_BLOCK
    b_ix = jnp.arange(B)[:, None, None]
    key_pos = jnp.arange(S)

    def block(i):
        start = i * Q_BLOCK
        qi = lax.dynamic_slice_in_dim(q_idx, start, Q_BLOCK, axis=1)
        wi = lax.dynamic_slice_in_dim(w_idx, start, Q_BLOCK, axis=1)
        qa = lax.dynamic_slice_in_dim(q, start, Q_BLOCK, axis=1)
        q_pos = start + jnp.arange(Q_BLOCK)
        rel = jax.nn.relu(jnp.einsum('bqhd,bsd->bqhs', qi, k_idx))
        score = jnp.einsum('bqh,bqhs->bqs', wi, rel).astype(jnp.float32) * INDEXER_SCALE
        causal = key_pos[None, :] <= q_pos[:, None]
        score = jnp.where(causal[None], score, NEG_INF)
        _, sel = lax.top_k(score, top_k)
        k_sel = k[b_ix, sel]
        v_sel = v[b_ix, sel]
        logits = jnp.einsum('bqhd,bqkhd->bhqk', qa, k_sel).astype(jnp.float32) * HEAD_DIM ** -0.5
        dist = q_pos[None, :, None] - sel
        bias = rel_bias[t5_bucket(jnp.maximum(dist, 0))]
        logits = logits + jnp.transpose(bias, (0, 3, 1, 2)).astype(jnp.float32)
        logits = jnp.where((dist >= 0)[:, None], logits, NEG_INF)
        p = jax.nn.softmax(logits, axis=-1).astype(v.dtype)
        return jnp.einsum('bhqk,bqkhd->bqhd', p, v_sel)

    out = lax.map(block, jnp.arange(n_blocks))
    return jnp.transpose(out, (1, 0, 2, 3, 4)).reshape(B, S, N_HEADS * HEAD_DIM)


def peer_ffn(x, wq, sub_keys, u, v):
    B, S, D = x.shape
    xt = x.reshape(-1, PEER_CHUNK, D)

    def chunk(xc):
        qh = (xc @ wq).reshape(PEER_CHUNK, PEER_HEADS, 2, PEER_HALF)
        s1 = jnp.einsum('thd,kd->thk', qh[:, :, 0], sub_keys[0]).astype(jnp.float32)
        s2 = jnp.einsum('thd,kd->thk', qh[:, :, 1], sub_keys[1]).astype(jnp.float32)
        v1, i1 = lax.top_k(s1, PEER_TOPK)
        v2, i2 = lax.top_k(s2, PEER_TOPK)
        cand = (v1[..., :, None] + v2[..., None, :]).reshape(PEER_CHUNK, PEER_HEADS, PEER_TOPK * PEER_TOPK)
        cidx = (i1[..., :, None] * PEER_KEYS + i2[..., None, :]).reshape(PEER_CHUNK, PEER_HEADS, PEER_TOPK * PEER_TOPK)
        best, pos = lax.top_k(cand, PEER_TOPK)
        expert = jnp.take_along_axis(cidx, pos, axis=-1)
        gate = jax.nn.softmax(best, axis=-1)
        u_sel = u[expert]
        v_sel = v[expert]
        act = jax.nn.gelu(jnp.einsum('td,thkd->thk', xc, u_sel).astype(jnp.float32), approximate=False)
        return jnp.einsum('thk,thkd->td', (gate * act).astype(x.dtype), v_sel)

    return lax.map(chunk, xt).reshape(B, S, D)


def setup_inputs(seed: int = 0) -> dict:
    key = jax.random.key(seed)
    ks = jax.random.split(key, 16)
    f32 = jnp.float32
    L = DEPTH
    x = jax.random.normal(ks[0], (BATCH, SEQ, D_MODEL), f32)
    w_in = jax.random.normal(ks[1], (L, D_MODEL, D_IN), f32) * D_MODEL ** -0.5
    pool_w = jax.random.normal(ks[2], (L, N_POOL_GROUPS, POOL_GROUP, POOL_GROUP), f32) * POOL_GROUP ** -0.5
    pool_scale = 1.0 + 0.1 * jax.random.normal(ks[3], (L, D_POOL), f32)
    rel_bias = 0.5 * jax.random.normal(ks[4], (N_BUCKETS, N_HEADS), f32)
    w_out = jax.random.normal(ks[5], (L, D_MIX, D_MODEL), f32) * (D_MIX ** -0.5 * BETA)
    ln1_g = 1.0 + 0.05 * jax.random.normal(ks[6], (L, D_MODEL), f32)
    ln1_b = 0.02 * jax.random.normal(ks[7], (L, D_MODEL), f32)
    peer_wq = jax.random.normal(ks[8], (L, D_MODEL, PEER_HEADS * PEER_QDIM), f32) * D_MODEL ** -0.5
    peer_subkeys = jax.random.normal(ks[9], (L, 2, PEER_KEYS, PEER_HALF), f32) * PEER_HALF ** -0.5
    peer_u = jax.random.normal(ks[10], (L, N_EXPERTS, D_MODEL), f32) * D_MODEL ** -0.5
    peer_v = jax.random.normal(ks[11], (L, N_EXPERTS, D_MODEL), f32) * (BETA * PEER_HEADS ** -0.5)
    ln2_g = 1.0 + 0.05 * jax.random.normal(ks[12], (L, D_MODEL), f32)
    ln2_b = 0.02 * jax.random.normal(ks[13], (L, D_MODEL), f32)
    return {"x": x, "w_in": w_in, "pool_w": pool_w, "pool_scale": pool_scale,
            "rel_bias": rel_bias, "w_out": w_out, "ln1_g": ln1_g, "ln1_b": ln1_b,
            "peer_wq": peer_wq, "peer_subkeys": peer_subkeys, "peer_u": peer_u,
            "peer_v": peer_v, "ln2_g": ln2_g, "ln2_b": ln2_b}


def reference(x, w_in, pool_w, pool_scale, rel_bias, w_out, ln1_g, ln1_b,
              peer_wq, peer_subkeys, peer_u, peer_v, ln2_g, ln2_b):
    B, S, _ = x.shape
    h = x
    for l in range(DEPTH):
        proj = h @ w_in[l]
        p_pool, p_q, p_k, p_v, p_qi, p_ki, p_wi = split_columns(proj)
        out_pool = pool_mixer(p_pool, pool_w[l], pool_scale[l])
        out_attn = sparse_attention(
            p_q.reshape(B, S, N_HEADS, HEAD_DIM),
            p_k.reshape(B, S, N_HEADS, HEAD_DIM),
            p_v.reshape(B, S, N_HEADS, HEAD_DIM),
            p_qi.reshape(B, S, IDX_HEADS, IDX_DIM), p_ki, p_wi, rel_bias)
        mix = jnp.concatenate([out_pool, out_attn], axis=-1) @ w_out[l]
        h = layer_norm(ALPHA * h + mix, ln1_g[l], ln1_b[l])
        ffn = peer_ffn(h, peer_wq[l], peer_subkeys[l], peer_u[l], peer_v[l])
        h = layer_norm(ALPHA * h + ffn, ln2_g[l], ln2_b[l])
    return h
```

```python
import numpy as np
from contextlib import ExitStack
import concourse.bass as bass
import concourse.mybir as mybir
from concourse.bass_utils import run_bass_kernel_spmd

F32 = mybir.dt.float32
BF16 = mybir.dt.bfloat16
U32 = mybir.dt.uint32
ALU = mybir.AluOpType
AF = mybir.ActivationFunctionType
AX = mybir.AxisListType

D = 2048
S = 4096
TOK = 2048
NEG = -1.0e30
ALPHA = 2.0 ** 0.25
LN_EPS = 1e-5
NIT = 22
TOPK = 256
ATT_SCALE = 128.0 ** -0.5
NSLOT = 6


class Buf:
    __slots__ = ("w", "r", "name")

    def __init__(self, name=""):
        self.w = None
        self.r = {}
        self.name = name


class KB:
    def __init__(self, nc, es):
        self.nc = nc
        self.engs = {"pe": nc.tensor, "dve": nc.vector, "act": nc.scalar, "pool": nc.gpsimd, "sp": nc.sync}
        self.psem = {e: es.enter_context(nc.semaphore("prog_" + e)) for e in ["pe", "dve", "act", "pool"]}
        self.cnt = {e: 0 for e in self.psem}
        self.seen = {e: {} for e in self.engs}
        self.pending = {e: [] for e in self.engs}
        self.dslots = {q: [[es.enter_context(nc.semaphore("dq_%s_%d" % (q, i))), 0, "dq_%s_%d" % (q, i)]
                           for i in range(NSLOT)] for q in ["sp", "pool"]}
        self.dnext = {q: 0 for q in self.dslots}
        self.nins = 0

    def wait(self, e, tok):
        if tok is None:
            return
        sem, val, key = tok
        if self.seen[e].get(key, 0) >= val:
            return
        self.engs[e].wait_ge(sem, val)
        self.seen[e][key] = val

    def _deps(self, e, reads, writes, deps):
        for b in reads:
            self.wait(e, b.w)
        for b in writes:
            self.wait(e, b.w)
            for t in b.r.values():
                self.wait(e, t)
        for t in deps:
            self.wait(e, t)

    def op(self, e, fn, reads=(), writes=(), deps=(), sig=True):
        self._deps(e, reads, writes, deps)
        ins = fn(self.engs[e])
        self.nins += 1
        if not sig:
            self.pending[e].append((list(reads), list(writes)))
            return None
        self.cnt[e] += 1
        ins.then_inc(self.psem[e], 1)
        key = "prog_" + e
        tok = (self.psem[e], self.cnt[e], key)
        allr = list(reads)
        allw = list(writes)
        for (r, w) in self.pending[e]:
            allr += r
            allw += w
        self.pending[e] = []
        for b in allw:
            b.w = tok
            b.r = {}
        for b in allr:
            if b not in allw:
                b.r[key] = tok
        return tok

    def barrier(self):
        toks = []
        for e in self.psem:
            if self.cnt[e] > 0:
                toks.append((self.psem[e], self.cnt[e], "prog_" + e))
        for q in self.dslots:
            for sem, cnt, key in self.dslots[q]:
                if cnt > 0:
                    toks.append((sem, cnt, key))
        for e in self.engs:
            for t in toks:
                self.wait(e, t)

    def dma(self, q, out, in_, reads=(), writes=(), deps=()):
        self._deps(q, reads, writes, deps)
        slot = self.dslots[q][self.dnext[q]]
        self.dnext[q] = (self.dnext[q] + 1) % NSLOT
        sem, cnt, key = slot
        if cnt > 0:
            self.wait(q, (sem, cnt, key))
        self.engs[q].dma_start(out=out, in_=in_).then_inc(sem, 16)
        self.nins += 1
        slot[1] = cnt + 16
        tok = (sem, cnt + 16, key)
        for b in writes:
            b.w = tok
            b.r = {}
        for b in reads:
            b.r[key] = tok
        return tok


def sap(t, dims, off=0, parts=128, p0=0):
    fs = 1
    for s_ in t.shape[1:]:
        fs *= int(s_)
    return bass.AP(t, p0 * fs + off, [[fs, parts]] + [[int(a), int(b)] for a, b in dims])


def bufs(n, name=""):
    return [Buf("%s%d" % (name, i)) for i in range(n)]


def build_nc(stop_after=None, small_peer=False):
    nc = bass.Bass("TRN2", target_bir_lowering=False)
    dbg = {}

    def din(name, shape, dt=F32):
        return nc.dram_tensor(name, list(shape), dt, kind="ExternalInput").ap()

    xT = din("xT", [2, 128, 16, TOK])
    x_tok = din("x_tok", [TOK, D])
    w_fm = din("w_fm", [33, 128, 16, 128])
    w_v = din("w_v", [4, 128, 16, 256])
    w_wi = din("w_wi", [128, 16, 16])
    pool_w = din("pool_w", [128, 4, 2, 256])
    consts = din("consts", [128, 1024])
    biasT = din("biasT", [128, 2, 8, 128])
    w_out = din("w_out", [128, 16, D])
    lnp = din("lnp", [4, 128, D])
    wq = din("wq", [128, 16, D])
    subkT = din("subkT", [128, 2, 128])
    uT = din("uT", [8 if small_peer else 128, 128, 16, 128])
    vL = din("vL", [8 if small_peer else 128, 128, D])
    y = nc.dram_tensor("y", [TOK, D], F32, kind="ExternalOutput").ap()
    maskT_d = nc.dram_tensor("maskT_d", [4, 128, 32, 512], BF16).ap()
    hT_d = nc.dram_tensor("hT_d", [128, 16, TOK], BF16).ap()
    h_tok = nc.dram_tensor("h_tok", [TOK, D], F32).ap()
    W1 = nc.dram_tensor("W1", [32, 128, 128, 64], BF16).ap()
    if stop_after is not None:
        dbg_out = nc.dram_tensor("dbg", [128, 8192], F32, kind="ExternalOutput").ap()

    with ExitStack() as es:
        kb = KB(nc, es)
        sb = lambda name, shape, dt=F32: es.enter_context(nc.sbuf_tensor(name, list(shape), dt))
        ps = es.enter_context(nc.psum_tensor("ps", [128, 8, 512], F32))
        PB = bufs(8, "psb")

        dbg_stg = sb("dbg_stg", [128, 1024]) if stop_after is not None else None
        cst = sb("cst", [128, 1024])
        b_cst = Buf("cst")
        kb.dma("sp", cst[:], consts, writes=[b_cst])
        C_ID = 0
        C_IOTA = 128
        C_TRI = 256
        C_FLAG = 384
        C_B31 = 392
        C_CORR = 400
        C_PSC = 464
        C_PW2 = 480
        ident = cst[:, C_ID:C_ID + 128]
        iota = cst[:, C_IOTA:C_IOTA + 128]
        tri = cst[:, C_TRI:C_TRI + 128]
        bT = sb("bT", [128, 2, 8, 128])
        b_bT = Buf("bT")
        kb.dma("sp", bT[:], biasT, writes=[b_bT])
        ones_b = sb("ones_b", [128, 128], BF16)
        b_ones = Buf("ones")
        kb.op("pool", lambda e: e.memset(ones_b[:], 1.0), writes=[b_ones])

        evac_rr = [0]

        def evac(out, in_, reads, writes, scale=None):
            evac_rr[0] ^= 1
            if scale is not None:
                return kb.op("act", lambda e: e.activation(out=out, in_=in_, func=AF.Copy, scale=scale),
                             reads=reads, writes=writes)
            if evac_rr[0]:
                return kb.op("act", lambda e: e.activation(out=out, in_=in_, func=AF.Copy), reads=reads, writes=writes)
            return kb.op("dve", lambda e: e.tensor_copy(out=out, in_=in_), reads=reads, writes=writes)

        pb_rr = [0]

        def next_bank(choices):
            pb_rr[0] += 1
            return choices[pb_rr[0] % len(choices)]

        def dump(ap_list):
            stg = dbg_stg
            b_stg = Buf("dbgstg")
            col = 0
            for ap, bl, n in ap_list:
                kb.op("dve", lambda e, ap=ap, n=n: e.tensor_copy(out=stg[:, 0:n], in_=ap),
                      reads=bl, writes=[b_stg])
                t = kb.dma("sp", dbg_out[:, col:col + n], stg[:, 0:n], reads=[b_stg])
                kb.wait("sp", t)
                col += n

        xg_t = [None, None]
        b_xg = bufs(2, "xg")
        xg_rr = [0]

        def load_xg(s, tg):
            i = xg_rr[0]
            xg_rr[0] ^= 1
            for q4 in range(4):
                kb.dma("pool", xg_t[i][:, 4 * q4:4 * q4 + 4, :], xT[s, :, 4 * q4:4 * q4 + 4, tg * 512:(tg + 1) * 512],
                       writes=[b_xg[i]])
            return xg_t[i], b_xg[i]

        def proj_fm(wt, wb, xg, bx, out_ap, out_bufs, scale=None):
            bk = next_bank([0, 1, 2, 7])
            for kc in range(16):
                kb.op("pe", lambda e, kc=kc: e.matmul(ps[:, bk, :], lhsT=wt(kc), rhs=xg[:, kc, :],
                                                      start=(kc == 0), stop=(kc == 15)),
                      reads=[wb, bx], writes=[PB[bk]], sig=(kc == 15))
            return evac(out_ap, ps[:, bk, :], [PB[bk]], out_bufs, scale=scale)

        phA = ExitStack()
        es.enter_context(phA)
        xg_t[0] = phA.enter_context(nc.sbuf_tensor("xgA0", [128, 16, 512], BF16))
        xg_t[1] = phA.enter_context(nc.sbuf_tensor("xgA1", [128, 16, 512], BF16))
        qiT = phA.enter_context(nc.sbuf_tensor("qiT", [128, 8, TOK], BF16))
        b_qiT = bufs(4, "qiT")
        kiT = phA.enter_context(nc.sbuf_tensor("kiT", [128, 2 * TOK], BF16))
        b_kiT = bufs(8, "kiT")
        widx = phA.enter_context(nc.sbuf_tensor("widx", [128, 16, 16], F32))
        b_widx = bufs(16, "widx")
        with ExitStack() as sc:
            wqi = sc.enter_context(nc.sbuf_tensor("wqi", [128, 8, 16, 128], BF16))
            b_wqi = Buf("wqi")
            wki = sc.enter_context(nc.sbuf_tensor("wki", [128, 16, 128], BF16))
            b_wki = Buf("wki")
            wwi = sc.enter_context(nc.sbuf_tensor("wwi", [128, 16, 16], BF16))
            b_wwi = Buf("wwi")
            kb.dma("pool", wki[:], w_fm[32], writes=[b_wki])
            for c in range(8):
                kb.dma("pool", wqi[:, c], w_fm[24 + c], writes=[b_wqi])
            kb.dma("pool", wwi[:], w_wi, writes=[b_wwi])
            for s in range(2):
                for tg in range(4):
                    xg, bx = load_xg(s, tg)
                    kg = s * 4 + tg
                    proj_fm(lambda kc: wki[:, kc, :], b_wki, xg, bx, kiT[:, kg * 512:(kg + 1) * 512], [b_kiT[kg]])
                    if stop_after == "A1":
                        dump([(kiT[:, 0:512], b_kiT[0:1], 512), (xg[:, 0, :], [bx], 512)])
                        return nc
                    if s == 1:
                        for c in range(8):
                            proj_fm(lambda kc, c=c: wqi[:, c, kc, :], b_wqi, xg, bx,
                                    qiT[:, c, tg * 512:(tg + 1) * 512], [b_qiT[tg]])
                        for tt in range(4):
                            bk = next_bank([0, 1, 2, 7])
                            for kc in range(16):
                                kb.op("pe", lambda e, kc=kc, tt=tt: e.matmul(
                                    ps[:, bk, 0:16], lhsT=xg[:, kc, tt * 128:(tt + 1) * 128], rhs=wwi[:, kc, :],
                                    start=(kc == 0), stop=(kc == 15)),
                                    reads=[b_wwi, bx], writes=[PB[bk]], sig=(kc == 15))
                            evac(widx[:, tg * 4 + tt, :], ps[:, bk, 0:16], [PB[bk]], [b_widx[tg * 4 + tt]])
        if stop_after == "A":
            dump([(kiT[:, 0:2048], b_kiT[0:4], 2048), (kiT[:, 2048:4096], b_kiT[4:8], 2048),
                  (qiT[:, 0, 0:2048], b_qiT, 2048), (widx[:, :, :].rearrange("p a b -> p (a b)"), b_widx, 256)])
            return nc

        print('sbuf remaining', nc.sbuf_bytes_remaining)
        kb.barrier()
        b_maskd = bufs(4, "maskd")
        with ExitStack() as sc:
            sbB = lambda name, shape, dt=F32: sc.enter_context(nc.sbuf_tensor(name, list(shape), dt))
            score = sbB("score", [128, 4096])
            b_score = bufs(8, "score")
            maskf = sbB("maskf", [128, 4096])
            b_maskf = Buf("maskf")
            junk = sbB("junk", [128, 4096], BF16)
            b_junk = Buf("junk")
            Rt = [sbB("R%d" % i, [128, 512]) for i in range(3)]
            b_R = bufs(3, "R")
            acc = sbB("accB", [128, 512])
            b_acc = Buf("acc")
            mxall = sbB("mxall", [128, 16])
            b_mx = Buf("mx")
            small = sbB("smallB", [128, 64])
            b_small = Buf("small")
            mid = small[:, 0:1]
            cntc = small[:, 1:2]
            tmpc = small[:, 2:3]
            Mp = small[:, 3:4]
            tau = small[:, 4:5]
            Dt = small[:, 8:8 + NIT + 1]
            mT = sbB("maskTg", [128, 32, 512], BF16)
            b_mT = Buf("maskTg")
            r_rr = 0
            for g in range(4):
                kb.op("pool", lambda e: e.memset(mT[:], 0.0), writes=[b_mT])
                for j in range(4):
                    i = 4 * g + j
                    NK = (17 + i) * 128
                    nkt = (NK + 511) // 512
                    for kt in range(nkt):
                        wk = min(512, NK - kt * 512)
                        direct = kt >= 4
                        for hn, h in enumerate([0, 2, 4, 6, 8, 10, 12, 14, 1, 3, 5, 7, 9, 11, 13, 15]):
                            cp, r0 = h // 2, 64 * (h % 2)
                            bk = next_bank([0, 1, 2])
                            kb.op("pe", lambda e, cp=cp, r0=r0, bk=bk, kt=kt, wk=wk, i=i: e.matmul(
                                ps[:, bk, 0:wk], lhsT=qiT[r0:r0 + 64, cp, i * 128:(i + 1) * 128],
                                rhs=kiT[r0:r0 + 64, kt * 512:kt * 512 + wk], start=True, stop=True),
                                reads=[b_qiT[g], b_kiT[kt]], writes=[PB[bk]])
                            ri = r_rr % 3
                            r_rr += 1
                            kb.op("act", lambda e, ri=ri, bk=bk, wk=wk: e.activation(
                                out=Rt[ri][:, 0:wk], in_=ps[:, bk, 0:wk], func=AF.Relu),
                                reads=[PB[bk]], writes=[b_R[ri]])
                            wcol = widx[:, i, h:h + 1]
                            if hn == 0:
                                kb.op("dve", lambda e, ri=ri, wk=wk, wcol=wcol: e.tensor_scalar(
                                    out=acc[:, 0:wk], in0=Rt[ri][:, 0:wk], scalar1=wcol, scalar2=None, op0=ALU.mult),
                                    reads=[b_R[ri], b_widx[i]], writes=[b_acc])
                            elif hn == 15 and direct:
                                kb.op("dve", lambda e, ri=ri, wk=wk, wcol=wcol, kt=kt: e.scalar_tensor_tensor(
                                    out=score[:, kt * 512:kt * 512 + wk], in0=Rt[ri][:, 0:wk], scalar=wcol,
                                    in1=acc[:, 0:wk], op0=ALU.mult, op1=ALU.add),
                                    reads=[b_R[ri], b_widx[i], b_acc], writes=[b_score[kt]])
                            else:
                                kb.op("dve", lambda e, ri=ri, wk=wk, wcol=wcol: e.scalar_tensor_tensor(
                                    out=acc[:, 0:wk], in0=Rt[ri][:, 0:wk], scalar=wcol,
                                    in1=acc[:, 0:wk], op0=ALU.mult, op1=ALU.add),
                                    reads=[b_R[ri], b_widx[i]], writes=[b_acc])
                        if not direct:
                            kb.op("dve", lambda e, wk=wk, kt=kt: e.tensor_reduce(
                                out=mxall[:, kt:kt + 1], in_=acc[:, 0:wk], axis=AX.X, op=ALU.max),
                                reads=[b_acc], writes=[b_mx])
                            kb.op("dve", lambda e, wk=wk, kt=kt: e.tensor_reduce(
                                out=mxall[:, 8 + kt:9 + kt], in_=acc[:, 0:wk], axis=AX.X, op=ALU.min),
                                reads=[b_acc], writes=[b_mx])
                            kb.op("dve", lambda e, wk=wk, kt=kt: e.tensor_scalar(
                                out=score[:, kt * 512:kt * 512 + wk], in0=acc[:, 0:wk],
                                scalar1=cst[:, C_FLAG:C_FLAG + 1], scalar2=cst[:, C_FLAG + 1:C_FLAG + 2],
                                op0=ALU.mult, op1=ALU.add),
                                reads=[b_acc, b_cst], writes=[b_score[kt]])
                        else:
                            kb.op("dve", lambda e, wk=wk, kt=kt: e.tensor_reduce(
                                out=mxall[:, kt:kt + 1], in_=score[:, kt * 512:kt * 512 + wk], axis=AX.X, op=ALU.max),
                                reads=[b_score[kt]], writes=[b_mx])
                            kb.op("dve", lambda e, wk=wk, kt=kt: e.tensor_reduce(
                                out=mxall[:, 8 + kt:9 + kt], in_=score[:, kt * 512:kt * 512 + wk], axis=AX.X, op=ALU.min),
                                reads=[b_score[kt]], writes=[b_mx])
                    kd = (NK - 128) // 512
                    kb.op("dve", lambda e, NK=NK: e.tensor_tensor(
                        out=score[:, NK - 128:NK], in0=score[:, NK - 128:NK], in1=tri, op=ALU.add),
                        reads=[b_cst], writes=[b_score[kd]])
                    kb.op("dve", lambda e, nkt=nkt: e.tensor_reduce(out=Mp, in_=mxall[:, 0:nkt], axis=AX.X, op=ALU.max),
                          reads=[b_mx], writes=[b_small])
                    kb.op("dve", lambda e, nkt=nkt: e.tensor_reduce(out=tmpc, in_=mxall[:, 8:8 + nkt], axis=AX.X, op=ALU.min),
                          reads=[b_mx], writes=[b_small])
                    kb.op("dve", lambda e: e.scalar_tensor_tensor(out=Mp, in0=tmpc, scalar=-1.0, in1=Mp, op0=ALU.mult,
                                                                  op1=ALU.max), writes=[b_small])
                    kb.op("dve", lambda e: e.tensor_scalar(out=Mp, in0=Mp, scalar1=1.001, scalar2=1e-20,
                                                           op0=ALU.mult, op1=ALU.add), writes=[b_small])
                    kb.op("dve", lambda e: e.tensor_scalar(out=Dt, in0=cst[:, C_PW2:C_PW2 + NIT + 1], scalar1=Mp,
                                                           scalar2=None, op0=ALU.mult), reads=[b_cst], writes=[b_small])
                    kb.op("dve", lambda e: e.memset(mid, 0.0), writes=[b_small])
                    sc_all = b_score[0:nkt]
                    for k in range(NIT):
                        kb.op("dve", lambda e, NK=NK: e.tensor_scalar(
                            out=junk[:, 0:NK], in0=score[:, 0:NK], scalar1=mid, scalar2=None,
                            op0=ALU.is_ge, op1=ALU.add, accum_out=cntc),
                            reads=sc_all, writes=[b_junk, b_small])
                        kb.op("dve", lambda e: e.tensor_scalar(out=tmpc, in0=cntc, scalar1=TOPK - 0.5, scalar2=0.5,
                                                               op0=ALU.is_ge, op1=ALU.subtract), writes=[b_small])
                        kb.op("dve", lambda e, k=k: e.scalar_tensor_tensor(
                            out=mid, in0=tmpc, scalar=Dt[:, k:k + 1], in1=mid, op0=ALU.mult, op1=ALU.add),
                            writes=[b_small])
                    kb.op("dve", lambda e: e.tensor_tensor(out=tau, in0=mid, in1=Dt[:, NIT:NIT + 1], op=ALU.subtract),
                          writes=[b_small])
                    kb.op("dve", lambda e, NK=NK: e.tensor_scalar(
                        out=maskf[:, 0:NK], in0=score[:, 0:NK], scalar1=tau, scalar2=None, op0=ALU.is_ge),
                        reads=sc_all + [b_small], writes=[b_maskf])
                    nch = 17 + i
                    for c0 in range(0, nch, 4):
                        n4 = min(4, nch - c0)
                        bk = next_bank([3, 4])
                        for cc in range(n4):
                            c = c0 + cc
                            kb.op("pe", lambda e, c=c, cc=cc, bk=bk: e.transpose(
                                ps[:, bk, cc * 128:(cc + 1) * 128], maskf[:, c * 128:(c + 1) * 128], ident),
                                reads=[b_maskf, b_cst], writes=[PB[bk]], sig=(cc == n4 - 1))
                        kb.op("act", lambda e, c0=c0, n4=n4, bk=bk, j=j: e.activation(
                            out=mT[:, c0:c0 + n4, j * 128:(j + 1) * 128],
                            in_=ps[:, bk, 0:n4 * 128].rearrange("p (c q) -> p c q", c=n4), func=AF.Copy),
                            reads=[PB[bk]], writes=[b_mT])
                    if stop_after == "B" and i == 0:
                        dump([(score[:, 0:1024], sc_all, 1024), (score[:, 1024:2048], sc_all, 1024),
                              (score[:, 2048:2176], sc_all, 128),
                              (maskf[:, 0:1024], [b_maskf], 1024), (maskf[:, 1024:2048], [b_maskf], 1024),
                              (maskf[:, 2048:2176], [b_maskf], 128), (small[:, 0:64], [b_small], 64),
                              (kiT[:, 1024:1536], [b_kiT[2]], 512), (kiT[:, 512:1024], [b_kiT[1]], 512)])
                        return nc
                kb.dma("sp", maskT_d[g], mT[:], reads=[b_mT], writes=[b_maskd[g]])
        phA.close()
        pers = ExitStack()
        es.enter_context(pers)
        psb = lambda name, shape, dt=F32: pers.enter_context(nc.sbuf_tensor(name, list(shape), dt))
        poolT = psb("poolT", [128, 8, TOK], BF16)
        b_poolT = bufs(4, "poolT")
        attnT = psb("attnT", [128, 8, TOK], BF16)
        b_attnT = [bufs(4, "attnT%d" % h) for h in range(8)]
        scCD = ExitStack()
        es.enter_context(scCD)
        xg_t[0] = scCD.enter_context(nc.sbuf_tensor("xgC0", [128, 16, 512], BF16))
        xg_t[1] = xg_t[0]
        b_xg[0] = Buf("xgc0")
        b_xg[1] = b_xg[0]

        print('sbuf remaining', nc.sbuf_bytes_remaining)
        kb.barrier()
        with ExitStack() as sc:
            sbC = lambda name, shape, dt=F32: sc.enter_context(nc.sbuf_tensor(name, list(shape), dt))
            wpl = sbC("wpl", [128, 8, 16, 128], BF16)
            b_wpl = Buf("wpl")
            for c in range(8):
                kb.dma("pool", wpl[:, c], w_fm[c], writes=[b_wpl])
            pw = sbC("pw", [128, 4, 2, 256], BF16)
            b_pw = Buf("pw")
            kb.dma("pool", pw[:], pool_w, writes=[b_pw])
            xh = sbC("xh", [128, 16, 16], BF16)
            b_xh = Buf("xh")
            for q4 in range(4):
                kb.dma("pool", xh[:, 4 * q4:4 * q4 + 4, :], xT[0, :, 4 * q4:4 * q4 + 4, TOK - 16:TOK], writes=[b_xh])
            hal = sbC("hal", [128, 8, 16])
            b_hal = bufs(8, "hal")
            vb = [sbC("vb%d" % i, [128, 528]) for i in range(2)]
            b_vb = bufs(2, "vb")
            sa = sbC("sa", [128, 528])
            sbb = sbC("sbb", [128, 528])
            b_sa, b_sb = Buf("sa"), Buf("sb")
            t16 = sbC("t16", [128, 16])
            b_t16 = Buf("t16")
            plb = [sbC("plb%d" % i, [128, 512], BF16) for i in range(2)]
            b_plb = bufs(2, "plb")
            for cp in range(8):
                bk = next_bank([0, 1, 2, 7])
                for kc in range(16):
                    kb.op("pe", lambda e, kc=kc, cp=cp, bk=bk: e.matmul(
                        ps[:, bk, 0:16], lhsT=wpl[:, cp, kc, :], rhs=xh[:, kc, :], start=(kc == 0), stop=(kc == 15)),
                        reads=[b_wpl, b_xh], writes=[PB[bk]], sig=(kc == 15))
                evac(hal[:, cp, :], ps[:, bk, 0:16], [PB[bk]], [b_hal[cp]])
            vrr = 0
            for tg in range(4):
                xg, bx = load_xg(1, tg)
                for gq in range(4):
                    wwin = (2, 4, 8, 16)[gq]
                    for cc in range(2):
                        cp = 2 * gq + cc
                        vi = vrr % 2
                        vrr += 1
                        V = vb[vi]
                        bV = b_vb[vi]
                        kb.op("dve", lambda e, V=V, cp=cp: e.tensor_copy(out=V[:, 0:16], in_=hal[:, cp, :]),
                              reads=[b_hal[cp]], writes=[bV])
                        proj_fm(lambda kc, cp=cp: wpl[:, cp, kc, :], b_wpl, xg, bx, V[:, 16:528], [bV])
                        kb.op("act", lambda e, V=V, cp=cp: e.activation(out=hal[:, cp, :], in_=V[:, 512:528], func=AF.Copy),
                              reads=[bV], writes=[b_hal[cp]])
                        kb.op("dve", lambda e, V=V: e.tensor_tensor(out=sa[:, 1:528], in0=V[:, 1:528], in1=V[:, 0:527],
                                                                     op=ALU.add), reads=[bV], writes=[b_sa])
                        Sfin, bS = sa, b_sa
                        if gq >= 1:
                            kb.op("dve", lambda e: e.tensor_tensor(out=sbb[:, 3:528], in0=sa[:, 3:528], in1=sa[:, 1:526],
                                                                   op=ALU.add), reads=[b_sa], writes=[b_sb])
                            Sfin, bS = sbb, b_sb
                        if gq >= 2:
                            kb.op("dve", lambda e: e.tensor_tensor(out=sa[:, 7:528], in0=sbb[:, 7:528], in1=sbb[:, 3:524],
                                                                   op=ALU.add), reads=[b_sb], writes=[b_sa])
                            Sfin, bS = sa, b_sa
                        if gq >= 3:
                            kb.op("dve", lambda e: e.tensor_tensor(out=sbb[:, 15:528], in0=sa[:, 15:528], in1=sa[:, 7:520],
                                                                   op=ALU.add), reads=[b_sa], writes=[b_sb])
                            Sfin, bS = sbb, b_sb
                        kb.op("dve", lambda e, Sfin=Sfin, V=V, cc=cc, wwin=wwin: e.scalar_tensor_tensor(
                            out=plb[cc][:, :], in0=Sfin[:, 16:528], scalar=1.0 / wwin, in1=V[:, 16:528],
                            op0=ALU.mult, op1=ALU.subtract), reads=[bS, bV], writes=[b_plb[cc]])
                        if tg == 0:
                            kb.op("dve", lambda e, Sfin=Sfin, gq=gq: e.tensor_tensor(
                                out=t16[:, :], in0=Sfin[:, 16:32], in1=cst[:, C_CORR + 16 * gq:C_CORR + 16 * gq + 16],
                                op=ALU.mult), reads=[bS, b_cst], writes=[b_t16])
                            kb.op("dve", lambda e, V=V, cc=cc, wwin=wwin: e.scalar_tensor_tensor(
                                out=plb[cc][:, 0:16], in0=t16[:, :], scalar=1.0 / wwin, in1=V[:, 16:32],
                                op0=ALU.mult, op1=ALU.subtract), reads=[b_t16, bV], writes=[b_plb[cc]])
                    for dc in range(2):
                        bk = next_bank([0, 1, 2, 7])
                        for cc in range(2):
                            kb.op("pe", lambda e, cc=cc, dc=dc, gq=gq, bk=bk: e.matmul(
                                ps[:, bk, :], lhsT=pw[:, gq, cc, dc * 128:(dc + 1) * 128], rhs=plb[cc][:, :],
                                start=(cc == 0), stop=(cc == 1)),
                                reads=[b_pw, b_plb[cc]], writes=[PB[bk]], sig=(cc == 1))
                        oc = 2 * gq + dc
                        kb.op("act", lambda e, oc=oc, bk=bk, tg=tg: e.activation(
                            out=poolT[:, oc, tg * 512:(tg + 1) * 512], in_=ps[:, bk, :], func=AF.Copy,
                            scale=cst[:, C_PSC + oc:C_PSC + oc + 1]),
                            reads=[PB[bk], b_cst], writes=[b_poolT[tg]])
        if stop_after == "C":
            dump([(poolT[:, 0, 0:1024], b_poolT, 1024), (poolT[:, 7, 0:1024], b_poolT, 1024),
                  (poolT[:, 3, 1024:2048], b_poolT, 1024)])
            return nc

        print('sbuf remaining', nc.sbuf_bytes_remaining)
        kb.barrier()
        with ExitStack() as sc:
            sbD = lambda name, shape, dt=F32: sc.enter_context(nc.sbuf_tensor(name, list(shape), dt))
            wq2 = sbD("wq2", [128, 2, 16, 128], BF16)
            wk2 = sbD("wk2", [128, 2, 16, 128], BF16)
            wv2 = sbD("wv2", [128, 16, 256], BF16)
            b_wq2, b_wk2, b_wv2 = Buf("wq2"), Buf("wk2"), Buf("wv2")
            kT2 = sbD("kT2", [128, 2, 2 * TOK], BF16)
            b_kT2 = bufs(8, "kT2")
            v2 = sbD("v2", [128, 32, 256], BF16)
            b_v2 = bufs(8, "v2")
            qT2 = sbD("qT2", [128, 2, TOK], BF16)
            b_qT2 = bufs(4, "qT2")
            mk = sbD("mk", [128, 32, 512], BF16)
            b_mk = Buf("mk")
            Et = [sbD("E%d" % i, [128, 512], BF16) for i in range(3)]
            b_E = bufs(3, "E")
            Pt = [sbD("P%d" % i, [128, 512], BF16) for i in range(3)]
            b_P = bufs(3, "P")
            tmpn = [sbD("tmpn%d" % i, [128, 128]) for i in range(2)]
            b_tmpn = bufs(2, "tmpn")
            rden = sbD("rden", [128, 512])
            b_rden = Buf("rden")
            SB_S = [0, 1, 2]
            for hg in range(4):
                for hl in range(2):
                    kb.dma("pool", wq2[:, hl], w_fm[8 + 2 * hg + hl], writes=[b_wq2])
                    kb.dma("pool", wk2[:, hl], w_fm[16 + 2 * hg + hl], writes=[b_wk2])
                kb.dma("pool", wv2[:], w_v[hg], writes=[b_wv2])
                for s in range(2):
                    for tg in range(4):
                        xg, bx = load_xg(s, tg)
                        kg = s * 4 + tg
                        for hl in range(2):
                            proj_fm(lambda kc, hl=hl: wk2[:, hl, kc, :], b_wk2, xg, bx,
                                    kT2[:, hl, kg * 512:(kg + 1) * 512], [b_kT2[kg]])
                            if s == 1:
                                proj_fm(lambda kc, hl=hl: wq2[:, hl, kc, :], b_wq2, xg, bx,
                                        qT2[:, hl, tg * 512:(tg + 1) * 512], [b_qT2[tg]])
                        for tt in range(4):
                            bk = 7
                            for kc in range(16):
                                kb.op("pe", lambda e, kc=kc, tt=tt, xg=xg: e.matmul(
                                    ps[:, bk, 0:256], lhsT=xg[:, kc, tt * 128:(tt + 1) * 128], rhs=wv2[:, kc, :],
                                    start=(kc == 0), stop=(kc == 15)),
                                    reads=[b_wv2, bx], writes=[PB[bk]], sig=(kc == 15))
                            evac(v2[:, kg * 4 + tt, :], ps[:, bk, 0:256], [PB[bk]], [b_v2[kg]])
                if stop_after == "D0":
                    dump([(kT2[:, 0, 0:1024], b_kT2[0:2], 1024), (qT2[:, 1, 0:1024], b_qT2[0:2], 1024),
                          (v2[:, 0:4, :].rearrange("p a b -> p (a b)"), b_v2[0:1], 1024)])
                    return nc
                for g in range(4):
                    kb.dma("sp", mk[:], maskT_d[g], reads=[b_maskd[g]], writes=[b_mk])
                    nch = 20 + 4 * g
                    for hl in range(2):
                        h = 2 * hg + hl
                        bO = 3 + 2 * (hl % 2)
                        bD = bO + 1

                        def issue_S(c, hl=hl, g=g):
                            bk = SB_S[c % 3]
                            kb.op("pe", lambda e: e.matmul(
                                ps[:, bk, :], lhsT=kT2[:, hl, c * 128:(c + 1) * 128],
                                rhs=qT2[:, hl, g * 512:(g + 1) * 512], start=True, stop=True),
                                reads=[b_kT2[c // 4], b_qT2[g]], writes=[PB[bk]])
                        issue_S(0)
                        if nch > 1:
                            issue_S(1)
                        for c in range(nch):
                            if c + 2 < nch:
                                issue_S(c + 2)
                            bk = SB_S[c % 3]
                            ei = c % 3
                            tokE = kb.op("act", lambda e, bk=bk, ei=ei, h=h: e.activation(
                                out=Et[ei][:, :], in_=ps[:, bk, :], func=AF.Exp, scale=ATT_SCALE,
                                bias=cst[:, C_B31 + h:C_B31 + h + 1]),
                                reads=[PB[bk], b_cst], writes=[b_E[ei]])
                            cn = c - (15 + 4 * g)
                            if 0 <= cn <= 4:
                                for j in range(4):
                                    dl = 1 + j - cn
                                    if dl in (0, 1):
                                        ti = (j + cn) % 2
                                        kb.op("dve", lambda e, bk=bk, j=j, dl=dl, h=h, ti=ti: e.scalar_tensor_tensor(
                                            out=tmpn[ti][:, :], in0=ps[:, bk, j * 128:(j + 1) * 128], scalar=ATT_SCALE,
                                            in1=bT[:, dl, h, :], op0=ALU.mult, op1=ALU.add),
                                            reads=[PB[bk], b_bT], writes=[b_tmpn[ti]], deps=[tokE])
                                        kb.op("act", lambda e, ei=ei, j=j, ti=ti: e.activation(
                                            out=Et[ei][:, j * 128:(j + 1) * 128], in_=tmpn[ti][:, :], func=AF.Exp),
                                            reads=[b_tmpn[ti]], writes=[b_E[ei]])
                            kb.op("dve", lambda e, ei=ei, c=c: e.tensor_tensor(
                                out=Pt[ei][:, :], in0=Et[ei][:, :], in1=mk[:, c, :], op=ALU.mult),
                                reads=[b_E[ei], b_mk], writes=[b_P[ei]])
                            kb.op("pe", lambda e, ei=ei, c=c, hl=hl, bO=bO, nch=nch: e.matmul(
                                ps[:, bO, :], lhsT=v2[:, c, hl * 128:(hl + 1) * 128], rhs=Pt[ei][:, :],
                                start=(c == 0), stop=(c == nch - 1)),
                                reads=[b_v2[c // 4], b_P[ei]], writes=[PB[bO]], sig=(c == nch - 1))
                            kb.op("pe", lambda e, ei=ei, c=c, bD=bD, nch=nch: e.matmul(
                                ps[:, bD, :], lhsT=ones_b[:, :], rhs=Pt[ei][:, :],
                                start=(c == 0), stop=(c == nch - 1)),
                                reads=[b_ones, b_P[ei]], writes=[PB[bD]], sig=(c == nch - 1))
                        kb.op("dve", lambda e, bD=bD: e.reciprocal(out=rden[:, :], in_=ps[:, bD, :]),
                              reads=[PB[bD]], writes=[b_rden])
                        kb.op("dve", lambda e, bO=bO, h=h, g=g: e.tensor_tensor(
                            out=attnT[:, h, g * 512:(g + 1) * 512], in0=ps[:, bO, :], in1=rden[:, :], op=ALU.mult),
                            reads=[PB[bO], b_rden], writes=[b_attnT[h][g]])
                        if stop_after == "D1":
                            dump([(attnT[:, 0, 0:512], [b_attnT[0][0]], 512), (rden[:, :], [b_rden], 512)])
                            return nc
        if stop_after == "D":
            dump([(attnT[:, 0, 0:1024], b_attnT[0], 1024), (attnT[:, 7, 0:1024], b_attnT[7], 1024),
                  (attnT[:, 3, 1024:2048], b_attnT[3], 1024)])
            return nc

        scCD.close()
        print('sbuf remaining', nc.sbuf_bytes_remaining)
        kb.barrier()
        b_hT_d = bufs(4, "hT_d")
        b_htok = bufs(16, "htok")

        def layer_norm(e_sb, src, bsrc, gam, bet, b_gb, outt, bout, xc, b_xc, junkf, b_jf, st, b_st):
            kb.op("dve", lambda e: e.tensor_scalar(out=junkf[:, :], in0=src, scalar1=1.0, scalar2=None, op0=ALU.mult,
                                                   op1=ALU.add, accum_out=st[:, 0:1]), reads=[bsrc], writes=[b_jf, b_st])
            kb.op("dve", lambda e: e.tensor_scalar(out=st[:, 1:2], in0=st[:, 0:1], scalar1=1.0 / D, scalar2=None,
                                                   op0=ALU.mult), writes=[b_st])
            kb.op("dve", lambda e: e.tensor_scalar(out=xc[:, :], in0=src, scalar1=st[:, 1:2], scalar2=None,
                                                   op0=ALU.subtract), reads=[bsrc, b_st], writes=[b_xc])
            kb.op("dve", lambda e: e.tensor_tensor(out=junkf[:, :], in0=xc[:, :], in1=xc[:, :], op=ALU.mult),
                  reads=[b_xc], writes=[b_jf])
            kb.op("dve", lambda e: e.tensor_scalar(out=junkf[:, :], in0=junkf[:, :], scalar1=1.0, scalar2=None,
                                                   op0=ALU.mult, op1=ALU.add, accum_out=st[:, 2:3]),
                  writes=[b_jf, b_st])
            kb.op("dve", lambda e: e.tensor_scalar(out=st[:, 3:4], in0=st[:, 2:3], scalar1=1.0 / D, scalar2=LN_EPS,
                                                   op0=ALU.mult, op1=ALU.add), writes=[b_st])
            kb.op("act", lambda e: e.activation(out=st[:, 4:5], in_=st[:, 3:4], func=AF.Sqrt), writes=[b_st])
            kb.op("dve", lambda e: e.reciprocal(out=st[:, 5:6], in_=st[:, 4:5]), writes=[b_st])
            kb.op("dve", lambda e: e.scalar_tensor_tensor(out=xc[:, :], in0=xc[:, :], scalar=st[:, 5:6], in1=gam,
                                                          op0=ALU.mult, op1=ALU.mult),
                  reads=[b_st, b_gb], writes=[b_xc])
            return kb.op("dve", lambda e: e.tensor_tensor(out=outt, in0=xc[:, :], in1=bet, op=ALU.add),
                         reads=[b_xc, b_gb], writes=[bout])

        with ExitStack() as sc:
            sbE = lambda name, shape, dt=F32: sc.enter_context(nc.sbuf_tensor(name, list(shape), dt))
            wo2 = [sbE("wo%d" % i, [128, 16, 512], BF16) for i in range(2)]
            b_wo2 = bufs(2, "wo")
            wo_rr = [0]
            gb = sbE("gb1", [128, 2, D])
            b_gb = Buf("gb1")
            kb.dma("sp", gb[:, 0, :], lnp[0], writes=[b_gb])
            kb.dma("sp", gb[:, 1, :], lnp[1], writes=[b_gb])
            xt = [sbE("xt0", [128, D])] * 2
            b_xt = [Buf("xt")] * 2
            hpre = sbE("hpre", [128, D])
            b_hpre = Buf("hpre")
            xc = sbE("xc", [128, D])
            b_xc = Buf("xc")
            junkf = sbE("junkf", [128, D])
            b_jf = Buf("junkf")
            hh = [sbE("hh0", [128, D])] * 2
            b_hh = [Buf("hh")] * 2
            st = sbE("st", [128, 8])
            b_st = Buf("st")
            hTs = sbE("hTs", [128, 16, 128], BF16)
            b_hTs = Buf("hTs")
            for tt in range(16):
                tg = tt // 4
                xi = tt % 2
                kb.dma("sp", xt[xi][:, :], x_tok[tt * 128:(tt + 1) * 128, :], writes=[b_xt[xi]])
                for dt_ in range(4):
                    bk = next_bank([0, 1, 2, 7])
                    wi_ = wo_rr[0] % 2
                    wo_rr[0] += 1
                    wo, b_wo = wo2[wi_], b_wo2[wi_]
                    for q4 in range(4):
                        kb.dma("pool", wo[:, 4 * q4:4 * q4 + 4, :], w_out[:, 4 * q4:4 * q4 + 4, dt_ * 512:(dt_ + 1) * 512],
                               writes=[b_wo])
                    for kc in range(16):
                        if kc < 8:
                            lh = poolT[:, kc, tt * 128:(tt + 1) * 128]
                            rb = b_poolT[tg]
                        else:
                            lh = attnT[:, kc - 8, tt * 128:(tt + 1) * 128]
                            rb = b_attnT[kc - 8][tg]
                        kb.op("pe", lambda e, lh=lh, kc=kc, dt_=dt_, bk=bk, wo=wo: e.matmul(
                            ps[:, bk, :], lhsT=lh, rhs=wo[:, kc, :],
                            start=(kc == 0), stop=(kc == 15)),
                            reads=[rb, b_wo], writes=[PB[bk]], sig=(kc == 15))
                    kb.op("dve", lambda e, xi=xi, dt_=dt_, bk=bk: e.scalar_tensor_tensor(
                        out=hpre[:, dt_ * 512:(dt_ + 1) * 512], in0=xt[xi][:, dt_ * 512:(dt_ + 1) * 512], scalar=ALPHA,
                        in1=ps[:, bk, :], op0=ALU.mult, op1=ALU.add),
                        reads=[b_xt[xi], PB[bk]], writes=[b_hpre])
                hi = tt % 2
                layer_norm(None, hpre[:, :], b_hpre, gb[:, 0, :], gb[:, 1, :], b_gb, hh[hi][:, :], b_hh[hi],
                           xc, b_xc, junkf, b_jf, st, b_st)
                kb.dma("sp", h_tok[tt * 128:(tt + 1) * 128, :], hh[hi][:, :], reads=[b_hh[hi]], writes=[b_htok[tt]])
                for c0 in range(0, 16, 4):
                    bk = next_bank([3, 4, 5, 6])
                    for cc in range(4):
                        kb.op("pe", lambda e, c0=c0, cc=cc, bk=bk, hi=hi: e.transpose(
                            ps[:, bk, cc * 128:(cc + 1) * 128], hh[hi][:, (c0 + cc) * 128:(c0 + cc + 1) * 128], ident),
                            reads=[b_hh[hi], b_cst], writes=[PB[bk]], sig=(cc == 3))
                    evac(hTs[:, c0:c0 + 4, :],
                         ps[:, bk, :].rearrange("p (c q) -> p c q", c=4), [PB[bk]], [b_hTs])
                for q4 in range(4):
                    kb.dma("sp", hT_d[:, 4 * q4:4 * q4 + 4, tt * 128:(tt + 1) * 128], hTs[:, 4 * q4:4 * q4 + 4, :],
                           reads=[b_hTs], writes=[b_hT_d[tg]])
        pers.close()
        if stop_after == "E":
            t1 = kb.dma("sp", dbg_out[:, 0:2048], h_tok[0:128, :], reads=[b_htok[0]])
            t2 = kb.dma("sp", dbg_out[:, 2048:4096], h_tok[1920:2048, :], reads=[b_htok[15]])
            kb.wait("sp", t1)
            kb.wait("sp", t2)
            return nc

        print('sbuf remaining', nc.sbuf_bytes_remaining)
        kb.barrier()
        b_W1 = bufs(32, "W1")
        with ExitStack() as sc:
            sbF = lambda name, shape, dt=F32: sc.enter_context(nc.sbuf_tensor(name, list(shape), dt))
            wqc = [sbF("wqc%d" % i, [128, 16, 128], BF16) for i in range(3)]
            b_wqc = bufs(3, "wqc")
            wq_rr = [0]
            skT = sbF("skT", [128, 2, 128], BF16)
            b_skT = Buf("skT")
            kb.dma("pool", skT[:], subkT, writes=[b_skT])
            hTg = [sbF("hTg0", [128, 16, 512], BF16)] * 2
            b_hTg = [Buf("hTg")] * 2
            qpT = sbF("qpT", [128, 16, 512], BF16)
            b_qpT = Buf("qpT")
            s_sb = sbF("s_sb", [128, 16, 128])
            s2_sb = sbF("s2_sb", [128, 16, 128])
            b_s, b_s2 = Buf("s"), Buf("s2")
            v16 = sbF("v16", [128, 16, 16])
            i16 = sbF("i16", [128, 16, 16], U32)
            i16f = sbF("i16f", [128, 16, 16])
            b_v16, b_i16, b_i16f = Buf("v16"), Buf("i16"), Buf("i16f")
            cand = sbF("cand", [128, 8, 256])
            cand2 = sbF("cand2", [128, 8, 256])
            b_cand, b_cand2 = Buf("cand"), Buf("cand2")
            best = sbF("best", [128, 8, 16])
            bidx = sbF("bidx", [128, 8, 16], U32)
            aiu = sbF("aiu", [128, 128], U32)
            biu = sbF("biu", [128, 128], U32)
            af = sbF("af", [128, 128])
            bf = sbF("bf", [128, 128])
            b_best, b_bidx, b_bidf, b_af, b_bf = Buf("best"), Buf("bidx"), Buf("bidf"), Buf("af"), Buf("bf")
            eb = sbF("eb", [128, 8, 16])
            zz = sbF("zz", [128, 16])
            b_eb, b_zz = Buf("eb"), Buf("zz")
            oh = sbF("oh", [128, 128, 16])
            b_oh = Buf("oh")
            IG = sbF("IG", [128, 3, 128])
            b_IG = Buf("IG")
            IGT = sbF("IGT", [128, 3, 128])
            b_IGT = Buf("IGT")
            eq = sbF("eq", [128, 32, 128], BF16)
            b_eq = Buf("eq")
            At = sbF("At", [128, 32, 128], BF16)
            Bt = sbF("Bt", [128, 32, 128], BF16)
            b_At, b_Bt = Buf("At"), Buf("Bt")
            stg = [sbF("stg0", [128, 128, 64], BF16)] * 2
            b_stg = [Buf("stg")] * 2
            for tg in range(4):
                hx = hTg[tg % 2]
                bhx = b_hTg[tg % 2]
                for q4 in range(4):
                    kb.dma("sp", hx[:, 4 * q4:4 * q4 + 4, :], hT_d[:, 4 * q4:4 * q4 + 4, tg * 512:(tg + 1) * 512],
                           reads=[b_hT_d[tg]], writes=[bhx])
                for n in range(16):
                    wi_ = wq_rr[0] % 3
                    wq_rr[0] += 1
                    for q4 in range(4):
                        kb.dma("pool", wqc[wi_][:, 4 * q4:4 * q4 + 4, :], wq[:, 4 * q4:4 * q4 + 4, n * 128:(n + 1) * 128],
                               writes=[b_wqc[wi_]])
                    proj_fm(lambda kc, wi_=wi_: wqc[wi_][:, kc, :], b_wqc[wi_], hx, bhx, qpT[:, n, :], [b_qpT])
                for tt in range(4):
                    T = 4 * tg + tt
                    for n4 in range(4):
                        bk = next_bank([3, 4, 5, 6])
                        for nn in range(4):
                            n = 4 * n4 + nn
                            kb.op("pe", lambda e, n=n, nn=nn, bk=bk, tt=tt: e.matmul(
                                ps[:, bk, nn * 128:(nn + 1) * 128], lhsT=qpT[:, n, tt * 128:(tt + 1) * 128],
                                rhs=skT[:, n % 2, :], start=True, stop=True),
                                reads=[b_qpT, b_skT], writes=[PB[bk]], sig=(nn == 3))
                        evac(s_sb[:, 4 * n4:4 * n4 + 4, :], ps[:, bk, :].rearrange("p (c q) -> p c q", c=4),
                             [PB[bk]], [b_s])
                    for n in range(16):
                        kb.op("dve", lambda e, n=n: e.max(out=v16[:, n, 0:8], in_=s_sb[:, n, :]),
                              reads=[b_s], writes=[b_v16])
                        kb.op("dve", lambda e, n=n: e.max_index(out=i16[:, n, 0:8], in_max=v16[:, n, 0:8],
                                                               in_values=s_sb[:, n, :]),
                              reads=[b_s, b_v16], writes=[b_i16])
                        kb.op("dve", lambda e, n=n: e.match_replace(out=s2_sb[:, n, :], in_to_replace=v16[:, n, 0:8],
                                                                   in_values=s_sb[:, n, :], imm_value=NEG),
                              reads=[b_s, b_v16], writes=[b_s2])
                        kb.op("dve", lambda e, n=n: e.max(out=v16[:, n, 8:16], in_=s2_sb[:, n, :]),
                              reads=[b_s2], writes=[b_v16])
                        kb.op("dve", lambda e, n=n: e.max_index(out=i16[:, n, 8:16], in_max=v16[:, n, 8:16],
                                                               in_values=s2_sb[:, n, :]),
                              reads=[b_s2, b_v16], writes=[b_i16])
                    kb.op("dve", lambda e: e.tensor_copy(out=i16f[:], in_=i16[:]), reads=[b_i16], writes=[b_i16f])
                    kb.op("dve", lambda e: e.tensor_tensor(
                        out=cand[:].rearrange("p h (a b) -> p h a b", a=16),
                        in0=sap(v16, [[32, 8], [1, 16], [0, 16]]),
                        in1=sap(v16, [[32, 8], [0, 16], [1, 16]], off=16), op=ALU.add),
                        reads=[b_v16], writes=[b_cand])
                    for h in range(8):
                        kb.op("dve", lambda e, h=h: e.max(out=best[:, h, 0:8], in_=cand[:, h, :]),
                              reads=[b_cand], writes=[b_best])
                        kb.op("dve", lambda e, h=h: e.max_index(out=bidx[:, h, 0:8], in_max=best[:, h, 0:8],
                                                               in_values=cand[:, h, :]),
                              reads=[b_cand, b_best], writes=[b_bidx])
                        kb.op("dve", lambda e, h=h: e.match_replace(out=cand2[:, h, :], in_to_replace=best[:, h, 0:8],
                                                                   in_values=cand[:, h, :], imm_value=NEG),
                              reads=[b_cand, b_best], writes=[b_cand2])
                        kb.op("dve", lambda e, h=h: e.max(out=best[:, h, 8:16], in_=cand2[:, h, :]),
                              reads=[b_cand2], writes=[b_best])
                        kb.op("dve", lambda e, h=h: e.max_index(out=bidx[:, h, 8:16], in_max=best[:, h, 8:16],
                                                               in_values=cand2[:, h, :]),
                              reads=[b_cand2, b_best], writes=[b_bidx])
                    kb.op("dve", lambda e: e.tensor_tensor(out=eb[:], in0=best[:], in1=sap(best, [[16, 8], [0, 16]]),
                                                           op=ALU.subtract), reads=[b_best], writes=[b_eb])
                    kb.op("act", lambda e: e.activation(out=eb[:], in_=eb[:], func=AF.Exp), writes=[b_eb])
                    kb.op("dve", lambda e: e.tensor_reduce(out=zz[:, 0:8], in_=eb[:], axis=AX.X, op=ALU.add),
                          reads=[b_eb], writes=[b_zz])
                    kb.op("dve", lambda e: e.reciprocal(out=zz[:, 8:16], in_=zz[:, 0:8]), writes=[b_zz])
                    kb.op("dve", lambda e: e.tensor_tensor(
                        out=IG[:, 2, :].rearrange("p (h r) -> p h r", h=8), in0=eb[:],
                        in1=sap(zz, [[1, 8], [0, 16]], off=8), op=ALU.mult),
                        reads=[b_eb, b_zz], writes=[b_IG])
                    kb.op("dve", lambda e: e.tensor_single_scalar(out=aiu[:], in_=bidx[:].rearrange("p h r -> p (h r)"),
                                                                  scalar=4, op=ALU.logical_shift_right),
                          reads=[b_bidx], writes=[b_bidf])
                    kb.op("dve", lambda e: e.tensor_single_scalar(out=biu[:], in_=bidx[:].rearrange("p h r -> p (h r)"),
                                                                  scalar=15, op=ALU.bitwise_and),
                          reads=[b_bidx], writes=[b_bidf])
                    kb.op("dve", lambda e: e.tensor_copy(out=af[:], in_=aiu[:]), reads=[b_bidf], writes=[b_af])
                    kb.op("dve", lambda e: e.tensor_copy(out=bf[:], in_=biu[:]), reads=[b_bidf], writes=[b_bf])
                    for which, sel, boff in ((0, af, 0), (1, bf, 16)):
                        bsel = b_af if which == 0 else b_bf
                        kb.op("dve", lambda e, sel=sel: e.tensor_tensor(
                            out=oh[:], in0=sap(cst, [[0, 128], [1, 16]], off=C_IOTA),
                            in1=sap(sel, [[1, 128], [0, 16]]), op=ALU.is_equal),
                            reads=[bsel, b_cst], writes=[b_oh])
                        kb.op("dve", lambda e, boff=boff: e.tensor_tensor(
                            out=oh[:].rearrange("p (h r) a -> p h r a", h=8),
                            in0=oh[:].rearrange("p (h r) a -> p h r a", h=8),
                            in1=sap(i16f, [[32, 8], [0, 16], [1, 16]], off=boff), op=ALU.mult),
                            reads=[b_i16f], writes=[b_oh])
                        kb.op("dve", lambda e, which=which: e.tensor_reduce(out=IG[:, which, :], in_=oh[:], axis=AX.X,
                                                                          op=ALU.add),
                              reads=[b_oh], writes=[b_IG])
                    bk = next_bank([3, 4, 5, 6])
                    for w3 in range(3):
                        kb.op("pe", lambda e, w3=w3, bk=bk: e.transpose(ps[:, bk, w3 * 128:(w3 + 1) * 128],
                                                                        IG[:, w3, :], ident),
                              reads=[b_IG, b_cst], writes=[PB[bk]], sig=(w3 == 2))
                    kb.op("act", lambda e, bk=bk: e.activation(out=IGT[:], in_=ps[:, bk, 0:384].rearrange(
                        "p (c q) -> p c q", c=3), func=AF.Copy), reads=[PB[bk]], writes=[b_IGT])
                    for sbk in range(2):
                        si = (2 * T + sbk) % 2
                        for s32 in range(2):
                            t0 = sbk * 64 + s32 * 32
                            kb.op("dve", lambda e, t0=t0: e.tensor_tensor(
                                out=eq[:], in0=sap(cst, [[0, 32], [1, 128]], off=C_IOTA),
                                in1=sap(IGT, [[1, 32], [0, 128]], off=0 * 128 + t0), op=ALU.is_equal),
                                reads=[b_IGT, b_cst], writes=[b_eq])
                            kb.op("dve", lambda e, t0=t0: e.tensor_tensor(
                                out=At[:], in0=eq[:], in1=sap(IGT, [[1, 32], [0, 128]], off=2 * 128 + t0), op=ALU.mult),
                                reads=[b_eq, b_IGT], writes=[b_At])
                            kb.op("dve", lambda e, t0=t0: e.tensor_tensor(
                                out=Bt[:], in0=sap(cst, [[0, 32], [1, 128]], off=C_IOTA),
                                in1=sap(IGT, [[1, 32], [0, 128]], off=1 * 128 + t0), op=ALU.is_equal),
                                reads=[b_IGT, b_cst], writes=[b_Bt])
                            for t4 in range(8):
                                bk = next_bank([0, 1, 2, 7])
                                for q4 in range(4):
                                    tl = 4 * t4 + q4
                                    kb.op("pe", lambda e, tl=tl, q4=q4, bk=bk: e.matmul(
                                        ps[:, bk, q4 * 128:(q4 + 1) * 128], lhsT=At[:, tl, :], rhs=Bt[:, tl, :],
                                        start=True, stop=True),
                                        reads=[b_At, b_Bt], writes=[PB[bk]], sig=(q4 == 3))
                                kb.op("act", lambda e, t4=t4, bk=bk, si=si, s32=s32: e.activation(
                                    out=sap(stg[si], [[1, 4], [64, 128]], off=s32 * 32 + 4 * t4),
                                    in_=ps[:, bk, :].rearrange("p (t j) -> p t j", t=4), func=AF.Copy),
                                    reads=[PB[bk]], writes=[b_stg[si]])
                        kb.dma("sp", W1[2 * T + sbk], stg[si][:], reads=[b_stg[si]], writes=[b_W1[2 * T + sbk]])
                    if stop_after == "F2" and T == 0:
                        wchk = sbF("wchk", [128, 1024], BF16)
                        b_wchk = Buf("wchk")
                        kb.dma("sp", wchk[:], W1[0, :, 0:16, :].rearrange("i j t -> i (j t)"), reads=[b_W1[0]], writes=[b_wchk])
                        dump([(wchk[:, :], [b_wchk], 1024), (IG[:, 0, :], [b_IG], 128), (IG[:, 1, :], [b_IG], 128),
                              (IG[:, 2, :], [b_IG], 128)])
                        return nc
                    if stop_after == "F" and T == 0:
                        dump([(IG[:, 0, :], [b_IG], 128), (IG[:, 1, :], [b_IG], 128), (IG[:, 2, :], [b_IG], 128),
                              (s_sb[:, 0, :], [b_s], 128), (s_sb[:, 1, :], [b_s], 128)])
                        return nc

        print('sbuf remaining', nc.sbuf_bytes_remaining)
        kb.barrier()
        NJ = 4
        NR = 0 if stop_after == "G4" else (2 if stop_after in ("G2", "G3") else 128 // NJ)
        for half in range(2):
            kb.barrier()
            with ExitStack() as sc:
                sbG = lambda name, shape, dt=F32: sc.enter_context(nc.sbuf_tensor(name + "_h%d" % half, list(shape), dt))
                hTh = sbG("hTh", [128, 16, 1024], BF16)
                b_hTh = Buf("hTh")
                for tg2 in range(2):
                    tg = 2 * half + tg2
                    for q4 in range(4):
                        kb.dma("sp", hTh[:, 4 * q4:4 * q4 + 4, tg2 * 512:(tg2 + 1) * 512],
                               hT_d[:, 4 * q4:4 * q4 + 4, tg * 512:(tg + 1) * 512],
                               reads=[b_hT_d[tg]], writes=[b_hTh])
                accG = sbG("accG", [128, 8, D])
                b_accG = [bufs(2, "accG%d" % t) for t in range(8)]
                with ExitStack() as sc2:
                    sbH = lambda name, shape, dt=F32: sc2.enter_context(nc.sbuf_tensor(name + "_h%d" % half, list(shape), dt))
                    Wr = [sbH("Wr%d" % i, [128, 16, NJ, 64], BF16) for i in range(2)]
                    b_Wr = bufs(2, "Wr")
                    vr = [sbH("vr%d" % i, [128, NJ, D], BF16) for i in range(2)]
                    b_vr = bufs(2, "vr")
                    uj = [sbH("uj%d" % i, [128, 16, 128], BF16) for i in range(2)]
                    b_uj = bufs(2, "uj")
                    Gr = [sbH("Gr%d" % i, [128, NJ, 1024], BF16) for i in range(2)]
                    b_Gr = bufs(2, "Gr")
                    ga = [sbH("ga%d" % i, [128, 512], BF16) for i in range(2)]
                    b_ga = bufs(2, "ga")
                    urr = [0]
                    grr = [0]

                    def phaseA(r):
                        ri = r % 2
                        j0 = r * NJ
                        for q4 in range(4):
                            b0_ = 16 * half + 4 * q4
                            kb.dma("sp", Wr[ri][:, 4 * q4:4 * q4 + 4], W1[b0_:b0_ + 4, :, j0:j0 + NJ, :].rearrange(
                                "b i j t -> i b j t"), reads=b_W1[b0_:b0_ + 4], writes=[b_Wr[ri]])
                        kb.dma("pool", vr[ri][:], vL[j0:j0 + NJ].rearrange("j i d -> i j d"), writes=[b_vr[ri]])
                        for jj in range(NJ):
                            ui = urr[0] % 2
                            urr[0] += 1
                            kb.dma("pool", uj[ui][:], uT[j0 + jj], writes=[b_uj[ui]])
                            for tg2 in range(2):
                                bk = next_bank([0, 1])
                                for kc in range(16):
                                    kb.op("pe", lambda e, kc=kc, ui=ui, tg2=tg2, bk=bk: e.matmul(
                                        ps[:, bk, :], lhsT=uj[ui][:, kc, :], rhs=hTh[:, kc, tg2 * 512:(tg2 + 1) * 512],
                                        start=(kc == 0), stop=(kc == 15)),
                                        reads=[b_uj[ui], b_hTh], writes=[PB[bk]], sig=(kc == 15))
                                gi = grr[0] % 2
                                grr[0] += 1
                                kb.op("act", lambda e, gi=gi, bk=bk: e.activation(out=ga[gi][:, :], in_=ps[:, bk, :],
                                                                                 func=AF.Gelu),
                                      reads=[PB[bk]], writes=[b_ga[gi]])
                                kb.op("dve", lambda e, gi=gi, ri=ri, jj=jj, tg2=tg2: e.tensor_tensor(
                                    out=Gr[ri][:, jj, tg2 * 512:(tg2 + 1) * 512].rearrange("p (b t) -> p b t", b=8),
                                    in0=ga[gi][:, :].rearrange("p (b t) -> p b t", b=8),
                                    in1=Wr[ri][:, tg2 * 8:(tg2 + 1) * 8, jj, :], op=ALU.mult),
                                    reads=[b_ga[gi], b_Wr[ri]], writes=[b_Gr[ri]])

                    def phaseB(r):
                        ri = r % 2
                        for tt in range(8):
                            for dh in range(2):
                                b0 = 2 + 2 * ((tt * 2 + dh) % 3)
                                for jj in range(NJ):
                                    for dq in range(2):
                                        kb.op("pe", lambda e, jj=jj, dq=dq, tt=tt, dh=dh, b0=b0: e.matmul(
                                            ps[:, b0 + dq, :], lhsT=Gr[ri][:, jj, tt * 128:(tt + 1) * 128],
                                            rhs=vr[ri][:, jj, dh * 1024 + dq * 512:dh * 1024 + (dq + 1) * 512],
                                            start=(jj == 0), stop=(jj == NJ - 1)),
                                            reads=[b_Gr[ri], b_vr[ri]], writes=[PB[b0 + dq]],
                                            sig=(jj == NJ - 1 and dq == 1))
                                pin = ps[:, b0:b0 + 2, :].rearrange("p a b -> p (a b)")
                                aout = accG[:, tt, dh * 1024:(dh + 1) * 1024]
                                if r == 0:
                                    kb.op("dve", lambda e, pin=pin, aout=aout: e.tensor_copy(out=aout, in_=pin),
                                          reads=[PB[b0], PB[b0 + 1]], writes=[b_accG[tt][dh]])
                                else:
                                    kb.op("dve", lambda e, pin=pin, aout=aout: e.tensor_tensor(
                                        out=aout, in0=aout, in1=pin, op=ALU.add),
                                        reads=[PB[b0], PB[b0 + 1]], writes=[b_accG[tt][dh]])

                    if stop_after == "G1":
                        phaseA(0)
                        phaseB(0)
                        dump([(accG[:, 0, 0:1024], b_accG[0], 1024), (accG[:, 7, 1024:2048], b_accG[7], 1024),
                              (Gr[0][:, 0, :], [b_Gr[0]], 1024), (Gr[0][:, 3, :], [b_Gr[0]], 1024)])
                        return nc
                    if NR > 0:
                        phaseA(0)
                    for r in range(NR):
                        if r + 1 < NR:
                            phaseA(r + 1)
                        phaseB(r)
                kb.barrier()
                if stop_after == "G3":
                    dump([(accG[:, 0, 0:1024], b_accG[0], 1024), (accG[:, 7, 1024:2048], b_accG[7], 1024)])
                    return nc
                with ExitStack() as sc3:
                    sbL = lambda name, shape, dt=F32: sc3.enter_context(nc.sbuf_tensor(name + "_h%d" % half, list(shape), dt))
                    gb2 = sbL("gb2", [128, 2, D])
                    b_gb2 = Buf("gb2")
                    kb.dma("sp", gb2[:, 0, :], lnp[2], writes=[b_gb2])
                    kb.dma("sp", gb2[:, 1, :], lnp[3], writes=[b_gb2])
                    hres = [sbL("hres%d" % i, [128, D]) for i in range(2)]
                    b_hres = bufs(2, "hres")
                    xc2 = sbL("xc2", [128, D])
                    b_xc2 = Buf("xc2")
                    jf2 = sbL("jf2", [128, D])
                    b_jf2 = Buf("jf2")
                    yo = [sbL("yo%d" % i, [128, D]) for i in range(2)]
                    b_yo = bufs(2, "yo")
                    st2 = sbL("st2", [128, 8])
                    b_st2 = Buf("st2")
                    outs = []
                    for tt in range(8):
                        T = 8 * half + tt
                        hi = tt % 2
                        kb.dma("sp", hres[hi][:, :], h_tok[T * 128:(T + 1) * 128, :], reads=[b_htok[T]],
                               writes=[b_hres[hi]])
                        kb.op("dve", lambda e, hi=hi, tt=tt: e.scalar_tensor_tensor(
                            out=hres[hi][:, :], in0=hres[hi][:, :], scalar=ALPHA, in1=accG[:, tt, :],
                            op0=ALU.mult, op1=ALU.add),
                            reads=b_accG[tt], writes=[b_hres[hi]])
                        layer_norm(None, hres[hi][:, :], b_hres[hi], gb2[:, 0, :], gb2[:, 1, :], b_gb2,
                                   yo[hi][:, :], b_yo[hi], xc2, b_xc2, jf2, b_jf2, st2, b_st2)
                        outs.append(kb.dma("sp", y[T * 128:(T + 1) * 128, :], yo[hi][:, :], reads=[b_yo[hi]]))
                    for t in outs:
                        kb.wait("sp", t)
                    if stop_after in ("G2", "G4"):
                        dump([(accG[:, 0, 0:1024], b_accG[0], 1024), (accG[:, 7, 1024:2048], b_accG[7], 1024),
                              (yo[1][:, 0:1024], [b_yo[1]], 1024), (hres[1][:, 0:1024], [b_hres[1]], 1024)])
                        return nc
        print("instructions:", kb.nins, "sbuf remaining:", nc.sbuf_bytes_remaining)
    return nc


def _t5_bucket(dist):
    dist = np.asarray(dist)
    d = np.maximum(dist, 1).astype(np.float32)
    large = 16 + (np.log(d / np.float32(16)) / np.float32(np.log(128 / 16)) * np.float32(16)).astype(np.int32)
    large = np.minimum(large, 31)
    return np.where(dist < 16, dist, large)


def _prep_shared(w_in, pool_w, pool_scale, rel_bias, w_out, ln1_g, ln1_b, peer_wq, peer_subkeys, peer_u, peer_v,
                 ln2_g, ln2_b):
    f = np.float32
    w = w_in[0]
    cols = []
    for c in range(8):
        cols.append(w[:, c * 128:(c + 1) * 128])
    for c in range(8):
        cols.append(w[:, 1024 + c * 128:1024 + (c + 1) * 128])
    for c in range(8):
        cols.append(w[:, 2048 + c * 128:2048 + (c + 1) * 128])
    for c in range(8):
        cols.append(w[:, 4096 + c * 128:4096 + (c + 1) * 128])
    ki = w[:, 5120:5184]
    cols.append(np.concatenate([ki, ki], axis=1))
    w_fm = np.stack([c.reshape(16, 128, 128).transpose(1, 0, 2) for c in cols]).astype(f)
    wv = w[:, 3072:4096]
    w_v = np.stack([wv[:, hg * 256:(hg + 1) * 256].reshape(16, 128, 256).transpose(1, 0, 2) for hg in range(4)])
    w_wi = w[:, 5184:5200].reshape(16, 128, 16).transpose(1, 0, 2)
    pw = pool_w[0].reshape(4, 2, 128, 256).transpose(2, 0, 1, 3)
    kk = np.arange(128)[:, None]
    qq = np.arange(128)[None, :]
    bt = np.zeros((128, 2, 8, 128), f)
    for dl in range(2):
        bkt = _t5_bucket(np.maximum(dl * 128 + qq - kk, 0))
        bt[:, dl, :, :] = rel_bias[bkt].transpose(0, 2, 1)
    wo = w_out[0].reshape(16, 128, D).transpose(1, 0, 2)
    lnp = np.stack([np.broadcast_to(a[0][None, :], (128, D)) for a in (ln1_g, ln1_b, ln2_g, ln2_b)])
    wqh = peer_wq[0].reshape(16, 128, D).transpose(1, 0, 2)
    skT = peer_subkeys[0].transpose(2, 0, 1)
    u = peer_u[0].reshape(128, 128, 16, 128)
    uT = u.transpose(1, 3, 2, 0)
    vv = peer_v[0].reshape(128, 128, D).transpose(1, 0, 2)
    c = lambda a: np.ascontiguousarray(a, dtype=f)
    return dict(w_fm=c(w_fm), w_v=c(w_v), w_wi=c(w_wi), pool_w=c(pw), biasT=c(bt), w_out=c(wo), lnp=c(lnp),
                wq=c(wqh), subkT=c(skT), uT=c(uT), vL=c(vv))


def _consts(hf, pool_scale, rel_bias):
    f = np.float32
    cst = np.zeros((128, 1024), f)
    cst[:, 0:128] = np.eye(128, dtype=f)
    cst[:, 128:256] = np.arange(128, dtype=f)[None, :]
    qq = np.arange(128)[:, None]
    kk = np.arange(128)[None, :]
    cst[:, 256:384] = np.where(kk <= qq, 0.0, NEG).astype(f)
    valid = 1.0 if hf == 1 else 0.0
    cst[:, 384] = valid
    cst[:, 385] = (valid - 1.0) * 1.0e30
    cst[:, 392:400] = rel_bias[31][None, :]
    for gq, wwin in enumerate((2, 4, 8, 16)):
        pos = np.arange(16)
        if hf == 0:
            corr = wwin / np.minimum(pos + 1, wwin).astype(f)
        else:
            corr = np.ones(16, f)
        cst[:, 400 + 16 * gq:400 + 16 * gq + 16] = corr[None, :]
    cst[:, 464:472] = pool_scale[0].reshape(8, 128).T
    cst[:, 480:512] = (2.0 ** -np.arange(32, dtype=np.float64)).astype(f)[None, :]
    return cst


def _core_inputs(x, shared, pool_scale, rel_bias):
    in_maps = []
    for c in range(8):
        b, hf = c // 2, c % 2
        own = x[b, hf * TOK:(hf + 1) * TOK]
        prev = x[b, 0:TOK] if hf == 1 else np.zeros_like(own)
        xT = np.stack([prev.T.reshape(16, 128, TOK).transpose(1, 0, 2), own.T.reshape(16, 128, TOK).transpose(1, 0, 2)])
        m = dict(shared)
        m["xT"] = np.ascontiguousarray(xT, dtype=np.float32)
        m["x_tok"] = np.ascontiguousarray(own, dtype=np.float32)
        m["consts"] = _consts(hf, pool_scale, rel_bias)
        in_maps.append(m)
    return in_maps


def kernel(x, w_in, pool_w, pool_scale, rel_bias, w_out, ln1_g, ln1_b, peer_wq, peer_subkeys, peer_u, peer_v,
           ln2_g, ln2_b):
    args = [np.asarray(a, dtype=np.float32) for a in (x, w_in, pool_w, pool_scale, rel_bias, w_out, ln1_g, ln1_b,
                                                      peer_wq, peer_subkeys, peer_u, peer_v, ln2_g, ln2_b)]
    (x, w_in, pool_w, pool_scale, rel_bias, w_out, ln1_g, ln1_b, peer_wq, peer_subkeys, peer_u, peer_v,
     ln2_g, ln2_b) = args
    shared = _prep_shared(w_in, pool_w, pool_scale, rel_bias, w_out, ln1_g, ln1_b, peer_wq, peer_subkeys,
                          peer_u, peer_v, ln2_g, ln2_b)
    in_maps = _core_inputs(x, shared, pool_scale, rel_bias)
    nc = build_nc()
    res = run_bass_kernel_spmd(nc, in_maps, core_ids=list(range(8)))
    out = np.zeros((4, S, D), np.float32)
    for c in range(8):
        b, hf = c // 2, c % 2
        out[b, hf * TOK:(hf + 1) * TOK] = res.results[c]["y"]
    return out
```

```python
import numpy as np
from contextlib import ExitStack
import concourse.bass as bass
import concourse.mybir as mybir
from concourse.bass_utils import run_bass_kernel_spmd

F32 = mybir.dt.float32
BF16 = mybir.dt.bfloat16
U32 = mybir.dt.uint32
ALU = mybir.AluOpType
AF = mybir.ActivationFunctionType
AX = mybir.AxisListType

D = 2048
S = 4096
TOK = 2048
NEG = -1.0e30
ALPHA = 2.0 ** 0.25
LN_EPS = 1e-5
NIT = 16
TOPK = 256
ATT_SCALE = 128.0 ** -0.5
NSLOT = 6


class Buf:
    __slots__ = ("w", "r", "name")

    def __init__(self, name=""):
        self.w = None
        self.r = {}
        self.name = name


class KB:
    def __init__(self, nc, es):
        self.nc = nc
        self.engs = {"pe": nc.tensor, "dve": nc.vector, "act": nc.scalar, "pool": nc.gpsimd, "sp": nc.sync}
        self.psem = {e: es.enter_context(nc.semaphore("prog_" + e)) for e in ["pe", "dve", "act", "pool"]}
        self.cnt = {e: 0 for e in self.psem}
        self.seen = {e: {} for e in self.engs}
        self.pending = {e: [] for e in self.engs}
        self.dslots = {q: [[es.enter_context(nc.semaphore("dq_%s_%d" % (q, i))), 0, "dq_%s_%d" % (q, i)]
                           for i in range(NSLOT)] for q in ["sp", "pool"]}
        self.dnext = {q: 0 for q in self.dslots}
        self.nins = 0

    def wait(self, e, tok):
        if tok is None:
            return
        sem, val, key = tok
        if self.seen[e].get(key, 0) >= val:
            return
        self.engs[e].wait_ge(sem, val)
        self.seen[e][key] = val

    def _deps(self, e, reads, writes, deps):
        for b in reads:
            self.wait(e, b.w)
        for b in writes:
            self.wait(e, b.w)
            for t in b.r.values():
                self.wait(e, t)
        for t in deps:
            self.wait(e, t)

    def op(self, e, fn, reads=(), writes=(), deps=(), sig=True):
        self._deps(e, reads, writes, deps)
        ins = fn(self.engs[e])
        self.nins += 1
        if not sig:
            self.pending[e].append((list(reads), list(writes)))
            return None
        self.cnt[e] += 1
        ins.then_inc(self.psem[e], 1)
        key = "prog_" + e
        tok = (self.psem[e], self.cnt[e], key)
        allr = list(reads)
        allw = list(writes)
        for (r, w) in self.pending[e]:
            allr += r
            allw += w
        self.pending[e] = []
        for b in allw:
            b.w = tok
            b.r = {}
        for b in allr:
            if b not in allw:
                b.r[key] = tok
        return tok

    def barrier(self):
        toks = []
        for e in self.psem:
            if self.cnt[e] > 0:
                toks.append((self.psem[e], self.cnt[e], "prog_" + e))
        for q in self.dslots:
            for sem, cnt, key in self.dslots[q]:
                if cnt > 0:
                    toks.append((sem, cnt, key))
        for e in self.engs:
            for t in toks:
                self.wait(e, t)

    def dma(self, q, out, in_, reads=(), writes=(), deps=()):
        self._deps(q, reads, writes, deps)
        slot = self.dslots[q][self.dnext[q]]
        self.dnext[q] = (self.dnext[q] + 1) % NSLOT
        sem, cnt, key = slot
        if cnt > 0:
            self.wait(q, (sem, cnt, key))
        self.engs[q].dma_start(out=out, in_=in_).then_inc(sem, 16)
        self.nins += 1
        slot[1] = cnt + 16
        tok = (sem, cnt + 16, key)
        for b in writes:
            b.w = tok
            b.r = {}
        for b in reads:
            b.r[key] = tok
        return tok


def sap(t, dims, off=0, parts=128, p0=0):
    fs = 1
    for s_ in t.shape[1:]:
        fs *= int(s_)
    return bass.AP(t, p0 * fs + off, [[fs, parts]] + [[int(a), int(b)] for a, b in dims])


def bufs(n, name=""):
    return [Buf("%s%d" % (name, i)) for i in range(n)]


def build_nc(stop_after=None, small_peer=False):
    nc = bass.Bass("TRN2", target_bir_lowering=False)
    dbg = {}

    def din(name, shape, dt=F32):
        return nc.dram_tensor(name, list(shape), dt, kind="ExternalInput").ap()

    xT = din("xT", [2, 128, 16, TOK])
    x_tok = din("x_tok", [TOK, D])
    w_fm = din("w_fm", [33, 128, 16, 128])
    w_v = din("w_v", [4, 128, 16, 256])
    w_wi = din("w_wi", [128, 16, 16])
    pool_w = din("pool_w", [128, 4, 2, 256])
    consts = din("consts", [128, 1024])
    biasT = din("biasT", [128, 2, 8, 128])
    w_out = din("w_out", [128, 16, D])
    lnp = din("lnp", [4, 128, D])
    wq = din("wq", [128, 16, D])
    subkT = din("subkT", [128, 2, 128])
    uT = din("uT", [8 if small_peer else 128, 128, 16, 128])
    vL = din("vL", [8 if small_peer else 128, 128, D])
    y = nc.dram_tensor("y", [TOK, D], F32, kind="ExternalOutput").ap()
    maskT_d = nc.dram_tensor("maskT_d", [4, 128, 32, 512], BF16).ap()
    hT_d = nc.dram_tensor("hT_d", [128, 16, TOK], BF16).ap()
    h_tok = nc.dram_tensor("h_tok", [TOK, D], F32).ap()
    W1 = nc.dram_tensor("W1", [32, 128, 128, 64], BF16).ap()
    if stop_after is not None:
        dbg_out = nc.dram_tensor("dbg", [128, 8192], F32, kind="ExternalOutput").ap()

    with ExitStack() as es:
        kb = KB(nc, es)
        sb = lambda name, shape, dt=F32: es.enter_context(nc.sbuf_tensor(name, list(shape), dt))
        ps = es.enter_context(nc.psum_tensor("ps", [128, 8, 512], F32))
        PB = bufs(8, "psb")

        dbg_stg = sb("dbg_stg", [128, 1024]) if stop_after is not None else None
        cst = sb("cst", [128, 1024])
        b_cst = Buf("cst")
        kb.dma("sp", cst[:], consts, writes=[b_cst])
        C_ID = 0
        C_IOTA = 128
        C_TRI = 256
        C_FLAG = 384
        C_B31 = 392
        C_CORR = 400
        C_PSC = 464
        C_PW2 = 480
        ident = cst[:, C_ID:C_ID + 128]
        iota = cst[:, C_IOTA:C_IOTA + 128]
        tri = cst[:, C_TRI:C_TRI + 128]
        bT = sb("bT", [128, 2, 8, 128])
        b_bT = Buf("bT")
        kb.dma("sp", bT[:], biasT, writes=[b_bT])
        ones_b = sb("ones_b", [128, 128], BF16)
        b_ones = Buf("ones")
        kb.op("pool", lambda e: e.memset(ones_b[:], 1.0), writes=[b_ones])

        evac_rr = [0]

        def evac(out, in_, reads, writes, scale=None):
            evac_rr[0] ^= 1
            if scale is not None:
                return kb.op("act", lambda e: e.activation(out=out, in_=in_, func=AF.Copy, scale=scale),
                             reads=reads, writes=writes)
            if evac_rr[0]:
                return kb.op("act", lambda e: e.activation(out=out, in_=in_, func=AF.Copy), reads=reads, writes=writes)
            return kb.op("dve", lambda e: e.tensor_copy(out=out, in_=in_), reads=reads, writes=writes)

        pb_rr = [0]

        def next_bank(choices):
            pb_rr[0] += 1
            return choices[pb_rr[0] % len(choices)]

        def dump(ap_list):
            stg = dbg_stg
            b_stg = Buf("dbgstg")
            col = 0
            for ap, bl, n in ap_list:
                kb.op("dve", lambda e, ap=ap, n=n: e.tensor_copy(out=stg[:, 0:n], in_=ap),
                      reads=bl, writes=[b_stg])
                t = kb.dma("sp", dbg_out[:, col:col + n], stg[:, 0:n], reads=[b_stg])
                kb.wait("sp", t)
                col += n

        xg_t = [None, None]
        b_xg = bufs(2, "xg")
        xg_rr = [0]

        def load_xg(s, tg):
            i = xg_rr[0]
            xg_rr[0] ^= 1
            for q4 in range(4):
                kb.dma("pool", xg_t[i][:, 4 * q4:4 * q4 + 4, :], xT[s, :, 4 * q4:4 * q4 + 4, tg * 512:(tg + 1) * 512],
                       writes=[b_xg[i]])
            return xg_t[i], b_xg[i]

        def proj_fm(wt, wb, xg, bx, out_ap, out_bufs, scale=None):
            bk = next_bank([0, 1, 2, 7])
            for kc in range(16):
                kb.op("pe", lambda e, kc=kc: e.matmul(ps[:, bk, :], lhsT=wt(kc), rhs=xg[:, kc, :],
                                                      start=(kc == 0), stop=(kc == 15)),
                      reads=[wb, bx], writes=[PB[bk]], sig=(kc == 15))
            return evac(out_ap, ps[:, bk, :], [PB[bk]], out_bufs, scale=scale)

        phA = ExitStack()
        es.enter_context(phA)
        xg_t[0] = phA.enter_context(nc.sbuf_tensor("xgA0", [128, 16, 512], BF16))
        xg_t[1] = phA.enter_context(nc.sbuf_tensor("xgA1", [128, 16, 512], BF16))
        qiT = phA.enter_context(nc.sbuf_tensor("qiT", [128, 8, TOK], BF16))
        b_qiT = bufs(4, "qiT")
        kiT = phA.enter_context(nc.sbuf_tensor("kiT", [128, 2 * TOK], BF16))
        b_kiT = bufs(8, "kiT")
        widx = phA.enter_context(nc.sbuf_tensor("widx", [128, 16, 16], F32))
        b_widx = bufs(16, "widx")
        with ExitStack() as sc:
            wqi = sc.enter_context(nc.sbuf_tensor("wqi", [128, 8, 16, 128], BF16))
            b_wqi = Buf("wqi")
            wki = sc.enter_context(nc.sbuf_tensor("wki", [128, 16, 128], BF16))
            b_wki = Buf("wki")
            wwi = sc.enter_context(nc.sbuf_tensor("wwi", [128, 16, 16], BF16))
            b_wwi = Buf("wwi")
            kb.dma("pool", wki[:], w_fm[32], writes=[b_wki])
            for c in range(8):
                kb.dma("pool", wqi[:, c], w_fm[24 + c], writes=[b_wqi])
            kb.dma("pool", wwi[:], w_wi, writes=[b_wwi])
            for s in range(2):
                for tg in range(4):
                    xg, bx = load_xg(s, tg)
                    kg = s * 4 + tg
                    proj_fm(lambda kc: wki[:, kc, :], b_wki, xg, bx, kiT[:, kg * 512:(kg + 1) * 512], [b_kiT[kg]])
                    if stop_after == "A1":
                        dump([(kiT[:, 0:512], b_kiT[0:1], 512), (xg[:, 0, :], [bx], 512)])
                        return nc
                    if s == 1:
                        for c in range(8):
                            proj_fm(lambda kc, c=c: wqi[:, c, kc, :], b_wqi, xg, bx,
                                    qiT[:, c, tg * 512:(tg + 1) * 512], [b_qiT[tg]])
                        for tt in range(4):
                            bk = next_bank([0, 1, 2, 7])
                            for kc in range(16):
                                kb.op("pe", lambda e, kc=kc, tt=tt: e.matmul(
                                    ps[:, bk, 0:16], lhsT=xg[:, kc, tt * 128:(tt + 1) * 128], rhs=wwi[:, kc, :],
                                    start=(kc == 0), stop=(kc == 15)),
                                    reads=[b_wwi, bx], writes=[PB[bk]], sig=(kc == 15))
                            evac(widx[:, tg * 4 + tt, :], ps[:, bk, 0:16], [PB[bk]], [b_widx[tg * 4 + tt]])
        if stop_after == "A":
            dump([(kiT[:, 0:2048], b_kiT[0:4], 2048), (kiT[:, 2048:4096], b_kiT[4:8], 2048),
                  (qiT[:, 0, 0:2048], b_qiT, 2048), (widx[:, :, :].rearrange("p a b -> p (a b)"), b_widx, 256)])
            return nc

        kb.barrier()
        b_maskd = bufs(4, "maskd")
        with ExitStack() as sc:
            sbB = lambda name, shape, dt=F32: sc.enter_context(nc.sbuf_tensor(name, list(shape), dt))
            score = sbB("score", [128, 4096])
            b_score = bufs(8, "score")
            maskf = sbB("maskf", [128, 4096])
            b_maskf = Buf("maskf")
            junk = sbB("junk", [128, 4096], BF16)
            b_junk = Buf("junk")
            Rt = [sbB("R%d" % i, [128, 512]) for i in range(3)]
            b_R = bufs(3, "R")
            acc = sbB("accB", [128, 512])
            b_acc = Buf("acc")
            mxall = sbB("mxall", [128, 16])
            b_mx = Buf("mx")
            small = sbB("smallB", [128, 64])
            b_small = Buf("small")
            mid = small[:, 0:1]
            cntc = small[:, 1:2]
            tmpc = small[:, 2:3]
            Mp = small[:, 3:4]
            tau = small[:, 4:5]
            Dt = small[:, 8:8 + NIT + 1]
            mT = sbB("maskTg", [128, 32, 512], BF16)
            b_mT = Buf("maskTg")
            r_rr = 0
            for g in range(4):
                kb.op("pool", lambda e: e.memset(mT[:], 0.0), writes=[b_mT])
                for j in range(4):
                    i = 4 * g + j
                    NK = (17 + i) * 128
                    nkt = (NK + 511) // 512
                    for kt in range(nkt):
                        wk = min(512, NK - kt * 512)
                        direct = kt >= 4
                        for hn, h in enumerate([0, 2, 4, 6, 8, 10, 12, 14, 1, 3, 5, 7, 9, 11, 13, 15]):
                            cp, r0 = h // 2, 64 * (h % 2)
                            bk = next_bank([0, 1, 2])
                            kb.op("pe", lambda e, cp=cp, r0=r0, bk=bk, kt=kt, wk=wk, i=i: e.matmul(
                                ps[:, bk, 0:wk], lhsT=qiT[r0:r0 + 64, cp, i * 128:(i + 1) * 128],
                                rhs=kiT[r0:r0 + 64, kt * 512:kt * 512 + wk], start=True, stop=True),
                                reads=[b_qiT[g], b_kiT[kt]], writes=[PB[bk]])
                            ri = r_rr % 3
                            r_rr += 1
                            kb.op("act", lambda e, ri=ri, bk=bk, wk=wk: e.activation(
                                out=Rt[ri][:, 0:wk], in_=ps[:, bk, 0:wk], func=AF.Relu),
                                reads=[PB[bk]], writes=[b_R[ri]])
                            wcol = widx[:, i, h:h + 1]
                            if hn == 0:
                                kb.op("dve", lambda e, ri=ri, wk=wk, wcol=wcol: e.tensor_scalar(
                                    out=acc[:, 0:wk], in0=Rt[ri][:, 0:wk], scalar1=wcol, scalar2=None, op0=ALU.mult),
                                    reads=[b_R[ri], b_widx[i]], writes=[b_acc])
                            elif hn == 15 and direct:
                                kb.op("dve", lambda e, ri=ri, wk=wk, wcol=wcol, kt=kt: e.scalar_tensor_tensor(
                                    out=score[:, kt * 512:kt * 512 + wk], in0=Rt[ri][:, 0:wk], scalar=wcol,
                                    in1=acc[:, 0:wk], op0=ALU.mult, op1=ALU.add),
                                    reads=[b_R[ri], b_widx[i], b_acc], writes=[b_score[kt]])
                            else:
                                kb.op("dve", lambda e, ri=ri, wk=wk, wcol=wcol: e.scalar_tensor_tensor(
                                    out=acc[:, 0:wk], in0=Rt[ri][:, 0:wk], scalar=wcol,
                                    in1=acc[:, 0:wk], op0=ALU.mult, op1=ALU.add),
                                    reads=[b_R[ri], b_widx[i]], writes=[b_acc])
                        if not direct:
                            kb.op("dve", lambda e, wk=wk, kt=kt: e.tensor_reduce(
                                out=mxall[:, kt:kt + 1], in_=acc[:, 0:wk], axis=AX.X, op=ALU.max),
                                reads=[b_acc], writes=[b_mx])
                            kb.op("dve", lambda e, wk=wk, kt=kt: e.tensor_reduce(
                                out=mxall[:, 8 + kt:9 + kt], in_=acc[:, 0:wk], axis=AX.X, op=ALU.min),
                                reads=[b_acc], writes=[b_mx])
                            kb.op("dve", lambda e, wk=wk, kt=kt: e.tensor_scalar(
                                out=score[:, kt * 512:kt * 512 + wk], in0=acc[:, 0:wk],
                                scalar1=cst[:, C_FLAG:C_FLAG + 1], scalar2=cst[:, C_FLAG + 1:C_FLAG + 2],
                                op0=ALU.mult, op1=ALU.add),
                                reads=[b_acc, b_cst], writes=[b_score[kt]])
                        else:
                            kb.op("dve", lambda e, wk=wk, kt=kt: e.tensor_reduce(
                                out=mxall[:, kt:kt + 1], in_=score[:, kt * 512:kt * 512 + wk], axis=AX.X, op=ALU.max),
                                reads=[b_score[kt]], writes=[b_mx])
                            kb.op("dve", lambda e, wk=wk, kt=kt: e.tensor_reduce(
                                out=mxall[:, 8 + kt:9 + kt], in_=score[:, kt * 512:kt * 512 + wk], axis=AX.X, op=ALU.min),
                                reads=[b_score[kt]], writes=[b_mx])
                    kd = (NK - 128) // 512
                    kb.op("dve", lambda e, NK=NK: e.tensor_tensor(
                        out=score[:, NK - 128:NK], in0=score[:, NK - 128:NK], in1=tri, op=ALU.add),
                        reads=[b_cst], writes=[b_score[kd]])
                    kb.op("dve", lambda e, nkt=nkt: e.tensor_reduce(out=Mp, in_=mxall[:, 0:nkt], axis=AX.X, op=ALU.max),
                          reads=[b_mx], writes=[b_small])
                    kb.op("dve", lambda e, nkt=nkt: e.tensor_reduce(out=tmpc, in_=mxall[:, 8:8 + nkt], axis=AX.X, op=ALU.min),
                          reads=[b_mx], writes=[b_small])
                    kb.op("dve", lambda e: e.scalar_tensor_tensor(out=Mp, in0=tmpc, scalar=-1.0, in1=Mp, op0=ALU.mult,
                                                                  op1=ALU.max), writes=[b_small])
                    kb.op("dve", lambda e: e.tensor_scalar(out=Mp, in0=Mp, scalar1=1.001, scalar2=1e-20,
                                                           op0=ALU.mult, op1=ALU.add), writes=[b_small])
                    kb.op("dve", lambda e: e.tensor_scalar(out=Dt, in0=cst[:, C_PW2:C_PW2 + NIT + 1], scalar1=Mp,
                                                           scalar2=None, op0=ALU.mult), reads=[b_cst], writes=[b_small])
                    kb.op("dve", lambda e: e.memset(mid, 0.0), writes=[b_small])
                    sc_all = b_score[0:nkt]
                    for k in range(NIT):
                        kb.op("dve", lambda e, NK=NK: e.tensor_scalar(
                            out=junk[:, 0:NK], in0=score[:, 0:NK], scalar1=mid, scalar2=None,
                            op0=ALU.is_ge, op1=ALU.add, accum_out=cntc),
                            reads=sc_all, writes=[b_junk, b_small])
                        kb.op("dve", lambda e: e.tensor_scalar(out=tmpc, in0=cntc, scalar1=TOPK - 0.5, scalar2=0.5,
                                                               op0=ALU.is_ge, op1=ALU.subtract), writes=[b_small])
                        kb.op("dve", lambda e, k=k: e.scalar_tensor_tensor(
                            out=mid, in0=tmpc, scalar=Dt[:, k:k + 1], in1=mid, op0=ALU.mult, op1=ALU.add),
                            writes=[b_small])
                    kb.op("dve", lambda e: e.tensor_tensor(out=tau, in0=mid, in1=Dt[:, NIT:NIT + 1], op=ALU.subtract),
                          writes=[b_small])
                    kb.op("dve", lambda e, NK=NK: e.tensor_scalar(
                        out=maskf[:, 0:NK], in0=score[:, 0:NK], scalar1=tau, scalar2=None, op0=ALU.is_ge),
                        reads=sc_all + [b_small], writes=[b_maskf])
                    nch = 17 + i
                    for c0 in range(0, nch, 4):
                        n4 = min(4, nch - c0)
                        bk = next_bank([3, 4])
                        for cc in range(n4):
                            c = c0 + cc
                            kb.op("pe", lambda e, c=c, cc=cc, bk=bk: e.transpose(
                                ps[:, bk, cc * 128:(cc + 1) * 128], maskf[:, c * 128:(c + 1) * 128], ident),
                                reads=[b_maskf, b_cst], writes=[PB[bk]], sig=(cc == n4 - 1))
                        kb.op("act", lambda e, c0=c0, n4=n4, bk=bk, j=j: e.activation(
                            out=mT[:, c0:c0 + n4, j * 128:(j + 1) * 128],
                            in_=ps[:, bk, 0:n4 * 128].rearrange("p (c q) -> p c q", c=n4), func=AF.Copy),
                            reads=[PB[bk]], writes=[b_mT])
                    if stop_after == "B" and i == 0:
                        dump([(score[:, 0:1024], sc_all, 1024), (score[:, 1024:2048], sc_all, 1024),
                              (score[:, 2048:2176], sc_all, 128),
                              (maskf[:, 0:1024], [b_maskf], 1024), (maskf[:, 1024:2048], [b_maskf], 1024),
                              (maskf[:, 2048:2176], [b_maskf], 128), (small[:, 0:64], [b_small], 64),
                              (kiT[:, 1024:1536], [b_kiT[2]], 512), (kiT[:, 512:1024], [b_kiT[1]], 512)])
                        return nc
                kb.dma("sp", maskT_d[g], mT[:], reads=[b_mT], writes=[b_maskd[g]])
        phA.close()
        pers = ExitStack()
        es.enter_context(pers)
        psb = lambda name, shape, dt=F32: pers.enter_context(nc.sbuf_tensor(name, list(shape), dt))
        poolT = psb("poolT", [128, 8, TOK], BF16)
        b_poolT = bufs(4, "poolT")
        attnT = psb("attnT", [128, 8, TOK], BF16)
        b_attnT = [bufs(4, "attnT%d" % h) for h in range(8)]
        scCD = ExitStack()
        es.enter_context(scCD)
        xg_t[0] = scCD.enter_context(nc.sbuf_tensor("xgC0", [128, 16, 512], BF16))
        xg_t[1] = xg_t[0]
        b_xg[0] = Buf("xgc0")
        b_xg[1] = b_xg[0]

        kb.barrier()
        with ExitStack() as sc:
            sbC = lambda name, shape, dt=F32: sc.enter_context(nc.sbuf_tensor(name, list(shape), dt))
            wpl = sbC("wpl", [128, 8, 16, 128], BF16)
            b_wpl = Buf("wpl")
            for c in range(8):
                kb.dma("pool", wpl[:, c], w_fm[c], writes=[b_wpl])
            pw = sbC("pw", [128, 4, 2, 256], BF16)
            b_pw = Buf("pw")
            kb.dma("pool", pw[:], pool_w, writes=[b_pw])
            xh = sbC("xh", [128, 16, 16], BF16)
            b_xh = Buf("xh")
            for q4 in range(4):
                kb.dma("pool", xh[:, 4 * q4:4 * q4 + 4, :], xT[0, :, 4 * q4:4 * q4 + 4, TOK - 16:TOK], writes=[b_xh])
            hal = sbC("hal", [128, 8, 16])
            b_hal = bufs(8, "hal")
            vb = [sbC("vb%d" % i, [128, 528]) for i in range(2)]
            b_vb = bufs(2, "vb")
            sa = sbC("sa", [128, 528])
            sbb = sbC("sbb", [128, 528])
            b_sa, b_sb = Buf("sa"), Buf("sb")
            t16 = sbC("t16", [128, 16])
            b_t16 = Buf("t16")
            plb = [sbC("plb%d" % i, [128, 512], BF16) for i in range(2)]
            b_plb = bufs(2, "plb")
            for cp in range(8):
                bk = next_bank([0, 1, 2, 7])
                for kc in range(16):
                    kb.op("pe", lambda e, kc=kc, cp=cp, bk=bk: e.matmul(
                        ps[:, bk, 0:16], lhsT=wpl[:, cp, kc, :], rhs=xh[:, kc, :], start=(kc == 0), stop=(kc == 15)),
                        reads=[b_wpl, b_xh], writes=[PB[bk]], sig=(kc == 15))
                evac(hal[:, cp, :], ps[:, bk, 0:16], [PB[bk]], [b_hal[cp]])
            vrr = 0
            for tg in range(4):
                xg, bx = load_xg(1, tg)
                for gq in range(4):
                    wwin = (2, 4, 8, 16)[gq]
                    for cc in range(2):
                        cp = 2 * gq + cc
                        vi = vrr % 2
                        vrr += 1
                        V = vb[vi]
                        bV = b_vb[vi]
                        kb.op("dve", lambda e, V=V, cp=cp: e.tensor_copy(out=V[:, 0:16], in_=hal[:, cp, :]),
                              reads=[b_hal[cp]], writes=[bV])
                        proj_fm(lambda kc, cp=cp: wpl[:, cp, kc, :], b_wpl, xg, bx, V[:, 16:528], [bV])
                        kb.op("act", lambda e, V=V, cp=cp: e.activation(out=hal[:, cp, :], in_=V[:, 512:528], func=AF.Copy),
                              reads=[bV], writes=[b_hal[cp]])
                        kb.op("dve", lambda e, V=V: e.tensor_tensor(out=sa[:, 1:528], in0=V[:, 1:528], in1=V[:, 0:527],
                                                                     op=ALU.add), reads=[bV], writes=[b_sa])
                        Sfin, bS = sa, b_sa
                        if gq >= 1:
                            kb.op("dve", lambda e: e.tensor_tensor(out=sbb[:, 3:528], in0=sa[:, 3:528], in1=sa[:, 1:526],
                                                                   op=ALU.add), reads=[b_sa], writes=[b_sb])
                            Sfin, bS = sbb, b_sb
                        if gq >= 2:
                            kb.op("dve", lambda e: e.tensor_tensor(out=sa[:, 7:528], in0=sbb[:, 7:528], in1=sbb[:, 3:524],
                                                                   op=ALU.add), reads=[b_sb], writes=[b_sa])
                            Sfin, bS = sa, b_sa
                        if gq >= 3:
                            kb.op("dve", lambda e: e.tensor_tensor(out=sbb[:, 15:528], in0=sa[:, 15:528], in1=sa[:, 7:520],
                                                                   op=ALU.add), reads=[b_sa], writes=[b_sb])
                            Sfin, bS = sbb, b_sb
                        kb.op("dve", lambda e, Sfin=Sfin, V=V, cc=cc, wwin=wwin: e.scalar_tensor_tensor(
                            out=plb[cc][:, :], in0=Sfin[:, 16:528], scalar=1.0 / wwin, in1=V[:, 16:528],
                            op0=ALU.mult, op1=ALU.subtract), reads=[bS, bV], writes=[b_plb[cc]])
                        if tg == 0:
                            kb.op("dve", lambda e, Sfin=Sfin, gq=gq: e.tensor_tensor(
                                out=t16[:, :], in0=Sfin[:, 16:32], in1=cst[:, C_CORR + 16 * gq:C_CORR + 16 * gq + 16],
                                op=ALU.mult), reads=[bS, b_cst], writes=[b_t16])
                            kb.op("dve", lambda e, V=V, cc=cc, wwin=wwin: e.scalar_tensor_tensor(
                                out=plb[cc][:, 0:16], in0=t16[:, :], scalar=1.0 / wwin, in1=V[:, 16:32],
                                op0=ALU.mult, op1=ALU.subtract), reads=[b_t16, bV], writes=[b_plb[cc]])
                    for dc in range(2):
                        bk = next_bank([0, 1, 2, 7])
                        for cc in range(2):
                            kb.op("pe", lambda e, cc=cc, dc=dc, gq=gq, bk=bk: e.matmul(
                                ps[:, bk, :], lhsT=pw[:, gq, cc, dc * 128:(dc + 1) * 128], rhs=plb[cc][:, :],
                                start=(cc == 0), stop=(cc == 1)),
                                reads=[b_pw, b_plb[cc]], writes=[PB[bk]], sig=(cc == 1))
                        oc = 2 * gq + dc
                        kb.op("act", lambda e, oc=oc, bk=bk, tg=tg: e.activation(
                            out=poolT[:, oc, tg * 512:(tg + 1) * 512], in_=ps[:, bk, :], func=AF.Copy,
                            scale=cst[:, C_PSC + oc:C_PSC + oc + 1]),
                            reads=[PB[bk], b_cst], writes=[b_poolT[tg]])
        if stop_after == "C":
            dump([(poolT[:, 0, 0:1024], b_poolT, 1024), (poolT[:, 7, 0:1024], b_poolT, 1024),
                  (poolT[:, 3, 1024:2048], b_poolT, 1024)])
            return nc

        kb.barrier()
        with ExitStack() as sc:
            sbD = lambda name, shape, dt=F32: sc.enter_context(nc.sbuf_tensor(name, list(shape), dt))
            wq2 = sbD("wq2", [128, 2, 16, 128], BF16)
            wk2 = sbD("wk2", [128, 2, 16, 128], BF16)
            wv2 = sbD("wv2", [128, 16, 256], BF16)
            b_wq2, b_wk2, b_wv2 = Buf("wq2"), Buf("wk2"), Buf("wv2")
            kT2 = sbD("kT2", [128, 2, 2 * TOK], BF16)
            b_kT2 = bufs(8, "kT2")
            v2 = sbD("v2", [128, 32, 256], BF16)
            b_v2 = bufs(8, "v2")
            qT2 = sbD("qT2", [128, 2, TOK], BF16)
            b_qT2 = bufs(4, "qT2")
            mk = sbD("mk", [128, 32, 512], BF16)
            b_mk = Buf("mk")
            Et = [sbD("E%d" % i, [128, 512], BF16) for i in range(3)]
            b_E = bufs(3, "E")
            Pt = [sbD("P%d" % i, [128, 512], BF16) for i in range(3)]
            b_P = bufs(3, "P")
            tmpn = [sbD("tmpn%d" % i, [128, 128]) for i in range(2)]
            b_tmpn = bufs(2, "tmpn")
            rden = sbD("rden", [128, 512])
            b_rden = Buf("rden")
            SB_S = [0, 1, 2]
            for hg in range(4):
                for hl in range(2):
                    kb.dma("pool", wq2[:, hl], w_fm[8 + 2 * hg + hl], writes=[b_wq2])
                    kb.dma("pool", wk2[:, hl], w_fm[16 + 2 * hg + hl], writes=[b_wk2])
                kb.dma("pool", wv2[:], w_v[hg], writes=[b_wv2])
                for s in range(2):
                    for tg in range(4):
                        xg, bx = load_xg(s, tg)
                        kg = s * 4 + tg
                        for hl in range(2):
                            proj_fm(lambda kc, hl=hl: wk2[:, hl, kc, :], b_wk2, xg, bx,
                                    kT2[:, hl, kg * 512:(kg + 1) * 512], [b_kT2[kg]])
                            if s == 1:
                                proj_fm(lambda kc, hl=hl: wq2[:, hl, kc, :], b_wq2, xg, bx,
                                        qT2[:, hl, tg * 512:(tg + 1) * 512], [b_qT2[tg]])
                        for tt in range(4):
                            bk = 7
                            for kc in range(16):
                                kb.op("pe", lambda e, kc=kc, tt=tt, xg=xg: e.matmul(
                                    ps[:, bk, 0:256], lhsT=xg[:, kc, tt * 128:(tt + 1) * 128], rhs=wv2[:, kc, :],
                                    start=(kc == 0), stop=(kc == 15)),
                                    reads=[b_wv2, bx], writes=[PB[bk]], sig=(kc == 15))
                            evac(v2[:, kg * 4 + tt, :], ps[:, bk, 0:256], [PB[bk]], [b_v2[kg]])
                if stop_after == "D0":
                    dump([(kT2[:, 0, 0:1024], b_kT2[0:2], 1024), (qT2[:, 1, 0:1024], b_qT2[0:2], 1024),
                          (v2[:, 0:4, :].rearrange("p a b -> p (a b)"), b_v2[0:1], 1024)])
                    return nc
                for g in range(4):
                    kb.dma("sp", mk[:], maskT_d[g], reads=[b_maskd[g]], writes=[b_mk])
                    nch = 20 + 4 * g
                    for hl in range(2):
                        h = 2 * hg + hl
                        bO = 3 + 2 * (hl % 2)
                        bD = bO + 1

                        def issue_S(c, hl=hl, g=g):
                            bk = SB_S[c % 3]
                            kb.op("pe", lambda e: e.matmul(
                                ps[:, bk, :], lhsT=kT2[:, hl, c * 128:(c + 1) * 128],
                                rhs=qT2[:, hl, g * 512:(g + 1) * 512], start=True, stop=True),
                                reads=[b_kT2[c // 4], b_qT2[g]], writes=[PB[bk]])
                        issue_S(0)
                        if nch > 1:
                            issue_S(1)
                        for c in range(nch):
                            if c + 2 < nch:
                                issue_S(c + 2)
                            bk = SB_S[c % 3]
                            ei = c % 3
                            tokE = kb.op("act", lambda e, bk=bk, ei=ei, h=h: e.activation(
                                out=Et[ei][:, :], in_=ps[:, bk, :], func=AF.Exp, scale=ATT_SCALE,
                                bias=cst[:, C_B31 + h:C_B31 + h + 1]),
                                reads=[PB[bk], b_cst], writes=[b_E[ei]])
                            cn = c - (15 + 4 * g)
                            if 0 <= cn <= 4:
                                for j in range(4):
                                    dl = 1 + j - cn
                                    if dl in (0, 1):
                                        ti = (j + cn) % 2
                                        kb.op("dve", lambda e, bk=bk, j=j, dl=dl, h=h, ti=ti: e.scalar_tensor_tensor(
                                            out=tmpn[ti][:, :], in0=ps[:, bk, j * 128:(j + 1) * 128], scalar=ATT_SCALE,
                                            in1=bT[:, dl, h, :], op0=ALU.mult, op1=ALU.add),
                                            reads=[PB[bk], b_bT], writes=[b_tmpn[ti]], deps=[tokE])
                                        kb.op("act", lambda e, ei=ei, j=j, ti=ti: e.activation(
                                            out=Et[ei][:, j * 128:(j + 1) * 128], in_=tmpn[ti][:, :], func=AF.Exp),
                                            reads=[b_tmpn[ti]], writes=[b_E[ei]])
                            kb.op("dve", lambda e, ei=ei, c=c: e.tensor_tensor(
                                out=Pt[ei][:, :], in0=Et[ei][:, :], in1=mk[:, c, :], op=ALU.mult),
                                reads=[b_E[ei], b_mk], writes=[b_P[ei]])
                            kb.op("pe", lambda e, ei=ei, c=c, hl=hl, bO=bO, nch=nch: e.matmul(
                                ps[:, bO, :], lhsT=v2[:, c, hl * 128:(hl + 1) * 128], rhs=Pt[ei][:, :],
                                start=(c == 0), stop=(c == nch - 1)),
                                reads=[b_v2[c // 4], b_P[ei]], writes=[PB[bO]], sig=(c == nch - 1))
                            kb.op("pe", lambda e, ei=ei, c=c, bD=bD, nch=nch: e.matmul(
                                ps[:, bD, :], lhsT=ones_b[:, :], rhs=Pt[ei][:, :],
                                start=(c == 0), stop=(c == nch - 1)),
                                reads=[b_ones, b_P[ei]], writes=[PB[bD]], sig=(c == nch - 1))
                        kb.op("dve", lambda e, bD=bD: e.reciprocal(out=rden[:, :], in_=ps[:, bD, :]),
                              reads=[PB[bD]], writes=[b_rden])
                        kb.op("dve", lambda e, bO=bO, h=h, g=g: e.tensor_tensor(
                            out=attnT[:, h, g * 512:(g + 1) * 512], in0=ps[:, bO, :], in1=rden[:, :], op=ALU.mult),
                            reads=[PB[bO], b_rden], writes=[b_attnT[h][g]])
                        if stop_after == "D1":
                            dump([(attnT[:, 0, 0:512], [b_attnT[0][0]], 512), (rden[:, :], [b_rden], 512)])
                            return nc
        if stop_after == "D":
            dump([(attnT[:, 0, 0:1024], b_attnT[0], 1024), (attnT[:, 7, 0:1024], b_attnT[7], 1024),
                  (attnT[:, 3, 1024:2048], b_attnT[3], 1024)])
            return nc

        scCD.close()
        kb.barrier()
        b_hT_d = bufs(4, "hT_d")
        b_htok = bufs(16, "htok")

        def layer_norm(e_sb, src, bsrc, gam, bet, b_gb, outt, bout, xc, b_xc, junkf, b_jf, st, b_st):
            kb.op("dve", lambda e: e.tensor_scalar(out=junkf[:, :], in0=src, scalar1=1.0, scalar2=None, op0=ALU.mult,
                                                   op1=ALU.add, accum_out=st[:, 0:1]), reads=[bsrc], writes=[b_jf, b_st])
            kb.op("dve", lambda e: e.tensor_scalar(out=st[:, 1:2], in0=st[:, 0:1], scalar1=1.0 / D, scalar2=None,
                                                   op0=ALU.mult), writes=[b_st])
            kb.op("dve", lambda e: e.tensor_scalar(out=xc[:, :], in0=src, scalar1=st[:, 1:2], scalar2=None,
                                                   op0=ALU.subtract), reads=[bsrc, b_st], writes=[b_xc])
            kb.op("dve", lambda e: e.tensor_tensor(out=junkf[:, :], in0=xc[:, :], in1=xc[:, :], op=ALU.mult),
                  reads=[b_xc], writes=[b_jf])
            kb.op("dve", lambda e: e.tensor_scalar(out=junkf[:, :], in0=junkf[:, :], scalar1=1.0, scalar2=None,
                                                   op0=ALU.mult, op1=ALU.add, accum_out=st[:, 2:3]),
                  writes=[b_jf, b_st])
            kb.op("dve", lambda e: e.tensor_scalar(out=st[:, 3:4], in0=st[:, 2:3], scalar1=1.0 / D, scalar2=LN_EPS,
                                                   op0=ALU.mult, op1=ALU.add), writes=[b_st])
            kb.op("act", lambda e: e.activation(out=st[:, 4:5], in_=st[:, 3:4], func=AF.Sqrt), writes=[b_st])
            kb.op("dve", lambda e: e.reciprocal(out=st[:, 5:6], in_=st[:, 4:5]), writes=[b_st])
            kb.op("dve", lambda e: e.scalar_tensor_tensor(out=xc[:, :], in0=xc[:, :], scalar=st[:, 5:6], in1=gam,
                                                          op0=ALU.mult, op1=ALU.mult),
                  reads=[b_st, b_gb], writes=[b_xc])
            return kb.op("dve", lambda e: e.tensor_tensor(out=outt, in0=xc[:, :], in1=bet, op=ALU.add),
                         reads=[b_xc, b_gb], writes=[bout])

        with ExitStack() as sc:
            sbE = lambda name, shape, dt=F32: sc.enter_context(nc.sbuf_tensor(name, list(shape), dt))
            wo2 = [sbE("wo%d" % i, [128, 16, 512], BF16) for i in range(2)]
            b_wo2 = bufs(2, "wo")
            wo_rr = [0]
            gb = sbE("gb1", [128, 2, D])
            b_gb = Buf("gb1")
            kb.dma("sp", gb[:, 0, :], lnp[0], writes=[b_gb])
            kb.dma("sp", gb[:, 1, :], lnp[1], writes=[b_gb])
            xt = [sbE("xt0", [128, D])] * 2
            b_xt = [Buf("xt")] * 2
            hpre = sbE("hpre", [128, D])
            b_hpre = Buf("hpre")
            xc = sbE("xc", [128, D])
            b_xc = Buf("xc")
            junkf = sbE("junkf", [128, D])
            b_jf = Buf("junkf")
            hh = [sbE("hh0", [128, D])] * 2
            b_hh = [Buf("hh")] * 2
            st = sbE("st", [128, 8])
            b_st = Buf("st")
            hTs = sbE("hTs", [128, 16, 128], BF16)
            b_hTs = Buf("hTs")
            for tt in range(16):
                tg = tt // 4
                xi = tt % 2
                kb.dma("sp", xt[xi][:, :], x_tok[tt * 128:(tt + 1) * 128, :], writes=[b_xt[xi]])
                for dt_ in range(4):
                    bk = next_bank([0, 1, 2, 7])
                    wi_ = wo_rr[0] % 2
                    wo_rr[0] += 1
                    wo, b_wo = wo2[wi_], b_wo2[wi_]
                    for q4 in range(4):
                        kb.dma("pool", wo[:, 4 * q4:4 * q4 + 4, :], w_out[:, 4 * q4:4 * q4 + 4, dt_ * 512:(dt_ + 1) * 512],
                               writes=[b_wo])
                    for kc in range(16):
                        if kc < 8:
                            lh = poolT[:, kc, tt * 128:(tt + 1) * 128]
                            rb = b_poolT[tg]
                        else:
                            lh = attnT[:, kc - 8, tt * 128:(tt + 1) * 128]
                            rb = b_attnT[kc - 8][tg]
                        kb.op("pe", lambda e, lh=lh, kc=kc, dt_=dt_, bk=bk, wo=wo: e.matmul(
                            ps[:, bk, :], lhsT=lh, rhs=wo[:, kc, :],
                            start=(kc == 0), stop=(kc == 15)),
                            reads=[rb, b_wo], writes=[PB[bk]], sig=(kc == 15))
                    kb.op("dve", lambda e, xi=xi, dt_=dt_, bk=bk: e.scalar_tensor_tensor(
                        out=hpre[:, dt_ * 512:(dt_ + 1) * 512], in0=xt[xi][:, dt_ * 512:(dt_ + 1) * 512], scalar=ALPHA,
                        in1=ps[:, bk, :], op0=ALU.mult, op1=ALU.add),
                        reads=[b_xt[xi], PB[bk]], writes=[b_hpre])
                hi = tt % 2
                layer_norm(None, hpre[:, :], b_hpre, gb[:, 0, :], gb[:, 1, :], b_gb, hh[hi][:, :], b_hh[hi],
                           xc, b_xc, junkf, b_jf, st, b_st)
                kb.dma("sp", h_tok[tt * 128:(tt + 1) * 128, :], hh[hi][:, :], reads=[b_hh[hi]], writes=[b_htok[tt]])
                for c0 in range(0, 16, 4):
                    bk = next_bank([3, 4, 5, 6])
                    for cc in range(4):
                        kb.op("pe", lambda e, c0=c0, cc=cc, bk=bk, hi=hi: e.transpose(
                            ps[:, bk, cc * 128:(cc + 1) * 128], hh[hi][:, (c0 + cc) * 128:(c0 + cc + 1) * 128], ident),
                            reads=[b_hh[hi], b_cst], writes=[PB[bk]], sig=(cc == 3))
                    evac(hTs[:, c0:c0 + 4, :],
                         ps[:, bk, :].rearrange("p (c q) -> p c q", c=4), [PB[bk]], [b_hTs])
                for q4 in range(4):
                    kb.dma("sp", hT_d[:, 4 * q4:4 * q4 + 4, tt * 128:(tt + 1) * 128], hTs[:, 4 * q4:4 * q4 + 4, :],
                           reads=[b_hTs], writes=[b_hT_d[tg]])
        pers.close()
        if stop_after == "E":
            t1 = kb.dma("sp", dbg_out[:, 0:2048], h_tok[0:128, :], reads=[b_htok[0]])
            t2 = kb.dma("sp", dbg_out[:, 2048:4096], h_tok[1920:2048, :], reads=[b_htok[15]])
            kb.wait("sp", t1)
            kb.wait("sp", t2)
            return nc

        kb.barrier()
        b_W1 = bufs(32, "W1")
        with ExitStack() as sc:
            sbF = lambda name, shape, dt=F32: sc.enter_context(nc.sbuf_tensor(name, list(shape), dt))
            wqc = [sbF("wqc%d" % i, [128, 16, 128], BF16) for i in range(3)]
            b_wqc = bufs(3, "wqc")
            wq_rr = [0]
            skT = sbF("skT", [128, 2, 128], BF16)
            b_skT = Buf("skT")
            kb.dma("pool", skT[:], subkT, writes=[b_skT])
            hTg = [sbF("hTg0", [128, 16, 512], BF16)] * 2
            b_hTg = [Buf("hTg")] * 2
            qpT = sbF("qpT", [128, 16, 512], BF16)
            b_qpT = Buf("qpT")
            s_sb = sbF("s_sb", [128, 16, 128])
            s2_sb = sbF("s2_sb", [128, 16, 128])
            b_s, b_s2 = Buf("s"), Buf("s2")
            v16 = sbF("v16", [128, 16, 16])
            i16 = sbF("i16", [128, 16, 16], U32)
            i16f = sbF("i16f", [128, 16, 16])
            b_v16, b_i16, b_i16f = Buf("v16"), Buf("i16"), Buf("i16f")
            cand = sbF("cand", [128, 8, 256])
            cand2 = sbF("cand2", [128, 8, 256])
            b_cand, b_cand2 = Buf("cand"), Buf("cand2")
            best = sbF("best", [128, 8, 16])
            bidx = sbF("bidx", [128, 8, 16], U32)
            aiu = sbF("aiu", [128, 128], U32)
            biu = sbF("biu", [128, 128], U32)
            af = sbF("af", [128, 128])
            bf = sbF("bf", [128, 128])
            b_best, b_bidx, b_bidf, b_af, b_bf = Buf("best"), Buf("bidx"), Buf("bidf"), Buf("af"), Buf("bf")
            eb = sbF("eb", [128, 8, 16])
            zz = sbF("zz", [128, 16])
            b_eb, b_zz = Buf("eb"), Buf("zz")
            oh = sbF("oh", [128, 128, 16])
            b_oh = Buf("oh")
            IG = sbF("IG", [128, 3, 128])
            b_IG = Buf("IG")
            IGT = sbF("IGT", [128, 3, 128])
            b_IGT = Buf("IGT")
            eq2 = [sbF("eq0", [128, 32, 128], BF16)] * 2
            b_eq2 = [Buf("eq")] * 2
            At2 = [sbF("At0", [128, 32, 128], BF16)] * 2
            Bt2 = [sbF("Bt0", [128, 32, 128], BF16)] * 2
            b_At2, b_Bt2 = [Buf("At")] * 2, [Buf("Bt")] * 2
            ab_rr = [0]
            stg = [sbF("stg0", [128, 128, 64], BF16)] * 2
            b_stg = [Buf("stg")] * 2
            for tg in range(4):
                hx = hTg[tg % 2]
                bhx = b_hTg[tg % 2]
                for q4 in range(4):
                    kb.dma("sp", hx[:, 4 * q4:4 * q4 + 4, :], hT_d[:, 4 * q4:4 * q4 + 4, tg * 512:(tg + 1) * 512],
                           reads=[b_hT_d[tg]], writes=[bhx])
                for n in range(16):
                    wi_ = wq_rr[0] % 3
                    wq_rr[0] += 1
                    for q4 in range(4):
                        kb.dma("pool", wqc[wi_][:, 4 * q4:4 * q4 + 4, :], wq[:, 4 * q4:4 * q4 + 4, n * 128:(n + 1) * 128],
                               writes=[b_wqc[wi_]])
                    proj_fm(lambda kc, wi_=wi_: wqc[wi_][:, kc, :], b_wqc[wi_], hx, bhx, qpT[:, n, :], [b_qpT])
                for tt in range(4):
                    T = 4 * tg + tt
                    for n4 in range(4):
                        bk = next_bank([3, 4, 5, 6])
                        for nn in range(4):
                            n = 4 * n4 + nn
                            kb.op("pe", lambda e, n=n, nn=nn, bk=bk, tt=tt: e.matmul(
                                ps[:, bk, nn * 128:(nn + 1) * 128], lhsT=qpT[:, n, tt * 128:(tt + 1) * 128],
                                rhs=skT[:, n % 2, :], start=True, stop=True),
                                reads=[b_qpT, b_skT], writes=[PB[bk]], sig=(nn == 3))
                        evac(s_sb[:, 4 * n4:4 * n4 + 4, :], ps[:, bk, :].rearrange("p (c q) -> p c q", c=4),
                             [PB[bk]], [b_s])
                    bv, bi, bs2 = bufs(16, "v16n"), bufs(16, "i16n"), bufs(16, "s2n")
                    for n in range(16):
                        kb.op("dve", lambda e, n=n: e.max(out=v16[:, n, 0:8], in_=s_sb[:, n, :]),
                              reads=[b_s], writes=[bv[n]], deps=[b_v16.w] + list(b_v16.r.values()))
                    for n in range(16):
                        kb.op("dve", lambda e, n=n: e.max_index(out=i16[:, n, 0:8], in_max=v16[:, n, 0:8],
                                                               in_values=s_sb[:, n, :]),
                              reads=[b_s, bv[n]], writes=[bi[n]], deps=[b_i16.w] + list(b_i16.r.values()))
                    for n in range(16):
                        kb.op("dve", lambda e, n=n: e.match_replace(out=s2_sb[:, n, :], in_to_replace=v16[:, n, 0:8],
                                                                   in_values=s_sb[:, n, :], imm_value=NEG),
                              reads=[b_s, bv[n]], writes=[bs2[n]])
                    for n in range(16):
                        kb.op("dve", lambda e, n=n: e.max(out=v16[:, n, 8:16], in_=s2_sb[:, n, :]),
                              reads=[bs2[n]], writes=[bv[n]])
                    for n in range(16):
                        kb.op("dve", lambda e, n=n: e.max_index(out=i16[:, n, 8:16], in_max=v16[:, n, 8:16],
                                                               in_values=s2_sb[:, n, :]),
                              reads=[bs2[n], bv[n]], writes=[bi[n]])
                    kb.op("dve", lambda e: e.tensor_copy(out=i16f[:], in_=i16[:]), reads=bi, writes=[b_i16f, b_i16])
                    kb.op("dve", lambda e: e.tensor_tensor(
                        out=cand[:].rearrange("p h (a b) -> p h a b", a=16),
                        in0=sap(v16, [[32, 8], [1, 16], [0, 16]]),
                        in1=sap(v16, [[32, 8], [0, 16], [1, 16]], off=16), op=ALU.add),
                        reads=bv, writes=[b_cand, b_v16])
                    bb, bx_, bc2 = bufs(8, "besth"), bufs(8, "bidxh"), bufs(8, "cand2h")
                    for h in range(8):
                        kb.op("dve", lambda e, h=h: e.max(out=best[:, h, 0:8], in_=cand[:, h, :]),
                              reads=[b_cand], writes=[bb[h]], deps=[b_best.w] + list(b_best.r.values()))
                    for h in range(8):
                        kb.op("dve", lambda e, h=h: e.max_index(out=bidx[:, h, 0:8], in_max=best[:, h, 0:8],
                                                               in_values=cand[:, h, :]),
                              reads=[b_cand, bb[h]], writes=[bx_[h]], deps=[b_bidx.w] + list(b_bidx.r.values()))
                    for h in range(8):
                        kb.op("dve", lambda e, h=h: e.match_replace(out=cand2[:, h, :], in_to_replace=best[:, h, 0:8],
                                                                   in_values=cand[:, h, :], imm_value=NEG),
                              reads=[b_cand, bb[h]], writes=[bc2[h]])
                    for h in range(8):
                        kb.op("dve", lambda e, h=h: e.max(out=best[:, h, 8:16], in_=cand2[:, h, :]),
                              reads=[bc2[h]], writes=[bb[h]])
                    for h in range(8):
                        kb.op("dve", lambda e, h=h: e.max_index(out=bidx[:, h, 8:16], in_max=best[:, h, 8:16],
                                                               in_values=cand2[:, h, :]),
                              reads=[bc2[h], bb[h]], writes=[bx_[h]])
                    kb.op("dve", lambda e: e.tensor_copy(out=zz[:, 0:1], in_=best[:, 0, 0:1]),
                          reads=bb + bx_, writes=[b_best, b_bidx, b_zz])
                    kb.op("dve", lambda e: e.tensor_tensor(out=eb[:], in0=best[:], in1=sap(best, [[16, 8], [0, 16]]),
                                                           op=ALU.subtract), reads=[b_best], writes=[b_eb])
                    kb.op("act", lambda e: e.activation(out=eb[:], in_=eb[:], func=AF.Exp), writes=[b_eb])
                    kb.op("dve", lambda e: e.tensor_reduce(out=zz[:, 0:8], in_=eb[:], axis=AX.X, op=ALU.add),
                          reads=[b_eb], writes=[b_zz])
                    kb.op("dve", lambda e: e.reciprocal(out=zz[:, 8:16], in_=zz[:, 0:8]), writes=[b_zz])
                    kb.op("dve", lambda e: e.tensor_tensor(
                        out=IG[:, 2, :].rearrange("p (h r) -> p h r", h=8), in0=eb[:],
                        in1=sap(zz, [[1, 8], [0, 16]], off=8), op=ALU.mult),
                        reads=[b_eb, b_zz], writes=[b_IG])
                    kb.op("dve", lambda e: e.tensor_single_scalar(out=aiu[:], in_=bidx[:].rearrange("p h r -> p (h r)"),
                                                                  scalar=4, op=ALU.logical_shift_right),
                          reads=[b_bidx], writes=[b_bidf])
                    kb.op("dve", lambda e: e.tensor_single_scalar(out=biu[:], in_=bidx[:].rearrange("p h r -> p (h r)"),
                                                                  scalar=15, op=ALU.bitwise_and),
                          reads=[b_bidx], writes=[b_bidf])
                    kb.op("dve", lambda e: e.tensor_copy(out=af[:], in_=aiu[:]), reads=[b_bidf], writes=[b_af])
                    kb.op("dve", lambda e: e.tensor_copy(out=bf[:], in_=biu[:]), reads=[b_bidf], writes=[b_bf])
                    for which, sel, boff in ((0, af, 0), (1, bf, 16)):
                        bsel = b_af if which == 0 else b_bf
                        kb.op("dve", lambda e, sel=sel: e.tensor_tensor(
                            out=oh[:], in0=sap(cst, [[0, 128], [1, 16]], off=C_IOTA),
                            in1=sap(sel, [[1, 128], [0, 16]]), op=ALU.is_equal),
                            reads=[bsel, b_cst], writes=[b_oh])
                        kb.op("dve", lambda e, boff=boff: e.tensor_tensor(
                            out=oh[:].rearrange("p (h r) a -> p h r a", h=8),
                            in0=oh[:].rearrange("p (h r) a -> p h r a", h=8),
                            in1=sap(i16f, [[32, 8], [0, 16], [1, 16]], off=boff), op=ALU.mult),
                            reads=[b_i16f], writes=[b_oh])
                        kb.op("dve", lambda e, which=which: e.tensor_reduce(out=IG[:, which, :], in_=oh[:], axis=AX.X,
                                                                          op=ALU.add),
                              reads=[b_oh], writes=[b_IG])
                    bk = next_bank([3, 4, 5, 6])
                    for w3 in range(3):
                        kb.op("pe", lambda e, w3=w3, bk=bk: e.transpose(ps[:, bk, w3 * 128:(w3 + 1) * 128],
                                                                        IG[:, w3, :], ident),
                              reads=[b_IG, b_cst], writes=[PB[bk]], sig=(w3 == 2))
                    kb.op("act", lambda e, bk=bk: e.activation(out=IGT[:], in_=ps[:, bk, 0:384].rearrange(
                        "p (c q) -> p c q", c=3), func=AF.Copy), reads=[PB[bk]], writes=[b_IGT])
                    for sbk in range(2):
                        si = (2 * T + sbk) % 2
                        for s32 in range(2):
                            t0 = sbk * 64 + s32 * 32
                            ai_ = ab_rr[0] % 2
                            ab_rr[0] += 1
                            eq, b_eq = eq2[ai_], b_eq2[ai_]
                            At, b_At = At2[ai_], b_At2[ai_]
                            Bt, b_Bt = Bt2[ai_], b_Bt2[ai_]
                            kb.op("dve", lambda e, t0=t0: e.tensor_tensor(
                                out=eq[:], in0=sap(cst, [[0, 32], [1, 128]], off=C_IOTA),
                                in1=sap(IGT, [[1, 32], [0, 128]], off=0 * 128 + t0), op=ALU.is_equal),
                                reads=[b_IGT, b_cst], writes=[b_eq])
                            kb.op("dve", lambda e, t0=t0: e.tensor_tensor(
                                out=At[:], in0=eq[:], in1=sap(IGT, [[1, 32], [0, 128]], off=2 * 128 + t0), op=ALU.mult),
                                reads=[b_eq, b_IGT], writes=[b_At])
                            kb.op("dve", lambda e, t0=t0: e.tensor_tensor(
                                out=Bt[:], in0=sap(cst, [[0, 32], [1, 128]], off=C_IOTA),
                                in1=sap(IGT, [[1, 32], [0, 128]], off=1 * 128 + t0), op=ALU.is_equal),
                                reads=[b_IGT, b_cst], writes=[b_Bt])
                            for t4 in range(8):
                                bk = next_bank([0, 1, 2, 7])
                                for q4 in range(4):
                                    tl = 4 * t4 + q4
                                    kb.op("pe", lambda e, tl=tl, q4=q4, bk=bk: e.matmul(
                                        ps[:, bk, q4 * 128:(q4 + 1) * 128], lhsT=At[:, tl, :], rhs=Bt[:, tl, :],
                                        start=True, stop=True),
                                        reads=[b_At, b_Bt], writes=[PB[bk]], sig=(q4 == 3))
                                kb.op("act", lambda e, t4=t4, bk=bk, si=si, s32=s32: e.activation(
                                    out=sap(stg[si], [[1, 4], [64, 128]], off=s32 * 32 + 4 * t4),
                                    in_=ps[:, bk, :].rearrange("p (t j) -> p t j", t=4), func=AF.Copy),
                                    reads=[PB[bk]], writes=[b_stg[si]])
                        kb.dma("sp", W1[2 * T + sbk], stg[si][:], reads=[b_stg[si]], writes=[b_W1[2 * T + sbk]])
                    if stop_after == "F2" and T == 0:
                        wchk = sbF("wchk", [128, 1024], BF16)
                        b_wchk = Buf("wchk")
                        kb.dma("sp", wchk[:], W1[0, :, 0:16, :].rearrange("i j t -> i (j t)"), reads=[b_W1[0]], writes=[b_wchk])
                        dump([(wchk[:, :], [b_wchk], 1024), (IG[:, 0, :], [b_IG], 128), (IG[:, 1, :], [b_IG], 128),
                              (IG[:, 2, :], [b_IG], 128)])
                        return nc
                    if stop_after == "F" and T == 0:
                        dump([(IG[:, 0, :], [b_IG], 128), (IG[:, 1, :], [b_IG], 128), (IG[:, 2, :], [b_IG], 128),
                              (s_sb[:, 0, :], [b_s], 128), (s_sb[:, 1, :], [b_s], 128)])
                        return nc

        kb.barrier()
        if stop_after == "Fend":
            kb.barrier()
            return nc
        NJ = 4
        NR = 0 if stop_after == "G4" else (2 if stop_after in ("G2", "G3") else 128 // NJ)
        for half in range(2):
            kb.barrier()
            with ExitStack() as sc:
                sbG = lambda name, shape, dt=F32: sc.enter_context(nc.sbuf_tensor(name + "_h%d" % half, list(shape), dt))
                hTh = sbG("hTh", [128, 16, 1024], BF16)
                b_hTh = Buf("hTh")
                for tg2 in range(2):
                    tg = 2 * half + tg2
                    for q4 in range(4):
                        kb.dma("sp", hTh[:, 4 * q4:4 * q4 + 4, tg2 * 512:(tg2 + 1) * 512],
                               hT_d[:, 4 * q4:4 * q4 + 4, tg * 512:(tg + 1) * 512],
                               reads=[b_hT_d[tg]], writes=[b_hTh])
                accG = sbG("accG", [128, 8, D])
                b_accG = [bufs(2, "accG%d" % t) for t in range(8)]
                with ExitStack() as sc2:
                    sbH = lambda name, shape, dt=F32: sc2.enter_context(nc.sbuf_tensor(name + "_h%d" % half, list(shape), dt))
                    Wr = [sbH("Wr%d" % i, [128, 16, NJ, 64], BF16) for i in range(2)]
                    b_Wr = bufs(2, "Wr")
                    vr = [sbH("vr%d" % i, [128, NJ, D], BF16) for i in range(2)]
                    b_vr = bufs(2, "vr")
                    uj = [sbH("uj%d" % i, [128, 16, 128], BF16) for i in range(2)]
                    b_uj = bufs(2, "uj")
                    Gr = [sbH("Gr%d" % i, [128, NJ, 1024], BF16) for i in range(2)]
                    b_Gr = bufs(2, "Gr")
                    ga = [sbH("ga%d" % i, [128, 512], BF16) for i in range(2)]
                    b_ga = bufs(2, "ga")
                    urr = [0]
                    grr = [0]

                    def phaseA(r):
                        ri = r % 2
                        j0 = r * NJ
                        for q4 in range(4):
                            b0_ = 16 * half + 4 * q4
                            kb.dma("sp", Wr[ri][:, 4 * q4:4 * q4 + 4], W1[b0_:b0_ + 4, :, j0:j0 + NJ, :].rearrange(
                                "b i j t -> i b j t"), reads=b_W1[b0_:b0_ + 4], writes=[b_Wr[ri]])
                        kb.dma("pool", vr[ri][:], vL[j0:j0 + NJ].rearrange("j i d -> i j d"), writes=[b_vr[ri]])
                        for jj in range(NJ):
                            ui = urr[0] % 2
                            urr[0] += 1
                            kb.dma("pool", uj[ui][:], uT[j0 + jj], writes=[b_uj[ui]])
                            for tg2 in range(2):
                                bk = next_bank([0, 1])
                                for kc in range(16):
                                    kb.op("pe", lambda e, kc=kc, ui=ui, tg2=tg2, bk=bk: e.matmul(
                                        ps[:, bk, :], lhsT=uj[ui][:, kc, :], rhs=hTh[:, kc, tg2 * 512:(tg2 + 1) * 512],
                                        start=(kc == 0), stop=(kc == 15)),
                                        reads=[b_uj[ui], b_hTh], writes=[PB[bk]], sig=(kc == 15))
                                gi = grr[0] % 2
                                grr[0] += 1
                                kb.op("act", lambda e, gi=gi, bk=bk: e.activation(out=ga[gi][:, :], in_=ps[:, bk, :],
                                                                                 func=AF.Gelu),
                                      reads=[PB[bk]], writes=[b_ga[gi]])
                                kb.op("dve", lambda e, gi=gi, ri=ri, jj=jj, tg2=tg2: e.tensor_tensor(
                                    out=Gr[ri][:, jj, tg2 * 512:(tg2 + 1) * 512].rearrange("p (b t) -> p b t", b=8),
                                    in0=ga[gi][:, :].rearrange("p (b t) -> p b t", b=8),
                                    in1=Wr[ri][:, tg2 * 8:(tg2 + 1) * 8, jj, :], op=ALU.mult),
                                    reads=[b_ga[gi], b_Wr[ri]], writes=[b_Gr[ri]])

                    def phaseB(r):
                        ri = r % 2
                        for tt in range(8):
                            for dh in range(2):
                                b0 = 2 + 2 * ((tt * 2 + dh) % 3)
                                for jj in range(NJ):
                                    for dq in range(2):
                                        kb.op("pe", lambda e, jj=jj, dq=dq, tt=tt, dh=dh, b0=b0: e.matmul(
                                            ps[:, b0 + dq, :], lhsT=Gr[ri][:, jj, tt * 128:(tt + 1) * 128],
                                            rhs=vr[ri][:, jj, dh * 1024 + dq * 512:dh * 1024 + (dq + 1) * 512],
                                            start=(jj == 0), stop=(jj == NJ - 1)),
                                            reads=[b_Gr[ri], b_vr[ri]], writes=[PB[b0 + dq]],
                                            sig=(jj == NJ - 1 and dq == 1))
                                pin = ps[:, b0:b0 + 2, :].rearrange("p a b -> p (a b)")
                                aout = accG[:, tt, dh * 1024:(dh + 1) * 1024]
                                if r == 0:
                                    kb.op("dve", lambda e, pin=pin, aout=aout: e.tensor_copy(out=aout, in_=pin),
                                          reads=[PB[b0], PB[b0 + 1]], writes=[b_accG[tt][dh]])
                                else:
                                    kb.op("dve", lambda e, pin=pin, aout=aout: e.tensor_tensor(
                                        out=aout, in0=aout, in1=pin, op=ALU.add),
                                        reads=[PB[b0], PB[b0 + 1]], writes=[b_accG[tt][dh]])

                    if stop_after == "G1":
                        phaseA(0)
                        phaseB(0)
                        dump([(accG[:, 0, 0:1024], b_accG[0], 1024), (accG[:, 7, 1024:2048], b_accG[7], 1024),
                              (Gr[0][:, 0, :], [b_Gr[0]], 1024), (Gr[0][:, 3, :], [b_Gr[0]], 1024)])
                        return nc
                    if NR > 0:
                        phaseA(0)
                    for r in range(NR):
                        if r + 1 < NR:
                            phaseA(r + 1)
                        phaseB(r)
                kb.barrier()
                if stop_after == "G3":
                    dump([(accG[:, 0, 0:1024], b_accG[0], 1024), (accG[:, 7, 1024:2048], b_accG[7], 1024)])
                    return nc
                with ExitStack() as sc3:
                    sbL = lambda name, shape, dt=F32: sc3.enter_context(nc.sbuf_tensor(name + "_h%d" % half, list(shape), dt))
                    gb2 = sbL("gb2", [128, 2, D])
                    b_gb2 = Buf("gb2")
                    kb.dma("sp", gb2[:, 0, :], lnp[2], writes=[b_gb2])
                    kb.dma("sp", gb2[:, 1, :], lnp[3], writes=[b_gb2])
                    hres = [sbL("hres%d" % i, [128, D]) for i in range(2)]
                    b_hres = bufs(2, "hres")
                    xc2 = sbL("xc2", [128, D])
                    b_xc2 = Buf("xc2")
                    jf2 = sbL("jf2", [128, D])
                    b_jf2 = Buf("jf2")
                    yo = [sbL("yo%d" % i, [128, D]) for i in range(2)]
                    b_yo = bufs(2, "yo")
                    st2 = sbL("st2", [128, 8])
                    b_st2 = Buf("st2")
                    outs = []
                    for tt in range(8):
                        T = 8 * half + tt
                        hi = tt % 2
                        kb.dma("sp", hres[hi][:, :], h_tok[T * 128:(T + 1) * 128, :], reads=[b_htok[T]],
                               writes=[b_hres[hi]])
                        kb.op("dve", lambda e, hi=hi, tt=tt: e.scalar_tensor_tensor(
                            out=hres[hi][:, :], in0=hres[hi][:, :], scalar=ALPHA, in1=accG[:, tt, :],
                            op0=ALU.mult, op1=ALU.add),
                            reads=b_accG[tt], writes=[b_hres[hi]])
                        layer_norm(None, hres[hi][:, :], b_hres[hi], gb2[:, 0, :], gb2[:, 1, :], b_gb2,
                                   yo[hi][:, :], b_yo[hi], xc2, b_xc2, jf2, b_jf2, st2, b_st2)
                        outs.append(kb.dma("sp", y[T * 128:(T + 1) * 128, :], yo[hi][:, :], reads=[b_yo[hi]]))
                    for t in outs:
                        kb.wait("sp", t)
                    if stop_after in ("G2", "G4"):
                        dump([(accG[:, 0, 0:1024], b_accG[0], 1024), (accG[:, 7, 1024:2048], b_accG[7], 1024),
                              (yo[1][:, 0:1024], [b_yo[1]], 1024), (hres[1][:, 0:1024], [b_hres[1]], 1024)])
                        return nc
        print("instructions:", kb.nins, "sbuf remaining:", nc.sbuf_bytes_remaining)
    return nc


def _t5_bucket(dist):
    dist = np.asarray(dist)
    d = np.maximum(dist, 1).astype(np.float32)
    large = 16 + (np.log(d / np.float32(16)) / np.float32(np.log(128 / 16)) * np.float32(16)).astype(np.int32)
    large = np.minimum(large, 31)
    return np.where(dist < 16, dist, large)


def _prep_shared(w_in, pool_w, pool_scale, rel_bias, w_out, ln1_g, ln1_b, peer_wq, peer_subkeys, peer_u, peer_v,
                 ln2_g, ln2_b):
    f = np.float32
    w = w_in[0]
    cols = []
    for c in range(8):
        cols.append(w[:, c * 128:(c + 1) * 128])
    for c in range(8):
        cols.append(w[:, 1024 + c * 128:1024 + (c + 1) * 128])
    for c in range(8):
        cols.append(w[:, 2048 + c * 128:2048 + (c + 1) * 128])
    for c in range(8):
        cols.append(w[:, 4096 + c * 128:4096 + (c + 1) * 128])
    ki = w[:, 5120:5184]
    cols.append(np.concatenate([ki, ki], axis=1))
    w_fm = np.stack([c.reshape(16, 128, 128).transpose(1, 0, 2) for c in cols]).astype(f)
    wv = w[:, 3072:4096]
    w_v = np.stack([wv[:, hg * 256:(hg + 1) * 256].reshape(16, 128, 256).transpose(1, 0, 2) for hg in range(4)])
    w_wi = w[:, 5184:5200].reshape(16, 128, 16).transpose(1, 0, 2)
    pw = pool_w[0].reshape(4, 2, 128, 256).transpose(2, 0, 1, 3)
    kk = np.arange(128)[:, None]
    qq = np.arange(128)[None, :]
    bt = np.zeros((128, 2, 8, 128), f)
    for dl in range(2):
        bkt = _t5_bucket(np.maximum(dl * 128 + qq - kk, 0))
        bt[:, dl, :, :] = rel_bias[bkt].transpose(0, 2, 1)
    wo = w_out[0].reshape(16, 128, D).transpose(1, 0, 2)
    lnp = np.stack([np.broadcast_to(a[0][None, :], (128, D)) for a in (ln1_g, ln1_b, ln2_g, ln2_b)])
    wqh = peer_wq[0].reshape(16, 128, D).transpose(1, 0, 2)
    skT = peer_subkeys[0].transpose(2, 0, 1)
    u = peer_u[0].reshape(128, 128, 16, 128)
    uT = u.transpose(1, 3, 2, 0)
    vv = peer_v[0].reshape(128, 128, D).transpose(1, 0, 2)
    c = lambda a: np.ascontiguousarray(a, dtype=f)
    return dict(w_fm=c(w_fm), w_v=c(w_v), w_wi=c(w_wi), pool_w=c(pw), biasT=c(bt), w_out=c(wo), lnp=c(lnp),
                wq=c(wqh), subkT=c(skT), uT=c(uT), vL=c(vv))


def _consts(hf, pool_scale, rel_bias):
    f = np.float32
    cst = np.zeros((128, 1024), f)
    cst[:, 0:128] = np.eye(128, dtype=f)
    cst[:, 128:256] = np.arange(128, dtype=f)[None, :]
    qq = np.arange(128)[:, None]
    kk = np.arange(128)[None, :]
    cst[:, 256:384] = np.where(kk <= qq, 0.0, NEG).astype(f)
    valid = 1.0 if hf == 1 else 0.0
    cst[:, 384] = valid
    cst[:, 385] = (valid - 1.0) * 1.0e30
    cst[:, 392:400] = rel_bias[31][None, :]
    for gq, wwin in enumerate((2, 4, 8, 16)):
        pos = np.arange(16)
        if hf == 0:
            corr = wwin / np.minimum(pos + 1, wwin).astype(f)
        else:
            corr = np.ones(16, f)
        cst[:, 400 + 16 * gq:400 + 16 * gq + 16] = corr[None, :]
    cst[:, 464:472] = pool_scale[0].reshape(8, 128).T
    cst[:, 480:512] = (2.0 ** -np.arange(32, dtype=np.float64)).astype(f)[None, :]
    return cst


def _core_inputs(x, shared, pool_scale, rel_bias):
    in_maps = []
    for c in range(8):
        b, hf = c // 2, c % 2
        own = x[b, hf * TOK:(hf + 1) * TOK]
        prev = x[b, 0:TOK] if hf == 1 else np.zeros_like(own)
        xT = np.stack([prev.T.reshape(16, 128, TOK).transpose(1, 0, 2), own.T.reshape(16, 128, TOK).transpose(1, 0, 2)])
        m = dict(shared)
        m["xT"] = np.ascontiguousarray(xT, dtype=np.float32)
        m["x_tok"] = np.ascontiguousarray(own, dtype=np.float32)
        m["consts"] = _consts(hf, pool_scale, rel_bias)
        in_maps.append(m)
    return in_maps


def kernel(x, w_in, pool_w, pool_scale, rel_bias, w_out, ln1_g, ln1_b, peer_wq, peer_subkeys, peer_u, peer_v,
           ln2_g, ln2_b):
    args = [np.asarray(a, dtype=np.float32) for a in (x, w_in, pool_w, pool_scale, rel_bias, w_out, ln1_g, ln1_b,
                                                      peer_wq, peer_subkeys, peer_u, peer_v, ln2_g, ln2_b)]
    (x, w_in, pool_w, pool_scale, rel_bias, w_out, ln1_g, ln1_b, peer_wq, peer_subkeys, peer_u, peer_v,
     ln2_g, ln2_b) = args
    shared = _prep_shared(w_in, pool_w, pool_scale, rel_bias, w_out, ln1_g, ln1_b, peer_wq, peer_subkeys,
                          peer_u, peer_v, ln2_g, ln2_b)
    in_maps = _core_inputs(x, shared, pool_scale, rel_bias)
    nc = build_nc()
    res = run_bass_kernel_spmd(nc, in_maps, core_ids=list(range(8)))
    out = np.zeros((4, S, D), np.float32)
    for c in range(8):
        b, hf = c // 2, c % 2
        out[b, hf * TOK:(hf + 1) * TOK] = res.results[c]["y"]
    return out
```

```python
import numpy as np
from contextlib import ExitStack
import concourse.bass as bass
import concourse.mybir as mybir
from concourse.bass_utils import run_bass_kernel_spmd

F32 = mybir.dt.float32
BF16 = mybir.dt.bfloat16
U32 = mybir.dt.uint32
ALU = mybir.AluOpType
AF = mybir.ActivationFunctionType
AX = mybir.AxisListType

D = 2048
S = 4096
TOK = 2048
NEG = -1.0e30
ALPHA = 2.0 ** 0.25
LN_EPS = 1e-5
NIT = 16
TOPK = 256
ATT_SCALE = 128.0 ** -0.5
NSLOT = 6


class Buf:
    __slots__ = ("w", "r", "name")

    def __init__(self, name=""):
        self.w = None
        self.r = {}
        self.name = name


class KB:
    def __init__(self, nc, es):
        self.nc = nc
        self.engs = {"pe": nc.tensor, "dve": nc.vector, "act": nc.scalar, "pool": nc.gpsimd, "sp": nc.sync}
        self.psem = {e: es.enter_context(nc.semaphore("prog_" + e)) for e in ["pe", "dve", "act", "pool"]}
        self.cnt = {e: 0 for e in self.psem}
        self.seen = {e: {} for e in self.engs}
        self.pending = {e: [] for e in self.engs}
        self.dslots = {q: [[es.enter_context(nc.semaphore("dq_%s_%d" % (q, i))), 0, "dq_%s_%d" % (q, i)]
                           for i in range(NSLOT)] for q in ["sp", "pool"]}
        self.dnext = {q: 0 for q in self.dslots}
        self.nins = 0

    def wait(self, e, tok):
        if tok is None:
            return
        sem, val, key = tok
        if self.seen[e].get(key, 0) >= val:
            return
        self.engs[e].wait_ge(sem, val)
        self.seen[e][key] = val

    def _deps(self, e, reads, writes, deps):
        for b in reads:
            self.wait(e, b.w)
        for b in writes:
            self.wait(e, b.w)
            for t in b.r.values():
                self.wait(e, t)
        for t in deps:
            self.wait(e, t)

    def op(self, e, fn, reads=(), writes=(), deps=(), sig=True):
        self._deps(e, reads, writes, deps)
        ins = fn(self.engs[e])
        self.nins += 1
        if not sig:
            self.pending[e].append((list(reads), list(writes)))
            return None
        self.cnt[e] += 1
        ins.then_inc(self.psem[e], 1)
        key = "prog_" + e
        tok = (self.psem[e], self.cnt[e], key)
        allr = list(reads)
        allw = list(writes)
        for (r, w) in self.pending[e]:
            allr += r
            allw += w
        self.pending[e] = []
        for b in allw:
            b.w = tok
            b.r = {}
        for b in allr:
            if b not in allw:
                b.r[key] = tok
        return tok

    def barrier(self):
        toks = []
        for e in self.psem:
            if self.cnt[e] > 0:
                toks.append((self.psem[e], self.cnt[e], "prog_" + e))
        for q in self.dslots:
            for sem, cnt, key in self.dslots[q]:
                if cnt > 0:
                    toks.append((sem, cnt, key))
        for e in self.engs:
            for t in toks:
                self.wait(e, t)

    def dma(self, q, out, in_, reads=(), writes=(), deps=()):
        self._deps(q, reads, writes, deps)
        slot = self.dslots[q][self.dnext[q]]
        self.dnext[q] = (self.dnext[q] + 1) % NSLOT
        sem, cnt, key = slot
        if cnt > 0:
            self.wait(q, (sem, cnt, key))
        self.engs[q].dma_start(out=out, in_=in_).then_inc(sem, 16)
        self.nins += 1
        slot[1] = cnt + 16
        tok = (sem, cnt + 16, key)
        for b in writes:
            b.w = tok
            b.r = {}
        for b in reads:
            b.r[key] = tok
        return tok


def sap(t, dims, off=0, parts=128, p0=0):
    fs = 1
    for s_ in t.shape[1:]:
        fs *= int(s_)
    return bass.AP(t, p0 * fs + off, [[fs, parts]] + [[int(a), int(b)] for a, b in dims])


def bufs(n, name=""):
    return [Buf("%s%d" % (name, i)) for i in range(n)]


def build_nc(stop_after=None, small_peer=False):
    nc = bass.Bass("TRN2", target_bir_lowering=False)
    dbg = {}

    def din(name, shape, dt=F32):
        return nc.dram_tensor(name, list(shape), dt, kind="ExternalInput").ap()

    xT = din("xT", [2, 128, 16, TOK])
    x_tok = din("x_tok", [TOK, D])
    w_fm = din("w_fm", [33, 128, 16, 128])
    w_v = din("w_v", [4, 128, 16, 256])
    w_wi = din("w_wi", [128, 16, 16])
    pool_w = din("pool_w", [128, 4, 2, 256])
    consts = din("consts", [128, 1024])
    biasT = din("biasT", [128, 2, 8, 128])
    w_out = din("w_out", [128, 16, D])
    lnp = din("lnp", [4, 128, D])
    wq = din("wq", [128, 16, D])
    subkT = din("subkT", [128, 2, 128])
    uT = din("uT", [8 if small_peer else 128, 128, 16, 128])
    vL = din("vL", [8 if small_peer else 128, 128, D])
    y = nc.dram_tensor("y", [TOK, D], F32, kind="ExternalOutput").ap()
    maskT_d = nc.dram_tensor("maskT_d", [4, 128, 32, 512], BF16).ap()
    hT_d = nc.dram_tensor("hT_d", [128, 16, TOK], BF16).ap()
    h_tok = nc.dram_tensor("h_tok", [TOK, D], F32).ap()
    W1 = nc.dram_tensor("W1", [32, 128, 128, 64], BF16).ap()
    if stop_after is not None:
        dbg_out = nc.dram_tensor("dbg", [128, 8192], F32, kind="ExternalOutput").ap()

    with ExitStack() as es:
        kb = KB(nc, es)
        sb = lambda name, shape, dt=F32: es.enter_context(nc.sbuf_tensor(name, list(shape), dt))
        ps = es.enter_context(nc.psum_tensor("ps", [128, 8, 512], F32))
        PB = bufs(8, "psb")

        dbg_stg = sb("dbg_stg", [128, 1024]) if stop_after is not None else None
        cst = sb("cst", [128, 1024])
        b_cst = Buf("cst")
        kb.dma("sp", cst[:], consts, writes=[b_cst])
        C_ID = 0
        C_IOTA = 128
        C_TRI = 256
        C_FLAG = 384
        C_B31 = 392
        C_CORR = 400
        C_PSC = 464
        C_PW2 = 480
        ident = cst[:, C_ID:C_ID + 128]
        iota = cst[:, C_IOTA:C_IOTA + 128]
        tri = cst[:, C_TRI:C_TRI + 128]
        bT = sb("bT", [128, 2, 8, 128])
        b_bT = Buf("bT")
        kb.dma("sp", bT[:], biasT, writes=[b_bT])
        ones_b = sb("ones_b", [128, 128], BF16)
        b_ones = Buf("ones")
        kb.op("pool", lambda e: e.memset(ones_b[:], 1.0), writes=[b_ones])

        evac_rr = [0]

        def evac(out, in_, reads, writes, scale=None):
            evac_rr[0] ^= 1
            if scale is not None:
                return kb.op("act", lambda e: e.activation(out=out, in_=in_, func=AF.Copy, scale=scale),
                             reads=reads, writes=writes)
            if evac_rr[0]:
                return kb.op("act", lambda e: e.activation(out=out, in_=in_, func=AF.Copy), reads=reads, writes=writes)
            return kb.op("dve", lambda e: e.tensor_copy(out=out, in_=in_), reads=reads, writes=writes)

        pb_rr = [0]

        def next_bank(choices):
            pb_rr[0] += 1
            return choices[pb_rr[0] % len(choices)]

        def dump(ap_list):
            stg = dbg_stg
            b_stg = Buf("dbgstg")
            col = 0
            for ap, bl, n in ap_list:
                kb.op("dve", lambda e, ap=ap, n=n: e.tensor_copy(out=stg[:, 0:n], in_=ap),
                      reads=bl, writes=[b_stg])
                t = kb.dma("sp", dbg_out[:, col:col + n], stg[:, 0:n], reads=[b_stg])
                kb.wait("sp", t)
                col += n

        xg_t = [None, None]
        b_xg = bufs(2, "xg")
        xg_rr = [0]

        def load_xg(s, tg):
            i = xg_rr[0]
            xg_rr[0] ^= 1
            for q4 in range(4):
                kb.dma("pool", xg_t[i][:, 4 * q4:4 * q4 + 4, :], xT[s, :, 4 * q4:4 * q4 + 4, tg * 512:(tg + 1) * 512],
                       writes=[b_xg[i]])
            return xg_t[i], b_xg[i]

        def proj_fm(wt, wb, xg, bx, out_ap, out_bufs, scale=None):
            bk = next_bank([0, 1, 2, 7])
            for kc in range(16):
                kb.op("pe", lambda e, kc=kc: e.matmul(ps[:, bk, :], lhsT=wt(kc), rhs=xg[:, kc, :],
                                                      start=(kc == 0), stop=(kc == 15)),
                      reads=[wb, bx], writes=[PB[bk]], sig=(kc == 15))
            return evac(out_ap, ps[:, bk, :], [PB[bk]], out_bufs, scale=scale)

        phA = ExitStack()
        es.enter_context(phA)
        xg_t[0] = phA.enter_context(nc.sbuf_tensor("xgA0", [128, 16, 512], BF16))
        xg_t[1] = phA.enter_context(nc.sbuf_tensor("xgA1", [128, 16, 512], BF16))
        qiT = phA.enter_context(nc.sbuf_tensor("qiT", [128, 8, TOK], BF16))
        b_qiT = bufs(4, "qiT")
        kiT = phA.enter_context(nc.sbuf_tensor("kiT", [128, 2 * TOK], BF16))
        b_kiT = bufs(8, "kiT")
        widx = phA.enter_context(nc.sbuf_tensor("widx", [128, 16, 16], F32))
        b_widx = bufs(16, "widx")
        with ExitStack() as sc:
            wqi = sc.enter_context(nc.sbuf_tensor("wqi", [128, 8, 16, 128], BF16))
            b_wqi = Buf("wqi")
            wki = sc.enter_context(nc.sbuf_tensor("wki", [128, 16, 128], BF16))
            b_wki = Buf("wki")
            wwi = sc.enter_context(nc.sbuf_tensor("wwi", [128, 16, 16], BF16))
            b_wwi = Buf("wwi")
            kb.dma("pool", wki[:], w_fm[32], writes=[b_wki])
            for c in range(8):
                kb.dma("pool", wqi[:, c], w_fm[24 + c], writes=[b_wqi])
            kb.dma("pool", wwi[:], w_wi, writes=[b_wwi])
            for s in range(2):
                for tg in range(4):
                    xg, bx = load_xg(s, tg)
                    kg = s * 4 + tg
                    proj_fm(lambda kc: wki[:, kc, :], b_wki, xg, bx, kiT[:, kg * 512:(kg + 1) * 512], [b_kiT[kg]])
                    if stop_after == "A1":
                        dump([(kiT[:, 0:512], b_kiT[0:1], 512), (xg[:, 0, :], [bx], 512)])
                        return nc
                    if s == 1:
                        for c in range(8):
                            proj_fm(lambda kc, c=c: wqi[:, c, kc, :], b_wqi, xg, bx,
                                    qiT[:, c, tg * 512:(tg + 1) * 512], [b_qiT[tg]])
                        for tt in range(4):
                            bk = next_bank([0, 1, 2, 7])
                            for kc in range(16):
                                kb.op("pe", lambda e, kc=kc, tt=tt: e.matmul(
                                    ps[:, bk, 0:16], lhsT=xg[:, kc, tt * 128:(tt + 1) * 128], rhs=wwi[:, kc, :],
                                    start=(kc == 0), stop=(kc == 15)),
                                    reads=[b_wwi, bx], writes=[PB[bk]], sig=(kc == 15))
                            evac(widx[:, tg * 4 + tt, :], ps[:, bk, 0:16], [PB[bk]], [b_widx[tg * 4 + tt]])
        if stop_after == "A":
            dump([(kiT[:, 0:2048], b_kiT[0:4], 2048), (kiT[:, 2048:4096], b_kiT[4:8], 2048),
                  (qiT[:, 0, 0:2048], b_qiT, 2048), (widx[:, :, :].rearrange("p a b -> p (a b)"), b_widx, 256)])
            return nc

        kb.barrier()
        b_maskd = bufs(4, "maskd")
        with ExitStack() as sc:
            sbB = lambda name, shape, dt=F32: sc.enter_context(nc.sbuf_tensor(name, list(shape), dt))
            score2 = [sbB("score%d" % p_, [128, 4096]) for p_ in range(2)]
            b_score2 = [bufs(8, "score%d_" % p_) for p_ in range(2)]
            maskf = sbB("maskf", [128, 4096])
            b_maskf = Buf("maskf")
            junkA = sbB("junkA", [128, 4096], BF16)
            b_junkA = Buf("junkA")
            Rt = [sbB("R%d" % i_, [128, 512]) for i_ in range(3)]
            b_R = bufs(3, "R")
            acc = sbB("accB", [128, 512])
            b_acc = Buf("acc")
            mx2 = [sbB("mxall%d" % p_, [128, 16]) for p_ in range(2)]
            b_mx2 = bufs(2, "mx")
            sm2 = [sbB("smallB%d" % p_, [128, 64]) for p_ in range(2)]
            b_sm2 = bufs(2, "small")
            b_nm2 = bufs(2, "negmid")
            b_cs2 = bufs(2, "cs")
            mT = sbB("maskTg", [128, 32, 512], BF16)
            b_mT = Buf("maskTg")
            r_rr = [0]

            def make_units(i):
                g = i // 4
                p_ = i % 2
                score, b_score, mxall, b_mx = score2[p_], b_score2[p_], mx2[p_], b_mx2[p_]
                NK = (17 + i) * 128
                nkt = (NK + 511) // 512
                units = []
                for kt in range(nkt):
                    wk = min(512, NK - kt * 512)
                    direct = kt >= 4
                    for hn, h in enumerate([0, 2, 4, 6, 8, 10, 12, 14, 1, 3, 5, 7, 9, 11, 13, 15]):
                        def unit(kt=kt, wk=wk, direct=direct, hn=hn, h=h):
                            cp, r0 = h // 2, 64 * (h % 2)
                            bk = next_bank([0, 1, 2])
                            kb.op("pe", lambda e: e.matmul(
                                ps[:, bk, 0:wk], lhsT=qiT[r0:r0 + 64, cp, i * 128:(i + 1) * 128],
                                rhs=kiT[r0:r0 + 64, kt * 512:kt * 512 + wk], start=True, stop=True),
                                reads=[b_qiT[g], b_kiT[kt]], writes=[PB[bk]])
                            ri = r_rr[0] % 3
                            r_rr[0] += 1
                            kb.op("act", lambda e: e.activation(
                                out=Rt[ri][:, 0:wk], in_=ps[:, bk, 0:wk], func=AF.Relu),
                                reads=[PB[bk]], writes=[b_R[ri]])
                            wcol = widx[:, i, h:h + 1]
                            if hn == 0:
                                kb.op("dve", lambda e: e.tensor_scalar(
                                    out=acc[:, 0:wk], in0=Rt[ri][:, 0:wk], scalar1=wcol, scalar2=None, op0=ALU.mult),
                                    reads=[b_R[ri], b_widx[i]], writes=[b_acc])
                            elif hn == 15 and direct:
                                kb.op("dve", lambda e: e.scalar_tensor_tensor(
                                    out=score[:, kt * 512:kt * 512 + wk], in0=Rt[ri][:, 0:wk], scalar=wcol,
                                    in1=acc[:, 0:wk], op0=ALU.mult, op1=ALU.add),
                                    reads=[b_R[ri], b_widx[i], b_acc], writes=[b_score[kt]])
                            else:
                                kb.op("dve", lambda e: e.scalar_tensor_tensor(
                                    out=acc[:, 0:wk], in0=Rt[ri][:, 0:wk], scalar=wcol,
                                    in1=acc[:, 0:wk], op0=ALU.mult, op1=ALU.add),
                                    reads=[b_R[ri], b_widx[i]], writes=[b_acc])
                            if hn == 15:
                                src = score[:, kt * 512:kt * 512 + wk] if direct else acc[:, 0:wk]
                                bsrc = b_score[kt] if direct else b_acc
                                kb.op("dve", lambda e: e.tensor_reduce(
                                    out=mxall[:, kt:kt + 1], in_=src, axis=AX.X, op=ALU.max),
                                    reads=[bsrc], writes=[b_mx])
                                kb.op("dve", lambda e: e.tensor_reduce(
                                    out=mxall[:, 8 + kt:9 + kt], in_=src, axis=AX.X, op=ALU.min),
                                    reads=[bsrc], writes=[b_mx])
                                if not direct:
                                    kb.op("dve", lambda e: e.tensor_scalar(
                                        out=score[:, kt * 512:kt * 512 + wk], in0=acc[:, 0:wk],
                                        scalar1=cst[:, C_FLAG:C_FLAG + 1], scalar2=cst[:, C_FLAG + 1:C_FLAG + 2],
                                        op0=ALU.mult, op1=ALU.add),
                                        reads=[b_acc, b_cst], writes=[b_score[kt]])
                        units.append(unit)
                return units

            def post_score(i):
                p_ = i % 2
                score, b_score, mxall, b_mx, sm, b_sm = score2[p_], b_score2[p_], mx2[p_], b_mx2[p_], sm2[p_], b_sm2[p_]
                NK = (17 + i) * 128
                nkt = (NK + 511) // 512
                negmid, tmpc, Mp, negD = sm[:, 0:1], sm[:, 2:3], sm[:, 3:4], sm[:, 8:8 + NIT + 1]
                kd = (NK - 128) // 512
                kb.op("dve", lambda e: e.tensor_tensor(
                    out=score[:, NK - 128:NK], in0=score[:, NK - 128:NK], in1=tri, op=ALU.add),
                    reads=[b_cst], writes=[b_score[kd]])
                kb.op("dve", lambda e: e.tensor_reduce(out=Mp, in_=mxall[:, 0:nkt], axis=AX.X, op=ALU.max),
                      reads=[b_mx], writes=[b_sm])
                kb.op("dve", lambda e: e.tensor_reduce(out=tmpc, in_=mxall[:, 8:8 + nkt], axis=AX.X, op=ALU.min),
                      reads=[b_mx], writes=[b_sm])
                kb.op("dve", lambda e: e.scalar_tensor_tensor(out=Mp, in0=tmpc, scalar=-1.0, in1=Mp, op0=ALU.mult,
                                                              op1=ALU.max), writes=[b_sm])
                kb.op("dve", lambda e: e.tensor_scalar(out=Mp, in0=Mp, scalar1=-1.001, scalar2=-1e-20,
                                                       op0=ALU.mult, op1=ALU.add), writes=[b_sm])
                kb.op("dve", lambda e: e.tensor_scalar(out=negD, in0=cst[:, C_PW2:C_PW2 + NIT + 1], scalar1=Mp,
                                                       scalar2=None, op0=ALU.mult), reads=[b_cst], writes=[b_sm])
                kb.op("dve", lambda e: e.memset(negmid, 0.0), writes=[b_nm2[p_]])

            def make_steps(i):
                p_ = i % 2
                score, b_score, sm, b_sm = score2[p_], b_score2[p_], sm2[p_], b_sm2[p_]
                NK = (17 + i) * 128
                nkt = (NK + 511) // 512
                negmid, cs, tmpc, negD = sm[:, 0:1], sm[:, 1:2], sm[:, 2:3], sm[:, 8:8 + NIT + 1]
                thr = 2.0 * TOPK - NK - 0.5
                steps = []
                for k in range(NIT):
                    def act_fn():
                        kb.op("act", lambda e: e.activation(
                            out=junkA[:, 0:NK], in_=score[:, 0:NK], func=AF.Sign, bias=negmid, scale=1.0,
                            accum_out=cs),
                            reads=b_score[0:nkt] + [b_nm2[p_]], writes=[b_junkA, b_cs2[p_]])

                    def dve_fn(k=k):
                        kb.op("dve", lambda e: e.tensor_scalar(out=tmpc, in0=cs, scalar1=thr, scalar2=0.5,
                                                               op0=ALU.is_ge, op1=ALU.subtract),
                              reads=[b_cs2[p_]], writes=[b_sm])
                        kb.op("dve", lambda e: e.scalar_tensor_tensor(
                            out=negmid, in0=tmpc, scalar=negD[:, k:k + 1], in1=negmid, op0=ALU.mult, op1=ALU.add),
                            reads=[b_sm], writes=[b_nm2[p_]])
                    steps.append((act_fn, dve_fn))
                return steps

            def finalize(i):
                g, j = i // 4, i % 4
                p_ = i % 2
                score, b_score, sm, b_sm = score2[p_], b_score2[p_], sm2[p_], b_sm2[p_]
                NK = (17 + i) * 128
                nkt = (NK + 511) // 512
                negmid, tau, negD = sm[:, 0:1], sm[:, 4:5], sm[:, 8:8 + NIT + 1]
                if j == 0:
                    kb.op("pool", lambda e: e.memset(mT[:], 0.0), writes=[b_mT])
                kb.op("dve", lambda e: e.tensor_tensor(out=tau, in0=negD[:, NIT:NIT + 1], in1=negmid, op=ALU.subtract),
                      reads=[b_nm2[p_]], writes=[b_sm])
                kb.op("dve", lambda e: e.tensor_scalar(
                    out=maskf[:, 0:NK], in0=score[:, 0:NK], scalar1=tau, scalar2=None, op0=ALU.is_ge),
                    reads=b_score[0:nkt] + [b_sm], writes=[b_maskf])
                nch = 17 + i
                for c0 in range(0, nch, 4):
                    n4 = min(4, nch - c0)
                    bk = next_bank([3, 4])
                    for cc in range(n4):
                        c = c0 + cc
                        kb.op("pe", lambda e, c=c, cc=cc: e.transpose(
                            ps[:, bk, cc * 128:(cc + 1) * 128], maskf[:, c * 128:(c + 1) * 128], ident),
                            reads=[b_maskf, b_cst], writes=[PB[bk]], sig=(cc == n4 - 1))
                    kb.op("act", lambda e, c0=c0, n4=n4: e.activation(
                        out=mT[:, c0:c0 + n4, j * 128:(j + 1) * 128],
                        in_=ps[:, bk, 0:n4 * 128].rearrange("p (c q) -> p c q", c=n4), func=AF.Copy),
                        reads=[PB[bk]], writes=[b_mT])
                if j == 3:
                    kb.dma("sp", maskT_d[g], mT[:], reads=[b_mT], writes=[b_maskd[g]])

            prev = None
            for i in range(16):
                units = make_units(i)
                steps = make_steps(prev) if prev is not None else []
                nU = len(units)
                spacing = max(4, nU // (NIT + 1))
                ka = 0
                kd_ = 0
                for u, unit in enumerate(units):
                    unit()
                    if ka < len(steps) and u == ka * spacing + 1:
                        steps[ka][0]()
                        ka += 1
                    if kd_ < len(steps) and kd_ < ka and u == kd_ * spacing + 1 + spacing // 2:
                        steps[kd_][1]()
                        kd_ += 1
                while kd_ < len(steps):
                    if ka == kd_:
                        steps[ka][0]()
                        ka += 1
                    steps[kd_][1]()
                    kd_ += 1
                post_score(i)
                if prev is not None:
                    finalize(prev)
                prev = i
            for st_ in make_steps(prev):
                st_[0]()
                st_[1]()
            finalize(prev)
        phA.close()
        pers = ExitStack()
        es.enter_context(pers)
        psb = lambda name, shape, dt=F32: pers.enter_context(nc.sbuf_tensor(name, list(shape), dt))
        poolT = psb("poolT", [128, 8, TOK], BF16)
        b_poolT = bufs(4, "poolT")
        attnT = psb("attnT", [128, 8, TOK], BF16)
        b_attnT = [bufs(4, "attnT%d" % h) for h in range(8)]
        scCD = ExitStack()
        es.enter_context(scCD)
        xg_t[0] = scCD.enter_context(nc.sbuf_tensor("xgC0", [128, 16, 512], BF16))
        xg_t[1] = xg_t[0]
        b_xg[0] = Buf("xgc0")
        b_xg[1] = b_xg[0]

        kb.barrier()
        with ExitStack() as sc:
            sbC = lambda name, shape, dt=F32: sc.enter_context(nc.sbuf_tensor(name, list(shape), dt))
            wpl = sbC("wpl", [128, 8, 16, 128], BF16)
            b_wpl = Buf("wpl")
            for c in range(8):
                kb.dma("pool", wpl[:, c], w_fm[c], writes=[b_wpl])
            pw = sbC("pw", [128, 4, 2, 256], BF16)
            b_pw = Buf("pw")
            kb.dma("pool", pw[:], pool_w, writes=[b_pw])
            xh = sbC("xh", [128, 16, 16], BF16)
            b_xh = Buf("xh")
            for q4 in range(4):
                kb.dma("pool", xh[:, 4 * q4:4 * q4 + 4, :], xT[0, :, 4 * q4:4 * q4 + 4, TOK - 16:TOK], writes=[b_xh])
            hal = sbC("hal", [128, 8, 16])
            b_hal = bufs(8, "hal")
            vb = [sbC("vb%d" % i, [128, 528]) for i in range(2)]
            b_vb = bufs(2, "vb")
            sa = sbC("sa", [128, 528])
            sbb = sbC("sbb", [128, 528])
            b_sa, b_sb = Buf("sa"), Buf("sb")
            t16 = sbC("t16", [128, 16])
            b_t16 = Buf("t16")
            plb = [sbC("plb%d" % i, [128, 512], BF16) for i in range(2)]
            b_plb = bufs(2, "plb")
            for cp in range(8):
                bk = next_bank([0, 1, 2, 7])
                for kc in range(16):
                    kb.op("pe", lambda e, kc=kc, cp=cp, bk=bk: e.matmul(
                        ps[:, bk, 0:16], lhsT=wpl[:, cp, kc, :], rhs=xh[:, kc, :], start=(kc == 0), stop=(kc == 15)),
                        reads=[b_wpl, b_xh], writes=[PB[bk]], sig=(kc == 15))
                evac(hal[:, cp, :], ps[:, bk, 0:16], [PB[bk]], [b_hal[cp]])
            vrr = 0
            for tg in range(4):
                xg, bx = load_xg(1, tg)
                for gq in range(4):
                    wwin = (2, 4, 8, 16)[gq]
                    for cc in range(2):
                        cp = 2 * gq + cc
                        vi = vrr % 2
                        vrr += 1
                        V = vb[vi]
                        bV = b_vb[vi]
                        kb.op("dve", lambda e, V=V, cp=cp: e.tensor_copy(out=V[:, 0:16], in_=hal[:, cp, :]),
                              reads=[b_hal[cp]], writes=[bV])
                        proj_fm(lambda kc, cp=cp: wpl[:, cp, kc, :], b_wpl, xg, bx, V[:, 16:528], [bV])
                        kb.op("act", lambda e, V=V, cp=cp: e.activation(out=hal[:, cp, :], in_=V[:, 512:528], func=AF.Copy),
                              reads=[bV], writes=[b_hal[cp]])
                        kb.op("dve", lambda e, V=V: e.tensor_tensor(out=sa[:, 1:528], in0=V[:, 1:528], in1=V[:, 0:527],
                                                                     op=ALU.add), reads=[bV], writes=[b_sa])
                        Sfin, bS = sa, b_sa
                        if gq >= 1:
                            kb.op("dve", lambda e: e.tensor_tensor(out=sbb[:, 3:528], in0=sa[:, 3:528], in1=sa[:, 1:526],
                                                                   op=ALU.add), reads=[b_sa], writes=[b_sb])
                            Sfin, bS = sbb, b_sb
                        if gq >= 2:
                            kb.op("dve", lambda e: e.tensor_tensor(out=sa[:, 7:528], in0=sbb[:, 7:528], in1=sbb[:, 3:524],
                                                                   op=ALU.add), reads=[b_sb], writes=[b_sa])
                            Sfin, bS = sa, b_sa
                        if gq >= 3:
                            kb.op("dve", lambda e: e.tensor_tensor(out=sbb[:, 15:528], in0=sa[:, 15:528], in1=sa[:, 7:520],
                                                                   op=ALU.add), reads=[b_sa], writes=[b_sb])
                            Sfin, bS = sbb, b_sb
                        kb.op("dve", lambda e, Sfin=Sfin, V=V, cc=cc, wwin=wwin: e.scalar_tensor_tensor(
                            out=plb[cc][:, :], in0=Sfin[:, 16:528], scalar=1.0 / wwin, in1=V[:, 16:528],
                            op0=ALU.mult, op1=ALU.subtract), reads=[bS, bV], writes=[b_plb[cc]])
                        if tg == 0:
                            kb.op("dve", lambda e, Sfin=Sfin, gq=gq: e.tensor_tensor(
                                out=t16[:, :], in0=Sfin[:, 16:32], in1=cst[:, C_CORR + 16 * gq:C_CORR + 16 * gq + 16],
                                op=ALU.mult), reads=[bS, b_cst], writes=[b_t16])
                            kb.op("dve", lambda e, V=V, cc=cc, wwin=wwin: e.scalar_tensor_tensor(
                                out=plb[cc][:, 0:16], in0=t16[:, :], scalar=1.0 / wwin, in1=V[:, 16:32],
                                op0=ALU.mult, op1=ALU.subtract), reads=[b_t16, bV], writes=[b_plb[cc]])
                    for dc in range(2):
                        bk = next_bank([0, 1, 2, 7])
                        for cc in range(2):
                            kb.op("pe", lambda e, cc=cc, dc=dc, gq=gq, bk=bk: e.matmul(
                                ps[:, bk, :], lhsT=pw[:, gq, cc, dc * 128:(dc + 1) * 128], rhs=plb[cc][:, :],
                                start=(cc == 0), stop=(cc == 1)),
                                reads=[b_pw, b_plb[cc]], writes=[PB[bk]], sig=(cc == 1))
                        oc = 2 * gq + dc
                        kb.op("act", lambda e, oc=oc, bk=bk, tg=tg: e.activation(
                            out=poolT[:, oc, tg * 512:(tg + 1) * 512], in_=ps[:, bk, :], func=AF.Copy,
                            scale=cst[:, C_PSC + oc:C_PSC + oc + 1]),
                            reads=[PB[bk], b_cst], writes=[b_poolT[tg]])
        if stop_after == "C":
            dump([(poolT[:, 0, 0:1024], b_poolT, 1024), (poolT[:, 7, 0:1024], b_poolT, 1024),
                  (poolT[:, 3, 1024:2048], b_poolT, 1024)])
            return nc

        kb.barrier()
        with ExitStack() as sc:
            sbD = lambda name, shape, dt=F32: sc.enter_context(nc.sbuf_tensor(name, list(shape), dt))
            wq2 = sbD("wq2", [128, 2, 16, 128], BF16)
            wk2 = sbD("wk2", [128, 2, 16, 128], BF16)
            wv2 = sbD("wv2", [128, 16, 256], BF16)
            b_wq2, b_wk2, b_wv2 = Buf("wq2"), Buf("wk2"), Buf("wv2")
            kT2 = sbD("kT2", [128, 2, 2 * TOK], BF16)
            b_kT2 = bufs(8, "kT2")
            v2 = sbD("v2", [128, 32, 256], BF16)
            b_v2 = bufs(8, "v2")
            qT2 = sbD("qT2", [128, 2, TOK], BF16)
            b_qT2 = bufs(4, "qT2")
            mk = sbD("mk", [128, 32, 512], BF16)
            b_mk = Buf("mk")
            Et = [sbD("E%d" % i, [128, 512], BF16) for i in range(3)]
            b_E = bufs(3, "E")
            Pt = [sbD("P%d" % i, [128, 512], BF16) for i in range(3)]
            b_P = bufs(3, "P")
            tmpn = [sbD("tmpn%d" % i, [128, 128]) for i in range(2)]
            b_tmpn = bufs(2, "tmpn")
            rden = sbD("rden", [128, 512])
            b_rden = Buf("rden")
            SB_S = [0, 1, 2]
            for hg in range(4):
                for hl in range(2):
                    kb.dma("pool", wq2[:, hl], w_fm[8 + 2 * hg + hl], writes=[b_wq2])
                    kb.dma("pool", wk2[:, hl], w_fm[16 + 2 * hg + hl], writes=[b_wk2])
                kb.dma("pool", wv2[:], w_v[hg], writes=[b_wv2])
                for s in range(2):
                    for tg in range(4):
                        xg, bx = load_xg(s, tg)
                        kg = s * 4 + tg
                        for hl in range(2):
                            proj_fm(lambda kc, hl=hl: wk2[:, hl, kc, :], b_wk2, xg, bx,
                                    kT2[:, hl, kg * 512:(kg + 1) * 512], [b_kT2[kg]])
                            if s == 1:
                                proj_fm(lambda kc, hl=hl: wq2[:, hl, kc, :], b_wq2, xg, bx,
                                        qT2[:, hl, tg * 512:(tg + 1) * 512], [b_qT2[tg]])
                        for tt in range(4):
                            bk = 7
                            for kc in range(16):
                                kb.op("pe", lambda e, kc=kc, tt=tt, xg=xg: e.matmul(
                                    ps[:, bk, 0:256], lhsT=xg[:, kc, tt * 128:(tt + 1) * 128], rhs=wv2[:, kc, :],
                                    start=(kc == 0), stop=(kc == 15)),
                                    reads=[b_wv2, bx], writes=[PB[bk]], sig=(kc == 15))
                            evac(v2[:, kg * 4 + tt, :], ps[:, bk, 0:256], [PB[bk]], [b_v2[kg]])
                if stop_after == "D0":
                    dump([(kT2[:, 0, 0:1024], b_kT2[0:2], 1024), (qT2[:, 1, 0:1024], b_qT2[0:2], 1024),
                          (v2[:, 0:4, :].rearrange("p a b -> p (a b)"), b_v2[0:1], 1024)])
                    return nc
                for g in range(4):
                    kb.dma("sp", mk[:], maskT_d[g], reads=[b_maskd[g]], writes=[b_mk])
                    nch = 20 + 4 * g
                    for hl in range(2):
                        h = 2 * hg + hl
                        bO = 3 + 2 * (hl % 2)
                        bD = bO + 1

                        def issue_S(c, hl=hl, g=g):
                            bk = SB_S[c % 3]
                            kb.op("pe", lambda e: e.matmul(
                                ps[:, bk, :], lhsT=kT2[:, hl, c * 128:(c + 1) * 128],
                                rhs=qT2[:, hl, g * 512:(g + 1) * 512], start=True, stop=True),
                                reads=[b_kT2[c // 4], b_qT2[g]], writes=[PB[bk]])
                        issue_S(0)
                        if nch > 1:
                            issue_S(1)
                        for c in range(nch):
                            if c + 2 < nch:
                                issue_S(c + 2)
                            bk = SB_S[c % 3]
                            ei = c % 3
                            tokE = kb.op("act", lambda e, bk=bk, ei=ei, h=h: e.activation(
                                out=Et[ei][:, :], in_=ps[:, bk, :], func=AF.Exp, scale=ATT_SCALE,
                                bias=cst[:, C_B31 + h:C_B31 + h + 1]),
                                reads=[PB[bk], b_cst], writes=[b_E[ei]])
                            cn = c - (15 + 4 * g)
                            if 0 <= cn <= 4:
                                for j in range(4):
                                    dl = 1 + j - cn
                                    if dl in (0, 1):
                                        ti = (j + cn) % 2
                                        kb.op("dve", lambda e, bk=bk, j=j, dl=dl, h=h, ti=ti: e.scalar_tensor_tensor(
                                            out=tmpn[ti][:, :], in0=ps[:, bk, j * 128:(j + 1) * 128], scalar=ATT_SCALE,
                                            in1=bT[:, dl, h, :], op0=ALU.mult, op1=ALU.add),
                                            reads=[PB[bk], b_bT], writes=[b_tmpn[ti]], deps=[tokE])
                                        kb.op("act", lambda e, ei=ei, j=j, ti=ti: e.activation(
                                            out=Et[ei][:, j * 128:(j + 1) * 128], in_=tmpn[ti][:, :], func=AF.Exp),
                                            reads=[b_tmpn[ti]], writes=[b_E[ei]])
                            kb.op("dve", lambda e, ei=ei, c=c: e.tensor_tensor(
                                out=Pt[ei][:, :], in0=Et[ei][:, :], in1=mk[:, c, :], op=ALU.mult),
                                reads=[b_E[ei], b_mk], writes=[b_P[ei]])
                            kb.op("pe", lambda e, ei=ei, c=c, hl=hl, bO=bO, nch=nch: e.matmul(
                                ps[:, bO, :], lhsT=v2[:, c, hl * 128:(hl + 1) * 128], rhs=Pt[ei][:, :],
                                start=(c == 0), stop=(c == nch - 1)),
                                reads=[b_v2[c // 4], b_P[ei]], writes=[PB[bO]], sig=(c == nch - 1))
                            kb.op("pe", lambda e, ei=ei, c=c, bD=bD, nch=nch: e.matmul(
                                ps[:, bD, :], lhsT=ones_b[:, :], rhs=Pt[ei][:, :],
                                start=(c == 0), stop=(c == nch - 1)),
                                reads=[b_ones, b_P[ei]], writes=[PB[bD]], sig=(c == nch - 1))
                        kb.op("dve", lambda e, bD=bD: e.reciprocal(out=rden[:, :], in_=ps[:, bD, :]),
                              reads=[PB[bD]], writes=[b_rden])
                        kb.op("dve", lambda e, bO=bO, h=h, g=g: e.tensor_tensor(
                            out=attnT[:, h, g * 512:(g + 1) * 512], in0=ps[:, bO, :], in1=rden[:, :], op=ALU.mult),
                            reads=[PB[bO], b_rden], writes=[b_attnT[h][g]])
                        if stop_after == "D1":
                            dump([(attnT[:, 0, 0:512], [b_attnT[0][0]], 512), (rden[:, :], [b_rden], 512)])
                            return nc
        if stop_after == "D":
            dump([(attnT[:, 0, 0:1024], b_attnT[0], 1024), (attnT[:, 7, 0:1024], b_attnT[7], 1024),
                  (attnT[:, 3, 1024:2048], b_attnT[3], 1024)])
            return nc

        scCD.close()
        kb.barrier()
        b_hT_d = bufs(4, "hT_d")
        b_htok = bufs(16, "htok")

        def layer_norm(e_sb, src, bsrc, gam, bet, b_gb, outt, bout, xc, b_xc, junkf, b_jf, st, b_st):
            kb.op("dve", lambda e: e.tensor_scalar(out=junkf[:, :], in0=src, scalar1=1.0, scalar2=None, op0=ALU.mult,
                                                   op1=ALU.add, accum_out=st[:, 0:1]), reads=[bsrc], writes=[b_jf, b_st])
            kb.op("dve", lambda e: e.tensor_scalar(out=st[:, 1:2], in0=st[:, 0:1], scalar1=1.0 / D, scalar2=None,
                                                   op0=ALU.mult), writes=[b_st])
            kb.op("dve", lambda e: e.tensor_scalar(out=xc[:, :], in0=src, scalar1=st[:, 1:2], scalar2=None,
                                                   op0=ALU.subtract), reads=[bsrc, b_st], writes=[b_xc])
            kb.op("dve", lambda e: e.tensor_tensor(out=junkf[:, :], in0=xc[:, :], in1=xc[:, :], op=ALU.mult),
                  reads=[b_xc], writes=[b_jf])
            kb.op("dve", lambda e: e.tensor_scalar(out=junkf[:, :], in0=junkf[:, :], scalar1=1.0, scalar2=None,
                                                   op0=ALU.mult, op1=ALU.add, accum_out=st[:, 2:3]),
                  writes=[b_jf, b_st])
            kb.op("dve", lambda e: e.tensor_scalar(out=st[:, 3:4], in0=st[:, 2:3], scalar1=1.0 / D, scalar2=LN_EPS,
                                                   op0=ALU.mult, op1=ALU.add), writes=[b_st])
            kb.op("act", lambda e: e.activation(out=st[:, 4:5], in_=st[:, 3:4], func=AF.Sqrt), writes=[b_st])
            kb.op("dve", lambda e: e.reciprocal(out=st[:, 5:6], in_=st[:, 4:5]), writes=[b_st])
            kb.op("dve", lambda e: e.scalar_tensor_tensor(out=xc[:, :], in0=xc[:, :], scalar=st[:, 5:6], in1=gam,
                                                          op0=ALU.mult, op1=ALU.mult),
                  reads=[b_st, b_gb], writes=[b_xc])
            return kb.op("dve", lambda e: e.tensor_tensor(out=outt, in0=xc[:, :], in1=bet, op=ALU.add),
                         reads=[b_xc, b_gb], writes=[bout])

        with ExitStack() as sc:
            sbE = lambda name, shape, dt=F32: sc.enter_context(nc.sbuf_tensor(name, list(shape), dt))
            wo2 = [sbE("wo%d" % i, [128, 16, 512], BF16) for i in range(2)]
            b_wo2 = bufs(2, "wo")
            wo_rr = [0]
            gb = sbE("gb1", [128, 2, D])
            b_gb = Buf("gb1")
            kb.dma("sp", gb[:, 0, :], lnp[0], writes=[b_gb])
            kb.dma("sp", gb[:, 1, :], lnp[1], writes=[b_gb])
            xt = [sbE("xt0", [128, D])] * 2
            b_xt = [Buf("xt")] * 2
            hpre = sbE("hpre", [128, D])
            b_hpre = Buf("hpre")
            xc = sbE("xc", [128, D])
            b_xc = Buf("xc")
            junkf = sbE("junkf", [128, D])
            b_jf = Buf("junkf")
            hh = [sbE("hh0", [128, D])] * 2
            b_hh = [Buf("hh")] * 2
            st = sbE("st", [128, 8])
            b_st = Buf("st")
            hTs = sbE("hTs", [128, 16, 128], BF16)
            b_hTs = Buf("hTs")
            for tt in range(16):
                tg = tt // 4
                xi = tt % 2
                kb.dma("sp", xt[xi][:, :], x_tok[tt * 128:(tt + 1) * 128, :], writes=[b_xt[xi]])
                for dt_ in range(4):
                    bk = next_bank([0, 1, 2, 7])
                    wi_ = wo_rr[0] % 2
                    wo_rr[0] += 1
                    wo, b_wo = wo2[wi_], b_wo2[wi_]
                    for q4 in range(4):
                        kb.dma("pool", wo[:, 4 * q4:4 * q4 + 4, :], w_out[:, 4 * q4:4 * q4 + 4, dt_ * 512:(dt_ + 1) * 512],
                               writes=[b_wo])
                    for kc in range(16):
                        if kc < 8:
                            lh = poolT[:, kc, tt * 128:(tt + 1) * 128]
                            rb = b_poolT[tg]
                        else:
                            lh = attnT[:, kc - 8, tt * 128:(tt + 1) * 128]
                            rb = b_attnT[kc - 8][tg]
                        kb.op("pe", lambda e, lh=lh, kc=kc, dt_=dt_, bk=bk, wo=wo: e.matmul(
                            ps[:, bk, :], lhsT=lh, rhs=wo[:, kc, :],
                            start=(kc == 0), stop=(kc == 15)),
                            reads=[rb, b_wo], writes=[PB[bk]], sig=(kc == 15))
                    kb.op("dve", lambda e, xi=xi, dt_=dt_, bk=bk: e.scalar_tensor_tensor(
                        out=hpre[:, dt_ * 512:(dt_ + 1) * 512], in0=xt[xi][:, dt_ * 512:(dt_ + 1) * 512], scalar=ALPHA,
                        in1=ps[:, bk, :], op0=ALU.mult, op1=ALU.add),
                        reads=[b_xt[xi], PB[bk]], writes=[b_hpre])
                hi = tt % 2
                layer_norm(None, hpre[:, :], b_hpre, gb[:, 0, :], gb[:, 1, :], b_gb, hh[hi][:, :], b_hh[hi],
                           xc, b_xc, junkf, b_jf, st, b_st)
                kb.dma("sp", h_tok[tt * 128:(tt + 1) * 128, :], hh[hi][:, :], reads=[b_hh[hi]], writes=[b_htok[tt]])
                for c0 in range(0, 16, 4):
                    bk = next_bank([3, 4, 5, 6])
                    for cc in range(4):
                        kb.op("pe", lambda e, c0=c0, cc=cc, bk=bk, hi=hi: e.transpose(
                            ps[:, bk, cc * 128:(cc + 1) * 128], hh[hi][:, (c0 + cc) * 128:(c0 + cc + 1) * 128], ident),
                            reads=[b_hh[hi], b_cst], writes=[PB[bk]], sig=(cc == 3))
                    evac(hTs[:, c0:c0 + 4, :],
                         ps[:, bk, :].rearrange("p (c q) -> p c q", c=4), [PB[bk]], [b_hTs])
                for q4 in range(4):
                    kb.dma("sp", hT_d[:, 4 * q4:4 * q4 + 4, tt * 128:(tt + 1) * 128], hTs[:, 4 * q4:4 * q4 + 4, :],
                           reads=[b_hTs], writes=[b_hT_d[tg]])
        pers.close()
        if stop_after == "E":
            t1 = kb.dma("sp", dbg_out[:, 0:2048], h_tok[0:128, :], reads=[b_htok[0]])
            t2 = kb.dma("sp", dbg_out[:, 2048:4096], h_tok[1920:2048, :], reads=[b_htok[15]])
            kb.wait("sp", t1)
            kb.wait("sp", t2)
            return nc

        kb.barrier()
        b_W1 = bufs(32, "W1")
        with ExitStack() as sc:
            sbF = lambda name, shape, dt=F32: sc.enter_context(nc.sbuf_tensor(name, list(shape), dt))
            wqc = [sbF("wqc%d" % i, [128, 16, 128], BF16) for i in range(3)]
            b_wqc = bufs(3, "wqc")
            wq_rr = [0]
            skT = sbF("skT", [128, 2, 128], BF16)
            b_skT = Buf("skT")
            kb.dma("pool", skT[:], subkT, writes=[b_skT])
            hTg = [sbF("hTg0", [128, 16, 512], BF16)] * 2
            b_hTg = [Buf("hTg")] * 2
            qpT = sbF("qpT", [128, 16, 512], BF16)
            b_qpT = Buf("qpT")
            s_sb = sbF("s_sb", [128, 16, 128])
            s2_sb = sbF("s2_sb", [128, 16, 128])
            b_s, b_s2 = Buf("s"), Buf("s2")
            v16 = sbF("v16", [128, 16, 16])
            i16 = sbF("i16", [128, 16, 16], U32)
            i16f = sbF("i16f", [128, 16, 16])
            b_v16, b_i16, b_i16f = Buf("v16"), Buf("i16"), Buf("i16f")
            cand = sbF("cand", [128, 8, 256])
            cand2 = sbF("cand2", [128, 8, 256])
            b_cand, b_cand2 = Buf("cand"), Buf("cand2")
            best = sbF("best", [128, 8, 16])
            bidx = sbF("bidx", [128, 8, 16], U32)
            aiu = sbF("aiu", [128, 128], U32)
            biu = sbF("biu", [128, 128], U32)
            af = sbF("af", [128, 128])
            bf = sbF("bf", [128, 128])
            b_best, b_bidx, b_bidf, b_af, b_bf = Buf("best"), Buf("bidx"), Buf("bidf"), Buf("af"), Buf("bf")
            eb = sbF("eb", [128, 8, 16])
            zz = sbF("zz", [128, 16])
            b_eb, b_zz = Buf("eb"), Buf("zz")
            oh = sbF("oh", [128, 128, 16])
            b_oh = Buf("oh")
            IG = sbF("IG", [128, 3, 128])
            b_IG = Buf("IG")
            IGT = sbF("IGT", [128, 3, 128])
            b_IGT = Buf("IGT")
            eq2 = [sbF("eq0", [128, 32, 128], BF16)] * 2
            b_eq2 = [Buf("eq")] * 2
            At2 = [sbF("At0", [128, 32, 128], BF16)] * 2
            Bt2 = [sbF("Bt0", [128, 32, 128], BF16)] * 2
            b_At2, b_Bt2 = [Buf("At")] * 2, [Buf("Bt")] * 2
            ab_rr = [0]
            stg = [sbF("stg0", [128, 128, 64], BF16)] * 2
            b_stg = [Buf("stg")] * 2
            for tg in range(4):
                hx = hTg[tg % 2]
                bhx = b_hTg[tg % 2]
                for q4 in range(4):
                    kb.dma("sp", hx[:, 4 * q4:4 * q4 + 4, :], hT_d[:, 4 * q4:4 * q4 + 4, tg * 512:(tg + 1) * 512],
                           reads=[b_hT_d[tg]], writes=[bhx])
                for n in range(16):
                    wi_ = wq_rr[0] % 3
                    wq_rr[0] += 1
                    for q4 in range(4):
                        kb.dma("pool", wqc[wi_][:, 4 * q4:4 * q4 + 4, :], wq[:, 4 * q4:4 * q4 + 4, n * 128:(n + 1) * 128],
                               writes=[b_wqc[wi_]])
                    proj_fm(lambda kc, wi_=wi_: wqc[wi_][:, kc, :], b_wqc[wi_], hx, bhx, qpT[:, n, :], [b_qpT])
                for tt in range(4):
                    T = 4 * tg + tt
                    for n4 in range(4):
                        bk = next_bank([3, 4, 5, 6])
                        for nn in range(4):
                            n = 4 * n4 + nn
                            kb.op("pe", lambda e, n=n, nn=nn, bk=bk, tt=tt: e.matmul(
                                ps[:, bk, nn * 128:(nn + 1) * 128], lhsT=qpT[:, n, tt * 128:(tt + 1) * 128],
                                rhs=skT[:, n % 2, :], start=True, stop=True),
                                reads=[b_qpT, b_skT], writes=[PB[bk]], sig=(nn == 3))
                        evac(s_sb[:, 4 * n4:4 * n4 + 4, :], ps[:, bk, :].rearrange("p (c q) -> p c q", c=4),
                             [PB[bk]], [b_s])
                    bv, bi, bs2 = bufs(16, "v16n"), bufs(16, "i16n"), bufs(16, "s2n")
                    for n in range(16):
                        kb.op("dve", lambda e, n=n: e.max(out=v16[:, n, 0:8], in_=s_sb[:, n, :]),
                              reads=[b_s], writes=[bv[n]], deps=[b_v16.w] + list(b_v16.r.values()))
                    for n in range(16):
                        kb.op("dve", lambda e, n=n: e.max_index(out=i16[:, n, 0:8], in_max=v16[:, n, 0:8],
                                                               in_values=s_sb[:, n, :]),
                              reads=[b_s, bv[n]], writes=[bi[n]], deps=[b_i16.w] + list(b_i16.r.values()))
                    for n in range(16):
                        kb.op("dve", lambda e, n=n: e.match_replace(out=s2_sb[:, n, :], in_to_replace=v16[:, n, 0:8],
                                                                   in_values=s_sb[:, n, :], imm_value=NEG),
                              reads=[b_s, bv[n]], writes=[bs2[n]])
                    for n in range(16):
                        kb.op("dve", lambda e, n=n: e.max(out=v16[:, n, 8:16], in_=s2_sb[:, n, :]),
                              reads=[bs2[n]], writes=[bv[n]])
                    for n in range(16):
                        kb.op("dve", lambda e, n=n: e.max_index(out=i16[:, n, 8:16], in_max=v16[:, n, 8:16],
                                                               in_values=s2_sb[:, n, :]),
                              reads=[bs2[n], bv[n]], writes=[bi[n]])
                    kb.op("dve", lambda e: e.tensor_copy(out=i16f[:], in_=i16[:]), reads=bi, writes=[b_i16f, b_i16])
                    kb.op("dve", lambda e: e.tensor_tensor(
                        out=cand[:].rearrange("p h (a b) -> p h a b", a=16),
                        in0=sap(v16, [[32, 8], [1, 16], [0, 16]]),
                        in1=sap(v16, [[32, 8], [0, 16], [1, 16]], off=16), op=ALU.add),
                        reads=bv, writes=[b_cand, b_v16])
                    bb, bx_, bc2 = bufs(8, "besth"), bufs(8, "bidxh"), bufs(8, "cand2h")
                    for h in range(8):
                        kb.op("dve", lambda e, h=h: e.max(out=best[:, h, 0:8], in_=cand[:, h, :]),
                              reads=[b_cand], writes=[bb[h]], deps=[b_best.w] + list(b_best.r.values()))
                    for h in range(8):
                        kb.op("dve", lambda e, h=h: e.max_index(out=bidx[:, h, 0:8], in_max=best[:, h, 0:8],
                                                               in_values=cand[:, h, :]),
                              reads=[b_cand, bb[h]], writes=[bx_[h]], deps=[b_bidx.w] + list(b_bidx.r.values()))
                    for h in range(8):
                        kb.op("dve", lambda e, h=h: e.match_replace(out=cand2[:, h, :], in_to_replace=best[:, h, 0:8],
                                                                   in_values=cand[:, h, :], imm_value=NEG),
                              reads=[b_cand, bb[h]], writes=[bc2[h]])
                    for h in range(8):
                        kb.op("dve", lambda e, h=h: e.max(out=best[:, h, 8:16], in_=cand2[:, h, :]),
                              reads=[bc2[h]], writes=[bb[h]])
                    for h in range(8):
                        kb.op("dve", lambda e, h=h: e.max_index(out=bidx[:, h, 8:16], in_max=best[:, h, 8:16],
                                                               in_values=cand2[:, h, :]),
                              reads=[bc2[h], bb[h]], writes=[bx_[h]])
                    kb.op("dve", lambda e: e.tensor_copy(out=zz[:, 0:1], in_=best[:, 0, 0:1]),
                          reads=bb + bx_, writes=[b_best, b_bidx, b_zz])
                    kb.op("dve", lambda e: e.tensor_tensor(out=eb[:], in0=best[:], in1=sap(best, [[16, 8], [0, 16]]),
                                                           op=ALU.subtract), reads=[b_best], writes=[b_eb])
                    kb.op("act", lambda e: e.activation(out=eb[:], in_=eb[:], func=AF.Exp), writes=[b_eb])
                    kb.op("dve", lambda e: e.tensor_reduce(out=zz[:, 0:8], in_=eb[:], axis=AX.X, op=ALU.add),
                          reads=[b_eb], writes=[b_zz])
                    kb.op("dve", lambda e: e.reciprocal(out=zz[:, 8:16], in_=zz[:, 0:8]), writes=[b_zz])
                    kb.op("dve", lambda e: e.tensor_tensor(
                        out=IG[:, 2, :].rearrange("p (h r) -> p h r", h=8), in0=eb[:],
                        in1=sap(zz, [[1, 8], [0, 16]], off=8), op=ALU.mult),
                        reads=[b_eb, b_zz], writes=[b_IG])
                    kb.op("dve", lambda e: e.tensor_single_scalar(out=aiu[:], in_=bidx[:].rearrange("p h r -> p (h r)"),
                                                                  scalar=4, op=ALU.logical_shift_right),
                          reads=[b_bidx], writes=[b_bidf])
                    kb.op("dve", lambda e: e.tensor_single_scalar(out=biu[:], in_=bidx[:].rearrange("p h r -> p (h r)"),
                                                                  scalar=15, op=ALU.bitwise_and),
                          reads=[b_bidx], writes=[b_bidf])
                    kb.op("dve", lambda e: e.tensor_copy(out=af[:], in_=aiu[:]), reads=[b_bidf], writes=[b_af])
                    kb.op("dve", lambda e: e.tensor_copy(out=bf[:], in_=biu[:]), reads=[b_bidf], writes=[b_bf])
                    for which, sel, boff in ((0, af, 0), (1, bf, 16)):
                        bsel = b_af if which == 0 else b_bf
                        kb.op("dve", lambda e, sel=sel: e.tensor_tensor(
                            out=oh[:], in0=sap(cst, [[0, 128], [1, 16]], off=C_IOTA),
                            in1=sap(sel, [[1, 128], [0, 16]]), op=ALU.is_equal),
                            reads=[bsel, b_cst], writes=[b_oh])
                        kb.op("dve", lambda e, boff=boff: e.tensor_tensor(
                            out=oh[:].rearrange("p (h r) a -> p h r a", h=8),
                            in0=oh[:].rearrange("p (h r) a -> p h r a", h=8),
                            in1=sap(i16f, [[32, 8], [0, 16], [1, 16]], off=boff), op=ALU.mult),
                            reads=[b_i16f], writes=[b_oh])
                        kb.op("dve", lambda e, which=which: e.tensor_reduce(out=IG[:, which, :], in_=oh[:], axis=AX.X,
                                                                          op=ALU.add),
                              reads=[b_oh], writes=[b_IG])
                    bk = next_bank([3, 4, 5, 6])
                    for w3 in range(3):
                        kb.op("pe", lambda e, w3=w3, bk=bk: e.transpose(ps[:, bk, w3 * 128:(w3 + 1) * 128],
                                                                        IG[:, w3, :], ident),
                              reads=[b_IG, b_cst], writes=[PB[bk]], sig=(w3 == 2))
                    kb.op("act", lambda e, bk=bk: e.activation(out=IGT[:], in_=ps[:, bk, 0:384].rearrange(
                        "p (c q) -> p c q", c=3), func=AF.Copy), reads=[PB[bk]], writes=[b_IGT])
                    for sbk in range(2):
                        si = (2 * T + sbk) % 2
                        for s32 in range(2):
                            t0 = sbk * 64 + s32 * 32
                            ai_ = ab_rr[0] % 2
                            ab_rr[0] += 1
                            eq, b_eq = eq2[ai_], b_eq2[ai_]
                            At, b_At = At2[ai_], b_At2[ai_]
                            Bt, b_Bt = Bt2[ai_], b_Bt2[ai_]
                            kb.op("dve", lambda e, t0=t0: e.tensor_tensor(
                                out=eq[:], in0=sap(cst, [[0, 32], [1, 128]], off=C_IOTA),
                                in1=sap(IGT, [[1, 32], [0, 128]], off=0 * 128 + t0), op=ALU.is_equal),
                                reads=[b_IGT, b_cst], writes=[b_eq])
                            kb.op("dve", lambda e, t0=t0: e.tensor_tensor(
                                out=At[:], in0=eq[:], in1=sap(IGT, [[1, 32], [0, 128]], off=2 * 128 + t0), op=ALU.mult),
                                reads=[b_eq, b_IGT], writes=[b_At])
                            kb.op("dve", lambda e, t0=t0: e.tensor_tensor(
                                out=Bt[:], in0=sap(cst, [[0, 32], [1, 128]], off=C_IOTA),
                                in1=sap(IGT, [[1, 32], [0, 128]], off=1 * 128 + t0), op=ALU.is_equal),
                                reads=[b_IGT, b_cst], writes=[b_Bt])
                            for t4 in range(8):
                                bk = next_bank([0, 1, 2, 7])
                                for q4 in range(4):
                                    tl = 4 * t4 + q4
                                    kb.op("pe", lambda e, tl=tl, q4=q4, bk=bk: e.matmul(
                                        ps[:, bk, q4 * 128:(q4 + 1) * 128], lhsT=At[:, tl, :], rhs=Bt[:, tl, :],
                                        start=True, stop=True),
                                        reads=[b_At, b_Bt], writes=[PB[bk]], sig=(q4 == 3))
                                kb.op("act", lambda e, t4=t4, bk=bk, si=si, s32=s32: e.activation(
                                    out=sap(stg[si], [[1, 4], [64, 128]], off=s32 * 32 + 4 * t4),
                                    in_=ps[:, bk, :].rearrange("p (t j) -> p t j", t=4), func=AF.Copy),
                                    reads=[PB[bk]], writes=[b_stg[si]])
                        kb.dma("sp", W1[2 * T + sbk], stg[si][:], reads=[b_stg[si]], writes=[b_W1[2 * T + sbk]])
                    if stop_after == "F2" and T == 0:
                        wchk = sbF("wchk", [128, 1024], BF16)
                        b_wchk = Buf("wchk")
                        kb.dma("sp", wchk[:], W1[0, :, 0:16, :].rearrange("i j t -> i (j t)"), reads=[b_W1[0]], writes=[b_wchk])
                        dump([(wchk[:, :], [b_wchk], 1024), (IG[:, 0, :], [b_IG], 128), (IG[:, 1, :], [b_IG], 128),
                              (IG[:, 2, :], [b_IG], 128)])
                        return nc
                    if stop_after == "F" and T == 0:
                        dump([(IG[:, 0, :], [b_IG], 128), (IG[:, 1, :], [b_IG], 128), (IG[:, 2, :], [b_IG], 128),
                              (s_sb[:, 0, :], [b_s], 128), (s_sb[:, 1, :], [b_s], 128)])
                        return nc

        kb.barrier()
        if stop_after == "Fend":
            kb.barrier()
            return nc
        NJ = 4
        NR = 0 if stop_after == "G4" else (2 if stop_after in ("G2", "G3") else 128 // NJ)
        for half in range(2):
            kb.barrier()
            with ExitStack() as sc:
                sbG = lambda name, shape, dt=F32: sc.enter_context(nc.sbuf_tensor(name + "_h%d" % half, list(shape), dt))
                hTh = sbG("hTh", [128, 16, 1024], BF16)
                b_hTh = Buf("hTh")
                for tg2 in range(2):
                    tg = 2 * half + tg2
                    for q4 in range(4):
                        kb.dma("sp", hTh[:, 4 * q4:4 * q4 + 4, tg2 * 512:(tg2 + 1) * 512],
                               hT_d[:, 4 * q4:4 * q4 + 4, tg * 512:(tg + 1) * 512],
                               reads=[b_hT_d[tg]], writes=[b_hTh])
                accG = sbG("accG", [128, 8, D])
                b_accG = [bufs(2, "accG%d" % t) for t in range(8)]
                with ExitStack() as sc2:
                    sbH = lambda name, shape, dt=F32: sc2.enter_context(nc.sbuf_tensor(name + "_h%d" % half, list(shape), dt))
                    Wr = [sbH("Wr%d" % i, [128, 16, NJ, 64], BF16) for i in range(2)]
                    b_Wr = bufs(2, "Wr")
                    vr = [sbH("vr%d" % i, [128, NJ, D], BF16) for i in range(2)]
                    b_vr = bufs(2, "vr")
                    uj = [sbH("uj%d" % i, [128, 16, 128], BF16) for i in range(2)]
                    b_uj = bufs(2, "uj")
                    Gr = [sbH("Gr%d" % i, [128, NJ, 1024], BF16) for i in range(2)]
                    b_Gr = bufs(2, "Gr")
                    ga = [sbH("ga%d" % i, [128, 512], BF16) for i in range(2)]
                    b_ga = bufs(2, "ga")
                    urr = [0]
                    grr = [0]

                    def phaseA(r):
                        ri = r % 2
                        j0 = r * NJ
                        for q4 in range(4):
                            b0_ = 16 * half + 4 * q4
                            kb.dma("sp", Wr[ri][:, 4 * q4:4 * q4 + 4], W1[b0_:b0_ + 4, :, j0:j0 + NJ, :].rearrange(
                                "b i j t -> i b j t"), reads=b_W1[b0_:b0_ + 4], writes=[b_Wr[ri]])
                        kb.dma("pool", vr[ri][:], vL[j0:j0 + NJ].rearrange("j i d -> i j d"), writes=[b_vr[ri]])
                        for jj in range(NJ):
                            ui = urr[0] % 2
                            urr[0] += 1
                            kb.dma("pool", uj[ui][:], uT[j0 + jj], writes=[b_uj[ui]])
                            for tg2 in range(2):
                                bk = next_bank([0, 1])
                                for kc in range(16):
                                    kb.op("pe", lambda e, kc=kc, ui=ui, tg2=tg2, bk=bk: e.matmul(
                                        ps[:, bk, :], lhsT=uj[ui][:, kc, :], rhs=hTh[:, kc, tg2 * 512:(tg2 + 1) * 512],
                                        start=(kc == 0), stop=(kc == 15)),
                                        reads=[b_uj[ui], b_hTh], writes=[PB[bk]], sig=(kc == 15))
                                gi = grr[0] % 2
                                grr[0] += 1
                                kb.op("act", lambda e, gi=gi, bk=bk: e.activation(out=ga[gi][:, :], in_=ps[:, bk, :],
                                                                                 func=AF.Gelu),
                                      reads=[PB[bk]], writes=[b_ga[gi]])
                                kb.op("dve", lambda e, gi=gi, ri=ri, jj=jj, tg2=tg2: e.tensor_tensor(
                                    out=Gr[ri][:, jj, tg2 * 512:(tg2 + 1) * 512].rearrange("p (b t) -> p b t", b=8),
                                    in0=ga[gi][:, :].rearrange("p (b t) -> p b t", b=8),
                                    in1=Wr[ri][:, tg2 * 8:(tg2 + 1) * 8, jj, :], op=ALU.mult),
                                    reads=[b_ga[gi], b_Wr[ri]], writes=[b_Gr[ri]])

                    def phaseB(r):
                        ri = r % 2
                        for tt in range(8):
                            for dh in range(2):
                                b0 = 2 + 2 * ((tt * 2 + dh) % 3)
                                for jj in range(NJ):
                                    for dq in range(2):
                                        kb.op("pe", lambda e, jj=jj, dq=dq, tt=tt, dh=dh, b0=b0: e.matmul(
                                            ps[:, b0 + dq, :], lhsT=Gr[ri][:, jj, tt * 128:(tt + 1) * 128],
                                            rhs=vr[ri][:, jj, dh * 1024 + dq * 512:dh * 1024 + (dq + 1) * 512],
                                            start=(jj == 0), stop=(jj == NJ - 1)),
                                            reads=[b_Gr[ri], b_vr[ri]], writes=[PB[b0 + dq]],
                                            sig=(jj == NJ - 1 and dq == 1))
                                pin = ps[:, b0:b0 + 2, :].rearrange("p a b -> p (a b)")
                                aout = accG[:, tt, dh * 1024:(dh + 1) * 1024]
                                if r == 0:
                                    kb.op("dve", lambda e, pin=pin, aout=aout: e.tensor_copy(out=aout, in_=pin),
                                          reads=[PB[b0], PB[b0 + 1]], writes=[b_accG[tt][dh]])
                                else:
                                    kb.op("dve", lambda e, pin=pin, aout=aout: e.tensor_tensor(
                                        out=aout, in0=aout, in1=pin, op=ALU.add),
                                        reads=[PB[b0], PB[b0 + 1]], writes=[b_accG[tt][dh]])

                    if stop_after == "G1":
                        phaseA(0)
                        phaseB(0)
                        dump([(accG[:, 0, 0:1024], b_accG[0], 1024), (accG[:, 7, 1024:2048], b_accG[7], 1024),
                              (Gr[0][:, 0, :], [b_Gr[0]], 1024), (Gr[0][:, 3, :], [b_Gr[0]], 1024)])
                        return nc
                    if NR > 0:
                        phaseA(0)
                    for r in range(NR):
                        if r + 1 < NR:
                            phaseA(r + 1)
                        phaseB(r)
                kb.barrier()
                if stop_after == "G3":
                    dump([(accG[:, 0, 0:1024], b_accG[0], 1024), (accG[:, 7, 1024:2048], b_accG[7], 1024)])
                    return nc
                with ExitStack() as sc3:
                    sbL = lambda name, shape, dt=F32: sc3.enter_context(nc.sbuf_tensor(name + "_h%d" % half, list(shape), dt))
                    gb2 = sbL("gb2", [128, 2, D])
                    b_gb2 = Buf("gb2")
                    kb.dma("sp", gb2[:, 0, :], lnp[2], writes=[b_gb2])
                    kb.dma("sp", gb2[:, 1, :], lnp[3], writes=[b_gb2])
                    hres = [sbL("hres%d" % i, [128, D]) for i in range(2)]
                    b_hres = bufs(2, "hres")
                    xc2 = sbL("xc2", [128, D])
                    b_xc2 = Buf("xc2")
                    jf2 = sbL("jf2", [128, D])
                    b_jf2 = Buf("jf2")
                    yo = [sbL("yo%d" % i, [128, D]) for i in range(2)]
                    b_yo = bufs(2, "yo")
                    st2 = sbL("st2", [128, 8])
                    b_st2 = Buf("st2")
                    outs = []
                    for tt in range(8):
                        T = 8 * half + tt
                        hi = tt % 2
                        kb.dma("sp", hres[hi][:, :], h_tok[T * 128:(T + 1) * 128, :], reads=[b_htok[T]],
                               writes=[b_hres[hi]])
                        kb.op("dve", lambda e, hi=hi, tt=tt: e.scalar_tensor_tensor(
                            out=hres[hi][:, :], in0=hres[hi][:, :], scalar=ALPHA, in1=accG[:, tt, :],
                            op0=ALU.mult, op1=ALU.add),
                            reads=b_accG[tt], writes=[b_hres[hi]])
                        layer_norm(None, hres[hi][:, :], b_hres[hi], gb2[:, 0, :], gb2[:, 1, :], b_gb2,
                                   yo[hi][:, :], b_yo[hi], xc2, b_xc2, jf2, b_jf2, st2, b_st2)
                        outs.append(kb.dma("sp", y[T * 128:(T + 1) * 128, :], yo[hi][:, :], reads=[b_yo[hi]]))
                    for t in outs:
                        kb.wait("sp", t)
                    if stop_after in ("G2", "G4"):
                        dump([(accG[:, 0, 0:1024], b_accG[0], 1024), (accG[:, 7, 1024:2048], b_accG[7], 1024),
                              (yo[1][:, 0:1024], [b_yo[1]], 1024), (hres[1][:, 0:1024], [b_hres[1]], 1024)])
                        return nc
        print("instructions:", kb.nins, "sbuf remaining:", nc.sbuf_bytes_remaining)
    return nc


def _t5_bucket(dist):
    dist = np.asarray(dist)
    d = np.maximum(dist, 1).astype(np.float32)
    large = 16 + (np.log(d / np.float32(16)) / np.float32(np.log(128 / 16)) * np.float32(16)).astype(np.int32)
    large = np.minimum(large, 31)
    return np.where(dist < 16, dist, large)


def _prep_shared(w_in, pool_w, pool_scale, rel_bias, w_out, ln1_g, ln1_b, peer_wq, peer_subkeys, peer_u, peer_v,
                 ln2_g, ln2_b):
    f = np.float32
    w = w_in[0]
    cols = []
    for c in range(8):
        cols.append(w[:, c * 128:(c + 1) * 128])
    for c in range(8):
        cols.append(w[:, 1024 + c * 128:1024 + (c + 1) * 128])
    for c in range(8):
        cols.append(w[:, 2048 + c * 128:2048 + (c + 1) * 128])
    for c in range(8):
        cols.append(w[:, 4096 + c * 128:4096 + (c + 1) * 128])
    ki = w[:, 5120:5184]
    cols.append(np.concatenate([ki, ki], axis=1))
    w_fm = np.stack([c.reshape(16, 128, 128).transpose(1, 0, 2) for c in cols]).astype(f)
    wv = w[:, 3072:4096]
    w_v = np.stack([wv[:, hg * 256:(hg + 1) * 256].reshape(16, 128, 256).transpose(1, 0, 2) for hg in range(4)])
    w_wi = w[:, 5184:5200].reshape(16, 128, 16).transpose(1, 0, 2)
    pw = pool_w[0].reshape(4, 2, 128, 256).transpose(2, 0, 1, 3)
    kk = np.arange(128)[:, None]
    qq = np.arange(128)[None, :]
    bt = np.zeros((128, 2, 8, 128), f)
    for dl in range(2):
        bkt = _t5_bucket(np.maximum(dl * 128 + qq - kk, 0))
        bt[:, dl, :, :] = rel_bias[bkt].transpose(0, 2, 1)
    wo = w_out[0].reshape(16, 128, D).transpose(1, 0, 2)
    lnp = np.stack([np.broadcast_to(a[0][None, :], (128, D)) for a in (ln1_g, ln1_b, ln2_g, ln2_b)])
    wqh = peer_wq[0].reshape(16, 128, D).transpose(1, 0, 2)
    skT = peer_subkeys[0].transpose(2, 0, 1)
    u = peer_u[0].reshape(128, 128, 16, 128)
    uT = u.transpose(1, 3, 2, 0)
    vv = peer_v[0].reshape(128, 128, D).transpose(1, 0, 2)
    c = lambda a: np.ascontiguousarray(a, dtype=f)
    return dict(w_fm=c(w_fm), w_v=c(w_v), w_wi=c(w_wi), pool_w=c(pw), biasT=c(bt), w_out=c(wo), lnp=c(lnp),
                wq=c(wqh), subkT=c(skT), uT=c(uT), vL=c(vv))


def _consts(hf, pool_scale, rel_bias):
    f = np.float32
    cst = np.zeros((128, 1024), f)
    cst[:, 0:128] = np.eye(128, dtype=f)
    cst[:, 128:256] = np.arange(128, dtype=f)[None, :]
    qq = np.arange(128)[:, None]
    kk = np.arange(128)[None, :]
    cst[:, 256:384] = np.where(kk <= qq, 0.0, NEG).astype(f)
    valid = 1.0 if hf == 1 else 0.0
    cst[:, 384] = valid
    cst[:, 385] = (valid - 1.0) * 1.0e30
    cst[:, 392:400] = rel_bias[31][None, :]
    for gq, wwin in enumerate((2, 4, 8, 16)):
        pos = np.arange(16)
        if hf == 0:
            corr = wwin / np.minimum(pos + 1, wwin).astype(f)
        else:
            corr = np.ones(16, f)
        cst[:, 400 + 16 * gq:400 + 16 * gq + 16] = corr[None, :]
    cst[:, 464:472] = pool_scale[0].reshape(8, 128).T
    cst[:, 480:512] = (2.0 ** -np.arange(32, dtype=np.float64)).astype(f)[None, :]
    return cst


def _core_inputs(x, shared, pool_scale, rel_bias):
    in_maps = []
    for c in range(8):
        b, hf = c // 2, c % 2
        own = x[b, hf * TOK:(hf + 1) * TOK]
        prev = x[b, 0:TOK] if hf == 1 else np.zeros_like(own)
        xT = np.stack([prev.T.reshape(16, 128, TOK).transpose(1, 0, 2), own.T.reshape(16, 128, TOK).transpose(1, 0, 2)])
        m = dict(shared)
        m["xT"] = np.ascontiguousarray(xT, dtype=np.float32)
        m["x_tok"] = np.ascontiguousarray(own, dtype=np.float32)
        m["consts"] = _consts(hf, pool_scale, rel_bias)
        in_maps.append(m)
    return in_maps


def kernel(x, w_in, pool_w, pool_scale, rel_bias, w_out, ln1_g, ln1_b, peer_wq, peer_subkeys, peer_u, peer_v,
           ln2_g, ln2_b):
    args = [np.asarray(a, dtype=np.float32) for a in (x, w_in, pool_w, pool_scale, rel_bias, w_out, ln1_g, ln1_b,
                                                      peer_wq, peer_subkeys, peer_u, peer_v, ln2_g, ln2_b)]
    (x, w_in, pool_w, pool_scale, rel_bias, w_out, ln1_g, ln1_b, peer_wq, peer_subkeys, peer_u, peer_v,
     ln2_g, ln2_b) = args
    shared = _prep_shared(w_in, pool_w, pool_scale, rel_bias, w_out, ln1_g, ln1_b, peer_wq, peer_subkeys,
                          peer_u, peer_v, ln2_g, ln2_b)
    in_maps = _core_inputs(x, shared, pool_scale, rel_bias)
    nc = build_nc()
    res = run_bass_kernel_spmd(nc, in_maps, core_ids=list(range(8)))
    out = np.zeros((4, S, D), np.float32)
    for c in range(8):
        b, hf = c // 2, c % 2
        out[b, hf * TOK:(hf + 1) * TOK] = res.results[c]["y"]
    return out
```

```python
import numpy as np
from contextlib import ExitStack
import concourse.bass as bass
import concourse.mybir as mybir
from concourse.bass_utils import run_bass_kernel_spmd

F32 = mybir.dt.float32
BF16 = mybir.dt.bfloat16
U32 = mybir.dt.uint32
ALU = mybir.AluOpType
AF = mybir.ActivationFunctionType
AX = mybir.AxisListType

D = 2048
S = 4096
TOK = 2048
NEG = -1.0e30
ALPHA = 2.0 ** 0.25
LN_EPS = 1e-5
NIT = 16
TOPK = 256
ATT_SCALE = 128.0 ** -0.5
NSLOT = 6


class Buf:
    __slots__ = ("w", "r", "name")

    def __init__(self, name=""):
        self.w = None
        self.r = {}
        self.name = name


class KB:
    def __init__(self, nc, es):
        self.nc = nc
        self.engs = {"pe": nc.tensor, "dve": nc.vector, "act": nc.scalar, "pool": nc.gpsimd, "sp": nc.sync}
        self.psem = {e: es.enter_context(nc.semaphore("prog_" + e)) for e in ["pe", "dve", "act", "pool"]}
        self.cnt = {e: 0 for e in self.psem}
        self.seen = {e: {} for e in self.engs}
        self.pending = {e: [] for e in self.engs}
        self.dslots = {q: [[es.enter_context(nc.semaphore("dq_%s_%d" % (q, i))), 0, "dq_%s_%d" % (q, i)]
                           for i in range(NSLOT)] for q in ["sp", "pool"]}
        self.dnext = {q: 0 for q in self.dslots}
        self.nins = 0

    def wait(self, e, tok):
        if tok is None:
            return
        sem, val, key = tok
        if self.seen[e].get(key, 0) >= val:
            return
        self.engs[e].wait_ge(sem, val)
        self.seen[e][key] = val

    def _deps(self, e, reads, writes, deps):
        for b in reads:
            self.wait(e, b.w)
        for b in writes:
            self.wait(e, b.w)
            for t in b.r.values():
                self.wait(e, t)
        for t in deps:
            self.wait(e, t)

    def op(self, e, fn, reads=(), writes=(), deps=(), sig=True):
        self._deps(e, reads, writes, deps)
        ins = fn(self.engs[e])
        self.nins += 1
        if not sig:
            self.pending[e].append((list(reads), list(writes)))
            return None
        self.cnt[e] += 1
        ins.then_inc(self.psem[e], 1)
        key = "prog_" + e
        tok = (self.psem[e], self.cnt[e], key)
        allr = list(reads)
        allw = list(writes)
        for (r, w) in self.pending[e]:
            allr += r
            allw += w
        self.pending[e] = []
        for b in allw:
            b.w = tok
            b.r = {}
        for b in allr:
            if b not in allw:
                b.r[key] = tok
        return tok

    def barrier(self):
        toks = []
        for e in self.psem:
            if self.cnt[e] > 0:
                toks.append((self.psem[e], self.cnt[e], "prog_" + e))
        for q in self.dslots:
            for sem, cnt, key in self.dslots[q]:
                if cnt > 0:
                    toks.append((sem, cnt, key))
        for e in self.engs:
            for t in toks:
                self.wait(e, t)

    def dma(self, q, out, in_, reads=(), writes=(), deps=()):
        self._deps(q, reads, writes, deps)
        slot = self.dslots[q][self.dnext[q]]
        self.dnext[q] = (self.dnext[q] + 1) % NSLOT
        sem, cnt, key = slot
        if cnt > 0:
            self.wait(q, (sem, cnt, key))
        self.engs[q].dma_start(out=out, in_=in_).then_inc(sem, 16)
        self.nins += 1
        slot[1] = cnt + 16
        tok = (sem, cnt + 16, key)
        for b in writes:
            b.w = tok
            b.r = {}
        for b in reads:
            b.r[key] = tok
        return tok


def sap(t, dims, off=0, parts=128, p0=0):
    fs = 1
    for s_ in t.shape[1:]:
        fs *= int(s_)
    return bass.AP(t, p0 * fs + off, [[fs, parts]] + [[int(a), int(b)] for a, b in dims])


def bufs(n, name=""):
    return [Buf("%s%d" % (name, i)) for i in range(n)]


def build_nc(stop_after=None, small_peer=False):
    nc = bass.Bass("TRN2", target_bir_lowering=False)
    dbg = {}

    def din(name, shape, dt=F32):
        return nc.dram_tensor(name, list(shape), dt, kind="ExternalInput").ap()

    xT = din("xT", [2, 128, 16, TOK])
    x_tok = din("x_tok", [TOK, D])
    w_fm = din("w_fm", [33, 128, 16, 128])
    w_v = din("w_v", [4, 128, 16, 256])
    w_wi = din("w_wi", [128, 16, 16])
    pool_w = din("pool_w", [128, 4, 2, 256])
    consts = din("consts", [128, 1024])
    biasT = din("biasT", [128, 2, 8, 128])
    w_out = din("w_out", [128, 16, D])
    lnp = din("lnp", [4, 128, D])
    wq = din("wq", [128, 16, D])
    subkT = din("subkT", [128, 2, 128])
    uT = din("uT", [8 if small_peer else 128, 128, 16, 128])
    vL = din("vL", [8 if small_peer else 128, 128, D])
    y = nc.dram_tensor("y", [TOK, D], F32, kind="ExternalOutput").ap()
    maskT_d = nc.dram_tensor("maskT_d", [4, 128, 32, 512], BF16).ap()
    hT_d = nc.dram_tensor("hT_d", [128, 16, TOK], BF16).ap()
    h_tok = nc.dram_tensor("h_tok", [TOK, D], F32).ap()
    W1 = nc.dram_tensor("W1", [32, 128, 128, 64], BF16).ap()
    if stop_after is not None:
        dbg_out = nc.dram_tensor("dbg", [128, 8192], F32, kind="ExternalOutput").ap()

    with ExitStack() as es:
        kb = KB(nc, es)
        sb = lambda name, shape, dt=F32: es.enter_context(nc.sbuf_tensor(name, list(shape), dt))
        ps = es.enter_context(nc.psum_tensor("ps", [128, 8, 512], F32))
        PB = bufs(8, "psb")

        dbg_stg = sb("dbg_stg", [128, 1024]) if stop_after is not None else None
        cst = sb("cst", [128, 1024])
        b_cst = Buf("cst")
        kb.dma("sp", cst[:], consts, writes=[b_cst])
        C_ID = 0
        C_IOTA = 128
        C_TRI = 256
        C_FLAG = 384
        C_B31 = 392
        C_CORR = 400
        C_PSC = 464
        C_PW2 = 480
        ident = cst[:, C_ID:C_ID + 128]
        iota = cst[:, C_IOTA:C_IOTA + 128]
        tri = cst[:, C_TRI:C_TRI + 128]
        bT = sb("bT", [128, 2, 8, 128])
        b_bT = Buf("bT")
        kb.dma("sp", bT[:], biasT, writes=[b_bT])
        ones_b = sb("ones_b", [128, 128], BF16)
        b_ones = Buf("ones")
        kb.op("pool", lambda e: e.memset(ones_b[:], 1.0), writes=[b_ones])

        evac_rr = [0]

        def evac(out, in_, reads, writes, scale=None):
            evac_rr[0] ^= 1
            if scale is not None:
                return kb.op("act", lambda e: e.activation(out=out, in_=in_, func=AF.Copy, scale=scale),
                             reads=reads, writes=writes)
            if evac_rr[0]:
                return kb.op("act", lambda e: e.activation(out=out, in_=in_, func=AF.Copy), reads=reads, writes=writes)
            return kb.op("dve", lambda e: e.tensor_copy(out=out, in_=in_), reads=reads, writes=writes)

        pb_rr = [0]

        def next_bank(choices):
            pb_rr[0] += 1
            return choices[pb_rr[0] % len(choices)]

        def dump(ap_list):
            stg = dbg_stg
            b_stg = Buf("dbgstg")
            col = 0
            for ap, bl, n in ap_list:
                kb.op("dve", lambda e, ap=ap, n=n: e.tensor_copy(out=stg[:, 0:n], in_=ap),
                      reads=bl, writes=[b_stg])
                t = kb.dma("sp", dbg_out[:, col:col + n], stg[:, 0:n], reads=[b_stg])
                kb.wait("sp", t)
                col += n

        xg_t = [None, None]
        b_xg = bufs(2, "xg")
        xg_rr = [0]

        def load_xg(s, tg):
            i = xg_rr[0]
            xg_rr[0] ^= 1
            for q4 in range(4):
                kb.dma("pool", xg_t[i][:, 4 * q4:4 * q4 + 4, :], xT[s, :, 4 * q4:4 * q4 + 4, tg * 512:(tg + 1) * 512],
                       writes=[b_xg[i]])
            return xg_t[i], b_xg[i]

        def proj_fm(wt, wb, xg, bx, out_ap, out_bufs, scale=None):
            bk = next_bank([0, 1, 2, 7])
            for kc in range(16):
                kb.op("pe", lambda e, kc=kc: e.matmul(ps[:, bk, :], lhsT=wt(kc), rhs=xg[:, kc, :],
                                                      start=(kc == 0), stop=(kc == 15)),
                      reads=[wb, bx], writes=[PB[bk]], sig=(kc == 15))
            return evac(out_ap, ps[:, bk, :], [PB[bk]], out_bufs, scale=scale)

        phA = ExitStack()
        es.enter_context(phA)
        xg_t[0] = phA.enter_context(nc.sbuf_tensor("xgA0", [128, 16, 512], BF16))
        xg_t[1] = phA.enter_context(nc.sbuf_tensor("xgA1", [128, 16, 512], BF16))
        qiT = phA.enter_context(nc.sbuf_tensor("qiT", [128, 8, TOK], BF16))
        b_qiT = bufs(4, "qiT")
        kiT = phA.enter_context(nc.sbuf_tensor("kiT", [128, 2 * TOK], BF16))
        b_kiT = bufs(8, "kiT")
        widx = phA.enter_context(nc.sbuf_tensor("widx", [128, 16, 16], F32))
        b_widx = bufs(16, "widx")
        with ExitStack() as sc:
            wqi = sc.enter_context(nc.sbuf_tensor("wqi", [128, 8, 16, 128], BF16))
            b_wqi = Buf("wqi")
            wki = sc.enter_context(nc.sbuf_tensor("wki", [128, 16, 128], BF16))
            b_wki = Buf("wki")
            wwi = sc.enter_context(nc.sbuf_tensor("wwi", [128, 16, 16], BF16))
            b_wwi = Buf("wwi")
            kb.dma("pool", wki[:], w_fm[32], writes=[b_wki])
            for c in range(8):
                kb.dma("pool", wqi[:, c], w_fm[24 + c], writes=[b_wqi])
            kb.dma("pool", wwi[:], w_wi, writes=[b_wwi])
            for s in range(2):
                for tg in range(4):
                    xg, bx = load_xg(s, tg)
                    kg = s * 4 + tg
                    proj_fm(lambda kc: wki[:, kc, :], b_wki, xg, bx, kiT[:, kg * 512:(kg + 1) * 512], [b_kiT[kg]])
                    if stop_after == "A1":
                        dump([(kiT[:, 0:512], b_kiT[0:1], 512), (xg[:, 0, :], [bx], 512)])
                        return nc
                    if s == 1:
                        for c in range(8):
                            proj_fm(lambda kc, c=c: wqi[:, c, kc, :], b_wqi, xg, bx,
                                    qiT[:, c, tg * 512:(tg + 1) * 512], [b_qiT[tg]])
                        for tt in range(4):
                            bk = next_bank([0, 1, 2, 7])
                            for kc in range(16):
                                kb.op("pe", lambda e, kc=kc, tt=tt: e.matmul(
                                    ps[:, bk, 0:16], lhsT=xg[:, kc, tt * 128:(tt + 1) * 128], rhs=wwi[:, kc, :],
                                    start=(kc == 0), stop=(kc == 15)),
                                    reads=[b_wwi, bx], writes=[PB[bk]], sig=(kc == 15))
                            evac(widx[:, tg * 4 + tt, :], ps[:, bk, 0:16], [PB[bk]], [b_widx[tg * 4 + tt]])
        if stop_after == "A":
            dump([(kiT[:, 0:2048], b_kiT[0:4], 2048), (kiT[:, 2048:4096], b_kiT[4:8], 2048),
                  (qiT[:, 0, 0:2048], b_qiT, 2048), (widx[:, :, :].rearrange("p a b -> p (a b)"), b_widx, 256)])
            return nc

        kb.barrier()
        b_maskd = bufs(4, "maskd")
        with ExitStack() as sc:
            sbB = lambda name, shape, dt=F32: sc.enter_context(nc.sbuf_tensor(name, list(shape), dt))
            score2 = [sbB("score%d" % p_, [128, 4096]) for p_ in range(2)]
            b_score2 = [bufs(8, "score%d_" % p_) for p_ in range(2)]
            maskf = sbB("maskf", [128, 4096])
            b_maskf = Buf("maskf")
            junkA = sbB("junkA", [128, 4096], BF16)
            b_junkA = Buf("junkA")
            Rt = [sbB("R%d" % i_, [128, 512]) for i_ in range(3)]
            b_R = bufs(3, "R")
            acc = sbB("accB", [128, 512])
            b_acc = Buf("acc")
            mx2 = [sbB("mxall%d" % p_, [128, 16]) for p_ in range(2)]
            b_mx2 = bufs(2, "mx")
            sm2 = [sbB("smallB%d" % p_, [128, 64]) for p_ in range(2)]
            b_sm2 = bufs(2, "small")
            b_nm2 = bufs(2, "negmid")
            b_cs2 = bufs(2, "cs")
            mT = sbB("maskTg", [128, 32, 512], BF16)
            b_mT = Buf("maskTg")
            r_rr = [0]

            def make_units(i):
                g = i // 4
                p_ = i % 2
                score, b_score, mxall, b_mx = score2[p_], b_score2[p_], mx2[p_], b_mx2[p_]
                NK = (17 + i) * 128
                nkt = (NK + 511) // 512
                units = []
                for kt in range(nkt):
                    wk = min(512, NK - kt * 512)
                    direct = kt >= 4
                    for hn, h in enumerate([0, 2, 4, 6, 8, 10, 12, 14, 1, 3, 5, 7, 9, 11, 13, 15]):
                        def unit(kt=kt, wk=wk, direct=direct, hn=hn, h=h):
                            cp, r0 = h // 2, 64 * (h % 2)
                            bk = next_bank([0, 1, 2])
                            kb.op("pe", lambda e: e.matmul(
                                ps[:, bk, 0:wk], lhsT=qiT[r0:r0 + 64, cp, i * 128:(i + 1) * 128],
                                rhs=kiT[r0:r0 + 64, kt * 512:kt * 512 + wk], start=True, stop=True),
                                reads=[b_qiT[g], b_kiT[kt]], writes=[PB[bk]])
                            ri = r_rr[0] % 3
                            r_rr[0] += 1
                            kb.op("act", lambda e: e.activation(
                                out=Rt[ri][:, 0:wk], in_=ps[:, bk, 0:wk], func=AF.Relu),
                                reads=[PB[bk]], writes=[b_R[ri]])
                            wcol = widx[:, i, h:h + 1]
                            if hn == 0:
                                kb.op("dve", lambda e: e.tensor_scalar(
                                    out=acc[:, 0:wk], in0=Rt[ri][:, 0:wk], scalar1=wcol, scalar2=None, op0=ALU.mult),
                                    reads=[b_R[ri], b_widx[i]], writes=[b_acc])
                            elif hn == 15 and direct:
                                kb.op("dve", lambda e: e.scalar_tensor_tensor(
                                    out=score[:, kt * 512:kt * 512 + wk], in0=Rt[ri][:, 0:wk], scalar=wcol,
                                    in1=acc[:, 0:wk], op0=ALU.mult, op1=ALU.add),
                                    reads=[b_R[ri], b_widx[i], b_acc], writes=[b_score[kt]])
                            else:
                                kb.op("dve", lambda e: e.scalar_tensor_tensor(
                                    out=acc[:, 0:wk], in0=Rt[ri][:, 0:wk], scalar=wcol,
                                    in1=acc[:, 0:wk], op0=ALU.mult, op1=ALU.add),
                                    reads=[b_R[ri], b_widx[i]], writes=[b_acc])
                            if hn == 15:
                                src = score[:, kt * 512:kt * 512 + wk] if direct else acc[:, 0:wk]
                                bsrc = b_score[kt] if direct else b_acc
                                kb.op("dve", lambda e: e.tensor_reduce(
                                    out=mxall[:, kt:kt + 1], in_=src, axis=AX.X, op=ALU.max),
                                    reads=[bsrc], writes=[b_mx])
                                kb.op("dve", lambda e: e.tensor_reduce(
                                    out=mxall[:, 8 + kt:9 + kt], in_=src, axis=AX.X, op=ALU.min),
                                    reads=[bsrc], writes=[b_mx])
                                if not direct:
                                    kb.op("dve", lambda e: e.tensor_scalar(
                                        out=score[:, kt * 512:kt * 512 + wk], in0=acc[:, 0:wk],
                                        scalar1=cst[:, C_FLAG:C_FLAG + 1], scalar2=cst[:, C_FLAG + 1:C_FLAG + 2],
                                        op0=ALU.mult, op1=ALU.add),
                                        reads=[b_acc, b_cst], writes=[b_score[kt]])
                        units.append(unit)
                return units

            def post_score(i):
                p_ = i % 2
                score, b_score, mxall, b_mx, sm, b_sm = score2[p_], b_score2[p_], mx2[p_], b_mx2[p_], sm2[p_], b_sm2[p_]
                NK = (17 + i) * 128
                nkt = (NK + 511) // 512
                negmid, tmpc, Mp, negD = sm[:, 0:1], sm[:, 2:3], sm[:, 3:4], sm[:, 8:8 + NIT + 1]
                kd = (NK - 128) // 512
                kb.op("dve", lambda e: e.tensor_tensor(
                    out=score[:, NK - 128:NK], in0=score[:, NK - 128:NK], in1=tri, op=ALU.add),
                    reads=[b_cst], writes=[b_score[kd]])
                kb.op("dve", lambda e: e.tensor_reduce(out=Mp, in_=mxall[:, 0:nkt], axis=AX.X, op=ALU.max),
                      reads=[b_mx], writes=[b_sm])
                kb.op("dve", lambda e: e.tensor_reduce(out=tmpc, in_=mxall[:, 8:8 + nkt], axis=AX.X, op=ALU.min),
                      reads=[b_mx], writes=[b_sm])
                kb.op("dve", lambda e: e.scalar_tensor_tensor(out=Mp, in0=tmpc, scalar=-1.0, in1=Mp, op0=ALU.mult,
                                                              op1=ALU.max), writes=[b_sm])
                kb.op("dve", lambda e: e.tensor_scalar(out=Mp, in0=Mp, scalar1=-1.001, scalar2=-1e-20,
                                                       op0=ALU.mult, op1=ALU.add), writes=[b_sm])
                kb.op("dve", lambda e: e.tensor_scalar(out=negD, in0=cst[:, C_PW2:C_PW2 + NIT + 1], scalar1=Mp,
                                                       scalar2=None, op0=ALU.mult), reads=[b_cst], writes=[b_sm])
                kb.op("dve", lambda e: e.memset(negmid, 0.0), writes=[b_nm2[p_]])

            def make_steps(i):
                p_ = i % 2
                score, b_score, sm, b_sm = score2[p_], b_score2[p_], sm2[p_], b_sm2[p_]
                NK = (17 + i) * 128
                nkt = (NK + 511) // 512
                negmid, cs, tmpc, negD = sm[:, 0:1], sm[:, 1:2], sm[:, 2:3], sm[:, 8:8 + NIT + 1]
                thr = 2.0 * TOPK - NK - 0.5
                steps = []
                for k in range(NIT):
                    def act_fn():
                        kb.op("act", lambda e: e.activation(
                            out=junkA[:, 0:NK], in_=score[:, 0:NK], func=AF.Sign, bias=negmid, scale=1.0,
                            accum_out=cs),
                            reads=b_score[0:nkt] + [b_nm2[p_]], writes=[b_junkA, b_cs2[p_]])

                    def dve_fn(k=k):
                        kb.op("dve", lambda e: e.tensor_scalar(out=tmpc, in0=cs, scalar1=thr, scalar2=0.5,
                                                               op0=ALU.is_ge, op1=ALU.subtract),
                              reads=[b_cs2[p_]], writes=[b_sm])
                        kb.op("dve", lambda e: e.scalar_tensor_tensor(
                            out=negmid, in0=tmpc, scalar=negD[:, k:k + 1], in1=negmid, op0=ALU.mult, op1=ALU.add),
                            reads=[b_sm], writes=[b_nm2[p_]])
                    steps.append((act_fn, dve_fn))
                return steps

            def finalize(i):
                g, j = i // 4, i % 4
                p_ = i % 2
                score, b_score, sm, b_sm = score2[p_], b_score2[p_], sm2[p_], b_sm2[p_]
                NK = (17 + i) * 128
                nkt = (NK + 511) // 512
                negmid, tau, negD = sm[:, 0:1], sm[:, 4:5], sm[:, 8:8 + NIT + 1]
                if j == 0:
                    kb.op("pool", lambda e: e.memset(mT[:], 0.0), writes=[b_mT])
                kb.op("dve", lambda e: e.tensor_tensor(out=tau, in0=negD[:, NIT:NIT + 1], in1=negmid, op=ALU.subtract),
                      reads=[b_nm2[p_]], writes=[b_sm])
                kb.op("dve", lambda e: e.tensor_scalar(
                    out=maskf[:, 0:NK], in0=score[:, 0:NK], scalar1=tau, scalar2=None, op0=ALU.is_ge),
                    reads=b_score[0:nkt] + [b_sm], writes=[b_maskf])
                nch = 17 + i
                for c0 in range(0, nch, 4):
                    n4 = min(4, nch - c0)
                    bk = next_bank([3, 4])
                    for cc in range(n4):
                        c = c0 + cc
                        kb.op("pe", lambda e, c=c, cc=cc: e.transpose(
                            ps[:, bk, cc * 128:(cc + 1) * 128], maskf[:, c * 128:(c + 1) * 128], ident),
                            reads=[b_maskf, b_cst], writes=[PB[bk]], sig=(cc == n4 - 1))
                    kb.op("act", lambda e, c0=c0, n4=n4: e.activation(
                        out=mT[:, c0:c0 + n4, j * 128:(j + 1) * 128],
                        in_=ps[:, bk, 0:n4 * 128].rearrange("p (c q) -> p c q", c=n4), func=AF.Copy),
                        reads=[PB[bk]], writes=[b_mT])
                if j == 3:
                    kb.dma("sp", maskT_d[g], mT[:], reads=[b_mT], writes=[b_maskd[g]])

            prev = None
            for i in range(16):
                units = make_units(i)
                steps = make_steps(prev) if prev is not None else []
                nU = len(units)
                spacing = max(4, nU // (NIT + 1))
                ka = 0
                kd_ = 0
                for u, unit in enumerate(units):
                    unit()
                    if ka < len(steps) and u == ka * spacing + 1:
                        steps[ka][0]()
                        ka += 1
                    if kd_ < len(steps) and kd_ < ka and u == kd_ * spacing + 1 + spacing // 2:
                        steps[kd_][1]()
                        kd_ += 1
                while kd_ < len(steps):
                    if ka == kd_:
                        steps[ka][0]()
                        ka += 1
                    steps[kd_][1]()
                    kd_ += 1
                post_score(i)
                if prev is not None:
                    finalize(prev)
                prev = i
            for st_ in make_steps(prev):
                st_[0]()
                st_[1]()
            finalize(prev)
        phA.close()
        pers = ExitStack()
        es.enter_context(pers)
        psb = lambda name, shape, dt=F32: pers.enter_context(nc.sbuf_tensor(name, list(shape), dt))
        poolT = psb("poolT", [128, 8, TOK], BF16)
        b_poolT = bufs(4, "poolT")
        attnT = psb("attnT", [128, 8, TOK], BF16)
        b_attnT = [bufs(4, "attnT%d" % h) for h in range(8)]
        scCD = ExitStack()
        es.enter_context(scCD)
        xg_t[0] = scCD.enter_context(nc.sbuf_tensor("xgC0", [128, 16, 512], BF16))
        xg_t[1] = xg_t[0]
        b_xg[0] = Buf("xgc0")
        b_xg[1] = b_xg[0]

        kb.barrier()
        with ExitStack() as sc:
            sbC = lambda name, shape, dt=F32: sc.enter_context(nc.sbuf_tensor(name, list(shape), dt))
            wpl = sbC("wpl", [128, 8, 16, 128], BF16)
            b_wpl = Buf("wpl")
            for c in range(8):
                kb.dma("pool", wpl[:, c], w_fm[c], writes=[b_wpl])
            pw = sbC("pw", [128, 4, 2, 256], BF16)
            b_pw = Buf("pw")
            kb.dma("pool", pw[:], pool_w, writes=[b_pw])
            xh = sbC("xh", [128, 16, 16], BF16)
            b_xh = Buf("xh")
            for q4 in range(4):
                kb.dma("pool", xh[:, 4 * q4:4 * q4 + 4, :], xT[0, :, 4 * q4:4 * q4 + 4, TOK - 16:TOK], writes=[b_xh])
            hal = sbC("hal", [128, 8, 16])
            b_hal = bufs(8, "hal")
            vb = [sbC("vb%d" % i, [128, 528]) for i in range(2)]
            b_vb = bufs(2, "vb")
            sa = sbC("sa", [128, 528])
            sbb = sbC("sbb", [128, 528])
            b_sa, b_sb = Buf("sa"), Buf("sb")
            t16 = sbC("t16", [128, 16])
            b_t16 = Buf("t16")
            plb = [sbC("plb%d" % i, [128, 512], BF16) for i in range(2)]
            b_plb = bufs(2, "plb")
            for cp in range(8):
                bk = next_bank([0, 1, 2, 7])
                for kc in range(16):
                    kb.op("pe", lambda e, kc=kc, cp=cp, bk=bk: e.matmul(
                        ps[:, bk, 0:16], lhsT=wpl[:, cp, kc, :], rhs=xh[:, kc, :], start=(kc == 0), stop=(kc == 15)),
                        reads=[b_wpl, b_xh], writes=[PB[bk]], sig=(kc == 15))
                evac(hal[:, cp, :], ps[:, bk, 0:16], [PB[bk]], [b_hal[cp]])
            vrr = 0
            for tg in range(4):
                xg, bx = load_xg(1, tg)
                for gq in range(4):
                    wwin = (2, 4, 8, 16)[gq]
                    for cc in range(2):
                        cp = 2 * gq + cc
                        vi = vrr % 2
                        vrr += 1
                        V = vb[vi]
                        bV = b_vb[vi]
                        kb.op("dve", lambda e, V=V, cp=cp: e.tensor_copy(out=V[:, 0:16], in_=hal[:, cp, :]),
                              reads=[b_hal[cp]], writes=[bV])
                        proj_fm(lambda kc, cp=cp: wpl[:, cp, kc, :], b_wpl, xg, bx, V[:, 16:528], [bV])
                        kb.op("act", lambda e, V=V, cp=cp: e.activation(out=hal[:, cp, :], in_=V[:, 512:528], func=AF.Copy),
                              reads=[bV], writes=[b_hal[cp]])
                        kb.op("dve", lambda e, V=V: e.tensor_tensor(out=sa[:, 1:528], in0=V[:, 1:528], in1=V[:, 0:527],
                                                                     op=ALU.add), reads=[bV], writes=[b_sa])
                        Sfin, bS = sa, b_sa
                        if gq >= 1:
                            kb.op("dve", lambda e: e.tensor_tensor(out=sbb[:, 3:528], in0=sa[:, 3:528], in1=sa[:, 1:526],
                                                                   op=ALU.add), reads=[b_sa], writes=[b_sb])
                            Sfin, bS = sbb, b_sb
                        if gq >= 2:
                            kb.op("dve", lambda e: e.tensor_tensor(out=sa[:, 7:528], in0=sbb[:, 7:528], in1=sbb[:, 3:524],
                                                                   op=ALU.add), reads=[b_sb], writes=[b_sa])
                            Sfin, bS = sa, b_sa
                        if gq >= 3:
                            kb.op("dve", lambda e: e.tensor_tensor(out=sbb[:, 15:528], in0=sa[:, 15:528], in1=sa[:, 7:520],
                                                                   op=ALU.add), reads=[b_sa], writes=[b_sb])
                            Sfin, bS = sbb, b_sb
                        kb.op("dve", lambda e, Sfin=Sfin, V=V, cc=cc, wwin=wwin: e.scalar_tensor_tensor(
                            out=plb[cc][:, :], in0=Sfin[:, 16:528], scalar=1.0 / wwin, in1=V[:, 16:528],
                            op0=ALU.mult, op1=ALU.subtract), reads=[bS, bV], writes=[b_plb[cc]])
                        if tg == 0:
                            kb.op("dve", lambda e, Sfin=Sfin, gq=gq: e.tensor_tensor(
                                out=t16[:, :], in0=Sfin[:, 16:32], in1=cst[:, C_CORR + 16 * gq:C_CORR + 16 * gq + 16],
                                op=ALU.mult), reads=[bS, b_cst], writes=[b_t16])
                            kb.op("dve", lambda e, V=V, cc=cc, wwin=wwin: e.scalar_tensor_tensor(
                                out=plb[cc][:, 0:16], in0=t16[:, :], scalar=1.0 / wwin, in1=V[:, 16:32],
                                op0=ALU.mult, op1=ALU.subtract), reads=[b_t16, bV], writes=[b_plb[cc]])
                    for dc in range(2):
                        bk = next_bank([0, 1, 2, 7])
                        for cc in range(2):
                            kb.op("pe", lambda e, cc=cc, dc=dc, gq=gq, bk=bk: e.matmul(
                                ps[:, bk, :], lhsT=pw[:, gq, cc, dc * 128:(dc + 1) * 128], rhs=plb[cc][:, :],
                                start=(cc == 0), stop=(cc == 1)),
                                reads=[b_pw, b_plb[cc]], writes=[PB[bk]], sig=(cc == 1))
                        oc = 2 * gq + dc
                        kb.op("act", lambda e, oc=oc, bk=bk, tg=tg: e.activation(
                            out=poolT[:, oc, tg * 512:(tg + 1) * 512], in_=ps[:, bk, :], func=AF.Copy,
                            scale=cst[:, C_PSC + oc:C_PSC + oc + 1]),
                            reads=[PB[bk], b_cst], writes=[b_poolT[tg]])
        if stop_after == "C":
            dump([(poolT[:, 0, 0:1024], b_poolT, 1024), (poolT[:, 7, 0:1024], b_poolT, 1024),
                  (poolT[:, 3, 1024:2048], b_poolT, 1024)])
            return nc

        kb.barrier()
        with ExitStack() as sc:
            sbD = lambda name, shape, dt=F32: sc.enter_context(nc.sbuf_tensor(name, list(shape), dt))
            wq2 = sbD("wq2", [128, 2, 16, 128], BF16)
            wk2 = sbD("wk2", [128, 2, 16, 128], BF16)
            wv2 = sbD("wv2", [128, 16, 256], BF16)
            b_wq2, b_wk2, b_wv2 = Buf("wq2"), Buf("wk2"), Buf("wv2")
            kT2 = sbD("kT2", [128, 2, 2 * TOK], BF16)
            b_kT2 = bufs(8, "kT2")
            v2 = sbD("v2", [128, 32, 256], BF16)
            b_v2 = bufs(8, "v2")
            qT2 = sbD("qT2", [128, 2, TOK], BF16)
            b_qT2 = bufs(4, "qT2")
            mk = sbD("mk", [128, 32, 512], BF16)
            b_mk = Buf("mk")
            Et = [sbD("E%d" % i, [128, 512], BF16) for i in range(3)]
            b_E = bufs(3, "E")
            Pt = [sbD("P%d" % i, [128, 512], BF16) for i in range(3)]
            b_P = bufs(3, "P")
            tmpn = [sbD("tmpn%d" % i, [128, 128]) for i in range(2)]
            b_tmpn = bufs(2, "tmpn")
            rden = sbD("rden", [128, 512])
            b_rden = Buf("rden")
            SB_S = [0, 1, 2]
            for hg in range(4):
                for hl in range(2):
                    kb.dma("pool", wq2[:, hl], w_fm[8 + 2 * hg + hl], writes=[b_wq2])
                    kb.dma("pool", wk2[:, hl], w_fm[16 + 2 * hg + hl], writes=[b_wk2])
                kb.dma("pool", wv2[:], w_v[hg], writes=[b_wv2])
                for s in range(2):
                    for tg in range(4):
                        xg, bx = load_xg(s, tg)
                        kg = s * 4 + tg
                        for hl in range(2):
                            proj_fm(lambda kc, hl=hl: wk2[:, hl, kc, :], b_wk2, xg, bx,
                                    kT2[:, hl, kg * 512:(kg + 1) * 512], [b_kT2[kg]])
                            if s == 1:
                                proj_fm(lambda kc, hl=hl: wq2[:, hl, kc, :], b_wq2, xg, bx,
                                        qT2[:, hl, tg * 512:(tg + 1) * 512], [b_qT2[tg]])
                        for tt in range(4):
                            bk = 7
                            for kc in range(16):
                                kb.op("pe", lambda e, kc=kc, tt=tt, xg=xg: e.matmul(
                                    ps[:, bk, 0:256], lhsT=xg[:, kc, tt * 128:(tt + 1) * 128], rhs=wv2[:, kc, :],
                                    start=(kc == 0), stop=(kc == 15)),
                                    reads=[b_wv2, bx], writes=[PB[bk]], sig=(kc == 15))
                            evac(v2[:, kg * 4 + tt, :], ps[:, bk, 0:256], [PB[bk]], [b_v2[kg]])
                if stop_after == "D0":
                    dump([(kT2[:, 0, 0:1024], b_kT2[0:2], 1024), (qT2[:, 1, 0:1024], b_qT2[0:2], 1024),
                          (v2[:, 0:4, :].rearrange("p a b -> p (a b)"), b_v2[0:1], 1024)])
                    return nc
                for g in range(4):
                    kb.dma("sp", mk[:], maskT_d[g], reads=[b_maskd[g]], writes=[b_mk])
                    nch = 20 + 4 * g
                    for hl in range(2):
                        h = 2 * hg + hl
                        bO = 3 + 2 * (hl % 2)
                        bD = bO + 1

                        def issue_S(c, hl=hl, g=g):
                            bk = SB_S[c % 3]
                            kb.op("pe", lambda e: e.matmul(
                                ps[:, bk, :], lhsT=kT2[:, hl, c * 128:(c + 1) * 128],
                                rhs=qT2[:, hl, g * 512:(g + 1) * 512], start=True, stop=True),
                                reads=[b_kT2[c // 4], b_qT2[g]], writes=[PB[bk]])
                        issue_S(0)
                        if nch > 1:
                            issue_S(1)
                        for c in range(nch):
                            if c + 2 < nch:
                                issue_S(c + 2)
                            bk = SB_S[c % 3]
                            ei = c % 3
                            tokE = kb.op("act", lambda e, bk=bk, ei=ei, h=h: e.activation(
                                out=Et[ei][:, :], in_=ps[:, bk, :], func=AF.Exp, scale=ATT_SCALE,
                                bias=cst[:, C_B31 + h:C_B31 + h + 1]),
                                reads=[PB[bk], b_cst], writes=[b_E[ei]])
                            cn = c - (15 + 4 * g)
                            if 0 <= cn <= 4:
                                for j in range(4):
                                    dl = 1 + j - cn
                                    if dl in (0, 1):
                                        ti = (j + cn) % 2
                                        kb.op("dve", lambda e, bk=bk, j=j, dl=dl, h=h, ti=ti: e.scalar_tensor_tensor(
                                            out=tmpn[ti][:, :], in0=ps[:, bk, j * 128:(j + 1) * 128], scalar=ATT_SCALE,
                                            in1=bT[:, dl, h, :], op0=ALU.mult, op1=ALU.add),
                                            reads=[PB[bk], b_bT], writes=[b_tmpn[ti]], deps=[tokE])
                                        kb.op("act", lambda e, ei=ei, j=j, ti=ti: e.activation(
                                            out=Et[ei][:, j * 128:(j + 1) * 128], in_=tmpn[ti][:, :], func=AF.Exp),
                                            reads=[b_tmpn[ti]], writes=[b_E[ei]])
                            kb.op("dve", lambda e, ei=ei, c=c: e.tensor_tensor(
                                out=Pt[ei][:, :], in0=Et[ei][:, :], in1=mk[:, c, :], op=ALU.mult),
                                reads=[b_E[ei], b_mk], writes=[b_P[ei]])
                            kb.op("pe", lambda e, ei=ei, c=c, hl=hl, bO=bO, nch=nch: e.matmul(
                                ps[:, bO, :], lhsT=v2[:, c, hl * 128:(hl + 1) * 128], rhs=Pt[ei][:, :],
                                start=(c == 0), stop=(c == nch - 1)),
                                reads=[b_v2[c // 4], b_P[ei]], writes=[PB[bO]], sig=(c == nch - 1))
                            kb.op("pe", lambda e, ei=ei, c=c, bD=bD, nch=nch: e.matmul(
                                ps[:, bD, :], lhsT=ones_b[:, :], rhs=Pt[ei][:, :],
                                start=(c == 0), stop=(c == nch - 1)),
                                reads=[b_ones, b_P[ei]], writes=[PB[bD]], sig=(c == nch - 1))
                        kb.op("dve", lambda e, bD=bD: e.reciprocal(out=rden[:, :], in_=ps[:, bD, :]),
                              reads=[PB[bD]], writes=[b_rden])
                        kb.op("dve", lambda e, bO=bO, h=h, g=g: e.tensor_tensor(
                            out=attnT[:, h, g * 512:(g + 1) * 512], in0=ps[:, bO, :], in1=rden[:, :], op=ALU.mult),
                            reads=[PB[bO], b_rden], writes=[b_attnT[h][g]])
                        if stop_after == "D1":
                            dump([(attnT[:, 0, 0:512], [b_attnT[0][0]], 512), (rden[:, :], [b_rden], 512)])
                            return nc
        if stop_after == "D":
            dump([(attnT[:, 0, 0:1024], b_attnT[0], 1024), (attnT[:, 7, 0:1024], b_attnT[7], 1024),
                  (attnT[:, 3, 1024:2048], b_attnT[3], 1024)])
            return nc

        scCD.close()
        kb.barrier()
        b_hT_d = bufs(4, "hT_d")
        b_htok = bufs(16, "htok")

        def layer_norm(e_sb, src, bsrc, gam, bet, b_gb, outt, bout, xc, b_xc, junkf, b_jf, st, b_st):
            kb.op("dve", lambda e: e.tensor_scalar(out=junkf[:, :], in0=src, scalar1=1.0, scalar2=None, op0=ALU.mult,
                                                   op1=ALU.add, accum_out=st[:, 0:1]), reads=[bsrc], writes=[b_jf, b_st])
            kb.op("dve", lambda e: e.tensor_scalar(out=st[:, 1:2], in0=st[:, 0:1], scalar1=1.0 / D, scalar2=None,
                                                   op0=ALU.mult), writes=[b_st])
            kb.op("dve", lambda e: e.tensor_scalar(out=xc[:, :], in0=src, scalar1=st[:, 1:2], scalar2=None,
                                                   op0=ALU.subtract), reads=[bsrc, b_st], writes=[b_xc])
            kb.op("dve", lambda e: e.tensor_tensor(out=junkf[:, :], in0=xc[:, :], in1=xc[:, :], op=ALU.mult),
                  reads=[b_xc], writes=[b_jf])
            kb.op("dve", lambda e: e.tensor_scalar(out=junkf[:, :], in0=junkf[:, :], scalar1=1.0, scalar2=None,
                                                   op0=ALU.mult, op1=ALU.add, accum_out=st[:, 2:3]),
                  writes=[b_jf, b_st])
            kb.op("dve", lambda e: e.tensor_scalar(out=st[:, 3:4], in0=st[:, 2:3], scalar1=1.0 / D, scalar2=LN_EPS,
                                                   op0=ALU.mult, op1=ALU.add), writes=[b_st])
            kb.op("act", lambda e: e.activation(out=st[:, 4:5], in_=st[:, 3:4], func=AF.Sqrt), writes=[b_st])
            kb.op("dve", lambda e: e.reciprocal(out=st[:, 5:6], in_=st[:, 4:5]), writes=[b_st])
            kb.op("dve", lambda e: e.scalar_tensor_tensor(out=xc[:, :], in0=xc[:, :], scalar=st[:, 5:6], in1=gam,
                                                          op0=ALU.mult, op1=ALU.mult),
                  reads=[b_st, b_gb], writes=[b_xc])
            return kb.op("dve", lambda e: e.tensor_tensor(out=outt, in0=xc[:, :], in1=bet, op=ALU.add),
                         reads=[b_xc, b_gb], writes=[bout])

        with ExitStack() as sc:
            sbE = lambda name, shape, dt=F32: sc.enter_context(nc.sbuf_tensor(name, list(shape), dt))
            wo2 = [sbE("wo%d" % i, [128, 16, 512], BF16) for i in range(2)]
            b_wo2 = bufs(2, "wo")
            wo_rr = [0]
            gb = sbE("gb1", [128, 2, D])
            b_gb = Buf("gb1")
            kb.dma("sp", gb[:, 0, :], lnp[0], writes=[b_gb])
            kb.dma("sp", gb[:, 1, :], lnp[1], writes=[b_gb])
            xt = [sbE("xt0", [128, D])] * 2
            b_xt = [Buf("xt")] * 2
            hpre = sbE("hpre", [128, D])
            b_hpre = Buf("hpre")
            xc = sbE("xc", [128, D])
            b_xc = Buf("xc")
            junkf = sbE("junkf", [128, D])
            b_jf = Buf("junkf")
            hh = [sbE("hh0", [128, D])] * 2
            b_hh = [Buf("hh")] * 2
            st = sbE("st", [128, 8])
            b_st = Buf("st")
            hTs = sbE("hTs", [128, 16, 128], BF16)
            b_hTs = Buf("hTs")
            for tt in range(16):
                tg = tt // 4
                xi = tt % 2
                kb.dma("sp", xt[xi][:, :], x_tok[tt * 128:(tt + 1) * 128, :], writes=[b_xt[xi]])
                for dt_ in range(4):
                    bk = next_bank([0, 1, 2, 7])
                    wi_ = wo_rr[0] % 2
                    wo_rr[0] += 1
                    wo, b_wo = wo2[wi_], b_wo2[wi_]
                    for q4 in range(4):
                        kb.dma("pool", wo[:, 4 * q4:4 * q4 + 4, :], w_out[:, 4 * q4:4 * q4 + 4, dt_ * 512:(dt_ + 1) * 512],
                               writes=[b_wo])
                    for kc in range(16):
                        if kc < 8:
                            lh = poolT[:, kc, tt * 128:(tt + 1) * 128]
                            rb = b_poolT[tg]
                        else:
                            lh = attnT[:, kc - 8, tt * 128:(tt + 1) * 128]
                            rb = b_attnT[kc - 8][tg]
                        kb.op("pe", lambda e, lh=lh, kc=kc, dt_=dt_, bk=bk, wo=wo: e.matmul(
                            ps[:, bk, :], lhsT=lh, rhs=wo[:, kc, :],
                            start=(kc == 0), stop=(kc == 15)),
                            reads=[rb, b_wo], writes=[PB[bk]], sig=(kc == 15))
                    kb.op("dve", lambda e, xi=xi, dt_=dt_, bk=bk: e.scalar_tensor_tensor(
                        out=hpre[:, dt_ * 512:(dt_ + 1) * 512], in0=xt[xi][:, dt_ * 512:(dt_ + 1) * 512], scalar=ALPHA,
                        in1=ps[:, bk, :], op0=ALU.mult, op1=ALU.add),
                        reads=[b_xt[xi], PB[bk]], writes=[b_hpre])
                hi = tt % 2
                layer_norm(None, hpre[:, :], b_hpre, gb[:, 0, :], gb[:, 1, :], b_gb, hh[hi][:, :], b_hh[hi],
                           xc, b_xc, junkf, b_jf, st, b_st)
                kb.dma("sp", h_tok[tt * 128:(tt + 1) * 128, :], hh[hi][:, :], reads=[b_hh[hi]], writes=[b_htok[tt]])
                for c0 in range(0, 16, 4):
                    bk = next_bank([3, 4, 5, 6])
                    for cc in range(4):
                        kb.op("pe", lambda e, c0=c0, cc=cc, bk=bk, hi=hi: e.transpose(
                            ps[:, bk, cc * 128:(cc + 1) * 128], hh[hi][:, (c0 + cc) * 128:(c0 + cc + 1) * 128], ident),
                            reads=[b_hh[hi], b_cst], writes=[PB[bk]], sig=(cc == 3))
                    evac(hTs[:, c0:c0 + 4, :],
                         ps[:, bk, :].rearrange("p (c q) -> p c q", c=4), [PB[bk]], [b_hTs])
                for q4 in range(4):
                    kb.dma("sp", hT_d[:, 4 * q4:4 * q4 + 4, tt * 128:(tt + 1) * 128], hTs[:, 4 * q4:4 * q4 + 4, :],
                           reads=[b_hTs], writes=[b_hT_d[tg]])
        pers.close()
        if stop_after == "E":
            t1 = kb.dma("sp", dbg_out[:, 0:2048], h_tok[0:128, :], reads=[b_htok[0]])
            t2 = kb.dma("sp", dbg_out[:, 2048:4096], h_tok[1920:2048, :], reads=[b_htok[15]])
            kb.wait("sp", t1)
            kb.wait("sp", t2)
            return nc

        kb.barrier()
        b_W1 = bufs(32, "W1")
        with ExitStack() as sc:
            sbF = lambda name, shape, dt=F32: sc.enter_context(nc.sbuf_tensor(name, list(shape), dt))
            wqc = [sbF("wqc%d" % i, [128, 16, 128], BF16) for i in range(3)]
            b_wqc = bufs(3, "wqc")
            wq_rr = [0]
            skT = sbF("skT", [128, 2, 128], BF16)
            b_skT = Buf("skT")
            kb.dma("pool", skT[:], subkT, writes=[b_skT])
            hTg = [sbF("hTg0", [128, 16, 512], BF16)] * 2
            b_hTg = [Buf("hTg")] * 2
            qpT = sbF("qpT", [128, 16, 512], BF16)
            b_qpT = Buf("qpT")
            s_sb = sbF("s_sb", [128, 16, 128])
            s2_sb = sbF("s2_sb", [128, 16, 128])
            b_s, b_s2 = Buf("s"), Buf("s2")
            v16 = sbF("v16", [128, 16, 16])
            i16 = sbF("i16", [128, 16, 16], U32)
            i16f = sbF("i16f", [128, 16, 16])
            b_v16, b_i16, b_i16f = Buf("v16"), Buf("i16"), Buf("i16f")
            cand = sbF("cand", [128, 8, 256])
            cand2 = sbF("cand2", [128, 8, 256])
            b_cand, b_cand2 = Buf("cand"), Buf("cand2")
            best = sbF("best", [128, 8, 16])
            bidx = sbF("bidx", [128, 8, 16], U32)
            aiu = sbF("aiu", [128, 128], U32)
            biu = sbF("biu", [128, 128], U32)
            af = sbF("af", [128, 128])
            bf = sbF("bf", [128, 128])
            b_best, b_bidx, b_bidf, b_af, b_bf = Buf("best"), Buf("bidx"), Buf("bidf"), Buf("af"), Buf("bf")
            eb = sbF("eb", [128, 8, 16])
            zz = sbF("zz", [128, 16])
            b_eb, b_zz = Buf("eb"), Buf("zz")
            oh = sbF("oh", [128, 128, 16])
            b_oh = Buf("oh")
            IG = sbF("IG", [128, 3, 128])
            b_IG = Buf("IG")
            IGT = sbF("IGT", [128, 3, 128])
            b_IGT = Buf("IGT")
            eq2 = [sbF("eq0", [128, 32, 128], BF16)] * 2
            b_eq2 = [Buf("eq")] * 2
            At2 = [sbF("At0", [128, 32, 128], BF16)] * 2
            Bt2 = [sbF("Bt0", [128, 32, 128], BF16)] * 2
            b_At2, b_Bt2 = [Buf("At")] * 2, [Buf("Bt")] * 2
            ab_rr = [0]
            stg = [sbF("stg0", [128, 128, 64], BF16)] * 2
            b_stg = [Buf("stg")] * 2
            for tg in range(4):
                hx = hTg[tg % 2]
                bhx = b_hTg[tg % 2]
                for q4 in range(4):
                    kb.dma("sp", hx[:, 4 * q4:4 * q4 + 4, :], hT_d[:, 4 * q4:4 * q4 + 4, tg * 512:(tg + 1) * 512],
                           reads=[b_hT_d[tg]], writes=[bhx])
                for n in range(16):
                    wi_ = wq_rr[0] % 3
                    wq_rr[0] += 1
                    for q4 in range(4):
                        kb.dma("pool", wqc[wi_][:, 4 * q4:4 * q4 + 4, :], wq[:, 4 * q4:4 * q4 + 4, n * 128:(n + 1) * 128],
                               writes=[b_wqc[wi_]])
                    proj_fm(lambda kc, wi_=wi_: wqc[wi_][:, kc, :], b_wqc[wi_], hx, bhx, qpT[:, n, :], [b_qpT])
                for tt in range(4):
                    T = 4 * tg + tt
                    for n4 in range(4):
                        bk = next_bank([3, 4, 5, 6])
                        for nn in range(4):
                            n = 4 * n4 + nn
                            kb.op("pe", lambda e, n=n, nn=nn, bk=bk, tt=tt: e.matmul(
                                ps[:, bk, nn * 128:(nn + 1) * 128], lhsT=qpT[:, n, tt * 128:(tt + 1) * 128],
                                rhs=skT[:, n % 2, :], start=True, stop=True),
                                reads=[b_qpT, b_skT], writes=[PB[bk]], sig=(nn == 3))
                        evac(s_sb[:, 4 * n4:4 * n4 + 4, :], ps[:, bk, :].rearrange("p (c q) -> p c q", c=4),
                             [PB[bk]], [b_s])
                    bv, bi, bs2 = bufs(16, "v16n"), bufs(16, "i16n"), bufs(16, "s2n")
                    for n in range(16):
                        kb.op("dve", lambda e, n=n: e.max(out=v16[:, n, 0:8], in_=s_sb[:, n, :]),
                              reads=[b_s], writes=[bv[n]], deps=[b_v16.w] + list(b_v16.r.values()))
                    for n in range(16):
                        kb.op("dve", lambda e, n=n: e.max_index(out=i16[:, n, 0:8], in_max=v16[:, n, 0:8],
                                                               in_values=s_sb[:, n, :]),
                              reads=[b_s, bv[n]], writes=[bi[n]], deps=[b_i16.w] + list(b_i16.r.values()))
                    for n in range(16):
                        kb.op("dve", lambda e, n=n: e.match_replace(out=s2_sb[:, n, :], in_to_replace=v16[:, n, 0:8],
                                                                   in_values=s_sb[:, n, :], imm_value=NEG),
                              reads=[b_s, bv[n]], writes=[bs2[n]])
                    for n in range(16):
                        kb.op("dve", lambda e, n=n: e.max(out=v16[:, n, 8:16], in_=s2_sb[:, n, :]),
                              reads=[bs2[n]], writes=[bv[n]])
                    for n in range(16):
                        kb.op("dve", lambda e, n=n: e.max_index(out=i16[:, n, 8:16], in_max=v16[:, n, 8:16],
                                                               in_values=s2_sb[:, n, :]),
                              reads=[bs2[n], bv[n]], writes=[bi[n]])
                    kb.op("dve", lambda e: e.tensor_copy(out=i16f[:], in_=i16[:]), reads=bi, writes=[b_i16f, b_i16])
                    kb.op("dve", lambda e: e.tensor_tensor(
                        out=cand[:].rearrange("p h (a b) -> p h a b", a=16),
                        in0=sap(v16, [[32, 8], [1, 16], [0, 16]]),
                        in1=sap(v16, [[32, 8], [0, 16], [1, 16]], off=16), op=ALU.add),
                        reads=bv, writes=[b_cand, b_v16])
                    bb, bx_, bc2 = bufs(8, "besth"), bufs(8, "bidxh"), bufs(8, "cand2h")
                    for h in range(8):
                        kb.op("dve", lambda e, h=h: e.max(out=best[:, h, 0:8], in_=cand[:, h, :]),
                              reads=[b_cand], writes=[bb[h]], deps=[b_best.w] + list(b_best.r.values()))
                    for h in range(8):
                        kb.op("dve", lambda e, h=h: e.max_index(out=bidx[:, h, 0:8], in_max=best[:, h, 0:8],
                                                               in_values=cand[:, h, :]),
                              reads=[b_cand, bb[h]], writes=[bx_[h]], deps=[b_bidx.w] + list(b_bidx.r.values()))
                    for h in range(8):
                        kb.op("dve", lambda e, h=h: e.match_replace(out=cand2[:, h, :], in_to_replace=best[:, h, 0:8],
                                                                   in_values=cand[:, h, :], imm_value=NEG),
                              reads=[b_cand, bb[h]], writes=[bc2[h]])
                    for h in range(8):
                        kb.op("dve", lambda e, h=h: e.max(out=best[:, h, 8:16], in_=cand2[:, h, :]),
                              reads=[bc2[h]], writes=[bb[h]])
                    for h in range(8):
                        kb.op("dve", lambda e, h=h: e.max_index(out=bidx[:, h, 8:16], in_max=best[:, h, 8:16],
                                                               in_values=cand2[:, h, :]),
                              reads=[bc2[h], bb[h]], writes=[bx_[h]])
                    kb.op("dve", lambda e: e.tensor_copy(out=zz[:, 0:1], in_=best[:, 0, 0:1]),
                          reads=bb + bx_, writes=[b_best, b_bidx, b_zz])
                    kb.op("dve", lambda e: e.tensor_tensor(out=eb[:], in0=best[:], in1=sap(best, [[16, 8], [0, 16]]),
                                                           op=ALU.subtract), reads=[b_best], writes=[b_eb])
                    kb.op("act", lambda e: e.activation(out=eb[:], in_=eb[:], func=AF.Exp), writes=[b_eb])
                    kb.op("dve", lambda e: e.tensor_reduce(out=zz[:, 0:8], in_=eb[:], axis=AX.X, op=ALU.add),
                          reads=[b_eb], writes=[b_zz])
                    kb.op("dve", lambda e: e.reciprocal(out=zz[:, 8:16], in_=zz[:, 0:8]), writes=[b_zz])
                    kb.op("dve", lambda e: e.tensor_tensor(
                        out=IG[:, 2, :].rearrange("p (h r) -> p h r", h=8), in0=eb[:],
                        in1=sap(zz, [[1, 8], [0, 16]], off=8), op=ALU.mult),
                        reads=[b_eb, b_zz], writes=[b_IG])
                    kb.op("dve", lambda e: e.tensor_single_scalar(out=aiu[:], in_=bidx[:].rearrange("p h r -> p (h r)"),
                                                                  scalar=4, op=ALU.logical_shift_right),
                          reads=[b_bidx], writes=[b_bidf])
                    kb.op("dve", lambda e: e.tensor_single_scalar(out=biu[:], in_=bidx[:].rearrange("p h r -> p (h r)"),
                                                                  scalar=15, op=ALU.bitwise_and),
                          reads=[b_bidx], writes=[b_bidf])
                    kb.op("dve", lambda e: e.tensor_copy(out=af[:], in_=aiu[:]), reads=[b_bidf], writes=[b_af])
                    kb.op("dve", lambda e: e.tensor_copy(out=bf[:], in_=biu[:]), reads=[b_bidf], writes=[b_bf])
                    for which, sel, boff in ((0, af, 0), (1, bf, 16)):
                        bsel = b_af if which == 0 else b_bf
                        kb.op("dve", lambda e, sel=sel: e.tensor_tensor(
                            out=oh[:], in0=sap(cst, [[0, 128], [1, 16]], off=C_IOTA),
                            in1=sap(sel, [[1, 128], [0, 16]]), op=ALU.is_equal),
                            reads=[bsel, b_cst], writes=[b_oh])
                        kb.op("dve", lambda e, boff=boff: e.tensor_tensor(
                            out=oh[:].rearrange("p (h r) a -> p h r a", h=8),
                            in0=oh[:].rearrange("p (h r) a -> p h r a", h=8),
                            in1=sap(i16f, [[32, 8], [0, 16], [1, 16]], off=boff), op=ALU.mult),
                            reads=[b_i16f], writes=[b_oh])
                        kb.op("dve", lambda e, which=which: e.tensor_reduce(out=IG[:, which, :], in_=oh[:], axis=AX.X,
                                                                          op=ALU.add),
                              reads=[b_oh], writes=[b_IG])
                    bk = next_bank([3, 4, 5, 6])
                    for w3 in range(3):
                        kb.op("pe", lambda e, w3=w3, bk=bk: e.transpose(ps[:, bk, w3 * 128:(w3 + 1) * 128],
                                                                        IG[:, w3, :], ident),
                              reads=[b_IG, b_cst], writes=[PB[bk]], sig=(w3 == 2))
                    kb.op("act", lambda e, bk=bk: e.activation(out=IGT[:], in_=ps[:, bk, 0:384].rearrange(
                        "p (c q) -> p c q", c=3), func=AF.Copy), reads=[PB[bk]], writes=[b_IGT])
                    for sbk in range(2):
                        si = (2 * T + sbk) % 2
                        for s32 in range(2):
                            t0 = sbk * 64 + s32 * 32
                            ai_ = ab_rr[0] % 2
                            ab_rr[0] += 1
                            eq, b_eq = eq2[ai_], b_eq2[ai_]
                            At, b_At = At2[ai_], b_At2[ai_]
                            Bt, b_Bt = Bt2[ai_], b_Bt2[ai_]
                            kb.op("dve", lambda e, t0=t0: e.tensor_tensor(
                                out=eq[:], in0=sap(cst, [[0, 32], [1, 128]], off=C_IOTA),
                                in1=sap(IGT, [[1, 32], [0, 128]], off=0 * 128 + t0), op=ALU.is_equal),
                                reads=[b_IGT, b_cst], writes=[b_eq])
                            kb.op("dve", lambda e, t0=t0: e.tensor_tensor(
                                out=At[:], in0=eq[:], in1=sap(IGT, [[1, 32], [0, 128]], off=2 * 128 + t0), op=ALU.mult),
                                reads=[b_eq, b_IGT], writes=[b_At])
                            kb.op("dve", lambda e, t0=t0: e.tensor_tensor(
                                out=Bt[:], in0=sap(cst, [[0, 32], [1, 128]], off=C_IOTA),
                                in1=sap(IGT, [[1, 32], [0, 128]], off=1 * 128 + t0), op=ALU.is_equal),
                                reads=[b_IGT, b_cst], writes=[b_Bt])
                            for t4 in range(8):
                                bk = next_bank([0, 1, 2, 7])
                                for q4 in range(4):
                                    tl = 4 * t4 + q4
                                    kb.op("pe", lambda e, tl=tl, q4=q4, bk=bk: e.matmul(
                                        ps[:, bk, q4 * 128:(q4 + 1) * 128], lhsT=At[:, tl, :], rhs=Bt[:, tl, :],
                                        start=True, stop=True),
                                        reads=[b_At, b_Bt], writes=[PB[bk]], sig=(q4 == 3))
                                kb.op("act", lambda e, t4=t4, bk=bk, si=si, s32=s32: e.activation(
                                    out=sap(stg[si], [[1, 4], [64, 128]], off=s32 * 32 + 4 * t4),
                                    in_=ps[:, bk, :].rearrange("p (t j) -> p t j", t=4), func=AF.Copy),
                                    reads=[PB[bk]], writes=[b_stg[si]])
                        kb.dma("sp", W1[2 * T + sbk], stg[si][:], reads=[b_stg[si]], writes=[b_W1[2 * T + sbk]])
                    if stop_after == "F2" and T == 0:
                        wchk = sbF("wchk", [128, 1024], BF16)
                        b_wchk = Buf("wchk")
                        kb.dma("sp", wchk[:], W1[0, :, 0:16, :].rearrange("i j t -> i (j t)"), reads=[b_W1[0]], writes=[b_wchk])
                        dump([(wchk[:, :], [b_wchk], 1024), (IG[:, 0, :], [b_IG], 128), (IG[:, 1, :], [b_IG], 128),
                              (IG[:, 2, :], [b_IG], 128)])
                        return nc
                    if stop_after == "F" and T == 0:
                        dump([(IG[:, 0, :], [b_IG], 128), (IG[:, 1, :], [b_IG], 128), (IG[:, 2, :], [b_IG], 128),
                              (s_sb[:, 0, :], [b_s], 128), (s_sb[:, 1, :], [b_s], 128)])
                        return nc

        kb.barrier()
        if stop_after == "Fend":
            kb.barrier()
            return nc
        NJ = 4
        NR = 0 if stop_after == "G4" else (2 if stop_after in ("G2", "G3") else 128 // NJ)
        for half in range(2):
            kb.barrier()
            with ExitStack() as sc:
                sbG = lambda name, shape, dt=F32: sc.enter_context(nc.sbuf_tensor(name + "_h%d" % half, list(shape), dt))
                hTh = sbG("hTh", [128, 16, 1024], BF16)
                b_hTh = Buf("hTh")
                for tg2 in range(2):
                    tg = 2 * half + tg2
                    for q4 in range(4):
                        kb.dma("sp", hTh[:, 4 * q4:4 * q4 + 4, tg2 * 512:(tg2 + 1) * 512],
                               hT_d[:, 4 * q4:4 * q4 + 4, tg * 512:(tg + 1) * 512],
                               reads=[b_hT_d[tg]], writes=[b_hTh])
                accG = sbG("accG", [128, 8, D])
                b_accG = [bufs(2, "accG%d" % t) for t in range(8)]
                with ExitStack() as sc2:
                    sbH = lambda name, shape, dt=F32: sc2.enter_context(nc.sbuf_tensor(name + "_h%d" % half, list(shape), dt))
                    Wr = [sbH("Wr%d" % i, [128, 16, NJ, 64], BF16) for i in range(2)]
                    b_Wr = bufs(2, "Wr")
                    vr = [sbH("vr%d" % i, [128, NJ, D], BF16) for i in range(2)]
                    b_vr = bufs(2, "vr")
                    uj = [sbH("uj%d" % i, [128, 16, 128], BF16) for i in range(2)]
                    b_uj = bufs(2, "uj")
                    Gr = [sbH("Gr%d" % i, [128, NJ, 1024], BF16) for i in range(2)]
                    b_Gr = bufs(2, "Gr")
                    ga = [sbH("ga%d" % i, [128, 512], BF16) for i in range(2)]
                    b_ga = bufs(2, "ga")
                    urr = [0]
                    grr = [0]

                    def phaseA(r):
                        ri = r % 2
                        j0 = r * NJ
                        for q4 in range(4):
                            b0_ = 16 * half + 4 * q4
                            kb.dma("sp", Wr[ri][:, 4 * q4:4 * q4 + 4], W1[b0_:b0_ + 4, :, j0:j0 + NJ, :].rearrange(
                                "b i j t -> i b j t"), reads=b_W1[b0_:b0_ + 4], writes=[b_Wr[ri]])
                        for jj in range(NJ):
                            ui = urr[0] % 2
                            urr[0] += 1
                            kb.dma("pool", uj[ui][:], uT[j0 + jj], writes=[b_uj[ui]])
                            for tg2 in range(2):
                                bk = next_bank([0, 1])
                                for kc in range(16):
                                    kb.op("pe", lambda e, kc=kc, ui=ui, tg2=tg2, bk=bk: e.matmul(
                                        ps[:, bk, :], lhsT=uj[ui][:, kc, :], rhs=hTh[:, kc, tg2 * 512:(tg2 + 1) * 512],
                                        start=(kc == 0), stop=(kc == 15)),
                                        reads=[b_uj[ui], b_hTh], writes=[PB[bk]], sig=(kc == 15))
                                gi = grr[0] % 2
                                grr[0] += 1
                                kb.op("act", lambda e, gi=gi, bk=bk: e.activation(out=ga[gi][:, :], in_=ps[:, bk, :],
                                                                                 func=AF.Gelu),
                                      reads=[PB[bk]], writes=[b_ga[gi]])
                                kb.op("dve", lambda e, gi=gi, ri=ri, jj=jj, tg2=tg2: e.tensor_tensor(
                                    out=Gr[ri][:, jj, tg2 * 512:(tg2 + 1) * 512].rearrange("p (b t) -> p b t", b=8),
                                    in0=ga[gi][:, :].rearrange("p (b t) -> p b t", b=8),
                                    in1=Wr[ri][:, tg2 * 8:(tg2 + 1) * 8, jj, :], op=ALU.mult),
                                    reads=[b_ga[gi], b_Wr[ri]], writes=[b_Gr[ri]])
                        kb.dma("pool", vr[ri][:], vL[j0:j0 + NJ].rearrange("j i d -> i j d"), writes=[b_vr[ri]])

                    def phaseB(r):
                        ri = r % 2
                        for tt in range(8):
                            for dh in range(2):
                                b0 = 2 + 2 * ((tt * 2 + dh) % 3)
                                for jj in range(NJ):
                                    for dq in range(2):
                                        kb.op("pe", lambda e, jj=jj, dq=dq, tt=tt, dh=dh, b0=b0: e.matmul(
                                            ps[:, b0 + dq, :], lhsT=Gr[ri][:, jj, tt * 128:(tt + 1) * 128],
                                            rhs=vr[ri][:, jj, dh * 1024 + dq * 512:dh * 1024 + (dq + 1) * 512],
                                            start=(jj == 0), stop=(jj == NJ - 1)),
                                            reads=[b_Gr[ri], b_vr[ri]], writes=[PB[b0 + dq]],
                                            sig=(jj == NJ - 1 and dq == 1))
                                pin = ps[:, b0:b0 + 2, :].rearrange("p a b -> p (a b)")
                                aout = accG[:, tt, dh * 1024:(dh + 1) * 1024]
                                if r == 0:
                                    kb.op("dve", lambda e, pin=pin, aout=aout: e.tensor_copy(out=aout, in_=pin),
                                          reads=[PB[b0], PB[b0 + 1]], writes=[b_accG[tt][dh]])
                                else:
                                    kb.op("dve", lambda e, pin=pin, aout=aout: e.tensor_tensor(
                                        out=aout, in0=aout, in1=pin, op=ALU.add),
                                        reads=[PB[b0], PB[b0 + 1]], writes=[b_accG[tt][dh]])

                    if stop_after == "G1":
                        phaseA(0)
                        phaseB(0)
                        dump([(accG[:, 0, 0:1024], b_accG[0], 1024), (accG[:, 7, 1024:2048], b_accG[7], 1024),
                              (Gr[0][:, 0, :], [b_Gr[0]], 1024), (Gr[0][:, 3, :], [b_Gr[0]], 1024)])
                        return nc
                    if NR > 0:
                        phaseA(0)
                    for r in range(NR):
                        if r + 1 < NR:
                            phaseA(r + 1)
                        phaseB(r)
                kb.barrier()
                if stop_after == "G3":
                    dump([(accG[:, 0, 0:1024], b_accG[0], 1024), (accG[:, 7, 1024:2048], b_accG[7], 1024)])
                    return nc
                with ExitStack() as sc3:
                    sbL = lambda name, shape, dt=F32: sc3.enter_context(nc.sbuf_tensor(name + "_h%d" % half, list(shape), dt))
                    gb2 = sbL("gb2", [128, 2, D])
                    b_gb2 = Buf("gb2")
                    kb.dma("sp", gb2[:, 0, :], lnp[2], writes=[b_gb2])
                    kb.dma("sp", gb2[:, 1, :], lnp[3], writes=[b_gb2])
                    hres = [sbL("hres%d" % i, [128, D]) for i in range(2)]
                    b_hres = bufs(2, "hres")
                    xc2 = sbL("xc2", [128, D])
                    b_xc2 = Buf("xc2")
                    jf2 = sbL("jf2", [128, D])
                    b_jf2 = Buf("jf2")
                    yo = [sbL("yo%d" % i, [128, D]) for i in range(2)]
                    b_yo = bufs(2, "yo")
                    st2 = sbL("st2", [128, 8])
                    b_st2 = Buf("st2")
                    outs = []
                    for tt in range(8):
                        T = 8 * half + tt
                        hi = tt % 2
                        kb.dma("sp", hres[hi][:, :], h_tok[T * 128:(T + 1) * 128, :], reads=[b_htok[T]],
                               writes=[b_hres[hi]])
                        kb.op("dve", lambda e, hi=hi, tt=tt: e.scalar_tensor_tensor(
                            out=hres[hi][:, :], in0=hres[hi][:, :], scalar=ALPHA, in1=accG[:, tt, :],
                            op0=ALU.mult, op1=ALU.add),
                            reads=b_accG[tt], writes=[b_hres[hi]])
                        layer_norm(None, hres[hi][:, :], b_hres[hi], gb2[:, 0, :], gb2[:, 1, :], b_gb2,
                                   yo[hi][:, :], b_yo[hi], xc2, b_xc2, jf2, b_jf2, st2, b_st2)
                        outs.append(kb.dma("sp", y[T * 128:(T + 1) * 128, :], yo[hi][:, :], reads=[b_yo[hi]]))
                    for t in outs:
                        kb.wait("sp", t)
                    if stop_after in ("G2", "G4"):
                        dump([(accG[:, 0, 0:1024], b_accG[0], 1024), (accG[:, 7, 1024:2048], b_accG[7], 1024),
                              (yo[1][:, 0:1024], [b_yo[1]], 1024), (hres[1][:, 0:1024], [b_hres[1]], 1024)])
                        return nc
        print("instructions:", kb.nins, "sbuf remaining:", nc.sbuf_bytes_remaining)
    return nc


def _t5_bucket(dist):
    dist = np.asarray(dist)
    d = np.maximum(dist, 1).astype(np.float32)
    large = 16 + (np.log(d / np.float32(16)) / np.float32(np.log(128 / 16)) * np.float32(16)).astype(np.int32)
    large = np.minimum(large, 31)
    return np.where(dist < 16, dist, large)


def _prep_shared(w_in, pool_w, pool_scale, rel_bias, w_out, ln1_g, ln1_b, peer_wq, peer_subkeys, peer_u, peer_v,
                 ln2_g, ln2_b):
    f = np.float32
    w = w_in[0]
    cols = []
    for c in range(8):
        cols.append(w[:, c * 128:(c + 1) * 128])
    for c in range(8):
        cols.append(w[:, 1024 + c * 128:1024 + (c + 1) * 128])
    for c in range(8):
        cols.append(w[:, 2048 + c * 128:2048 + (c + 1) * 128])
    for c in range(8):
        cols.append(w[:, 4096 + c * 128:4096 + (c + 1) * 128])
    ki = w[:, 5120:5184]
    cols.append(np.concatenate([ki, ki], axis=1))
    w_fm = np.stack([c.reshape(16, 128, 128).transpose(1, 0, 2) for c in cols]).astype(f)
    wv = w[:, 3072:4096]
    w_v = np.stack([wv[:, hg * 256:(hg + 1) * 256].reshape(16, 128, 256).transpose(1, 0, 2) for hg in range(4)])
    w_wi = w[:, 5184:5200].reshape(16, 128, 16).transpose(1, 0, 2)
    pw = pool_w[0].reshape(4, 2, 128, 256).transpose(2, 0, 1, 3)
    kk = np.arange(128)[:, None]
    qq = np.arange(128)[None, :]
    bt = np.zeros((128, 2, 8, 128), f)
    for dl in range(2):
        bkt = _t5_bucket(np.maximum(dl * 128 + qq - kk, 0))
        bt[:, dl, :, :] = rel_bias[bkt].transpose(0, 2, 1)
    wo = w_out[0].reshape(16, 128, D).transpose(1, 0, 2)
    lnp = np.stack([np.broadcast_to(a[0][None, :], (128, D)) for a in (ln1_g, ln1_b, ln2_g, ln2_b)])
    wqh = peer_wq[0].reshape(16, 128, D).transpose(1, 0, 2)
    skT = peer_subkeys[0].transpose(2, 0, 1)
    u = peer_u[0].reshape(128, 128, 16, 128)
    uT = u.transpose(1, 3, 2, 0)
    vv = peer_v[0].reshape(128, 128, D).transpose(1, 0, 2)
    c = lambda a: np.ascontiguousarray(a, dtype=f)
    return dict(w_fm=c(w_fm), w_v=c(w_v), w_wi=c(w_wi), pool_w=c(pw), biasT=c(bt), w_out=c(wo), lnp=c(lnp),
                wq=c(wqh), subkT=c(skT), uT=c(uT), vL=c(vv))


def _consts(hf, pool_scale, rel_bias):
    f = np.float32
    cst = np.zeros((128, 1024), f)
    cst[:, 0:128] = np.eye(128, dtype=f)
    cst[:, 128:256] = np.arange(128, dtype=f)[None, :]
    qq = np.arange(128)[:, None]
    kk = np.arange(128)[None, :]
    cst[:, 256:384] = np.where(kk <= qq, 0.0, NEG).astype(f)
    valid = 1.0 if hf == 1 else 0.0
    cst[:, 384] = valid
    cst[:, 385] = (valid - 1.0) * 1.0e30
    cst[:, 392:400] = rel_bias[31][None, :]
    for gq, wwin in enumerate((2, 4, 8, 16)):
        pos = np.arange(16)
        if hf == 0:
            corr = wwin / np.minimum(pos + 1, wwin).astype(f)
        else:
            corr = np.ones(16, f)
        cst[:, 400 + 16 * gq:400 + 16 * gq + 16] = corr[None, :]
    cst[:, 464:472] = pool_scale[0].reshape(8, 128).T
    cst[:, 480:512] = (2.0 ** -np.arange(32, dtype=np.float64)).astype(f)[None, :]
    return cst


def _core_inputs(x, shared, pool_scale, rel_bias):
    in_maps = []
    for c in range(8):
        b, hf = c // 2, c % 2
        own = x[b, hf * TOK:(hf + 1) * TOK]
        prev = x[b, 0:TOK] if hf == 1 else np.zeros_like(own)
        xT = np.stack([prev.T.reshape(16, 128, TOK).transpose(1, 0, 2), own.T.reshape(16, 128, TOK).transpose(1, 0, 2)])
        m = dict(shared)
        m["xT"] = np.ascontiguousarray(xT, dtype=np.float32)
        m["x_tok"] = np.ascontiguousarray(own, dtype=np.float32)
        m["consts"] = _consts(hf, pool_scale, rel_bias)
        in_maps.append(m)
    return in_maps


def kernel(x, w_in, pool_w, pool_scale, rel_bias, w_out, ln1_g, ln1_b, peer_wq, peer_subkeys, peer_u, peer_v,
           ln2_g, ln2_b):
    args = [np.asarray(a, dtype=np.float32) for a in (x, w_in, pool_w, pool_scale, rel_bias, w_out, ln1_g, ln1_b,
                                                      peer_wq, peer_subkeys, peer_u, peer_v, ln2_g, ln2_b)]
    (x, w_in, pool_w, pool_scale, rel_bias, w_out, ln1_g, ln1_b, peer_wq, peer_subkeys, peer_u, peer_v,
     ln2_g, ln2_b) = args
    shared = _prep_shared(w_in, pool_w, pool_scale, rel_bias, w_out, ln1_g, ln1_b, peer_wq, peer_subkeys,
                          peer_u, peer_v, ln2_g, ln2_b)
    in_maps = _core_inputs(x, shared, pool_scale, rel_bias)
    nc = build_nc()
    res = run_bass_kernel_spmd(nc, in_maps, core_ids=list(range(8)))
    out = np.zeros((4, S, D), np.float32)
    for c in range(8):
        b, hf = c // 2, c % 2
        out[b, hf * TOK:(hf + 1) * TOK] = res.results[c]["y"]
    return out
```

```python
import numpy as np
from contextlib import ExitStack
import concourse.bass as bass
import concourse.mybir as mybir
from concourse.bass_utils import run_bass_kernel_spmd

F32 = mybir.dt.float32
BF16 = mybir.dt.bfloat16
U32 = mybir.dt.uint32
ALU = mybir.AluOpType
AF = mybir.ActivationFunctionType
AX = mybir.AxisListType

D = 2048
S = 4096
TOK = 2048
NEG = -1.0e30
ALPHA = 2.0 ** 0.25
LN_EPS = 1e-5
NIT = 16
TOPK = 256
ATT_SCALE = 128.0 ** -0.5
NSLOT = 6


class Buf:
    __slots__ = ("w", "r", "name")

    def __init__(self, name=""):
        self.w = None
        self.r = {}
        self.name = name


class KB:
    def __init__(self, nc, es):
        self.nc = nc
        self.engs = {"pe": nc.tensor, "dve": nc.vector, "act": nc.scalar, "pool": nc.gpsimd, "sp": nc.sync}
        self.psem = {e: es.enter_context(nc.semaphore("prog_" + e)) for e in ["pe", "dve", "act", "pool"]}
        self.cnt = {e: 0 for e in self.psem}
        self.seen = {e: {} for e in self.engs}
        self.pending = {e: [] for e in self.engs}
        self.dslots = {q: [[es.enter_context(nc.semaphore("dq_%s_%d" % (q, i))), 0, "dq_%s_%d" % (q, i)]
                           for i in range(NSLOT)] for q in ["sp", "pool"]}
        self.dnext = {q: 0 for q in self.dslots}
        self.nins = 0

    def wait(self, e, tok):
        if tok is None:
            return
        sem, val, key = tok
        if self.seen[e].get(key, 0) >= val:
            return
        self.engs[e].wait_ge(sem, val)
        self.seen[e][key] = val

    def _deps(self, e, reads, writes, deps):
        for b in reads:
            self.wait(e, b.w)
        for b in writes:
            self.wait(e, b.w)
            for t in b.r.values():
                self.wait(e, t)
        for t in deps:
            self.wait(e, t)

    def op(self, e, fn, reads=(), writes=(), deps=(), sig=True):
        self._deps(e, reads, writes, deps)
        ins = fn(self.engs[e])
        self.nins += 1
        if not sig:
            self.pending[e].append((list(reads), list(writes)))
            return None
        self.cnt[e] += 1
        ins.then_inc(self.psem[e], 1)
        key = "prog_" + e
        tok = (self.psem[e], self.cnt[e], key)
        allr = list(reads)
        allw = list(writes)
        for (r, w) in self.pending[e]:
            allr += r
            allw += w
        self.pending[e] = []
        for b in allw:
            b.w = tok
            b.r = {}
        for b in allr:
            if b not in allw:
                b.r[key] = tok
        return tok

    def barrier(self):
        toks = []
        for e in self.psem:
            if self.cnt[e] > 0:
                toks.append((self.psem[e], self.cnt[e], "prog_" + e))
        for q in self.dslots:
            for sem, cnt, key in self.dslots[q]:
                if cnt > 0:
                    toks.append((sem, cnt, key))
        for e in self.engs:
            for t in toks:
                self.wait(e, t)

    def dma(self, q, out, in_, reads=(), writes=(), deps=()):
        self._deps(q, reads, writes, deps)
        slot = self.dslots[q][self.dnext[q]]
        self.dnext[q] = (self.dnext[q] + 1) % NSLOT
        sem, cnt, key = slot
        if cnt > 0:
            self.wait(q, (sem, cnt, key))
        self.engs[q].dma_start(out=out, in_=in_).then_inc(sem, 16)
        self.nins += 1
        slot[1] = cnt + 16
        tok = (sem, cnt + 16, key)
        for b in writes:
            b.w = tok
            b.r = {}
        for b in reads:
            b.r[key] = tok
        return tok


def sap(t, dims, off=0, parts=128, p0=0):
    fs = 1
    for s_ in t.shape[1:]:
        fs *= int(s_)
    return bass.AP(t, p0 * fs + off, [[fs, parts]] + [[int(a), int(b)] for a, b in dims])


def bufs(n, name=""):
    return [Buf("%s%d" % (name, i)) for i in range(n)]


def build_nc(stop_after=None, small_peer=False):
    nc = bass.Bass("TRN2", target_bir_lowering=False)
    dbg = {}

    def din(name, shape, dt=F32):
        return nc.dram_tensor(name, list(shape), dt, kind="ExternalInput").ap()

    xT = din("xT", [2, 128, 16, TOK])
    x_tok = din("x_tok", [TOK, D])
    w_fm = din("w_fm", [33, 128, 16, 128])
    w_v = din("w_v", [4, 128, 16, 256])
    w_wi = din("w_wi", [128, 16, 16])
    pool_w = din("pool_w", [128, 4, 2, 256])
    consts = din("consts", [128, 1024])
    biasT = din("biasT", [128, 2, 8, 128])
    w_out = din("w_out", [128, 16, D])
    lnp = din("lnp", [4, 128, D])
    wq = din("wq", [128, 16, D])
    subkT = din("subkT", [128, 2, 128])
    uT = din("uT", [8 if small_peer else 128, 128, 16, 128])
    vL = din("vL", [8 if small_peer else 128, 128, D])
    y = nc.dram_tensor("y", [TOK, D], F32, kind="ExternalOutput").ap()
    maskT_d = nc.dram_tensor("maskT_d", [4, 128, 32, 512], BF16).ap()
    hT_d = nc.dram_tensor("hT_d", [128, 16, TOK], BF16).ap()
    h_tok = nc.dram_tensor("h_tok", [TOK, D], F32).ap()
    W1 = nc.dram_tensor("W1", [32, 128, 128, 64], BF16).ap()
    if stop_after is not None:
        dbg_out = nc.dram_tensor("dbg", [128, 8192], F32, kind="ExternalOutput").ap()

    with ExitStack() as es:
        kb = KB(nc, es)
        sb = lambda name, shape, dt=F32: es.enter_context(nc.sbuf_tensor(name, list(shape), dt))
        ps = es.enter_context(nc.psum_tensor("ps", [128, 8, 512], F32))
        PB = bufs(8, "psb")

        dbg_stg = sb("dbg_stg", [128, 1024]) if stop_after is not None else None
        cst = sb("cst", [128, 1024])
        b_cst = Buf("cst")
        kb.dma("sp", cst[:], consts, writes=[b_cst])
        C_ID = 0
        C_IOTA = 128
        C_TRI = 256
        C_FLAG = 384
        C_B31 = 392
        C_CORR = 400
        C_PSC = 464
        C_PW2 = 480
        ident = cst[:, C_ID:C_ID + 128]
        iota = cst[:, C_IOTA:C_IOTA + 128]
        tri = cst[:, C_TRI:C_TRI + 128]
        bT = sb("bT", [128, 2, 8, 128])
        b_bT = Buf("bT")
        kb.dma("sp", bT[:], biasT, writes=[b_bT])
        ones_b = sb("ones_b", [128, 128], BF16)
        b_ones = Buf("ones")
        kb.op("pool", lambda e: e.memset(ones_b[:], 1.0), writes=[b_ones])

        evac_rr = [0]

        def evac(out, in_, reads, writes, scale=None):
            evac_rr[0] ^= 1
            if scale is not None:
                return kb.op("act", lambda e: e.activation(out=out, in_=in_, func=AF.Copy, scale=scale),
                             reads=reads, writes=writes)
            if evac_rr[0]:
                return kb.op("act", lambda e: e.activation(out=out, in_=in_, func=AF.Copy), reads=reads, writes=writes)
            return kb.op("dve", lambda e: e.tensor_copy(out=out, in_=in_), reads=reads, writes=writes)

        pb_rr = [0]

        def next_bank(choices):
            pb_rr[0] += 1
            return choices[pb_rr[0] % len(choices)]

        def dump(ap_list):
            stg = dbg_stg
            b_stg = Buf("dbgstg")
            col = 0
            for ap, bl, n in ap_list:
                kb.op("dve", lambda e, ap=ap, n=n: e.tensor_copy(out=stg[:, 0:n], in_=ap),
                      reads=bl, writes=[b_stg])
                t = kb.dma("sp", dbg_out[:, col:col + n], stg[:, 0:n], reads=[b_stg])
                kb.wait("sp", t)
                col += n

        xg_t = [None, None]
        b_xg = bufs(2, "xg")
        xg_rr = [0]

        def load_xg(s, tg):
            i = xg_rr[0]
            xg_rr[0] ^= 1
            for q4 in range(4):
                kb.dma("pool", xg_t[i][:, 4 * q4:4 * q4 + 4, :], xT[s, :, 4 * q4:4 * q4 + 4, tg * 512:(tg + 1) * 512],
                       writes=[b_xg[i]])
            return xg_t[i], b_xg[i]

        def proj_fm(wt, wb, xg, bx, out_ap, out_bufs, scale=None):
            bk = next_bank([0, 1, 2, 7])
            for kc in range(16):
                kb.op("pe", lambda e, kc=kc: e.matmul(ps[:, bk, :], lhsT=wt(kc), rhs=xg[:, kc, :],
                                                      start=(kc == 0), stop=(kc == 15)),
                      reads=[wb, bx], writes=[PB[bk]], sig=(kc == 15))
            return evac(out_ap, ps[:, bk, :], [PB[bk]], out_bufs, scale=scale)

        phA = ExitStack()
        es.enter_context(phA)
        xg_t[0] = phA.enter_context(nc.sbuf_tensor("xgA0", [128, 16, 512], BF16))
        xg_t[1] = phA.enter_context(nc.sbuf_tensor("xgA1", [128, 16, 512], BF16))
        qiT = phA.enter_context(nc.sbuf_tensor("qiT", [128, 8, TOK], BF16))
        b_qiT = bufs(4, "qiT")
        kiT = phA.enter_context(nc.sbuf_tensor("kiT", [128, 2 * TOK], BF16))
        b_kiT = bufs(8, "kiT")
        widx = phA.enter_context(nc.sbuf_tensor("widx", [128, 16, 16], F32))
        b_widx = bufs(16, "widx")
        with ExitStack() as sc:
            wqi = sc.enter_context(nc.sbuf_tensor("wqi", [128, 8, 16, 128], BF16))
            b_wqi = Buf("wqi")
            wki = sc.enter_context(nc.sbuf_tensor("wki", [128, 16, 128], BF16))
            b_wki = Buf("wki")
            wwi = sc.enter_context(nc.sbuf_tensor("wwi", [128, 16, 16], BF16))
            b_wwi = Buf("wwi")
            kb.dma("pool", wki[:], w_fm[32], writes=[b_wki])
            for c in range(8):
                kb.dma("pool", wqi[:, c], w_fm[24 + c], writes=[b_wqi])
            kb.dma("pool", wwi[:], w_wi, writes=[b_wwi])
            for s in range(2):
                for tg in range(4):
                    xg, bx = load_xg(s, tg)
                    kg = s * 4 + tg
                    proj_fm(lambda kc: wki[:, kc, :], b_wki, xg, bx, kiT[:, kg * 512:(kg + 1) * 512], [b_kiT[kg]])
                    if stop_after == "A1":
                        dump([(kiT[:, 0:512], b_kiT[0:1], 512), (xg[:, 0, :], [bx], 512)])
                        return nc
                    if s == 1:
                        for c in range(8):
                            proj_fm(lambda kc, c=c: wqi[:, c, kc, :], b_wqi, xg, bx,
                                    qiT[:, c, tg * 512:(tg + 1) * 512], [b_qiT[tg]])
                        for tt in range(4):
                            bk = next_bank([0, 1, 2, 7])
                            for kc in range(16):
                                kb.op("pe", lambda e, kc=kc, tt=tt: e.matmul(
                                    ps[:, bk, 0:16], lhsT=xg[:, kc, tt * 128:(tt + 1) * 128], rhs=wwi[:, kc, :],
                                    start=(kc == 0), stop=(kc == 15)),
                                    reads=[b_wwi, bx], writes=[PB[bk]], sig=(kc == 15))
                            evac(widx[:, tg * 4 + tt, :], ps[:, bk, 0:16], [PB[bk]], [b_widx[tg * 4 + tt]])
        if stop_after == "A":
            dump([(kiT[:, 0:2048], b_kiT[0:4], 2048), (kiT[:, 2048:4096], b_kiT[4:8], 2048),
                  (qiT[:, 0, 0:2048], b_qiT, 2048), (widx[:, :, :].rearrange("p a b -> p (a b)"), b_widx, 256)])
            return nc

        kb.barrier()
        b_maskd = bufs(4, "maskd")
        with ExitStack() as sc:
            sbB = lambda name, shape, dt=F32: sc.enter_context(nc.sbuf_tensor(name, list(shape), dt))
            score2 = [sbB("score%d" % p_, [128, 4096]) for p_ in range(2)]
            b_score2 = [bufs(8, "score%d_" % p_) for p_ in range(2)]
            maskf = sbB("maskf", [128, 4096])
            b_maskf = Buf("maskf")
            junkA = sbB("junkA", [128, 4096], BF16)
            b_junkA = Buf("junkA")
            Rt = [sbB("R%d" % i_, [128, 512]) for i_ in range(3)]
            b_R = bufs(3, "R")
            acc = sbB("accB", [128, 512])
            b_acc = Buf("acc")
            mx2 = [sbB("mxall%d" % p_, [128, 16]) for p_ in range(2)]
            b_mx2 = bufs(2, "mx")
            sm2 = [sbB("smallB%d" % p_, [128, 64]) for p_ in range(2)]
            b_sm2 = bufs(2, "small")
            b_nm2 = bufs(2, "negmid")
            b_cs2 = bufs(2, "cs")
            mT = sbB("maskTg", [128, 32, 512], BF16)
            b_mT = Buf("maskTg")
            r_rr = [0]

            def make_units(i):
                g = i // 4
                p_ = i % 2
                score, b_score, mxall, b_mx = score2[p_], b_score2[p_], mx2[p_], b_mx2[p_]
                NK = (17 + i) * 128
                nkt = (NK + 511) // 512
                units = []
                for kt in range(nkt):
                    wk = min(512, NK - kt * 512)
                    direct = kt >= 4
                    for hn, h in enumerate([0, 2, 4, 6, 8, 10, 12, 14, 1, 3, 5, 7, 9, 11, 13, 15]):
                        def unit(kt=kt, wk=wk, direct=direct, hn=hn, h=h):
                            cp, r0 = h // 2, 64 * (h % 2)
                            bk = next_bank([0, 1, 2])
                            kb.op("pe", lambda e: e.matmul(
                                ps[:, bk, 0:wk], lhsT=qiT[r0:r0 + 64, cp, i * 128:(i + 1) * 128],
                                rhs=kiT[r0:r0 + 64, kt * 512:kt * 512 + wk], start=True, stop=True),
                                reads=[b_qiT[g], b_kiT[kt]], writes=[PB[bk]])
                            ri = r_rr[0] % 3
                            r_rr[0] += 1
                            kb.op("act", lambda e: e.activation(
                                out=Rt[ri][:, 0:wk], in_=ps[:, bk, 0:wk], func=AF.Relu),
                                reads=[PB[bk]], writes=[b_R[ri]])
                            wcol = widx[:, i, h:h + 1]
                            if hn == 0:
                                kb.op("dve", lambda e: e.tensor_scalar(
                                    out=acc[:, 0:wk], in0=Rt[ri][:, 0:wk], scalar1=wcol, scalar2=None, op0=ALU.mult),
                                    reads=[b_R[ri], b_widx[i]], writes=[b_acc])
                            elif hn == 15 and direct:
                                kb.op("dve", lambda e: e.scalar_tensor_tensor(
                                    out=score[:, kt * 512:kt * 512 + wk], in0=Rt[ri][:, 0:wk], scalar=wcol,
                                    in1=acc[:, 0:wk], op0=ALU.mult, op1=ALU.add),
                                    reads=[b_R[ri], b_widx[i], b_acc], writes=[b_score[kt]])
                            else:
                                kb.op("dve", lambda e: e.scalar_tensor_tensor(
                                    out=acc[:, 0:wk], in0=Rt[ri][:, 0:wk], scalar=wcol,
                                    in1=acc[:, 0:wk], op0=ALU.mult, op1=ALU.add),
                                    reads=[b_R[ri], b_widx[i]], writes=[b_acc])
                            if hn == 15:
                                src = score[:, kt * 512:kt * 512 + wk] if direct else acc[:, 0:wk]
                                bsrc = b_score[kt] if direct else b_acc
                                kb.op("dve", lambda e: e.tensor_reduce(
                                    out=mxall[:, kt:kt + 1], in_=src, axis=AX.X, op=ALU.max),
                                    reads=[bsrc], writes=[b_mx])
                                kb.op("dve", lambda e: e.tensor_reduce(
                                    out=mxall[:, 8 + kt:9 + kt], in_=src, axis=AX.X, op=ALU.min),
                                    reads=[bsrc], writes=[b_mx])
                                if not direct:
                                    kb.op("dve", lambda e: e.tensor_scalar(
                                        out=score[:, kt * 512:kt * 512 + wk], in0=acc[:, 0:wk],
                                        scalar1=cst[:, C_FLAG:C_FLAG + 1], scalar2=cst[:, C_FLAG + 1:C_FLAG + 2],
                                        op0=ALU.mult, op1=ALU.add),
                                        reads=[b_acc, b_cst], writes=[b_score[kt]])
                        units.append(unit)
                return units

            def post_score(i):
                p_ = i % 2
                score, b_score, mxall, b_mx, sm, b_sm = score2[p_], b_score2[p_], mx2[p_], b_mx2[p_], sm2[p_], b_sm2[p_]
                NK = (17 + i) * 128
                nkt = (NK + 511) // 512
                negmid, tmpc, Mp, negD = sm[:, 0:1], sm[:, 2:3], sm[:, 3:4], sm[:, 8:8 + NIT + 1]
                kd = (NK - 128) // 512
                kb.op("dve", lambda e: e.tensor_tensor(
                    out=score[:, NK - 128:NK], in0=score[:, NK - 128:NK], in1=tri, op=ALU.add),
                    reads=[b_cst], writes=[b_score[kd]])
                kb.op("dve", lambda e: e.tensor_reduce(out=Mp, in_=mxall[:, 0:nkt], axis=AX.X, op=ALU.max),
                      reads=[b_mx], writes=[b_sm])
                kb.op("dve", lambda e: e.tensor_reduce(out=tmpc, in_=mxall[:, 8:8 + nkt], axis=AX.X, op=ALU.min),
                      reads=[b_mx], writes=[b_sm])
                kb.op("dve", lambda e: e.scalar_tensor_tensor(out=Mp, in0=tmpc, scalar=-1.0, in1=Mp, op0=ALU.mult,
                                                              op1=ALU.max), writes=[b_sm])
                kb.op("dve", lambda e: e.tensor_scalar(out=Mp, in0=Mp, scalar1=-1.001, scalar2=-1e-20,
                                                       op0=ALU.mult, op1=ALU.add), writes=[b_sm])
                kb.op("dve", lambda e: e.tensor_scalar(out=negD, in0=cst[:, C_PW2:C_PW2 + NIT + 1], scalar1=Mp,
                                                       scalar2=None, op0=ALU.mult), reads=[b_cst], writes=[b_sm])
                kb.op("dve", lambda e: e.memset(negmid, 0.0), writes=[b_nm2[p_]])

            def make_steps(i):
                p_ = i % 2
                score, b_score, sm, b_sm = score2[p_], b_score2[p_], sm2[p_], b_sm2[p_]
                NK = (17 + i) * 128
                nkt = (NK + 511) // 512
                negmid, cs, tmpc, negD = sm[:, 0:1], sm[:, 1:2], sm[:, 2:3], sm[:, 8:8 + NIT + 1]
                thr = 2.0 * TOPK - NK - 0.5
                steps = []
                for k in range(NIT):
                    def act_fn():
                        kb.op("act", lambda e: e.activation(
                            out=junkA[:, 0:NK], in_=score[:, 0:NK], func=AF.Sign, bias=negmid, scale=1.0,
                            accum_out=cs),
                            reads=b_score[0:nkt] + [b_nm2[p_]], writes=[b_junkA, b_cs2[p_]])

                    def dve_fn(k=k):
                        kb.op("dve", lambda e: e.tensor_scalar(out=tmpc, in0=cs, scalar1=thr, scalar2=0.5,
                                                               op0=ALU.is_ge, op1=ALU.subtract),
                              reads=[b_cs2[p_]], writes=[b_sm])
                        kb.op("dve", lambda e: e.scalar_tensor_tensor(
                            out=negmid, in0=tmpc, scalar=negD[:, k:k + 1], in1=negmid, op0=ALU.mult, op1=ALU.add),
                            reads=[b_sm], writes=[b_nm2[p_]])
                    steps.append((act_fn, dve_fn))
                return steps

            def finalize(i):
                g, j = i // 4, i % 4
                p_ = i % 2
                score, b_score, sm, b_sm = score2[p_], b_score2[p_], sm2[p_], b_sm2[p_]
                NK = (17 + i) * 128
                nkt = (NK + 511) // 512
                negmid, tau, negD = sm[:, 0:1], sm[:, 4:5], sm[:, 8:8 + NIT + 1]
                if j == 0:
                    kb.op("pool", lambda e: e.memset(mT[:], 0.0), writes=[b_mT])
                kb.op("dve", lambda e: e.tensor_tensor(out=tau, in0=negD[:, NIT:NIT + 1], in1=negmid, op=ALU.subtract),
                      reads=[b_nm2[p_]], writes=[b_sm])
                kb.op("dve", lambda e: e.tensor_scalar(
                    out=maskf[:, 0:NK], in0=score[:, 0:NK], scalar1=tau, scalar2=None, op0=ALU.is_ge),
                    reads=b_score[0:nkt] + [b_sm], writes=[b_maskf])
                nch = 17 + i
                for c0 in range(0, nch, 4):
                    n4 = min(4, nch - c0)
                    bk = next_bank([3, 4])
                    for cc in range(n4):
                        c = c0 + cc
                        kb.op("pe", lambda e, c=c, cc=cc: e.transpose(
                            ps[:, bk, cc * 128:(cc + 1) * 128], maskf[:, c * 128:(c + 1) * 128], ident),
                            reads=[b_maskf, b_cst], writes=[PB[bk]], sig=(cc == n4 - 1))
                    kb.op("act", lambda e, c0=c0, n4=n4: e.activation(
                        out=mT[:, c0:c0 + n4, j * 128:(j + 1) * 128],
                        in_=ps[:, bk, 0:n4 * 128].rearrange("p (c q) -> p c q", c=n4), func=AF.Copy),
                        reads=[PB[bk]], writes=[b_mT])
                if j == 3:
                    kb.dma("sp", maskT_d[g], mT[:], reads=[b_mT], writes=[b_maskd[g]])

            prev = None
            for i in range(16):
                units = make_units(i)
                steps = make_steps(prev) if prev is not None else []
                nU = len(units)
                spacing = max(4, nU // (NIT + 1))
                ka = 0
                kd_ = 0
                for u, unit in enumerate(units):
                    unit()
                    if ka < len(steps) and u == ka * spacing + 1:
                        steps[ka][0]()
                        ka += 1
                    if kd_ < len(steps) and kd_ < ka and u == kd_ * spacing + 1 + spacing // 2:
                        steps[kd_][1]()
                        kd_ += 1
                while kd_ < len(steps):
                    if ka == kd_:
                        steps[ka][0]()
                        ka += 1
                    steps[kd_][1]()
                    kd_ += 1
                post_score(i)
                if prev is not None:
                    finalize(prev)
                prev = i
            for st_ in make_steps(prev):
                st_[0]()
                st_[1]()
            finalize(prev)
        phA.close()
        pers = ExitStack()
        es.enter_context(pers)
        psb = lambda name, shape, dt=F32: pers.enter_context(nc.sbuf_tensor(name, list(shape), dt))
        poolT = psb("poolT", [128, 8, TOK], BF16)
        b_poolT = bufs(4, "poolT")
        attnT = psb("attnT", [128, 8, TOK], BF16)
        b_attnT = [bufs(4, "attnT%d" % h) for h in range(8)]
        scCD = ExitStack()
        es.enter_context(scCD)
        xg_t[0] = scCD.enter_context(nc.sbuf_tensor("xgC0", [128, 16, 512], BF16))
        xg_t[1] = xg_t[0]
        b_xg[0] = Buf("xgc0")
        b_xg[1] = b_xg[0]

        kb.barrier()
        with ExitStack() as sc:
            sbC = lambda name, shape, dt=F32: sc.enter_context(nc.sbuf_tensor(name, list(shape), dt))
            wpl = sbC("wpl", [128, 8, 16, 128], BF16)
            b_wpl = Buf("wpl")
            for c in range(8):
                kb.dma("pool", wpl[:, c], w_fm[c], writes=[b_wpl])
            pw = sbC("pw", [128, 4, 2, 256], BF16)
            b_pw = Buf("pw")
            kb.dma("pool", pw[:], pool_w, writes=[b_pw])
            xh = sbC("xh", [128, 16, 16], BF16)
            b_xh = Buf("xh")
            for q4 in range(4):
                kb.dma("pool", xh[:, 4 * q4:4 * q4 + 4, :], xT[0, :, 4 * q4:4 * q4 + 4, TOK - 16:TOK], writes=[b_xh])
            hal = sbC("hal", [128, 8, 16])
            b_hal = bufs(8, "hal")
            vb = [sbC("vb%d" % i, [128, 528]) for i in range(2)]
            b_vb = bufs(2, "vb")
            sa = sbC("sa", [128, 528])
            sbb = sbC("sbb", [128, 528])
            b_sa, b_sb = Buf("sa"), Buf("sb")
            t16 = sbC("t16", [128, 16])
            b_t16 = Buf("t16")
            plb = [sbC("plb%d" % i, [128, 512], BF16) for i in range(2)]
            b_plb = bufs(2, "plb")
            for cp in range(8):
                bk = next_bank([0, 1, 2, 7])
                for kc in range(16):
                    kb.op("pe", lambda e, kc=kc, cp=cp, bk=bk: e.matmul(
                        ps[:, bk, 0:16], lhsT=wpl[:, cp, kc, :], rhs=xh[:, kc, :], start=(kc == 0), stop=(kc == 15)),
                        reads=[b_wpl, b_xh], writes=[PB[bk]], sig=(kc == 15))
                evac(hal[:, cp, :], ps[:, bk, 0:16], [PB[bk]], [b_hal[cp]])
            vrr = 0
            for tg in range(4):
                xg, bx = load_xg(1, tg)
                for gq in range(4):
                    wwin = (2, 4, 8, 16)[gq]
                    for cc in range(2):
                        cp = 2 * gq + cc
                        vi = vrr % 2
                        vrr += 1
                        V = vb[vi]
                        bV = b_vb[vi]
                        kb.op("dve", lambda e, V=V, cp=cp: e.tensor_copy(out=V[:, 0:16], in_=hal[:, cp, :]),
                              reads=[b_hal[cp]], writes=[bV])
                        proj_fm(lambda kc, cp=cp: wpl[:, cp, kc, :], b_wpl, xg, bx, V[:, 16:528], [bV])
                        kb.op("act", lambda e, V=V, cp=cp: e.activation(out=hal[:, cp, :], in_=V[:, 512:528], func=AF.Copy),
                              reads=[bV], writes=[b_hal[cp]])
                        kb.op("dve", lambda e, V=V: e.tensor_tensor(out=sa[:, 1:528], in0=V[:, 1:528], in1=V[:, 0:527],
                                                                     op=ALU.add), reads=[bV], writes=[b_sa])
                        Sfin, bS = sa, b_sa
                        if gq >= 1:
                            kb.op("dve", lambda e: e.tensor_tensor(out=sbb[:, 3:528], in0=sa[:, 3:528], in1=sa[:, 1:526],
                                                                   op=ALU.add), reads=[b_sa], writes=[b_sb])
                            Sfin, bS = sbb, b_sb
                        if gq >= 2:
                            kb.op("dve", lambda e: e.tensor_tensor(out=sa[:, 7:528], in0=sbb[:, 7:528], in1=sbb[:, 3:524],
                                                                   op=ALU.add), reads=[b_sb], writes=[b_sa])
                            Sfin, bS = sa, b_sa
                        if gq >= 3:
                            kb.op("dve", lambda e: e.tensor_tensor(out=sbb[:, 15:528], in0=sa[:, 15:528], in1=sa[:, 7:520],
                                                                   op=ALU.add), reads=[b_sa], writes=[b_sb])
                            Sfin, bS = sbb, b_sb
                        kb.op("dve", lambda e, Sfin=Sfin, V=V, cc=cc, wwin=wwin: e.scalar_tensor_tensor(
                            out=plb[cc][:, :], in0=Sfin[:, 16:528], scalar=1.0 / wwin, in1=V[:, 16:528],
                            op0=ALU.mult, op1=ALU.subtract), reads=[bS, bV], writes=[b_plb[cc]])
                        if tg == 0:
                            kb.op("dve", lambda e, Sfin=Sfin, gq=gq: e.tensor_tensor(
                                out=t16[:, :], in0=Sfin[:, 16:32], in1=cst[:, C_CORR + 16 * gq:C_CORR + 16 * gq + 16],
                                op=ALU.mult), reads=[bS, b_cst], writes=[b_t16])
                            kb.op("dve", lambda e, V=V, cc=cc, wwin=wwin: e.scalar_tensor_tensor(
                                out=plb[cc][:, 0:16], in0=t16[:, :], scalar=1.0 / wwin, in1=V[:, 16:32],
                                op0=ALU.mult, op1=ALU.subtract), reads=[b_t16, bV], writes=[b_plb[cc]])
                    for dc in range(2):
                        bk = next_bank([0, 1, 2, 7])
                        for cc in range(2):
                            kb.op("pe", lambda e, cc=cc, dc=dc, gq=gq, bk=bk: e.matmul(
                                ps[:, bk, :], lhsT=pw[:, gq, cc, dc * 128:(dc + 1) * 128], rhs=plb[cc][:, :],
                                start=(cc == 0), stop=(cc == 1)),
                                reads=[b_pw, b_plb[cc]], writes=[PB[bk]], sig=(cc == 1))
                        oc = 2 * gq + dc
                        kb.op("act", lambda e, oc=oc, bk=bk, tg=tg: e.activation(
                            out=poolT[:, oc, tg * 512:(tg + 1) * 512], in_=ps[:, bk, :], func=AF.Copy,
                            scale=cst[:, C_PSC + oc:C_PSC + oc + 1]),
                            reads=[PB[bk], b_cst], writes=[b_poolT[tg]])
        if stop_after == "C":
            dump([(poolT[:, 0, 0:1024], b_poolT, 1024), (poolT[:, 7, 0:1024], b_poolT, 1024),
                  (poolT[:, 3, 1024:2048], b_poolT, 1024)])
            return nc

        kb.barrier()
        with ExitStack() as sc:
            sbD = lambda name, shape, dt=F32: sc.enter_context(nc.sbuf_tensor(name, list(shape), dt))
            wq2 = sbD("wq2", [128, 2, 16, 128], BF16)
            wk2 = sbD("wk2", [128, 2, 16, 128], BF16)
            wv2 = sbD("wv2", [128, 16, 256], BF16)
            b_wq2, b_wk2, b_wv2 = Buf("wq2"), Buf("wk2"), Buf("wv2")
            kT2 = sbD("kT2", [128, 2, 2 * TOK], BF16)
            b_kT2 = bufs(8, "kT2")
            v2 = sbD("v2", [128, 32, 256], BF16)
            b_v2 = bufs(8, "v2")
            qT2 = sbD("qT2", [128, 2, TOK], BF16)
            b_qT2 = bufs(4, "qT2")
            mk = sbD("mk", [128, 32, 512], BF16)
            b_mk = Buf("mk")
            Et = [sbD("E%d" % i, [128, 512], BF16) for i in range(3)]
            b_E = bufs(3, "E")
            Pt = [sbD("P%d" % i, [128, 512], BF16) for i in range(3)]
            b_P = bufs(3, "P")
            tmpn = [sbD("tmpn%d" % i, [128, 128]) for i in range(2)]
            b_tmpn = bufs(2, "tmpn")
            rden = sbD("rden", [128, 512])
            b_rden = Buf("rden")
            SB_S = [0, 1, 2]
            for hg in range(4):
                for hl in range(2):
                    kb.dma("pool", wq2[:, hl], w_fm[8 + 2 * hg + hl], writes=[b_wq2])
                    kb.dma("pool", wk2[:, hl], w_fm[16 + 2 * hg + hl], writes=[b_wk2])
                kb.dma("pool", wv2[:], w_v[hg], writes=[b_wv2])
                for s in range(2):
                    for tg in range(4):
                        xg, bx = load_xg(s, tg)
                        kg = s * 4 + tg
                        for hl in range(2):
                            proj_fm(lambda kc, hl=hl: wk2[:, hl, kc, :], b_wk2, xg, bx,
                                    kT2[:, hl, kg * 512:(kg + 1) * 512], [b_kT2[kg]])
                            if s == 1:
                                proj_fm(lambda kc, hl=hl: wq2[:, hl, kc, :], b_wq2, xg, bx,
                                        qT2[:, hl, tg * 512:(tg + 1) * 512], [b_qT2[tg]])
                        for tt in range(4):
                            bk = 7
                            for kc in range(16):
                                kb.op("pe", lambda e, kc=kc, tt=tt, xg=xg: e.matmul(
                                    ps[:, bk, 0:256], lhsT=xg[:, kc, tt * 128:(tt + 1) * 128], rhs=wv2[:, kc, :],
                                    start=(kc == 0), stop=(kc == 15)),
                                    reads=[b_wv2, bx], writes=[PB[bk]], sig=(kc == 15))
                            evac(v2[:, kg * 4 + tt, :], ps[:, bk, 0:256], [PB[bk]], [b_v2[kg]])
                if stop_after == "D0":
                    dump([(kT2[:, 0, 0:1024], b_kT2[0:2], 1024), (qT2[:, 1, 0:1024], b_qT2[0:2], 1024),
                          (v2[:, 0:4, :].rearrange("p a b -> p (a b)"), b_v2[0:1], 1024)])
                    return nc
                for g in range(4):
                    kb.dma("sp", mk[:], maskT_d[g], reads=[b_maskd[g]], writes=[b_mk])
                    nch = 20 + 4 * g
                    for hl in range(2):
                        h = 2 * hg + hl
                        bO = 3 + 2 * (hl % 2)
                        bD = bO + 1

                        def issue_S(c, hl=hl, g=g):
                            bk = SB_S[c % 3]
                            kb.op("pe", lambda e: e.matmul(
                                ps[:, bk, :], lhsT=kT2[:, hl, c * 128:(c + 1) * 128],
                                rhs=qT2[:, hl, g * 512:(g + 1) * 512], start=True, stop=True),
                                reads=[b_kT2[c // 4], b_qT2[g]], writes=[PB[bk]])
                        issue_S(0)
                        if nch > 1:
                            issue_S(1)
                        for c in range(nch):
                            if c + 2 < nch:
                                issue_S(c + 2)
                            bk = SB_S[c % 3]
                            ei = c % 3
                            tokE = kb.op("act", lambda e, bk=bk, ei=ei, h=h: e.activation(
                                out=Et[ei][:, :], in_=ps[:, bk, :], func=AF.Exp, scale=ATT_SCALE,
                                bias=cst[:, C_B31 + h:C_B31 + h + 1]),
                                reads=[PB[bk], b_cst], writes=[b_E[ei]])
                            cn = c - (15 + 4 * g)
                            if 0 <= cn <= 4:
                                for j in range(4):
                                    dl = 1 + j - cn
                                    if dl in (0, 1):
                                        ti = (j + cn) % 2
                                        kb.op("dve", lambda e, bk=bk, j=j, dl=dl, h=h, ti=ti: e.scalar_tensor_tensor(
                                            out=tmpn[ti][:, :], in0=ps[:, bk, j * 128:(j + 1) * 128], scalar=ATT_SCALE,
                                            in1=bT[:, dl, h, :], op0=ALU.mult, op1=ALU.add),
                                            reads=[PB[bk], b_bT], writes=[b_tmpn[ti]], deps=[tokE])
                                        kb.op("act", lambda e, ei=ei, j=j, ti=ti: e.activation(
                                            out=Et[ei][:, j * 128:(j + 1) * 128], in_=tmpn[ti][:, :], func=AF.Exp),
                                            reads=[b_tmpn[ti]], writes=[b_E[ei]])
                            kb.op("dve", lambda e, ei=ei, c=c: e.tensor_tensor(
                                out=Pt[ei][:, :], in0=Et[ei][:, :], in1=mk[:, c, :], op=ALU.mult),
                                reads=[b_E[ei], b_mk], writes=[b_P[ei]])
                            kb.op("pe", lambda e, ei=ei, c=c, hl=hl, bO=bO, nch=nch: e.matmul(
                                ps[:, bO, :], lhsT=v2[:, c, hl * 128:(hl + 1) * 128], rhs=Pt[ei][:, :],
                                start=(c == 0), stop=(c == nch - 1)),
                                reads=[b_v2[c // 4], b_P[ei]], writes=[PB[bO]], sig=(c == nch - 1))
                            kb.op("pe", lambda e, ei=ei, c=c, bD=bD, nch=nch: e.matmul(
                                ps[:, bD, :], lhsT=ones_b[:, :], rhs=Pt[ei][:, :],
                                start=(c == 0), stop=(c == nch - 1)),
                                reads=[b_ones, b_P[ei]], writes=[PB[bD]], sig=(c == nch - 1))
                        kb.op("dve", lambda e, bD=bD: e.reciprocal(out=rden[:, :], in_=ps[:, bD, :]),
                              reads=[PB[bD]], writes=[b_rden])
                        kb.op("dve", lambda e, bO=bO, h=h, g=g: e.tensor_tensor(
                            out=attnT[:, h, g * 512:(g + 1) * 512], in0=ps[:, bO, :], in1=rden[:, :], op=ALU.mult),
                            reads=[PB[bO], b_rden], writes=[b_attnT[h][g]])
                        if stop_after == "D1":
                            dump([(attnT[:, 0, 0:512], [b_attnT[0][0]], 512), (rden[:, :], [b_rden], 512)])
                            return nc
        if stop_after == "D":
            dump([(attnT[:, 0, 0:1024], b_attnT[0], 1024), (attnT[:, 7, 0:1024], b_attnT[7], 1024),
                  (attnT[:, 3, 1024:2048], b_attnT[3], 1024)])
            return nc

        scCD.close()
        kb.barrier()
        b_hT_d = bufs(4, "hT_d")
        b_htok = bufs(16, "htok")

        def layer_norm(e_sb, src, bsrc, gam, bet, b_gb, outt, bout, xc, b_xc, junkf, b_jf, st, b_st):
            kb.op("dve", lambda e: e.tensor_scalar(out=junkf[:, :], in0=src, scalar1=1.0, scalar2=None, op0=ALU.mult,
                                                   op1=ALU.add, accum_out=st[:, 0:1]), reads=[bsrc], writes=[b_jf, b_st])
            kb.op("dve", lambda e: e.tensor_scalar(out=st[:, 1:2], in0=st[:, 0:1], scalar1=1.0 / D, scalar2=None,
                                                   op0=ALU.mult), writes=[b_st])
            kb.op("dve", lambda e: e.tensor_scalar(out=xc[:, :], in0=src, scalar1=st[:, 1:2], scalar2=None,
                                                   op0=ALU.subtract), reads=[bsrc, b_st], writes=[b_xc])
            kb.op("dve", lambda e: e.tensor_tensor(out=junkf[:, :], in0=xc[:, :], in1=xc[:, :], op=ALU.mult),
                  reads=[b_xc], writes=[b_jf])
            kb.op("dve", lambda e: e.tensor_scalar(out=junkf[:, :], in0=junkf[:, :], scalar1=1.0, scalar2=None,
                                                   op0=ALU.mult, op1=ALU.add, accum_out=st[:, 2:3]),
                  writes=[b_jf, b_st])
            kb.op("dve", lambda e: e.tensor_scalar(out=st[:, 3:4], in0=st[:, 2:3], scalar1=1.0 / D, scalar2=LN_EPS,
                                                   op0=ALU.mult, op1=ALU.add), writes=[b_st])
            kb.op("act", lambda e: e.activation(out=st[:, 4:5], in_=st[:, 3:4], func=AF.Sqrt), writes=[b_st])
            kb.op("dve", lambda e: e.reciprocal(out=st[:, 5:6], in_=st[:, 4:5]), writes=[b_st])
            kb.op("dve", lambda e: e.scalar_tensor_tensor(out=xc[:, :], in0=xc[:, :], scalar=st[:, 5:6], in1=gam,
                                                          op0=ALU.mult, op1=ALU.mult),
                  reads=[b_st, b_gb], writes=[b_xc])
            return kb.op("dve", lambda e: e.tensor_tensor(out=outt, in0=xc[:, :], in1=bet, op=ALU.add),
                         reads=[b_xc, b_gb], writes=[bout])

        with ExitStack() as sc:
            sbE = lambda name, shape, dt=F32: sc.enter_context(nc.sbuf_tensor(name, list(shape), dt))
            wo2 = [sbE("wo%d" % i, [128, 16, 512], BF16) for i in range(2)]
            b_wo2 = bufs(2, "wo")
            wo_rr = [0]
            gb = sbE("gb1", [128, 2, D])
            b_gb = Buf("gb1")
            kb.dma("sp", gb[:, 0, :], lnp[0], writes=[b_gb])
            kb.dma("sp", gb[:, 1, :], lnp[1], writes=[b_gb])
            xt = [sbE("xt0", [128, D])] * 2
            b_xt = [Buf("xt")] * 2
            hpre = sbE("hpre", [128, D])
            b_hpre = Buf("hpre")
            xc = sbE("xc", [128, D])
            b_xc = Buf("xc")
            junkf = sbE("junkf", [128, D])
            b_jf = Buf("junkf")
            hh = [sbE("hh0", [128, D])] * 2
            b_hh = [Buf("hh")] * 2
            st = sbE("st", [128, 8])
            b_st = Buf("st")
            hTs = sbE("hTs", [128, 16, 128], BF16)
            b_hTs = Buf("hTs")
            kb.dma("sp", xt[0][:, :], x_tok[0:128, :], writes=[b_xt[0]])
            for tt in range(16):
                tg = tt // 4
                xi = tt % 2
                for dt_ in range(4):
                    bk = next_bank([0, 1, 2, 7])
                    wi_ = wo_rr[0] % 2
                    wo_rr[0] += 1
                    wo, b_wo = wo2[wi_], b_wo2[wi_]
                    for q4 in range(4):
                        kb.dma("pool", wo[:, 4 * q4:4 * q4 + 4, :], w_out[:, 4 * q4:4 * q4 + 4, dt_ * 512:(dt_ + 1) * 512],
                               writes=[b_wo])
                    for kc in range(16):
                        if kc < 8:
                            lh = poolT[:, kc, tt * 128:(tt + 1) * 128]
                            rb = b_poolT[tg]
                        else:
                            lh = attnT[:, kc - 8, tt * 128:(tt + 1) * 128]
                            rb = b_attnT[kc - 8][tg]
                        kb.op("pe", lambda e, lh=lh, kc=kc, dt_=dt_, bk=bk, wo=wo: e.matmul(
                            ps[:, bk, :], lhsT=lh, rhs=wo[:, kc, :],
                            start=(kc == 0), stop=(kc == 15)),
                            reads=[rb, b_wo], writes=[PB[bk]], sig=(kc == 15))
                    kb.op("dve", lambda e, xi=xi, dt_=dt_, bk=bk: e.scalar_tensor_tensor(
                        out=hpre[:, dt_ * 512:(dt_ + 1) * 512], in0=xt[xi][:, dt_ * 512:(dt_ + 1) * 512], scalar=ALPHA,
                        in1=ps[:, bk, :], op0=ALU.mult, op1=ALU.add),
                        reads=[b_xt[xi], PB[bk]], writes=[b_hpre])
                hi = tt % 2
                if tt + 1 < 16:
                    kb.dma("sp", xt[(tt + 1) % 2][:, :], x_tok[(tt + 1) * 128:(tt + 2) * 128, :],
                           writes=[b_xt[(tt + 1) % 2]])
                layer_norm(None, hpre[:, :], b_hpre, gb[:, 0, :], gb[:, 1, :], b_gb, hh[hi][:, :], b_hh[hi],
                           xc, b_xc, junkf, b_jf, st, b_st)
                kb.dma("sp", h_tok[tt * 128:(tt + 1) * 128, :], hh[hi][:, :], reads=[b_hh[hi]], writes=[b_htok[tt]])
                for c0 in range(0, 16, 4):
                    bk = next_bank([3, 4, 5, 6])
                    for cc in range(4):
                        kb.op("pe", lambda e, c0=c0, cc=cc, bk=bk, hi=hi: e.transpose(
                            ps[:, bk, cc * 128:(cc + 1) * 128], hh[hi][:, (c0 + cc) * 128:(c0 + cc + 1) * 128], ident),
                            reads=[b_hh[hi], b_cst], writes=[PB[bk]], sig=(cc == 3))
                    evac(hTs[:, c0:c0 + 4, :],
                         ps[:, bk, :].rearrange("p (c q) -> p c q", c=4), [PB[bk]], [b_hTs])
                for q4 in range(4):
                    kb.dma("sp", hT_d[:, 4 * q4:4 * q4 + 4, tt * 128:(tt + 1) * 128], hTs[:, 4 * q4:4 * q4 + 4, :],
                           reads=[b_hTs], writes=[b_hT_d[tg]])
        pers.close()
        if stop_after == "E":
            t1 = kb.dma("sp", dbg_out[:, 0:2048], h_tok[0:128, :], reads=[b_htok[0]])
            t2 = kb.dma("sp", dbg_out[:, 2048:4096], h_tok[1920:2048, :], reads=[b_htok[15]])
            kb.wait("sp", t1)
            kb.wait("sp", t2)
            return nc

        kb.barrier()
        b_W1 = bufs(32, "W1")
        with ExitStack() as sc:
            sbF = lambda name, shape, dt=F32: sc.enter_context(nc.sbuf_tensor(name, list(shape), dt))
            wqc = [sbF("wqc%d" % i, [128, 16, 128], BF16) for i in range(3)]
            b_wqc = bufs(3, "wqc")
            wq_rr = [0]
            skT = sbF("skT", [128, 2, 128], BF16)
            b_skT = Buf("skT")
            kb.dma("pool", skT[:], subkT, writes=[b_skT])
            hTg = [sbF("hTg0", [128, 16, 512], BF16)] * 2
            b_hTg = [Buf("hTg")] * 2
            qpT = sbF("qpT", [128, 16, 512], BF16)
            b_qpT = Buf("qpT")
            s_sb = sbF("s_sb", [128, 16, 128])
            s2_sb = sbF("s2_sb", [128, 16, 128])
            b_s, b_s2 = Buf("s"), Buf("s2")
            v16 = sbF("v16", [128, 16, 16])
            i16 = sbF("i16", [128, 16, 16], U32)
            i16f = sbF("i16f", [128, 16, 16])
            b_v16, b_i16, b_i16f = Buf("v16"), Buf("i16"), Buf("i16f")
            cand = sbF("cand", [128, 8, 256])
            cand2 = sbF("cand2", [128, 8, 256])
            b_cand, b_cand2 = Buf("cand"), Buf("cand2")
            best = sbF("best", [128, 8, 16])
            bidx = sbF("bidx", [128, 8, 16], U32)
            aiu = sbF("aiu", [128, 128], U32)
            biu = sbF("biu", [128, 128], U32)
            af = sbF("af", [128, 128])
            bf = sbF("bf", [128, 128])
            b_best, b_bidx, b_bidf, b_af, b_bf = Buf("best"), Buf("bidx"), Buf("bidf"), Buf("af"), Buf("bf")
            eb = sbF("eb", [128, 8, 16])
            zz = sbF("zz", [128, 16])
            b_eb, b_zz = Buf("eb"), Buf("zz")
            oh = sbF("oh", [128, 128, 16])
            b_oh = Buf("oh")
            IG = sbF("IG", [128, 3, 128])
            b_IG = Buf("IG")
            IGT = sbF("IGT", [128, 3, 128])
            b_IGT = Buf("IGT")
            eq2 = [sbF("eq0", [128, 32, 128], BF16)] * 2
            b_eq2 = [Buf("eq")] * 2
            At2 = [sbF("At0", [128, 32, 128], BF16)] * 2
            Bt2 = [sbF("Bt0", [128, 32, 128], BF16)] * 2
            b_At2, b_Bt2 = [Buf("At")] * 2, [Buf("Bt")] * 2
            ab_rr = [0]
            stg = [sbF("stg0", [128, 128, 64], BF16)] * 2
            b_stg = [Buf("stg")] * 2
            for tg in range(4):
                hx = hTg[tg % 2]
                bhx = b_hTg[tg % 2]
                for q4 in range(4):
                    kb.dma("sp", hx[:, 4 * q4:4 * q4 + 4, :], hT_d[:, 4 * q4:4 * q4 + 4, tg * 512:(tg + 1) * 512],
                           reads=[b_hT_d[tg]], writes=[bhx])
                for n in range(16):
                    wi_ = wq_rr[0] % 3
                    wq_rr[0] += 1
                    for q4 in range(4):
                        kb.dma("pool", wqc[wi_][:, 4 * q4:4 * q4 + 4, :], wq[:, 4 * q4:4 * q4 + 4, n * 128:(n + 1) * 128],
                               writes=[b_wqc[wi_]])
                    proj_fm(lambda kc, wi_=wi_: wqc[wi_][:, kc, :], b_wqc[wi_], hx, bhx, qpT[:, n, :], [b_qpT])
                for tt in range(4):
                    T = 4 * tg + tt
                    for n4 in range(4):
                        bk = next_bank([3, 4, 5, 6])
                        for nn in range(4):
                            n = 4 * n4 + nn
                            kb.op("pe", lambda e, n=n, nn=nn, bk=bk, tt=tt: e.matmul(
                                ps[:, bk, nn * 128:(nn + 1) * 128], lhsT=qpT[:, n, tt * 128:(tt + 1) * 128],
                                rhs=skT[:, n % 2, :], start=True, stop=True),
                                reads=[b_qpT, b_skT], writes=[PB[bk]], sig=(nn == 3))
                        evac(s_sb[:, 4 * n4:4 * n4 + 4, :], ps[:, bk, :].rearrange("p (c q) -> p c q", c=4),
                             [PB[bk]], [b_s])
                    bv, bi, bs2 = bufs(16, "v16n"), bufs(16, "i16n"), bufs(16, "s2n")
                    for n in range(16):
                        kb.op("dve", lambda e, n=n: e.max(out=v16[:, n, 0:8], in_=s_sb[:, n, :]),
                              reads=[b_s], writes=[bv[n]], deps=[b_v16.w] + list(b_v16.r.values()))
                    for n in range(16):
                        kb.op("dve", lambda e, n=n: e.max_index(out=i16[:, n, 0:8], in_max=v16[:, n, 0:8],
                                                               in_values=s_sb[:, n, :]),
                              reads=[b_s, bv[n]], writes=[bi[n]], deps=[b_i16.w] + list(b_i16.r.values()))
                    for n in range(16):
                        kb.op("dve", lambda e, n=n: e.match_replace(out=s2_sb[:, n, :], in_to_replace=v16[:, n, 0:8],
                                                                   in_values=s_sb[:, n, :], imm_value=NEG),
                              reads=[b_s, bv[n]], writes=[bs2[n]])
                    for n in range(16):
                        kb.op("dve", lambda e, n=n: e.max(out=v16[:, n, 8:16], in_=s2_sb[:, n, :]),
                              reads=[bs2[n]], writes=[bv[n]])
                    for n in range(16):
                        kb.op("dve", lambda e, n=n: e.max_index(out=i16[:, n, 8:16], in_max=v16[:, n, 8:16],
                                                               in_values=s2_sb[:, n, :]),
                              reads=[bs2[n], bv[n]], writes=[bi[n]])
                    kb.op("dve", lambda e: e.tensor_copy(out=i16f[:], in_=i16[:]), reads=bi, writes=[b_i16f, b_i16])
                    kb.op("dve", lambda e: e.tensor_tensor(
                        out=cand[:].rearrange("p h (a b) -> p h a b", a=16),
                        in0=sap(v16, [[32, 8], [1, 16], [0, 16]]),
                        in1=sap(v16, [[32, 8], [0, 16], [1, 16]], off=16), op=ALU.add),
                        reads=bv, writes=[b_cand, b_v16])
                    bb, bx_, bc2 = bufs(8, "besth"), bufs(8, "bidxh"), bufs(8, "cand2h")
                    for h in range(8):
                        kb.op("dve", lambda e, h=h: e.max(out=best[:, h, 0:8], in_=cand[:, h, :]),
                              reads=[b_cand], writes=[bb[h]], deps=[b_best.w] + list(b_best.r.values()))
                    for h in range(8):
                        kb.op("dve", lambda e, h=h: e.max_index(out=bidx[:, h, 0:8], in_max=best[:, h, 0:8],
                                                               in_values=cand[:, h, :]),
                              reads=[b_cand, bb[h]], writes=[bx_[h]], deps=[b_bidx.w] + list(b_bidx.r.values()))
                    for h in range(8):
                        kb.op("dve", lambda e, h=h: e.match_replace(out=cand2[:, h, :], in_to_replace=best[:, h, 0:8],
                                                                   in_values=cand[:, h, :], imm_value=NEG),
                              reads=[b_cand, bb[h]], writes=[bc2[h]])
                    for h in range(8):
                        kb.op("dve", lambda e, h=h: e.max(out=best[:, h, 8:16], in_=cand2[:, h, :]),
                              reads=[bc2[h]], writes=[bb[h]])
                    for h in range(8):
                        kb.op("dve", lambda e, h=h: e.max_index(out=bidx[:, h, 8:16], in_max=best[:, h, 8:16],
                                                               in_values=cand2[:, h, :]),
                              reads=[bc2[h], bb[h]], writes=[bx_[h]])
                    kb.op("dve", lambda e: e.tensor_copy(out=zz[:, 0:1], in_=best[:, 0, 0:1]),
                          reads=bb + bx_, writes=[b_best, b_bidx, b_zz])
                    kb.op("dve", lambda e: e.tensor_tensor(out=eb[:], in0=best[:], in1=sap(best, [[16, 8], [0, 16]]),
                                                           op=ALU.subtract), reads=[b_best], writes=[b_eb])
                    kb.op("act", lambda e: e.activation(out=eb[:], in_=eb[:], func=AF.Exp), writes=[b_eb])
                    kb.op("dve", lambda e: e.tensor_reduce(out=zz[:, 0:8], in_=eb[:], axis=AX.X, op=ALU.add),
                          reads=[b_eb], writes=[b_zz])
                    kb.op("dve", lambda e: e.reciprocal(out=zz[:, 8:16], in_=zz[:, 0:8]), writes=[b_zz])
                    kb.op("dve", lambda e: e.tensor_tensor(
                        out=IG[:, 2, :].rearrange("p (h r) -> p h r", h=8), in0=eb[:],
                        in1=sap(zz, [[1, 8], [0, 16]], off=8), op=ALU.mult),
                        reads=[b_eb, b_zz], writes=[b_IG])
                    kb.op("dve", lambda e: e.tensor_single_scalar(out=aiu[:], in_=bidx[:].rearrange("p h r -> p (h r)"),
                                                                  scalar=4, op=ALU.logical_shift_right),
                          reads=[b_bidx], writes=[b_bidf])
                    kb.op("dve", lambda e: e.tensor_single_scalar(out=biu[:], in_=bidx[:].rearrange("p h r -> p (h r)"),
                                                                  scalar=15, op=ALU.bitwise_and),
                          reads=[b_bidx], writes=[b_bidf])
                    kb.op("dve", lambda e: e.tensor_copy(out=af[:], in_=aiu[:]), reads=[b_bidf], writes=[b_af])
                    kb.op("dve", lambda e: e.tensor_copy(out=bf[:], in_=biu[:]), reads=[b_bidf], writes=[b_bf])
                    for which, sel, boff in ((0, af, 0), (1, bf, 16)):
                        bsel = b_af if which == 0 else b_bf
                        kb.op("dve", lambda e, sel=sel: e.tensor_tensor(
                            out=oh[:], in0=sap(cst, [[0, 128], [1, 16]], off=C_IOTA),
                            in1=sap(sel, [[1, 128], [0, 16]]), op=ALU.is_equal),
                            reads=[bsel, b_cst], writes=[b_oh])
                        kb.op("dve", lambda e, boff=boff: e.tensor_tensor(
                            out=oh[:].rearrange("p (h r) a -> p h r a", h=8),
                            in0=oh[:].rearrange("p (h r) a -> p h r a", h=8),
                            in1=sap(i16f, [[32, 8], [0, 16], [1, 16]], off=boff), op=ALU.mult),
                            reads=[b_i16f], writes=[b_oh])
                        kb.op("dve", lambda e, which=which: e.tensor_reduce(out=IG[:, which, :], in_=oh[:], axis=AX.X,
                                                                          op=ALU.add),
                              reads=[b_oh], writes=[b_IG])
                    bk = next_bank([3, 4, 5, 6])
                    for w3 in range(3):
                        kb.op("pe", lambda e, w3=w3, bk=bk: e.transpose(ps[:, bk, w3 * 128:(w3 + 1) * 128],
                                                                        IG[:, w3, :], ident),
                              reads=[b_IG, b_cst], writes=[PB[bk]], sig=(w3 == 2))
                    kb.op("act", lambda e, bk=bk: e.activation(out=IGT[:], in_=ps[:, bk, 0:384].rearrange(
                        "p (c q) -> p c q", c=3), func=AF.Copy), reads=[PB[bk]], writes=[b_IGT])
                    for sbk in range(2):
                        si = (2 * T + sbk) % 2
                        for s32 in range(2):
                            t0 = sbk * 64 + s32 * 32
                            ai_ = ab_rr[0] % 2
                            ab_rr[0] += 1
                            eq, b_eq = eq2[ai_], b_eq2[ai_]
                            At, b_At = At2[ai_], b_At2[ai_]
                            Bt, b_Bt = Bt2[ai_], b_Bt2[ai_]
                            kb.op("dve", lambda e, t0=t0: e.tensor_tensor(
                                out=eq[:], in0=sap(cst, [[0, 32], [1, 128]], off=C_IOTA),
                                in1=sap(IGT, [[1, 32], [0, 128]], off=0 * 128 + t0), op=ALU.is_equal),
                                reads=[b_IGT, b_cst], writes=[b_eq])
                            kb.op("dve", lambda e, t0=t0: e.tensor_tensor(
                                out=At[:], in0=eq[:], in1=sap(IGT, [[1, 32], [0, 128]], off=2 * 128 + t0), op=ALU.mult),
                                reads=[b_eq, b_IGT], writes=[b_At])
                            kb.op("dve", lambda e, t0=t0: e.tensor_tensor(
                                out=Bt[:], in0=sap(cst, [[0, 32], [1, 128]], off=C_IOTA),
                                in1=sap(IGT, [[1, 32], [0, 128]], off=1 * 128 + t0), op=ALU.is_equal),
                                reads=[b_IGT, b_cst], writes=[b_Bt])
                            for t4 in range(8):
                                bk = next_bank([0, 1, 2, 7])
                                for q4 in range(4):
                                    tl = 4 * t4 + q4
                                    kb.op("pe", lambda e, tl=tl, q4=q4, bk=bk: e.matmul(
                                        ps[:, bk, q4 * 128:(q4 + 1) * 128], lhsT=At[:, tl, :], rhs=Bt[:, tl, :],
                                        start=True, stop=True),
                                        reads=[b_At, b_Bt], writes=[PB[bk]], sig=(q4 == 3))
                                kb.op("act", lambda e, t4=t4, bk=bk, si=si, s32=s32: e.activation(
                                    out=sap(stg[si], [[1, 4], [64, 128]], off=s32 * 32 + 4 * t4),
                                    in_=ps[:, bk, :].rearrange("p (t j) -> p t j", t=4), func=AF.Copy),
                                    reads=[PB[bk]], writes=[b_stg[si]])
                        kb.dma("sp", W1[2 * T + sbk], stg[si][:], reads=[b_stg[si]], writes=[b_W1[2 * T + sbk]])
                    if stop_after == "F2" and T == 0:
                        wchk = sbF("wchk", [128, 1024], BF16)
                        b_wchk = Buf("wchk")
                        kb.dma("sp", wchk[:], W1[0, :, 0:16, :].rearrange("i j t -> i (j t)"), reads=[b_W1[0]], writes=[b_wchk])
                        dump([(wchk[:, :], [b_wchk], 1024), (IG[:, 0, :], [b_IG], 128), (IG[:, 1, :], [b_IG], 128),
                              (IG[:, 2, :], [b_IG], 128)])
                        return nc
                    if stop_after == "F" and T == 0:
                        dump([(IG[:, 0, :], [b_IG], 128), (IG[:, 1, :], [b_IG], 128), (IG[:, 2, :], [b_IG], 128),
                              (s_sb[:, 0, :], [b_s], 128), (s_sb[:, 1, :], [b_s], 128)])
                        return nc

        kb.barrier()
        if stop_after == "Fend":
            kb.barrier()
            return nc
        NJ = 4
        NR = 0 if stop_after == "G4" else (2 if stop_after in ("G2", "G3") else 128 // NJ)
        for half in range(2):
            kb.barrier()
            with ExitStack() as sc:
                sbG = lambda name, shape, dt=F32: sc.enter_context(nc.sbuf_tensor(name + "_h%d" % half, list(shape), dt))
                hTh = sbG("hTh", [128, 16, 1024], BF16)
                b_hTh = Buf("hTh")
                for tg2 in range(2):
                    tg = 2 * half + tg2
                    for q4 in range(4):
                        kb.dma("sp", hTh[:, 4 * q4:4 * q4 + 4, tg2 * 512:(tg2 + 1) * 512],
                               hT_d[:, 4 * q4:4 * q4 + 4, tg * 512:(tg + 1) * 512],
                               reads=[b_hT_d[tg]], writes=[b_hTh])
                accG = sbG("accG", [128, 8, D])
                b_accG = [bufs(2, "accG%d" % t) for t in range(8)]
                with ExitStack() as sc2:
                    sbH = lambda name, shape, dt=F32: sc2.enter_context(nc.sbuf_tensor(name + "_h%d" % half, list(shape), dt))
                    Wr = [sbH("Wr%d" % i, [128, 16, NJ, 64], BF16) for i in range(2)]
                    b_Wr = bufs(2, "Wr")
                    vr = [sbH("vr%d" % i, [128, NJ, D], BF16) for i in range(2)]
                    b_vr = bufs(2, "vr")
                    uj = [sbH("uj%d" % i, [128, 16, 128], BF16) for i in range(2)]
                    b_uj = bufs(2, "uj")
                    Gr = [sbH("Gr%d" % i, [128, NJ, 1024], BF16) for i in range(2)]
                    b_Gr = bufs(2, "Gr")
                    ga = [sbH("ga%d" % i, [128, 512], BF16) for i in range(2)]
                    b_ga = bufs(2, "ga")
                    urr = [0]
                    grr = [0]

                    def phaseA(r):
                        ri = r % 2
                        j0 = r * NJ
                        for q4 in range(4):
                            b0_ = 16 * half + 4 * q4
                            kb.dma("sp", Wr[ri][:, 4 * q4:4 * q4 + 4], W1[b0_:b0_ + 4, :, j0:j0 + NJ, :].rearrange(
                                "b i j t -> i b j t"), reads=b_W1[b0_:b0_ + 4], writes=[b_Wr[ri]])
                        for jj in range(NJ):
                            ui = urr[0] % 2
                            urr[0] += 1
                            kb.dma("pool", uj[ui][:], uT[j0 + jj], writes=[b_uj[ui]])
                            for tg2 in range(2):
                                bk = next_bank([0, 1])
                                for kc in range(16):
                                    kb.op("pe", lambda e, kc=kc, ui=ui, tg2=tg2, bk=bk: e.matmul(
                                        ps[:, bk, :], lhsT=uj[ui][:, kc, :], rhs=hTh[:, kc, tg2 * 512:(tg2 + 1) * 512],
                                        start=(kc == 0), stop=(kc == 15)),
                                        reads=[b_uj[ui], b_hTh], writes=[PB[bk]], sig=(kc == 15))
                                gi = grr[0] % 2
                                grr[0] += 1
                                kb.op("act", lambda e, gi=gi, bk=bk: e.activation(out=ga[gi][:, :], in_=ps[:, bk, :],
                                                                                 func=AF.Gelu),
                                      reads=[PB[bk]], writes=[b_ga[gi]])
                                kb.op("dve", lambda e, gi=gi, ri=ri, jj=jj, tg2=tg2: e.tensor_tensor(
                                    out=Gr[ri][:, jj, tg2 * 512:(tg2 + 1) * 512].rearrange("p (b t) -> p b t", b=8),
                                    in0=ga[gi][:, :].rearrange("p (b t) -> p b t", b=8),
                                    in1=Wr[ri][:, tg2 * 8:(tg2 + 1) * 8, jj, :], op=ALU.mult),
                                    reads=[b_ga[gi], b_Wr[ri]], writes=[b_Gr[ri]])
                        kb.dma("pool", vr[ri][:], vL[j0:j0 + NJ].rearrange("j i d -> i j d"), writes=[b_vr[ri]])

                    def phaseB(r):
                        ri = r % 2
                        for tt in range(8):
                            for dh in range(2):
                                b0 = 2 + 2 * ((tt * 2 + dh) % 3)
                                for jj in range(NJ):
                                    for dq in range(2):
                                        kb.op("pe", lambda e, jj=jj, dq=dq, tt=tt, dh=dh, b0=b0: e.matmul(
                                            ps[:, b0 + dq, :], lhsT=Gr[ri][:, jj, tt * 128:(tt + 1) * 128],
                                            rhs=vr[ri][:, jj, dh * 1024 + dq * 512:dh * 1024 + (dq + 1) * 512],
                                            start=(jj == 0), stop=(jj == NJ - 1)),
                                            reads=[b_Gr[ri], b_vr[ri]], writes=[PB[b0 + dq]],
                                            sig=(jj == NJ - 1 and dq == 1))
                                pin = ps[:, b0:b0 + 2, :].rearrange("p a b -> p (a b)")
                                aout = accG[:, tt, dh * 1024:(dh + 1) * 1024]
                                if r == 0:
                                    kb.op("dve", lambda e, pin=pin, aout=aout: e.tensor_copy(out=aout, in_=pin),
                                          reads=[PB[b0], PB[b0 + 1]], writes=[b_accG[tt][dh]])
                                else:
                                    kb.op("dve", lambda e, pin=pin, aout=aout: e.tensor_tensor(
                                        out=aout, in0=aout, in1=pin, op=ALU.add),
                                        reads=[PB[b0], PB[b0 + 1]], writes=[b_accG[tt][dh]])

                    if stop_after == "G1":
                        phaseA(0)
                        phaseB(0)
                        dump([(accG[:, 0, 0:1024], b_accG[0], 1024), (accG[:, 7, 1024:2048], b_accG[7], 1024),
                              (Gr[0][:, 0, :], [b_Gr[0]], 1024), (Gr[0][:, 3, :], [b_Gr[0]], 1024)])
                        return nc
                    if NR > 0:
                        phaseA(0)
                    for r in range(NR):
                        if r + 1 < NR:
                            phaseA(r + 1)
                        phaseB(r)
                kb.barrier()
                if stop_after == "G3":
                    dump([(accG[:, 0, 0:1024], b_accG[0], 1024), (accG[:, 7, 1024:2048], b_accG[7], 1024)])
                    return nc
                with ExitStack() as sc3:
                    sbL = lambda name, shape, dt=F32: sc3.enter_context(nc.sbuf_tensor(name + "_h%d" % half, list(shape), dt))
                    gb2 = sbL("gb2", [128, 2, D])
                    b_gb2 = Buf("gb2")
                    kb.dma("sp", gb2[:, 0, :], lnp[2], writes=[b_gb2])
                    kb.dma("sp", gb2[:, 1, :], lnp[3], writes=[b_gb2])
                    hres = [sbL("hres%d" % i, [128, D]) for i in range(2)]
                    b_hres = bufs(2, "hres")
                    xc2 = sbL("xc2", [128, D])
                    b_xc2 = Buf("xc2")
                    jf2 = sbL("jf2", [128, D])
                    b_jf2 = Buf("jf2")
                    yo = [sbL("yo%d" % i, [128, D]) for i in range(2)]
                    b_yo = bufs(2, "yo")
                    st2 = sbL("st2", [128, 8])
                    b_st2 = Buf("st2")
                    outs = []
                    kb.dma("sp", hres[0][:, :], h_tok[(8 * half) * 128:(8 * half + 1) * 128, :],
                           reads=[b_htok[8 * half]], writes=[b_hres[0]])
                    for tt in range(8):
                        T = 8 * half + tt
                        hi = tt % 2
                        if tt + 1 < 8:
                            kb.dma("sp", hres[1 - hi][:, :], h_tok[(T + 1) * 128:(T + 2) * 128, :],
                                   reads=[b_htok[T + 1]], writes=[b_hres[1 - hi]])
                        kb.op("dve", lambda e, hi=hi, tt=tt: e.scalar_tensor_tensor(
                            out=hres[hi][:, :], in0=hres[hi][:, :], scalar=ALPHA, in1=accG[:, tt, :],
                            op0=ALU.mult, op1=ALU.add),
                            reads=b_accG[tt], writes=[b_hres[hi]])
                        layer_norm(None, hres[hi][:, :], b_hres[hi], gb2[:, 0, :], gb2[:, 1, :], b_gb2,
                                   yo[hi][:, :], b_yo[hi], xc2, b_xc2, jf2, b_jf2, st2, b_st2)
                        outs.append(kb.dma("sp", y[T * 128:(T + 1) * 128, :], yo[hi][:, :], reads=[b_yo[hi]]))
                    for t in outs:
                        kb.wait("sp", t)
                    if stop_after in ("G2", "G4"):
                        dump([(accG[:, 0, 0:1024], b_accG[0], 1024), (accG[:, 7, 1024:2048], b_accG[7], 1024),
                              (yo[1][:, 0:1024], [b_yo[1]], 1024), (hres[1][:, 0:1024], [b_hres[1]], 1024)])
                        return nc
        print("instructions:", kb.nins, "sbuf remaining:", nc.sbuf_bytes_remaining)
    return nc


def _t5_bucket(dist):
    dist = np.asarray(dist)
    d = np.maximum(dist, 1).astype(np.float32)
    large = 16 + (np.log(d / np.float32(16)) / np.float32(np.log(128 / 16)) * np.float32(16)).astype(np.int32)
    large = np.minimum(large, 31)
    return np.where(dist < 16, dist, large)


def _prep_shared(w_in, pool_w, pool_scale, rel_bias, w_out, ln1_g, ln1_b, peer_wq, peer_subkeys, peer_u, peer_v,
                 ln2_g, ln2_b):
    f = np.float32
    w = w_in[0]
    cols = []
    for c in range(8):
        cols.append(w[:, c * 128:(c + 1) * 128])
    for c in range(8):
        cols.append(w[:, 1024 + c * 128:1024 + (c + 1) * 128])
    for c in range(8):
        cols.append(w[:, 2048 + c * 128:2048 + (c + 1) * 128])
    for c in range(8):
        cols.append(w[:, 4096 + c * 128:4096 + (c + 1) * 128])
    ki = w[:, 5120:5184]
    cols.append(np.concatenate([ki, ki], axis=1))
    w_fm = np.stack([c.reshape(16, 128, 128).transpose(1, 0, 2) for c in cols]).astype(f)
    wv = w[:, 3072:4096]
    w_v = np.stack([wv[:, hg * 256:(hg + 1) * 256].reshape(16, 128, 256).transpose(1, 0, 2) for hg in range(4)])
    w_wi = w[:, 5184:5200].reshape(16, 128, 16).transpose(1, 0, 2)
    pw = pool_w[0].reshape(4, 2, 128, 256).transpose(2, 0, 1, 3)
    kk = np.arange(128)[:, None]
    qq = np.arange(128)[None, :]
    bt = np.zeros((128, 2, 8, 128), f)
    for dl in range(2):
        bkt = _t5_bucket(np.maximum(dl * 128 + qq - kk, 0))
        bt[:, dl, :, :] = rel_bias[bkt].transpose(0, 2, 1)
    wo = w_out[0].reshape(16, 128, D).transpose(1, 0, 2)
    lnp = np.stack([np.broadcast_to(a[0][None, :], (128, D)) for a in (ln1_g, ln1_b, ln2_g, ln2_b)])
    wqh = peer_wq[0].reshape(16, 128, D).transpose(1, 0, 2)
    skT = peer_subkeys[0].transpose(2, 0, 1)
    u = peer_u[0].reshape(128, 128, 16, 128)
    uT = u.transpose(1, 3, 2, 0)
    vv = peer_v[0].reshape(128, 128, D).transpose(1, 0, 2)
    c = lambda a: np.ascontiguousarray(a, dtype=f)
    return dict(w_fm=c(w_fm), w_v=c(w_v), w_wi=c(w_wi), pool_w=c(pw), biasT=c(bt), w_out=c(wo), lnp=c(lnp),
                wq=c(wqh), subkT=c(skT), uT=c(uT), vL=c(vv))


def _consts(hf, pool_scale, rel_bias):
    f = np.float32
    cst = np.zeros((128, 1024), f)
    cst[:, 0:128] = np.eye(128, dtype=f)
    cst[:, 128:256] = np.arange(128, dtype=f)[None, :]
    qq = np.arange(128)[:, None]
    kk = np.arange(128)[None, :]
    cst[:, 256:384] = np.where(kk <= qq, 0.0, NEG).astype(f)
    valid = 1.0 if hf == 1 else 0.0
    cst[:, 384] = valid
    cst[:, 385] = (valid - 1.0) * 1.0e30
    cst[:, 392:400] = rel_bias[31][None, :]
    for gq, wwin in enumerate((2, 4, 8, 16)):
        pos = np.arange(16)
        if hf == 0:
            corr = wwin / np.minimum(pos + 1, wwin).astype(f)
        else:
            corr = np.ones(16, f)
        cst[:, 400 + 16 * gq:400 + 16 * gq + 16] = corr[None, :]
    cst[:, 464:472] = pool_scale[0].reshape(8, 128).T
    cst[:, 480:512] = (2.0 ** -np.arange(32, dtype=np.float64)).astype(f)[None, :]
    return cst


def _core_inputs(x, shared, pool_scale, rel_bias):
    in_maps = []
    for c in range(8):
        b, hf = c // 2, c % 2
        own = x[b, hf * TOK:(hf + 1) * TOK]
        prev = x[b, 0:TOK] if hf == 1 else np.zeros_like(own)
        xT = np.stack([prev.T.reshape(16, 128, TOK).transpose(1, 0, 2), own.T.reshape(16, 128, TOK).transpose(1, 0, 2)])
        m = dict(shared)
        m["xT"] = np.ascontiguousarray(xT, dtype=np.float32)
        m["x_tok"] = np.ascontiguousarray(own, dtype=np.float32)
        m["consts"] = _consts(hf, pool_scale, rel_bias)
        in_maps.append(m)
    return in_maps


def kernel(x, w_in, pool_w, pool_scale, rel_bias, w_out, ln1_g, ln1_b, peer_wq, peer_subkeys, peer_u, peer_v,
           ln2_g, ln2_b):
    args = [np.asarray(a, dtype=np.float32) for a in (x, w_in, pool_w, pool_scale, rel_bias, w_out, ln1_g, ln1_b,
                                                      peer_wq, peer_subkeys, peer_u, peer_v, ln2_g, ln2_b)]
    (x, w_in, pool_w, pool_scale, rel_bias, w_out, ln1_g, ln1_b, peer_wq, peer_subkeys, peer_u, peer_v,
     ln2_g, ln2_b) = args
    shared = _prep_shared(w_in, pool_w, pool_scale, rel_bias, w_out, ln1_g, ln1_b, peer_wq, peer_subkeys,
                          peer_u, peer_v, ln2_g, ln2_b)
    in_maps = _core_inputs(x, shared, pool_scale, rel_bias)
    nc = build_nc()
    res = run_bass_kernel_spmd(nc, in_maps, core_ids=list(range(8)))
    out = np.zeros((4, S, D), np.float32)
    for c in range(8):
        b, hf = c // 2, c % 2
        out[b, hf * TOK:(hf + 1) * TOK] = res.results[c]["y"]
    return out
```

```python
import numpy as np
from contextlib import ExitStack
import concourse.bass as bass
import concourse.mybir as mybir
from concourse.bass_utils import run_bass_kernel_spmd

F32 = mybir.dt.float32
BF16 = mybir.dt.bfloat16
U32 = mybir.dt.uint32
ALU = mybir.AluOpType
AF = mybir.ActivationFunctionType
AX = mybir.AxisListType

D = 2048
S = 4096
TOK = 2048
NEG = -1.0e30
ALPHA = 2.0 ** 0.25
LN_EPS = 1e-5
NIT = 16
TOPK = 256
ATT_SCALE = 128.0 ** -0.5
NSLOT = 6


class Buf:
    __slots__ = ("w", "r", "name")

    def __init__(self, name=""):
        self.w = None
        self.r = {}
        self.name = name


class KB:
    def __init__(self, nc, es):
        self.nc = nc
        self.engs = {"pe": nc.tensor, "dve": nc.vector, "act": nc.scalar, "pool": nc.gpsimd, "sp": nc.sync}
        self.psem = {e: es.enter_context(nc.semaphore("prog_" + e)) for e in ["pe", "dve", "act", "pool"]}
        self.cnt = {e: 0 for e in self.psem}
        self.seen = {e: {} for e in self.engs}
        self.pending = {e: [] for e in self.engs}
        self.dslots = {q: [[es.enter_context(nc.semaphore("dq_%s_%d" % (q, i))), 0, "dq_%s_%d" % (q, i)]
                           for i in range(NSLOT)] for q in ["sp", "pool"]}
        self.dnext = {q: 0 for q in self.dslots}
        self.nins = 0

    def wait(self, e, tok):
        if tok is None:
            return
        sem, val, key = tok
        if self.seen[e].get(key, 0) >= val:
            return
        self.engs[e].wait_ge(sem, val)
        self.seen[e][key] = val

    def _deps(self, e, reads, writes, deps):
        for b in reads:
            self.wait(e, b.w)
        for b in writes:
            self.wait(e, b.w)
            for t in b.r.values():
                self.wait(e, t)
        for t in deps:
            self.wait(e, t)

    def op(self, e, fn, reads=(), writes=(), deps=(), sig=True):
        self._deps(e, reads, writes, deps)
        ins = fn(self.engs[e])
        self.nins += 1
        if not sig:
            self.pending[e].append((list(reads), list(writes)))
            return None
        self.cnt[e] += 1
        ins.then_inc(self.psem[e], 1)
        key = "prog_" + e
        tok = (self.psem[e], self.cnt[e], key)
        allr = list(reads)
        allw = list(writes)
        for (r, w) in self.pending[e]:
            allr += r
            allw += w
        self.pending[e] = []
        for b in allw:
            b.w = tok
            b.r = {}
        for b in allr:
            if b not in allw:
                b.r[key] = tok
        return tok

    def barrier(self):
        toks = []
        for e in self.psem:
            if self.cnt[e] > 0:
                toks.append((self.psem[e], self.cnt[e], "prog_" + e))
        for q in self.dslots:
            for sem, cnt, key in self.dslots[q]:
                if cnt > 0:
                    toks.append((sem, cnt, key))
        for e in self.engs:
            for t in toks:
                self.wait(e, t)

    def dma(self, q, out, in_, reads=(), writes=(), deps=()):
        self._deps(q, reads, writes, deps)
        slot = self.dslots[q][self.dnext[q]]
        self.dnext[q] = (self.dnext[q] + 1) % NSLOT
        sem, cnt, key = slot
        if cnt > 0:
            self.wait(q, (sem, cnt, key))
        self.engs[q].dma_start(out=out, in_=in_).then_inc(sem, 16)
        self.nins += 1
        slot[1] = cnt + 16
        tok = (sem, cnt + 16, key)
        for b in writes:
            b.w = tok
            b.r = {}
        for b in reads:
            b.r[key] = tok
        return tok


def sap(t, dims, off=0, parts=128, p0=0):
    fs = 1
    for s_ in t.shape[1:]:
        fs *= int(s_)
    return bass.AP(t, p0 * fs + off, [[fs, parts]] + [[int(a), int(b)] for a, b in dims])


def bufs(n, name=""):
    return [Buf("%s%d" % (name, i)) for i in range(n)]


def build_nc(stop_after=None, small_peer=False):
    nc = bass.Bass("TRN2", target_bir_lowering=False)
    dbg = {}

    def din(name, shape, dt=F32):
        return nc.dram_tensor(name, list(shape), dt, kind="ExternalInput").ap()

    xT = din("xT", [2, 128, 16, TOK])
    x_tok = din("x_tok", [TOK, D])
    w_fm = din("w_fm", [33, 128, 16, 128])
    w_v = din("w_v", [4, 128, 16, 256])
    w_wi = din("w_wi", [128, 16, 16])
    pool_w = din("pool_w", [128, 4, 2, 256])
    consts = din("consts", [128, 1024])
    biasT = din("biasT", [128, 2, 8, 128])
    w_out = din("w_out", [128, 16, D])
    lnp = din("lnp", [4, 128, D])
    wq = din("wq", [128, 16, D])
    subkT = din("subkT", [128, 2, 128])
    uT = din("uT", [8 if small_peer else 128, 128, 16, 128])
    vL = din("vL", [8 if small_peer else 128, 128, D])
    y = nc.dram_tensor("y", [TOK, D], F32, kind="ExternalOutput").ap()
    maskT_d = nc.dram_tensor("maskT_d", [4, 128, 32, 512], BF16).ap()
    hT_d = nc.dram_tensor("hT_d", [128, 16, TOK], BF16).ap()
    h_tok = nc.dram_tensor("h_tok", [TOK, D], F32).ap()
    W1 = nc.dram_tensor("W1", [32, 128, 128, 64], BF16).ap()
    if stop_after is not None:
        dbg_out = nc.dram_tensor("dbg", [128, 8192], F32, kind="ExternalOutput").ap()

    with ExitStack() as es:
        kb = KB(nc, es)
        sb = lambda name, shape, dt=F32: es.enter_context(nc.sbuf_tensor(name, list(shape), dt))
        ps = es.enter_context(nc.psum_tensor("ps", [128, 8, 512], F32))
        PB = bufs(8, "psb")

        dbg_stg = sb("dbg_stg", [128, 1024]) if stop_after is not None else None
        cst = sb("cst", [128, 1024])
        b_cst = Buf("cst")
        kb.dma("sp", cst[:], consts, writes=[b_cst])
        C_ID = 0
        C_IOTA = 128
        C_TRI = 256
        C_FLAG = 384
        C_B31 = 392
        C_CORR = 400
        C_PSC = 464
        C_PW2 = 480
        ident = cst[:, C_ID:C_ID + 128]
        iota = cst[:, C_IOTA:C_IOTA + 128]
        tri = cst[:, C_TRI:C_TRI + 128]
        bT = sb("bT", [128, 2, 8, 128])
        b_bT = Buf("bT")
        kb.dma("sp", bT[:], biasT, writes=[b_bT])
        ones_b = sb("ones_b", [128, 128], BF16)
        b_ones = Buf("ones")
        kb.op("pool", lambda e: e.memset(ones_b[:], 1.0), writes=[b_ones])

        evac_rr = [0]

        def evac(out, in_, reads, writes, scale=None):
            evac_rr[0] ^= 1
            if scale is not None:
                return kb.op("act", lambda e: e.activation(out=out, in_=in_, func=AF.Copy, scale=scale),
                             reads=reads, writes=writes)
            if evac_rr[0]:
                return kb.op("act", lambda e: e.activation(out=out, in_=in_, func=AF.Copy), reads=reads, writes=writes)
            return kb.op("dve", lambda e: e.tensor_copy(out=out, in_=in_), reads=reads, writes=writes)

        pb_rr = [0]

        def next_bank(choices):
            pb_rr[0] += 1
            return choices[pb_rr[0] % len(choices)]

        def dump(ap_list):
            stg = dbg_stg
            b_stg = Buf("dbgstg")
            col = 0
            for ap, bl, n in ap_list:
                kb.op("dve", lambda e, ap=ap, n=n: e.tensor_copy(out=stg[:, 0:n], in_=ap),
                      reads=bl, writes=[b_stg])
                t = kb.dma("sp", dbg_out[:, col:col + n], stg[:, 0:n], reads=[b_stg])
                kb.wait("sp", t)
                col += n

        xg_t = [None, None]
        b_xg = bufs(2, "xg")
        xg_rr = [0]

        def load_xg(s, tg):
            i = xg_rr[0]
            xg_rr[0] ^= 1
            for q4 in range(4):
                kb.dma("pool", xg_t[i][:, 4 * q4:4 * q4 + 4, :], xT[s, :, 4 * q4:4 * q4 + 4, tg * 512:(tg + 1) * 512],
                       writes=[b_xg[i]])
            return xg_t[i], b_xg[i]

        def proj_fm(wt, wb, xg, bx, out_ap, out_bufs, scale=None):
            bk = next_bank([0, 1, 2, 7])
            for kc in range(16):
                kb.op("pe", lambda e, kc=kc: e.matmul(ps[:, bk, :], lhsT=wt(kc), rhs=xg[:, kc, :],
                                                      start=(kc == 0), stop=(kc == 15)),
                      reads=[wb, bx], writes=[PB[bk]], sig=(kc == 15))
            return evac(out_ap, ps[:, bk, :], [PB[bk]], out_bufs, scale=scale)

        phA = ExitStack()
        es.enter_context(phA)
        xg_t[0] = phA.enter_context(nc.sbuf_tensor("xgA0", [128, 16, 512], BF16))
        xg_t[1] = phA.enter_context(nc.sbuf_tensor("xgA1", [128, 16, 512], BF16))
        qiT = phA.enter_context(nc.sbuf_tensor("qiT", [128, 8, TOK], BF16))
        b_qiT = bufs(4, "qiT")
        kiT = phA.enter_context(nc.sbuf_tensor("kiT", [128, 2 * TOK], BF16))
        b_kiT = bufs(8, "kiT")
        widx = phA.enter_context(nc.sbuf_tensor("widx", [128, 16, 16], F32))
        b_widx = bufs(16, "widx")
        with ExitStack() as sc:
            wqi = sc.enter_context(nc.sbuf_tensor("wqi", [128, 8, 16, 128], BF16))
            b_wqi = Buf("wqi")
            wki = sc.enter_context(nc.sbuf_tensor("wki", [128, 16, 128], BF16))
            b_wki = Buf("wki")
            wwi = sc.enter_context(nc.sbuf_tensor("wwi", [128, 16, 16], BF16))
            b_wwi = Buf("wwi")
            kb.dma("pool", wki[:], w_fm[32], writes=[b_wki])
            for c in range(8):
                kb.dma("pool", wqi[:, c], w_fm[24 + c], writes=[b_wqi])
            kb.dma("pool", wwi[:], w_wi, writes=[b_wwi])
            for s in range(2):
                for tg in range(4):
                    xg, bx = load_xg(s, tg)
                    kg = s * 4 + tg
                    proj_fm(lambda kc: wki[:, kc, :], b_wki, xg, bx, kiT[:, kg * 512:(kg + 1) * 512], [b_kiT[kg]])
                    if stop_after == "A1":
                        dump([(kiT[:, 0:512], b_kiT[0:1], 512), (xg[:, 0, :], [bx], 512)])
                        return nc
                    if s == 1:
                        for c in range(8):
                            proj_fm(lambda kc, c=c: wqi[:, c, kc, :], b_wqi, xg, bx,
                                    qiT[:, c, tg * 512:(tg + 1) * 512], [b_qiT[tg]])
                        for tt in range(4):
                            bk = next_bank([0, 1, 2, 7])
                            for kc in range(16):
                                kb.op("pe", lambda e, kc=kc, tt=tt: e.matmul(
                                    ps[:, bk, 0:16], lhsT=xg[:, kc, tt * 128:(tt + 1) * 128], rhs=wwi[:, kc, :],
                                    start=(kc == 0), stop=(kc == 15)),
                                    reads=[b_wwi, bx], writes=[PB[bk]], sig=(kc == 15))
                            evac(widx[:, tg * 4 + tt, :], ps[:, bk, 0:16], [PB[bk]], [b_widx[tg * 4 + tt]])
        if stop_after == "A":
            dump([(kiT[:, 0:2048], b_kiT[0:4], 2048), (kiT[:, 2048:4096], b_kiT[4:8], 2048),
                  (qiT[:, 0, 0:2048], b_qiT, 2048), (widx[:, :, :].rearrange("p a b -> p (a b)"), b_widx, 256)])
            return nc

        kb.barrier()
        b_maskd = bufs(4, "maskd")
        with ExitStack() as sc:
            sbB = lambda name, shape, dt=F32: sc.enter_context(nc.sbuf_tensor(name, list(shape), dt))
            score2 = [sbB("score%d" % p_, [128, 4096]) for p_ in range(2)]
            b_score2 = [bufs(8, "score%d_" % p_) for p_ in range(2)]
            maskf = sbB("maskf", [128, 4096])
            b_maskf = Buf("maskf")
            junkA = sbB("junkA", [128, 4096], BF16)
            b_junkA = Buf("junkA")
            Rt = [sbB("R%d" % i_, [128, 512]) for i_ in range(3)]
            b_R = bufs(3, "R")
            acc = sbB("accB", [128, 512])
            b_acc = Buf("acc")
            mx2 = [sbB("mxall%d" % p_, [128, 16]) for p_ in range(2)]
            b_mx2 = bufs(2, "mx")
            sm2 = [sbB("smallB%d" % p_, [128, 64]) for p_ in range(2)]
            b_sm2 = bufs(2, "small")
            b_nm2 = bufs(2, "negmid")
            b_cs2 = bufs(2, "cs")
            mT = sbB("maskTg", [128, 32, 512], BF16)
            b_mT = Buf("maskTg")
            r_rr = [0]

            def make_units(i):
                g = i // 4
                p_ = i % 2
                score, b_score, mxall, b_mx = score2[p_], b_score2[p_], mx2[p_], b_mx2[p_]
                NK = (17 + i) * 128
                nkt = (NK + 511) // 512
                units = []
                for kt in range(nkt):
                    wk = min(512, NK - kt * 512)
                    direct = kt >= 4
                    for hn, h in enumerate([0, 2, 4, 6, 8, 10, 12, 14, 1, 3, 5, 7, 9, 11, 13, 15]):
                        def unit(kt=kt, wk=wk, direct=direct, hn=hn, h=h):
                            cp, r0 = h // 2, 64 * (h % 2)
                            bk = next_bank([0, 1, 2])
                            kb.op("pe", lambda e: e.matmul(
                                ps[:, bk, 0:wk], lhsT=qiT[r0:r0 + 64, cp, i * 128:(i + 1) * 128],
                                rhs=kiT[r0:r0 + 64, kt * 512:kt * 512 + wk], start=True, stop=True),
                                reads=[b_qiT[g], b_kiT[kt]], writes=[PB[bk]])
                            ri = r_rr[0] % 3
                            r_rr[0] += 1
                            kb.op("act", lambda e: e.activation(
                                out=Rt[ri][:, 0:wk], in_=ps[:, bk, 0:wk], func=AF.Relu),
                                reads=[PB[bk]], writes=[b_R[ri]])
                            wcol = widx[:, i, h:h + 1]
                            if hn == 0:
                                kb.op("dve", lambda e: e.tensor_scalar(
                                    out=acc[:, 0:wk], in0=Rt[ri][:, 0:wk], scalar1=wcol, scalar2=None, op0=ALU.mult),
                                    reads=[b_R[ri], b_widx[i]], writes=[b_acc])
                            elif hn == 15 and direct:
                                kb.op("dve", lambda e: e.scalar_tensor_tensor(
                                    out=score[:, kt * 512:kt * 512 + wk], in0=Rt[ri][:, 0:wk], scalar=wcol,
                                    in1=acc[:, 0:wk], op0=ALU.mult, op1=ALU.add),
                                    reads=[b_R[ri], b_widx[i], b_acc], writes=[b_score[kt]])
                            else:
                                kb.op("dve", lambda e: e.scalar_tensor_tensor(
                                    out=acc[:, 0:wk], in0=Rt[ri][:, 0:wk], scalar=wcol,
                                    in1=acc[:, 0:wk], op0=ALU.mult, op1=ALU.add),
                                    reads=[b_R[ri], b_widx[i]], writes=[b_acc])
                            if hn == 15:
                                src = score[:, kt * 512:kt * 512 + wk] if direct else acc[:, 0:wk]
                                bsrc = b_score[kt] if direct else b_acc
                                kb.op("dve", lambda e: e.tensor_reduce(
                                    out=mxall[:, kt:kt + 1], in_=src, axis=AX.X, op=ALU.max),
                                    reads=[bsrc], writes=[b_mx])
                                kb.op("dve", lambda e: e.tensor_reduce(
                                    out=mxall[:, 8 + kt:9 + kt], in_=src, axis=AX.X, op=ALU.min),
                                    reads=[bsrc], writes=[b_mx])
                                if not direct:
                                    kb.op("dve", lambda e: e.tensor_scalar(
                                        out=score[:, kt * 512:kt * 512 + wk], in0=acc[:, 0:wk],
                                        scalar1=cst[:, C_FLAG:C_FLAG + 1], scalar2=cst[:, C_FLAG + 1:C_FLAG + 2],
                                        op0=ALU.mult, op1=ALU.add),
                                        reads=[b_acc, b_cst], writes=[b_score[kt]])
                        units.append(unit)
                return units

            def post_score(i):
                p_ = i % 2
                score, b_score, mxall, b_mx, sm, b_sm = score2[p_], b_score2[p_], mx2[p_], b_mx2[p_], sm2[p_], b_sm2[p_]
                NK = (17 + i) * 128
                nkt = (NK + 511) // 512
                negmid, tmpc, Mp, negD = sm[:, 0:1], sm[:, 2:3], sm[:, 3:4], sm[:, 8:8 + NIT + 1]
                kd = (NK - 128) // 512
                kb.op("dve", lambda e: e.tensor_tensor(
                    out=score[:, NK - 128:NK], in0=score[:, NK - 128:NK], in1=tri, op=ALU.add),
                    reads=[b_cst], writes=[b_score[kd]])
                kb.op("dve", lambda e: e.tensor_reduce(out=Mp, in_=mxall[:, 0:nkt], axis=AX.X, op=ALU.max),
                      reads=[b_mx], writes=[b_sm])
                kb.op("dve", lambda e: e.tensor_reduce(out=tmpc, in_=mxall[:, 8:8 + nkt], axis=AX.X, op=ALU.min),
                      reads=[b_mx], writes=[b_sm])
                kb.op("dve", lambda e: e.scalar_tensor_tensor(out=Mp, in0=tmpc, scalar=-1.0, in1=Mp, op0=ALU.mult,
                                                              op1=ALU.max), writes=[b_sm])
                kb.op("dve", lambda e: e.tensor_scalar(out=Mp, in0=Mp, scalar1=-1.001, scalar2=-1e-20,
                                                       op0=ALU.mult, op1=ALU.add), writes=[b_sm])
                kb.op("dve", lambda e: e.tensor_scalar(out=negD, in0=cst[:, C_PW2:C_PW2 + NIT + 1], scalar1=Mp,
                                                       scalar2=None, op0=ALU.mult), reads=[b_cst], writes=[b_sm])
                kb.op("dve", lambda e: e.memset(negmid, 0.0), writes=[b_nm2[p_]])

            def make_steps(i):
                p_ = i % 2
                score, b_score, sm, b_sm = score2[p_], b_score2[p_], sm2[p_], b_sm2[p_]
                NK = (17 + i) * 128
                nkt = (NK + 511) // 512
                negmid, cs, tmpc, negD = sm[:, 0:1], sm[:, 1:2], sm[:, 2:3], sm[:, 8:8 + NIT + 1]
                thr = 2.0 * TOPK - NK - 0.5
                steps = []
                for k in range(NIT):
                    def act_fn():
                        kb.op("act", lambda e: e.activation(
                            out=junkA[:, 0:NK], in_=score[:, 0:NK], func=AF.Sign, bias=negmid, scale=1.0,
                            accum_out=cs),
                            reads=b_score[0:nkt] + [b_nm2[p_]], writes=[b_junkA, b_cs2[p_]])

                    def dve_fn(k=k):
                        kb.op("dve", lambda e: e.tensor_scalar(out=tmpc, in0=cs, scalar1=thr, scalar2=0.5,
                                                               op0=ALU.is_ge, op1=ALU.subtract),
                              reads=[b_cs2[p_]], writes=[b_sm])
                        kb.op("dve", lambda e: e.scalar_tensor_tensor(
                            out=negmid, in0=tmpc, scalar=negD[:, k:k + 1], in1=negmid, op0=ALU.mult, op1=ALU.add),
                            reads=[b_sm], writes=[b_nm2[p_]])
                    steps.append((act_fn, dve_fn))
                return steps

            def finalize(i):
                g, j = i // 4, i % 4
                p_ = i % 2
                score, b_score, sm, b_sm = score2[p_], b_score2[p_], sm2[p_], b_sm2[p_]
                NK = (17 + i) * 128
                nkt = (NK + 511) // 512
                negmid, tau, negD = sm[:, 0:1], sm[:, 4:5], sm[:, 8:8 + NIT + 1]
                if j == 0:
                    kb.op("pool", lambda e: e.memset(mT[:], 0.0), writes=[b_mT])
                kb.op("dve", lambda e: e.tensor_tensor(out=tau, in0=negD[:, NIT:NIT + 1], in1=negmid, op=ALU.subtract),
                      reads=[b_nm2[p_]], writes=[b_sm])
                kb.op("dve", lambda e: e.tensor_scalar(
                    out=maskf[:, 0:NK], in0=score[:, 0:NK], scalar1=tau, scalar2=None, op0=ALU.is_ge),
                    reads=b_score[0:nkt] + [b_sm], writes=[b_maskf])
                nch = 17 + i
                for c0 in range(0, nch, 4):
                    n4 = min(4, nch - c0)
                    bk = next_bank([3, 4])
                    for cc in range(n4):
                        c = c0 + cc
                        kb.op("pe", lambda e, c=c, cc=cc: e.transpose(
                            ps[:, bk, cc * 128:(cc + 1) * 128], maskf[:, c * 128:(c + 1) * 128], ident),
                            reads=[b_maskf, b_cst], writes=[PB[bk]], sig=(cc == n4 - 1))
                    kb.op("act", lambda e, c0=c0, n4=n4: e.activation(
                        out=mT[:, c0:c0 + n4, j * 128:(j + 1) * 128],
                        in_=ps[:, bk, 0:n4 * 128].rearrange("p (c q) -> p c q", c=n4), func=AF.Copy),
                        reads=[PB[bk]], writes=[b_mT])
                if j == 3:
                    kb.dma("sp", maskT_d[g], mT[:], reads=[b_mT], writes=[b_maskd[g]])

            prev = None
            for i in range(16):
                units = make_units(i)
                steps = make_steps(prev) if prev is not None else []
                nU = len(units)
                spacing = max(4, nU // (NIT + 1))
                ka = 0
                kd_ = 0
                for u, unit in enumerate(units):
                    unit()
                    if ka < len(steps) and u == ka * spacing + 1:
                        steps[ka][0]()
                        ka += 1
                    if kd_ < len(steps) and kd_ < ka and u == kd_ * spacing + 1 + spacing // 2:
                        steps[kd_][1]()
                        kd_ += 1
                while kd_ < len(steps):
                    if ka == kd_:
                        steps[ka][0]()
                        ka += 1
                    steps[kd_][1]()
                    kd_ += 1
                post_score(i)
                if prev is not None:
                    finalize(prev)
                prev = i
            for st_ in make_steps(prev):
                st_[0]()
                st_[1]()
            finalize(prev)
        phA.close()
        pers = ExitStack()
        es.enter_context(pers)
        psb = lambda name, shape, dt=F32: pers.enter_context(nc.sbuf_tensor(name, list(shape), dt))
        poolT = psb("poolT", [128, 8, TOK], BF16)
        b_poolT = bufs(4, "poolT")
        attnT = psb("attnT", [128, 8, TOK], BF16)
        b_attnT = [bufs(4, "attnT%d" % h) for h in range(8)]
        scCD = ExitStack()
        es.enter_context(scCD)
        xg_t[0] = scCD.enter_context(nc.sbuf_tensor("xgC0", [128, 16, 512], BF16))
        xg_t[1] = scCD.enter_context(nc.sbuf_tensor("xgC1", [128, 16, 512], BF16))
        b_xg[0], b_xg[1] = Buf("xgc0"), Buf("xgc1")

        kb.barrier()
        with ExitStack() as sc:
            sbC = lambda name, shape, dt=F32: sc.enter_context(nc.sbuf_tensor(name, list(shape), dt))
            wpl = sbC("wpl", [128, 8, 16, 128], BF16)
            b_wpl = Buf("wpl")
            for c in range(8):
                kb.dma("pool", wpl[:, c], w_fm[c], writes=[b_wpl])
            pw = sbC("pw", [128, 4, 2, 256], BF16)
            b_pw = Buf("pw")
            kb.dma("pool", pw[:], pool_w, writes=[b_pw])
            xh = sbC("xh", [128, 16, 16], BF16)
            b_xh = Buf("xh")
            for q4 in range(4):
                kb.dma("pool", xh[:, 4 * q4:4 * q4 + 4, :], xT[0, :, 4 * q4:4 * q4 + 4, TOK - 16:TOK], writes=[b_xh])
            hal = sbC("hal", [128, 8, 16])
            b_hal = bufs(8, "hal")
            vb = [sbC("vb%d" % i, [128, 528]) for i in range(2)]
            b_vb = bufs(2, "vb")
            sa = sbC("sa", [128, 528])
            sbb = sbC("sbb", [128, 528])
            b_sa, b_sb = Buf("sa"), Buf("sb")
            t16 = sbC("t16", [128, 16])
            b_t16 = Buf("t16")
            plb = [sbC("plb%d" % i, [128, 512], BF16) for i in range(2)]
            b_plb = bufs(2, "plb")
            for cp in range(8):
                bk = next_bank([0, 1, 2, 7])
                for kc in range(16):
                    kb.op("pe", lambda e, kc=kc, cp=cp, bk=bk: e.matmul(
                        ps[:, bk, 0:16], lhsT=wpl[:, cp, kc, :], rhs=xh[:, kc, :], start=(kc == 0), stop=(kc == 15)),
                        reads=[b_wpl, b_xh], writes=[PB[bk]], sig=(kc == 15))
                evac(hal[:, cp, :], ps[:, bk, 0:16], [PB[bk]], [b_hal[cp]])
            vrr = 0
            for tg in range(4):
                xg, bx = load_xg(1, tg)
                for gq in range(4):
                    wwin = (2, 4, 8, 16)[gq]
                    for cc in range(2):
                        cp = 2 * gq + cc
                        vi = vrr % 2
                        vrr += 1
                        V = vb[vi]
                        bV = b_vb[vi]
                        kb.op("dve", lambda e, V=V, cp=cp: e.tensor_copy(out=V[:, 0:16], in_=hal[:, cp, :]),
                              reads=[b_hal[cp]], writes=[bV])
                        proj_fm(lambda kc, cp=cp: wpl[:, cp, kc, :], b_wpl, xg, bx, V[:, 16:528], [bV])
                        kb.op("act", lambda e, V=V, cp=cp: e.activation(out=hal[:, cp, :], in_=V[:, 512:528], func=AF.Copy),
                              reads=[bV], writes=[b_hal[cp]])
                        kb.op("dve", lambda e, V=V: e.tensor_tensor(out=sa[:, 1:528], in0=V[:, 1:528], in1=V[:, 0:527],
                                                                     op=ALU.add), reads=[bV], writes=[b_sa])
                        Sfin, bS = sa, b_sa
                        if gq >= 1:
                            kb.op("dve", lambda e: e.tensor_tensor(out=sbb[:, 3:528], in0=sa[:, 3:528], in1=sa[:, 1:526],
                                                                   op=ALU.add), reads=[b_sa], writes=[b_sb])
                            Sfin, bS = sbb, b_sb
                        if gq >= 2:
                            kb.op("dve", lambda e: e.tensor_tensor(out=sa[:, 7:528], in0=sbb[:, 7:528], in1=sbb[:, 3:524],
                                                                   op=ALU.add), reads=[b_sb], writes=[b_sa])
                            Sfin, bS = sa, b_sa
                        if gq >= 3:
                            kb.op("dve", lambda e: e.tensor_tensor(out=sbb[:, 15:528], in0=sa[:, 15:528], in1=sa[:, 7:520],
                                                                   op=ALU.add), reads=[b_sa], writes=[b_sb])
                            Sfin, bS = sbb, b_sb
                        kb.op("dve", lambda e, Sfin=Sfin, V=V, cc=cc, wwin=wwin: e.scalar_tensor_tensor(
                            out=plb[cc][:, :], in0=Sfin[:, 16:528], scalar=1.0 / wwin, in1=V[:, 16:528],
                            op0=ALU.mult, op1=ALU.subtract), reads=[bS, bV], writes=[b_plb[cc]])
                        if tg == 0:
                            kb.op("dve", lambda e, Sfin=Sfin, gq=gq: e.tensor_tensor(
                                out=t16[:, :], in0=Sfin[:, 16:32], in1=cst[:, C_CORR + 16 * gq:C_CORR + 16 * gq + 16],
                                op=ALU.mult), reads=[bS, b_cst], writes=[b_t16])
                            kb.op("dve", lambda e, V=V, cc=cc, wwin=wwin: e.scalar_tensor_tensor(
                                out=plb[cc][:, 0:16], in0=t16[:, :], scalar=1.0 / wwin, in1=V[:, 16:32],
                                op0=ALU.mult, op1=ALU.subtract), reads=[b_t16, bV], writes=[b_plb[cc]])
                    for dc in range(2):
                        bk = next_bank([0, 1, 2, 7])
                        for cc in range(2):
                            kb.op("pe", lambda e, cc=cc, dc=dc, gq=gq, bk=bk: e.matmul(
                                ps[:, bk, :], lhsT=pw[:, gq, cc, dc * 128:(dc + 1) * 128], rhs=plb[cc][:, :],
                                start=(cc == 0), stop=(cc == 1)),
                                reads=[b_pw, b_plb[cc]], writes=[PB[bk]], sig=(cc == 1))
                        oc = 2 * gq + dc
                        kb.op("act", lambda e, oc=oc, bk=bk, tg=tg: e.activation(
                            out=poolT[:, oc, tg * 512:(tg + 1) * 512], in_=ps[:, bk, :], func=AF.Copy,
                            scale=cst[:, C_PSC + oc:C_PSC + oc + 1]),
                            reads=[PB[bk], b_cst], writes=[b_poolT[tg]])
        if stop_after == "C":
            dump([(poolT[:, 0, 0:1024], b_poolT, 1024), (poolT[:, 7, 0:1024], b_poolT, 1024),
                  (poolT[:, 3, 1024:2048], b_poolT, 1024)])
            return nc

        kb.barrier()
        with ExitStack() as sc:
            sbD = lambda name, shape, dt=F32: sc.enter_context(nc.sbuf_tensor(name, list(shape), dt))
            kT2 = sbD("kT2", [128, 2, 2 * TOK], BF16)
            b_kT2 = bufs(8, "kT2")
            v2 = sbD("v2", [128, 32, 256], BF16)
            b_v2 = bufs(8, "v2")
            qT2 = sbD("qT2", [128, 2, TOK], BF16)
            b_qT2 = bufs(4, "qT2")
            SB_S = [0, 1, 2]
            for hg in range(4):
                kb.barrier()
                scP = ExitStack()
                sbP = lambda name, shape, dt=F32: scP.enter_context(nc.sbuf_tensor(name + "_g%d" % hg, list(shape), dt))
                wq2 = sbP("wq2", [128, 2, 16, 128], BF16)
                wk2 = sbP("wk2", [128, 2, 16, 128], BF16)
                wv2 = sbP("wv2", [128, 16, 256], BF16)
                b_wq2, b_wk2, b_wv2 = Buf("wq2"), Buf("wk2"), Buf("wv2")
                for hl in range(2):
                    kb.dma("pool", wq2[:, hl], w_fm[8 + 2 * hg + hl], writes=[b_wq2])
                    kb.dma("pool", wk2[:, hl], w_fm[16 + 2 * hg + hl], writes=[b_wk2])
                kb.dma("pool", wv2[:], w_v[hg], writes=[b_wv2])
                for s in range(2):
                    for tg in range(4):
                        xg, bx = load_xg(s, tg)
                        kg = s * 4 + tg
                        for hl in range(2):
                            proj_fm(lambda kc, hl=hl: wk2[:, hl, kc, :], b_wk2, xg, bx,
                                    kT2[:, hl, kg * 512:(kg + 1) * 512], [b_kT2[kg]])
                            if s == 1:
                                proj_fm(lambda kc, hl=hl: wq2[:, hl, kc, :], b_wq2, xg, bx,
                                        qT2[:, hl, tg * 512:(tg + 1) * 512], [b_qT2[tg]])
                        for tt in range(4):
                            bk = 7
                            for kc in range(16):
                                kb.op("pe", lambda e, kc=kc, tt=tt, xg=xg: e.matmul(
                                    ps[:, bk, 0:256], lhsT=xg[:, kc, tt * 128:(tt + 1) * 128], rhs=wv2[:, kc, :],
                                    start=(kc == 0), stop=(kc == 15)),
                                    reads=[b_wv2, bx], writes=[PB[bk]], sig=(kc == 15))
                            evac(v2[:, kg * 4 + tt, :], ps[:, bk, 0:256], [PB[bk]], [b_v2[kg]])
                if stop_after == "D0":
                    dump([(kT2[:, 0, 0:1024], b_kT2[0:2], 1024), (qT2[:, 1, 0:1024], b_qT2[0:2], 1024),
                          (v2[:, 0:4, :].rearrange("p a b -> p (a b)"), b_v2[0:1], 1024)])
                    return nc
                kb.barrier()
                scP.close()
                scT = ExitStack()
                sbT = lambda name, shape, dt=F32: scT.enter_context(nc.sbuf_tensor(name + "_g%d" % hg, list(shape), dt))
                mk = sbT("mk", [128, 32, 512], BF16)
                b_mk = Buf("mk")
                Et = [sbT("E%d" % i_, [128, 512], BF16) for i_ in range(3)]
                b_E = bufs(3, "E")
                Pt = [sbT("P%d" % i_, [128, 512], BF16) for i_ in range(3)]
                b_P = bufs(3, "P")
                tmpn = [sbT("tmpn%d" % i_, [128, 128]) for i_ in range(2)]
                b_tmpn = bufs(2, "tmpn")
                rden = sbT("rden", [128, 512])
                b_rden = Buf("rden")
                for g in range(4):
                    kb.dma("sp", mk[:], maskT_d[g], reads=[b_maskd[g]], writes=[b_mk])
                    nch = 20 + 4 * g
                    for hl in range(2):
                        h = 2 * hg + hl
                        bO = 3 + 2 * (hl % 2)
                        bD = bO + 1

                        def issue_S(c, hl=hl, g=g):
                            bk = SB_S[c % 3]
                            kb.op("pe", lambda e: e.matmul(
                                ps[:, bk, :], lhsT=kT2[:, hl, c * 128:(c + 1) * 128],
                                rhs=qT2[:, hl, g * 512:(g + 1) * 512], start=True, stop=True),
                                reads=[b_kT2[c // 4], b_qT2[g]], writes=[PB[bk]])
                        issue_S(0)
                        if nch > 1:
                            issue_S(1)
                        for c in range(nch):
                            if c + 2 < nch:
                                issue_S(c + 2)
                            bk = SB_S[c % 3]
                            ei = c % 3
                            tokE = kb.op("act", lambda e, bk=bk, ei=ei, h=h: e.activation(
                                out=Et[ei][:, :], in_=ps[:, bk, :], func=AF.Exp, scale=ATT_SCALE,
                                bias=cst[:, C_B31 + h:C_B31 + h + 1]),
                                reads=[PB[bk], b_cst], writes=[b_E[ei]])
                            cn = c - (15 + 4 * g)
                            if 0 <= cn <= 4:
                                for j in range(4):
                                    dl = 1 + j - cn
                                    if dl in (0, 1):
                                        ti = (j + cn) % 2
                                        kb.op("dve", lambda e, bk=bk, j=j, dl=dl, h=h, ti=ti: e.scalar_tensor_tensor(
                                            out=tmpn[ti][:, :], in0=ps[:, bk, j * 128:(j + 1) * 128], scalar=ATT_SCALE,
                                            in1=bT[:, dl, h, :], op0=ALU.mult, op1=ALU.add),
                                            reads=[PB[bk], b_bT], writes=[b_tmpn[ti]], deps=[tokE])
                                        kb.op("act", lambda e, ei=ei, j=j, ti=ti: e.activation(
                                            out=Et[ei][:, j * 128:(j + 1) * 128], in_=tmpn[ti][:, :], func=AF.Exp),
                                            reads=[b_tmpn[ti]], writes=[b_E[ei]])
                            kb.op("dve", lambda e, ei=ei, c=c: e.tensor_tensor(
                                out=Pt[ei][:, :], in0=Et[ei][:, :], in1=mk[:, c, :], op=ALU.mult),
                                reads=[b_E[ei], b_mk], writes=[b_P[ei]])
                            kb.op("pe", lambda e, ei=ei, c=c, hl=hl, bO=bO, nch=nch: e.matmul(
                                ps[:, bO, :], lhsT=v2[:, c, hl * 128:(hl + 1) * 128], rhs=Pt[ei][:, :],
                                start=(c == 0), stop=(c == nch - 1)),
                                reads=[b_v2[c // 4], b_P[ei]], writes=[PB[bO]], sig=(c == nch - 1))
                            kb.op("pe", lambda e, ei=ei, c=c, bD=bD, nch=nch: e.matmul(
                                ps[:, bD, :], lhsT=ones_b[:, :], rhs=Pt[ei][:, :],
                                start=(c == 0), stop=(c == nch - 1)),
                                reads=[b_ones, b_P[ei]], writes=[PB[bD]], sig=(c == nch - 1))
                        kb.op("dve", lambda e, bD=bD: e.reciprocal(out=rden[:, :], in_=ps[:, bD, :]),
                              reads=[PB[bD]], writes=[b_rden])
                        kb.op("dve", lambda e, bO=bO, h=h, g=g: e.tensor_tensor(
                            out=attnT[:, h, g * 512:(g + 1) * 512], in0=ps[:, bO, :], in1=rden[:, :], op=ALU.mult),
                            reads=[PB[bO], b_rden], writes=[b_attnT[h][g]])
                        if stop_after == "D1":
                            dump([(attnT[:, 0, 0:512], [b_attnT[0][0]], 512), (rden[:, :], [b_rden], 512)])
                            return nc
                scT.close()
        if stop_after == "D":
            dump([(attnT[:, 0, 0:1024], b_attnT[0], 1024), (attnT[:, 7, 0:1024], b_attnT[7], 1024),
                  (attnT[:, 3, 1024:2048], b_attnT[3], 1024)])
            return nc

        scCD.close()
        kb.barrier()
        b_hT_d = bufs(4, "hT_d")
        b_htok = bufs(16, "htok")

        def layer_norm(e_sb, src, bsrc, gam, bet, b_gb, outt, bout, xc, b_xc, junkf, b_jf, st, b_st):
            kb.op("dve", lambda e: e.tensor_scalar(out=junkf[:, :], in0=src, scalar1=1.0, scalar2=None, op0=ALU.mult,
                                                   op1=ALU.add, accum_out=st[:, 0:1]), reads=[bsrc], writes=[b_jf, b_st])
            kb.op("dve", lambda e: e.tensor_scalar(out=st[:, 1:2], in0=st[:, 0:1], scalar1=1.0 / D, scalar2=None,
                                                   op0=ALU.mult), writes=[b_st])
            kb.op("dve", lambda e: e.tensor_scalar(out=xc[:, :], in0=src, scalar1=st[:, 1:2], scalar2=None,
                                                   op0=ALU.subtract), reads=[bsrc, b_st], writes=[b_xc])
            kb.op("dve", lambda e: e.tensor_tensor(out=junkf[:, :], in0=xc[:, :], in1=xc[:, :], op=ALU.mult),
                  reads=[b_xc], writes=[b_jf])
            kb.op("dve", lambda e: e.tensor_scalar(out=junkf[:, :], in0=junkf[:, :], scalar1=1.0, scalar2=None,
                                                   op0=ALU.mult, op1=ALU.add, accum_out=st[:, 2:3]),
                  writes=[b_jf, b_st])
            kb.op("dve", lambda e: e.tensor_scalar(out=st[:, 3:4], in0=st[:, 2:3], scalar1=1.0 / D, scalar2=LN_EPS,
                                                   op0=ALU.mult, op1=ALU.add), writes=[b_st])
            kb.op("act", lambda e: e.activation(out=st[:, 4:5], in_=st[:, 3:4], func=AF.Sqrt), writes=[b_st])
            kb.op("dve", lambda e: e.reciprocal(out=st[:, 5:6], in_=st[:, 4:5]), writes=[b_st])
            kb.op("dve", lambda e: e.scalar_tensor_tensor(out=xc[:, :], in0=xc[:, :], scalar=st[:, 5:6], in1=gam,
                                                          op0=ALU.mult, op1=ALU.mult),
                  reads=[b_st, b_gb], writes=[b_xc])
            return kb.op("dve", lambda e: e.tensor_tensor(out=outt, in0=xc[:, :], in1=bet, op=ALU.add),
                         reads=[b_xc, b_gb], writes=[bout])

        with ExitStack() as sc:
            sbE = lambda name, shape, dt=F32: sc.enter_context(nc.sbuf_tensor(name, list(shape), dt))
            wo2 = [sbE("wo%d" % i, [128, 16, 512], BF16) for i in range(2)]
            b_wo2 = bufs(2, "wo")
            wo_rr = [0]
            gb = sbE("gb1", [128, 2, D])
            b_gb = Buf("gb1")
            kb.dma("sp", gb[:, 0, :], lnp[0], writes=[b_gb])
            kb.dma("sp", gb[:, 1, :], lnp[1], writes=[b_gb])
            xt = [sbE("xt0", [128, D])] * 2
            b_xt = [Buf("xt")] * 2
            hpre = sbE("hpre", [128, D])
            b_hpre = Buf("hpre")
            xc = sbE("xc", [128, D])
            b_xc = Buf("xc")
            junkf = sbE("junkf", [128, D])
            b_jf = Buf("junkf")
            hh = [sbE("hh0", [128, D])] * 2
            b_hh = [Buf("hh")] * 2
            st = sbE("st", [128, 8])
            b_st = Buf("st")
            hTs = sbE("hTs", [128, 16, 128], BF16)
            b_hTs = Buf("hTs")
            kb.dma("sp", xt[0][:, :], x_tok[0:128, :], writes=[b_xt[0]])
            for tt in range(16):
                tg = tt // 4
                xi = tt % 2
                for dt_ in range(4):
                    bk = next_bank([0, 1, 2, 7])
                    wi_ = wo_rr[0] % 2
                    wo_rr[0] += 1
                    wo, b_wo = wo2[wi_], b_wo2[wi_]
                    for q4 in range(4):
                        kb.dma("pool", wo[:, 4 * q4:4 * q4 + 4, :], w_out[:, 4 * q4:4 * q4 + 4, dt_ * 512:(dt_ + 1) * 512],
                               writes=[b_wo])
                    for kc in range(16):
                        if kc < 8:
                            lh = poolT[:, kc, tt * 128:(tt + 1) * 128]
                            rb = b_poolT[tg]
                        else:
                            lh = attnT[:, kc - 8, tt * 128:(tt + 1) * 128]
                            rb = b_attnT[kc - 8][tg]
                        kb.op("pe", lambda e, lh=lh, kc=kc, dt_=dt_, bk=bk, wo=wo: e.matmul(
                            ps[:, bk, :], lhsT=lh, rhs=wo[:, kc, :],
                            start=(kc == 0), stop=(kc == 15)),
                            reads=[rb, b_wo], writes=[PB[bk]], sig=(kc == 15))
                    kb.op("dve", lambda e, xi=xi, dt_=dt_, bk=bk: e.scalar_tensor_tensor(
                        out=hpre[:, dt_ * 512:(dt_ + 1) * 512], in0=xt[xi][:, dt_ * 512:(dt_ + 1) * 512], scalar=ALPHA,
                        in1=ps[:, bk, :], op0=ALU.mult, op1=ALU.add),
                        reads=[b_xt[xi], PB[bk]], writes=[b_hpre])
                hi = tt % 2
                if tt + 1 < 16:
                    kb.dma("sp", xt[(tt + 1) % 2][:, :], x_tok[(tt + 1) * 128:(tt + 2) * 128, :],
                           writes=[b_xt[(tt + 1) % 2]])
                layer_norm(None, hpre[:, :], b_hpre, gb[:, 0, :], gb[:, 1, :], b_gb, hh[hi][:, :], b_hh[hi],
                           xc, b_xc, junkf, b_jf, st, b_st)
                kb.dma("sp", h_tok[tt * 128:(tt + 1) * 128, :], hh[hi][:, :], reads=[b_hh[hi]], writes=[b_htok[tt]])
                for c0 in range(0, 16, 4):
                    bk = next_bank([3, 4, 5, 6])
                    for cc in range(4):
                        kb.op("pe", lambda e, c0=c0, cc=cc, bk=bk, hi=hi: e.transpose(
                            ps[:, bk, cc * 128:(cc + 1) * 128], hh[hi][:, (c0 + cc) * 128:(c0 + cc + 1) * 128], ident),
                            reads=[b_hh[hi], b_cst], writes=[PB[bk]], sig=(cc == 3))
                    evac(hTs[:, c0:c0 + 4, :],
                         ps[:, bk, :].rearrange("p (c q) -> p c q", c=4), [PB[bk]], [b_hTs])
                for q4 in range(4):
                    kb.dma("sp", hT_d[:, 4 * q4:4 * q4 + 4, tt * 128:(tt + 1) * 128], hTs[:, 4 * q4:4 * q4 + 4, :],
                           reads=[b_hTs], writes=[b_hT_d[tg]])
        pers.close()
        if stop_after == "E":
            t1 = kb.dma("sp", dbg_out[:, 0:2048], h_tok[0:128, :], reads=[b_htok[0]])
            t2 = kb.dma("sp", dbg_out[:, 2048:4096], h_tok[1920:2048, :], reads=[b_htok[15]])
            kb.wait("sp", t1)
            kb.wait("sp", t2)
            return nc

        kb.barrier()
        b_W1 = bufs(32, "W1")
        with ExitStack() as sc:
            sbF = lambda name, shape, dt=F32: sc.enter_context(nc.sbuf_tensor(name, list(shape), dt))
            wqc = [sbF("wqc%d" % i, [128, 16, 128], BF16) for i in range(3)]
            b_wqc = bufs(3, "wqc")
            wq_rr = [0]
            skT = sbF("skT", [128, 2, 128], BF16)
            b_skT = Buf("skT")
            kb.dma("pool", skT[:], subkT, writes=[b_skT])
            hTg = [sbF("hTg0", [128, 16, 512], BF16)] * 2
            b_hTg = [Buf("hTg")] * 2
            qpT = sbF("qpT", [128, 16, 512], BF16)
            b_qpT = Buf("qpT")
            s_sb = sbF("s_sb", [128, 16, 128])
            s2_sb = sbF("s2_sb", [128, 16, 128])
            b_s, b_s2 = Buf("s"), Buf("s2")
            v16 = sbF("v16", [128, 16, 16])
            i16 = sbF("i16", [128, 16, 16], U32)
            i16f = sbF("i16f", [128, 16, 16])
            b_v16, b_i16, b_i16f = Buf("v16"), Buf("i16"), Buf("i16f")
            cand = sbF("cand", [128, 8, 256])
            cand2 = sbF("cand2", [128, 8, 256])
            b_cand, b_cand2 = Buf("cand"), Buf("cand2")
            best = sbF("best", [128, 8, 16])
            bidx = sbF("bidx", [128, 8, 16], U32)
            aiu = sbF("aiu", [128, 128], U32)
            biu = sbF("biu", [128, 128], U32)
            af = sbF("af", [128, 128])
            bf = sbF("bf", [128, 128])
            b_best, b_bidx, b_bidf, b_af, b_bf = Buf("best"), Buf("bidx"), Buf("bidf"), Buf("af"), Buf("bf")
            eb = sbF("eb", [128, 8, 16])
            zz = sbF("zz", [128, 16])
            b_eb, b_zz = Buf("eb"), Buf("zz")
            oh = sbF("oh", [128, 128, 16])
            b_oh = Buf("oh")
            IG = sbF("IG", [128, 3, 128])
            b_IG = Buf("IG")
            IGT = sbF("IGT", [128, 3, 128])
            b_IGT = Buf("IGT")
            eq2 = [sbF("eq0", [128, 32, 128], BF16)] * 2
            b_eq2 = [Buf("eq")] * 2
            At2 = [sbF("At0", [128, 32, 128], BF16)] * 2
            Bt2 = [sbF("Bt0", [128, 32, 128], BF16)] * 2
            b_At2, b_Bt2 = [Buf("At")] * 2, [Buf("Bt")] * 2
            ab_rr = [0]
            stg = [sbF("stg0", [128, 128, 64], BF16)] * 2
            b_stg = [Buf("stg")] * 2
            for tg in range(4):
                hx = hTg[tg % 2]
                bhx = b_hTg[tg % 2]
                for q4 in range(4):
                    kb.dma("sp", hx[:, 4 * q4:4 * q4 + 4, :], hT_d[:, 4 * q4:4 * q4 + 4, tg * 512:(tg + 1) * 512],
                           reads=[b_hT_d[tg]], writes=[bhx])
                for n in range(16):
                    wi_ = wq_rr[0] % 3
                    wq_rr[0] += 1
                    for q4 in range(4):
                        kb.dma("pool", wqc[wi_][:, 4 * q4:4 * q4 + 4, :], wq[:, 4 * q4:4 * q4 + 4, n * 128:(n + 1) * 128],
                               writes=[b_wqc[wi_]])
                    proj_fm(lambda kc, wi_=wi_: wqc[wi_][:, kc, :], b_wqc[wi_], hx, bhx, qpT[:, n, :], [b_qpT])
                for tt in range(4):
                    T = 4 * tg + tt
                    for n4 in range(4):
                        bk = next_bank([3, 4, 5, 6])
                        for nn in range(4):
                            n = 4 * n4 + nn
                            kb.op("pe", lambda e, n=n, nn=nn, bk=bk, tt=tt: e.matmul(
                                ps[:, bk, nn * 128:(nn + 1) * 128], lhsT=qpT[:, n, tt * 128:(tt + 1) * 128],
                                rhs=skT[:, n % 2, :], start=True, stop=True),
                                reads=[b_qpT, b_skT], writes=[PB[bk]], sig=(nn == 3))
                        evac(s_sb[:, 4 * n4:4 * n4 + 4, :], ps[:, bk, :].rearrange("p (c q) -> p c q", c=4),
                             [PB[bk]], [b_s])
                    bv, bi, bs2 = bufs(16, "v16n"), bufs(16, "i16n"), bufs(16, "s2n")
                    for n in range(16):
                        kb.op("dve", lambda e, n=n: e.max(out=v16[:, n, 0:8], in_=s_sb[:, n, :]),
                              reads=[b_s], writes=[bv[n]], deps=[b_v16.w] + list(b_v16.r.values()))
                    for n in range(16):
                        kb.op("dve", lambda e, n=n: e.max_index(out=i16[:, n, 0:8], in_max=v16[:, n, 0:8],
                                                               in_values=s_sb[:, n, :]),
                              reads=[b_s, bv[n]], writes=[bi[n]], deps=[b_i16.w] + list(b_i16.r.values()))
                    for n in range(16):
                        kb.op("dve", lambda e, n=n: e.match_replace(out=s2_sb[:, n, :], in_to_replace=v16[:, n, 0:8],
                                                                   in_values=s_sb[:, n, :], imm_value=NEG),
                              reads=[b_s, bv[n]], writes=[bs2[n]])
                    for n in range(16):
                        kb.op("dve", lambda e, n=n: e.max(out=v16[:, n, 8:16], in_=s2_sb[:, n, :]),
                              reads=[bs2[n]], writes=[bv[n]])
                    for n in range(16):
                        kb.op("dve", lambda e, n=n: e.max_index(out=i16[:, n, 8:16], in_max=v16[:, n, 8:16],
                                                               in_values=s2_sb[:, n, :]),
                              reads=[bs2[n], bv[n]], writes=[bi[n]])
                    kb.op("dve", lambda e: e.tensor_copy(out=i16f[:], in_=i16[:]), reads=bi, writes=[b_i16f, b_i16])
                    kb.op("dve", lambda e: e.tensor_tensor(
                        out=cand[:].rearrange("p h (a b) -> p h a b", a=16),
                        in0=sap(v16, [[32, 8], [1, 16], [0, 16]]),
                        in1=sap(v16, [[32, 8], [0, 16], [1, 16]], off=16), op=ALU.add),
                        reads=bv, writes=[b_cand, b_v16])
                    bb, bx_, bc2 = bufs(8, "besth"), bufs(8, "bidxh"), bufs(8, "cand2h")
                    for h in range(8):
                        kb.op("dve", lambda e, h=h: e.max(out=best[:, h, 0:8], in_=cand[:, h, :]),
                              reads=[b_cand], writes=[bb[h]], deps=[b_best.w] + list(b_best.r.values()))
                    for h in range(8):
                        kb.op("dve", lambda e, h=h: e.max_index(out=bidx[:, h, 0:8], in_max=best[:, h, 0:8],
                                                               in_values=cand[:, h, :]),
                              reads=[b_cand, bb[h]], writes=[bx_[h]], deps=[b_bidx.w] + list(b_bidx.r.values()))
                    for h in range(8):
                        kb.op("dve", lambda e, h=h: e.match_replace(out=cand2[:, h, :], in_to_replace=best[:, h, 0:8],
                                                                   in_values=cand[:, h, :], imm_value=NEG),
                              reads=[b_cand, bb[h]], writes=[bc2[h]])
                    for h in range(8):
                        kb.op("dve", lambda e, h=h: e.max(out=best[:, h, 8:16], in_=cand2[:, h, :]),
                              reads=[bc2[h]], writes=[bb[h]])
                    for h in range(8):
                        kb.op("dve", lambda e, h=h: e.max_index(out=bidx[:, h, 8:16], in_max=best[:, h, 8:16],
                                                               in_values=cand2[:, h, :]),
                              reads=[bc2[h], bb[h]], writes=[bx_[h]])
                    kb.op("dve", lambda e: e.tensor_copy(out=zz[:, 0:1], in_=best[:, 0, 0:1]),
                          reads=bb + bx_, writes=[b_best, b_bidx, b_zz])
                    kb.op("dve", lambda e: e.tensor_tensor(out=eb[:], in0=best[:], in1=sap(best, [[16, 8], [0, 16]]),
                                                           op=ALU.subtract), reads=[b_best], writes=[b_eb])
                    kb.op("act", lambda e: e.activation(out=eb[:], in_=eb[:], func=AF.Exp), writes=[b_eb])
                    kb.op("dve", lambda e: e.tensor_reduce(out=zz[:, 0:8], in_=eb[:], axis=AX.X, op=ALU.add),
                          reads=[b_eb], writes=[b_zz])
                    kb.op("dve", lambda e: e.reciprocal(out=zz[:, 8:16], in_=zz[:, 0:8]), writes=[b_zz])
                    kb.op("dve", lambda e: e.tensor_tensor(
                        out=IG[:, 2, :].rearrange("p (h r) -> p h r", h=8), in0=eb[:],
                        in1=sap(zz, [[1, 8], [0, 16]], off=8), op=ALU.mult),
                        reads=[b_eb, b_zz], writes=[b_IG])
                    kb.op("dve", lambda e: e.tensor_single_scalar(out=aiu[:], in_=bidx[:].rearrange("p h r -> p (h r)"),
                                                                  scalar=4, op=ALU.logical_shift_right),
                          reads=[b_bidx], writes=[b_bidf])
                    kb.op("dve", lambda e: e.tensor_single_scalar(out=biu[:], in_=bidx[:].rearrange("p h r -> p (h r)"),
                                                                  scalar=15, op=ALU.bitwise_and),
                          reads=[b_bidx], writes=[b_bidf])
                    kb.op("dve", lambda e: e.tensor_copy(out=af[:], in_=aiu[:]), reads=[b_bidf], writes=[b_af])
                    kb.op("dve", lambda e: e.tensor_copy(out=bf[:], in_=biu[:]), reads=[b_bidf], writes=[b_bf])
                    for which, sel, boff in ((0, af, 0), (1, bf, 16)):
                        bsel = b_af if which == 0 else b_bf
                        kb.op("dve", lambda e, sel=sel: e.tensor_tensor(
                            out=oh[:], in0=sap(cst, [[0, 128], [1, 16]], off=C_IOTA),
                            in1=sap(sel, [[1, 128], [0, 16]]), op=ALU.is_equal),
                            reads=[bsel, b_cst], writes=[b_oh])
                        kb.op("dve", lambda e, boff=boff: e.tensor_tensor(
                            out=oh[:].rearrange("p (h r) a -> p h r a", h=8),
                            in0=oh[:].rearrange("p (h r) a -> p h r a", h=8),
                            in1=sap(i16f, [[32, 8], [0, 16], [1, 16]], off=boff), op=ALU.mult),
                            reads=[b_i16f], writes=[b_oh])
                        kb.op("dve", lambda e, which=which: e.tensor_reduce(out=IG[:, which, :], in_=oh[:], axis=AX.X,
                                                                          op=ALU.add),
                              reads=[b_oh], writes=[b_IG])
                    bk = next_bank([3, 4, 5, 6])
                    for w3 in range(3):
                        kb.op("pe", lambda e, w3=w3, bk=bk: e.transpose(ps[:, bk, w3 * 128:(w3 + 1) * 128],
                                                                        IG[:, w3, :], ident),
                              reads=[b_IG, b_cst], writes=[PB[bk]], sig=(w3 == 2))
                    kb.op("act", lambda e, bk=bk: e.activation(out=IGT[:], in_=ps[:, bk, 0:384].rearrange(
                        "p (c q) -> p c q", c=3), func=AF.Copy), reads=[PB[bk]], writes=[b_IGT])
                    for sbk in range(2):
                        si = (2 * T + sbk) % 2
                        for s32 in range(2):
                            t0 = sbk * 64 + s32 * 32
                            ai_ = ab_rr[0] % 2
                            ab_rr[0] += 1
                            eq, b_eq = eq2[ai_], b_eq2[ai_]
                            At, b_At = At2[ai_], b_At2[ai_]
                            Bt, b_Bt = Bt2[ai_], b_Bt2[ai_]
                            kb.op("dve", lambda e, t0=t0: e.tensor_tensor(
                                out=eq[:], in0=sap(cst, [[0, 32], [1, 128]], off=C_IOTA),
                                in1=sap(IGT, [[1, 32], [0, 128]], off=0 * 128 + t0), op=ALU.is_equal),
                                reads=[b_IGT, b_cst], writes=[b_eq])
                            kb.op("dve", lambda e, t0=t0: e.tensor_tensor(
                                out=At[:], in0=eq[:], in1=sap(IGT, [[1, 32], [0, 128]], off=2 * 128 + t0), op=ALU.mult),
                                reads=[b_eq, b_IGT], writes=[b_At])
                            kb.op("dve", lambda e, t0=t0: e.tensor_tensor(
                                out=Bt[:], in0=sap(cst, [[0, 32], [1, 128]], off=C_IOTA),
                                in1=sap(IGT, [[1, 32], [0, 128]], off=1 * 128 + t0), op=ALU.is_equal),
                                reads=[b_IGT, b_cst], writes=[b_Bt])
                            for t4 in range(8):
                                bk = next_bank([0, 1, 2, 7])
                                for q4 in range(4):
                                    tl = 4 * t4 + q4
                                    kb.op("pe", lambda e, tl=tl, q4=q4, bk=bk: e.matmul(
                                        ps[:, bk, q4 * 128:(q4 + 1) * 128], lhsT=At[:, tl, :], rhs=Bt[:, tl, :],
                                        start=True, stop=True),
                                        reads=[b_At, b_Bt], writes=[PB[bk]], sig=(q4 == 3))
                                kb.op("act", lambda e, t4=t4, bk=bk, si=si, s32=s32: e.activation(
                                    out=sap(stg[si], [[1, 4], [64, 128]], off=s32 * 32 + 4 * t4),
                                    in_=ps[:, bk, :].rearrange("p (t j) -> p t j", t=4), func=AF.Copy),
                                    reads=[PB[bk]], writes=[b_stg[si]])
                        kb.dma("sp", W1[2 * T + sbk], stg[si][:], reads=[b_stg[si]], writes=[b_W1[2 * T + sbk]])
                    if stop_after == "F2" and T == 0:
                        wchk = sbF("wchk", [128, 1024], BF16)
                        b_wchk = Buf("wchk")
                        kb.dma("sp", wchk[:], W1[0, :, 0:16, :].rearrange("i j t -> i (j t)"), reads=[b_W1[0]], writes=[b_wchk])
                        dump([(wchk[:, :], [b_wchk], 1024), (IG[:, 0, :], [b_IG], 128), (IG[:, 1, :], [b_IG], 128),
                              (IG[:, 2, :], [b_IG], 128)])
                        return nc
                    if stop_after == "F" and T == 0:
                        dump([(IG[:, 0, :], [b_IG], 128), (IG[:, 1, :], [b_IG], 128), (IG[:, 2, :], [b_IG], 128),
                              (s_sb[:, 0, :], [b_s], 128), (s_sb[:, 1, :], [b_s], 128)])
                        return nc

        kb.barrier()
        if stop_after == "Fend":
            kb.barrier()
            return nc
        NJ = 4
        NR = 0 if stop_after == "G4" else (2 if stop_after in ("G2", "G3") else 128 // NJ)
        for half in range(2):
            kb.barrier()
            with ExitStack() as sc:
                sbG = lambda name, shape, dt=F32: sc.enter_context(nc.sbuf_tensor(name + "_h%d" % half, list(shape), dt))
                hTh = sbG("hTh", [128, 16, 1024], BF16)
                b_hTh = Buf("hTh")
                for tg2 in range(2):
                    tg = 2 * half + tg2
                    for q4 in range(4):
                        kb.dma("sp", hTh[:, 4 * q4:4 * q4 + 4, tg2 * 512:(tg2 + 1) * 512],
                               hT_d[:, 4 * q4:4 * q4 + 4, tg * 512:(tg + 1) * 512],
                               reads=[b_hT_d[tg]], writes=[b_hTh])
                accG = sbG("accG", [128, 8, D])
                b_accG = [bufs(2, "accG%d" % t) for t in range(8)]
                with ExitStack() as sc2:
                    sbH = lambda name, shape, dt=F32: sc2.enter_context(nc.sbuf_tensor(name + "_h%d" % half, list(shape), dt))
                    Wr = [sbH("Wr%d" % i, [128, 16, NJ, 64], BF16) for i in range(2)]
                    b_Wr = bufs(2, "Wr")
                    vr = [sbH("vr%d" % i, [128, NJ, D], BF16) for i in range(2)]
                    b_vr = bufs(2, "vr")
                    uj = [sbH("uj%d" % i, [128, 16, 128], BF16) for i in range(2)]
                    b_uj = bufs(2, "uj")
                    Gr = [sbH("Gr%d" % i, [128, NJ, 1024], BF16) for i in range(2)]
                    b_Gr = bufs(2, "Gr")
                    ga = [sbH("ga%d" % i, [128, 512], BF16) for i in range(2)]
                    b_ga = bufs(2, "ga")
                    urr = [0]
                    grr = [0]

                    def phaseA(r):
                        ri = r % 2
                        j0 = r * NJ
                        for q4 in range(4):
                            b0_ = 16 * half + 4 * q4
                            kb.dma("sp", Wr[ri][:, 4 * q4:4 * q4 + 4], W1[b0_:b0_ + 4, :, j0:j0 + NJ, :].rearrange(
                                "b i j t -> i b j t"), reads=b_W1[b0_:b0_ + 4], writes=[b_Wr[ri]])
                        for jj in range(NJ):
                            ui = urr[0] % 2
                            urr[0] += 1
                            kb.dma("pool", uj[ui][:], uT[j0 + jj], writes=[b_uj[ui]])
                            for tg2 in range(2):
                                bk = next_bank([0, 1])
                                for kc in range(16):
                                    kb.op("pe", lambda e, kc=kc, ui=ui, tg2=tg2, bk=bk: e.matmul(
                                        ps[:, bk, :], lhsT=uj[ui][:, kc, :], rhs=hTh[:, kc, tg2 * 512:(tg2 + 1) * 512],
                                        start=(kc == 0), stop=(kc == 15)),
                                        reads=[b_uj[ui], b_hTh], writes=[PB[bk]], sig=(kc == 15))
                                gi = grr[0] % 2
                                grr[0] += 1
                                kb.op("act", lambda e, gi=gi, bk=bk: e.activation(out=ga[gi][:, :], in_=ps[:, bk, :],
                                                                                 func=AF.Gelu),
                                      reads=[PB[bk]], writes=[b_ga[gi]])
                                kb.op("dve", lambda e, gi=gi, ri=ri, jj=jj, tg2=tg2: e.tensor_tensor(
                                    out=Gr[ri][:, jj, tg2 * 512:(tg2 + 1) * 512].rearrange("p (b t) -> p b t", b=8),
                                    in0=ga[gi][:, :].rearrange("p (b t) -> p b t", b=8),
                                    in1=Wr[ri][:, tg2 * 8:(tg2 + 1) * 8, jj, :], op=ALU.mult),
                                    reads=[b_ga[gi], b_Wr[ri]], writes=[b_Gr[ri]])
                        kb.dma("pool", vr[ri][:], vL[j0:j0 + NJ].rearrange("j i d -> i j d"), writes=[b_vr[ri]])

                    def phaseB(r):
                        ri = r % 2
                        for tt in range(8):
                            for dh in range(2):
                                b0 = 2 + 2 * ((tt * 2 + dh) % 3)
                                for jj in range(NJ):
                                    for dq in range(2):
                                        kb.op("pe", lambda e, jj=jj, dq=dq, tt=tt, dh=dh, b0=b0: e.matmul(
                                            ps[:, b0 + dq, :], lhsT=Gr[ri][:, jj, tt * 128:(tt + 1) * 128],
                                            rhs=vr[ri][:, jj, dh * 1024 + dq * 512:dh * 1024 + (dq + 1) * 512],
                                            start=(jj == 0), stop=(jj == NJ - 1)),
                                            reads=[b_Gr[ri], b_vr[ri]], writes=[PB[b0 + dq]],
                                            sig=(jj == NJ - 1 and dq == 1))
                                pin = ps[:, b0:b0 + 2, :].rearrange("p a b -> p (a b)")
                                aout = accG[:, tt, dh * 1024:(dh + 1) * 1024]
                                if r == 0:
                                    kb.op("dve", lambda e, pin=pin, aout=aout: e.tensor_copy(out=aout, in_=pin),
                                          reads=[PB[b0], PB[b0 + 1]], writes=[b_accG[tt][dh]])
                                else:
                                    kb.op("dve", lambda e, pin=pin, aout=aout: e.tensor_tensor(
                                        out=aout, in0=aout, in1=pin, op=ALU.add),
                                        reads=[PB[b0], PB[b0 + 1]], writes=[b_accG[tt][dh]])

                    if stop_after == "G1":
                        phaseA(0)
                        phaseB(0)
                        dump([(accG[:, 0, 0:1024], b_accG[0], 1024), (accG[:, 7, 1024:2048], b_accG[7], 1024),
                              (Gr[0][:, 0, :], [b_Gr[0]], 1024), (Gr[0][:, 3, :], [b_Gr[0]], 1024)])
                        return nc
                    if NR > 0:
                        phaseA(0)
                    for r in range(NR):
                        if r + 1 < NR:
                            phaseA(r + 1)
                        phaseB(r)
                kb.barrier()
                if stop_after == "G3":
                    dump([(accG[:, 0, 0:1024], b_accG[0], 1024), (accG[:, 7, 1024:2048], b_accG[7], 1024)])
                    return nc
                with ExitStack() as sc3:
                    sbL = lambda name, shape, dt=F32: sc3.enter_context(nc.sbuf_tensor(name + "_h%d" % half, list(shape), dt))
                    gb2 = sbL("gb2", [128, 2, D])
                    b_gb2 = Buf("gb2")
                    kb.dma("sp", gb2[:, 0, :], lnp[2], writes=[b_gb2])
                    kb.dma("sp", gb2[:, 1, :], lnp[3], writes=[b_gb2])
                    hres = [sbL("hres%d" % i, [128, D]) for i in range(2)]
                    b_hres = bufs(2, "hres")
                    xc2 = sbL("xc2", [128, D])
                    b_xc2 = Buf("xc2")
                    jf2 = sbL("jf2", [128, D])
                    b_jf2 = Buf("jf2")
                    yo = [sbL("yo%d" % i, [128, D]) for i in range(2)]
                    b_yo = bufs(2, "yo")
                    st2 = sbL("st2", [128, 8])
                    b_st2 = Buf("st2")
                    outs = []
                    kb.dma("sp", hres[0][:, :], h_tok[(8 * half) * 128:(8 * half + 1) * 128, :],
                           reads=[b_htok[8 * half]], writes=[b_hres[0]])
                    for tt in range(8):
                        T = 8 * half + tt
                        hi = tt % 2
                        if tt + 1 < 8:
                            kb.dma("sp", hres[1 - hi][:, :], h_tok[(T + 1) * 128:(T + 2) * 128, :],
                                   reads=[b_htok[T + 1]], writes=[b_hres[1 - hi]])
                        kb.op("dve", lambda e, hi=hi, tt=tt: e.scalar_tensor_tensor(
                            out=hres[hi][:, :], in0=hres[hi][:, :], scalar=ALPHA, in1=accG[:, tt, :],
                            op0=ALU.mult, op1=ALU.add),
                            reads=b_accG[tt], writes=[b_hres[hi]])
                        layer_norm(None, hres[hi][:, :], b_hres[hi], gb2[:, 0, :], gb2[:, 1, :], b_gb2,
                                   yo[hi][:, :], b_yo[hi], xc2, b_xc2, jf2, b_jf2, st2, b_st2)
                        outs.append(kb.dma("sp", y[T * 128:(T + 1) * 128, :], yo[hi][:, :], reads=[b_yo[hi]]))
                    for t in outs:
                        kb.wait("sp", t)
                    if stop_after in ("G2", "G4"):
                        dump([(accG[:, 0, 0:1024], b_accG[0], 1024), (accG[:, 7, 1024:2048], b_accG[7], 1024),
                              (yo[1][:, 0:1024], [b_yo[1]], 1024), (hres[1][:, 0:1024], [b_hres[1]], 1024)])
                        return nc
        print("instructions:", kb.nins, "sbuf remaining:", nc.sbuf_bytes_remaining)
    return nc


def _t5_bucket(dist):
    dist = np.asarray(dist)
    d = np.maximum(dist, 1).astype(np.float32)
    large = 16 + (np.log(d / np.float32(16)) / np.float32(np.log(128 / 16)) * np.float32(16)).astype(np.int32)
    large = np.minimum(large, 31)
    return np.where(dist < 16, dist, large)


def _prep_shared(w_in, pool_w, pool_scale, rel_bias, w_out, ln1_g, ln1_b, peer_wq, peer_subkeys, peer_u, peer_v,
                 ln2_g, ln2_b):
    f = np.float32
    w = w_in[0]
    cols = []
    for c in range(8):
        cols.append(w[:, c * 128:(c + 1) * 128])
    for c in range(8):
        cols.append(w[:, 1024 + c * 128:1024 + (c + 1) * 128])
    for c in range(8):
        cols.append(w[:, 2048 + c * 128:2048 + (c + 1) * 128])
    for c in range(8):
        cols.append(w[:, 4096 + c * 128:4096 + (c + 1) * 128])
    ki = w[:, 5120:5184]
    cols.append(np.concatenate([ki, ki], axis=1))
    w_fm = np.stack([c.reshape(16, 128, 128).transpose(1, 0, 2) for c in cols]).astype(f)
    wv = w[:, 3072:4096]
    w_v = np.stack([wv[:, hg * 256:(hg + 1) * 256].reshape(16, 128, 256).transpose(1, 0, 2) for hg in range(4)])
    w_wi = w[:, 5184:5200].reshape(16, 128, 16).transpose(1, 0, 2)
    pw = pool_w[0].reshape(4, 2, 128, 256).transpose(2, 0, 1, 3)
    kk = np.arange(128)[:, None]
    qq = np.arange(128)[None, :]
    bt = np.zeros((128, 2, 8, 128), f)
    for dl in range(2):
        bkt = _t5_bucket(np.maximum(dl * 128 + qq - kk, 0))
        bt[:, dl, :, :] = rel_bias[bkt].transpose(0, 2, 1)
    wo = w_out[0].reshape(16, 128, D).transpose(1, 0, 2)
    lnp = np.stack([np.broadcast_to(a[0][None, :], (128, D)) for a in (ln1_g, ln1_b, ln2_g, ln2_b)])
    wqh = peer_wq[0].reshape(16, 128, D).transpose(1, 0, 2)
    skT = peer_subkeys[0].transpose(2, 0, 1)
    u = peer_u[0].reshape(128, 128, 16, 128)
    uT = u.transpose(1, 3, 2, 0)
    vv = peer_v[0].reshape(128, 128, D).transpose(1, 0, 2)
    c = lambda a: np.ascontiguousarray(a, dtype=f)
    return dict(w_fm=c(w_fm), w_v=c(w_v), w_wi=c(w_wi), pool_w=c(pw), biasT=c(bt), w_out=c(wo), lnp=c(lnp),
                wq=c(wqh), subkT=c(skT), uT=c(uT), vL=c(vv))


def _consts(hf, pool_scale, rel_bias):
    f = np.float32
    cst = np.zeros((128, 1024), f)
    cst[:, 0:128] = np.eye(128, dtype=f)
    cst[:, 128:256] = np.arange(128, dtype=f)[None, :]
    qq = np.arange(128)[:, None]
    kk = np.arange(128)[None, :]
    cst[:, 256:384] = np.where(kk <= qq, 0.0, NEG).astype(f)
    valid = 1.0 if hf == 1 else 0.0
    cst[:, 384] = valid
    cst[:, 385] = (valid - 1.0) * 1.0e30
    cst[:, 392:400] = rel_bias[31][None, :]
    for gq, wwin in enumerate((2, 4, 8, 16)):
        pos = np.arange(16)
        if hf == 0:
            corr = wwin / np.minimum(pos + 1, wwin).astype(f)
        else:
            corr = np.ones(16, f)
        cst[:, 400 + 16 * gq:400 + 16 * gq + 16] = corr[None, :]
    cst[:, 464:472] = pool_scale[0].reshape(8, 128).T
    cst[:, 480:512] = (2.0 ** -np.arange(32, dtype=np.float64)).astype(f)[None, :]
    return cst


def _core_inputs(x, shared, pool_scale, rel_bias):
    in_maps = []
    for c in range(8):
        b, hf = c // 2, c % 2
        own = x[b, hf * TOK:(hf + 1) * TOK]
        prev = x[b, 0:TOK] if hf == 1 else np.zeros_like(own)
        xT = np.stack([prev.T.reshape(16, 128, TOK).transpose(1, 0, 2), own.T.reshape(16, 128, TOK).transpose(1, 0, 2)])
        m = dict(shared)
        m["xT"] = np.ascontiguousarray(xT, dtype=np.float32)
        m["x_tok"] = np.ascontiguousarray(own, dtype=np.float32)
        m["consts"] = _consts(hf, pool_scale, rel_bias)
        in_maps.append(m)
    return in_maps


def kernel(x, w_in, pool_w, pool_scale, rel_bias, w_out, ln1_g, ln1_b, peer_wq, peer_subkeys, peer_u, peer_v,
           ln2_g, ln2_b):
    args = [np.asarray(a, dtype=np.float32) for a in (x, w_in, pool_w, pool_scale, rel_bias, w_out, ln1_g, ln1_b,
                                                      peer_wq, peer_subkeys, peer_u, peer_v, ln2_g, ln2_b)]
    (x, w_in, pool_w, pool_scale, rel_bias, w_out, ln1_g, ln1_b, peer_wq, peer_subkeys, peer_u, peer_v,
     ln2_g, ln2_b) = args
    shared = _prep_shared(w_in, pool_w, pool_scale, rel_bias, w_out, ln1_g, ln1_b, peer_wq, peer_subkeys,
                          peer_u, peer_v, ln2_g, ln2_b)
    in_maps = _core_inputs(x, shared, pool_scale, rel_bias)
    nc = build_nc()
    res = run_bass_kernel_spmd(nc, in_maps, core_ids=list(range(8)))
    out = np.zeros((4, S, D), np.float32)
    for c in range(8):
        b, hf = c // 2, c % 2
        out[b, hf * TOK:(hf + 1) * TOK] = res.results[c]["y"]
    return out
```

```python
import numpy as np
from contextlib import ExitStack
import concourse.bass as bass
import concourse.mybir as mybir
from concourse.bass_utils import run_bass_kernel_spmd

F32 = mybir.dt.float32
BF16 = mybir.dt.bfloat16
U32 = mybir.dt.uint32
ALU = mybir.AluOpType
AF = mybir.ActivationFunctionType
AX = mybir.AxisListType

D = 2048
S = 4096
TOK = 2048
NEG = -1.0e30
ALPHA = 2.0 ** 0.25
LN_EPS = 1e-5
NIT = 16
TOPK = 256
ATT_SCALE = 128.0 ** -0.5
NSLOT = 6


class Buf:
    __slots__ = ("w", "r", "name")

    def __init__(self, name=""):
        self.w = None
        self.r = {}
        self.name = name


class KB:
    def __init__(self, nc, es):
        self.nc = nc
        self.engs = {"pe": nc.tensor, "dve": nc.vector, "act": nc.scalar, "pool": nc.gpsimd, "sp": nc.sync}
        self.psem = {e: es.enter_context(nc.semaphore("prog_" + e)) for e in ["pe", "dve", "act", "pool"]}
        self.cnt = {e: 0 for e in self.psem}
        self.seen = {e: {} for e in self.engs}
        self.pending = {e: [] for e in self.engs}
        self.dslots = {q: [[es.enter_context(nc.semaphore("dq_%s_%d" % (q, i))), 0, "dq_%s_%d" % (q, i)]
                           for i in range(NSLOT)] for q in ["sp", "pool"]}
        self.dnext = {q: 0 for q in self.dslots}
        self.nins = 0

    def wait(self, e, tok):
        if tok is None:
            return
        sem, val, key = tok
        if self.seen[e].get(key, 0) >= val:
            return
        self.engs[e].wait_ge(sem, val)
        self.seen[e][key] = val

    def _deps(self, e, reads, writes, deps):
        for b in reads:
            self.wait(e, b.w)
        for b in writes:
            self.wait(e, b.w)
            for t in b.r.values():
                self.wait(e, t)
        for t in deps:
            self.wait(e, t)

    def op(self, e, fn, reads=(), writes=(), deps=(), sig=True):
        self._deps(e, reads, writes, deps)
        ins = fn(self.engs[e])
        self.nins += 1
        if not sig:
            self.pending[e].append((list(reads), list(writes)))
            return None
        self.cnt[e] += 1
        ins.then_inc(self.psem[e], 1)
        key = "prog_" + e
        tok = (self.psem[e], self.cnt[e], key)
        allr = list(reads)
        allw = list(writes)
        for (r, w) in self.pending[e]:
            allr += r
            allw += w
        self.pending[e] = []
        for b in allw:
            b.w = tok
            b.r = {}
        for b in allr:
            if b not in allw:
                b.r[key] = tok
        return tok

    def barrier(self):
        toks = []
        for e in self.psem:
            if self.cnt[e] > 0:
                toks.append((self.psem[e], self.cnt[e], "prog_" + e))
        for q in self.dslots:
            for sem, cnt, key in self.dslots[q]:
                if cnt > 0:
                    toks.append((sem, cnt, key))
        for e in self.engs:
            for t in toks:
                self.wait(e, t)

    def dma(self, q, out, in_, reads=(), writes=(), deps=()):
        self._deps(q, reads, writes, deps)
        slot = self.dslots[q][self.dnext[q]]
        self.dnext[q] = (self.dnext[q] + 1) % NSLOT
        sem, cnt, key = slot
        if cnt > 0:
            self.wait(q, (sem, cnt, key))
        self.engs[q].dma_start(out=out, in_=in_).then_inc(sem, 16)
        self.nins += 1
        slot[1] = cnt + 16
        tok = (sem, cnt + 16, key)
        for b in writes:
            b.w = tok
            b.r = {}
        for b in reads:
            b.r[key] = tok
        return tok


def sap(t, dims, off=0, parts=128, p0=0):
    fs = 1
    for s_ in t.shape[1:]:
        fs *= int(s_)
    return bass.AP(t, p0 * fs + off, [[fs, parts]] + [[int(a), int(b)] for a, b in dims])


def bufs(n, name=""):
    return [Buf("%s%d" % (name, i)) for i in range(n)]


def build_nc(stop_after=None, small_peer=False):
    nc = bass.Bass("TRN2", target_bir_lowering=False)
    dbg = {}

    def din(name, shape, dt=F32):
        return nc.dram_tensor(name, list(shape), dt, kind="ExternalInput").ap()

    xT = din("xT", [2, 128, 16, TOK])
    x_tok = din("x_tok", [TOK, D])
    w_fm = din("w_fm", [33, 128, 16, 128])
    w_v = din("w_v", [4, 128, 16, 256])
    w_wi = din("w_wi", [128, 16, 16])
    pool_w = din("pool_w", [128, 4, 2, 256])
    consts = din("consts", [128, 1024])
    biasT = din("biasT", [128, 2, 8, 128])
    w_out = din("w_out", [128, 16, D])
    lnp = din("lnp", [4, 128, D])
    wq = din("wq", [128, 16, D])
    subkT = din("subkT", [128, 2, 128])
    uT = din("uT", [8 if small_peer else 128, 128, 16, 128])
    vL = din("vL", [8 if small_peer else 128, 128, D])
    y = nc.dram_tensor("y", [TOK, D], F32, kind="ExternalOutput").ap()
    maskT_d = nc.dram_tensor("maskT_d", [4, 128, 32, 512], BF16).ap()
    hT_d = nc.dram_tensor("hT_d", [128, 16, TOK], BF16).ap()
    h_tok = nc.dram_tensor("h_tok", [TOK, D], F32).ap()
    W1 = nc.dram_tensor("W1", [32, 128, 128, 64], BF16).ap()
    if stop_after is not None:
        dbg_out = nc.dram_tensor("dbg", [128, 8192], F32, kind="ExternalOutput").ap()

    with ExitStack() as es:
        kb = KB(nc, es)
        sb = lambda name, shape, dt=F32: es.enter_context(nc.sbuf_tensor(name, list(shape), dt))
        ps = es.enter_context(nc.psum_tensor("ps", [128, 8, 512], F32))
        PB = bufs(8, "psb")

        dbg_stg = sb("dbg_stg", [128, 1024]) if stop_after is not None else None
        cst = sb("cst", [128, 1024])
        b_cst = Buf("cst")
        kb.dma("sp", cst[:], consts, writes=[b_cst])
        C_ID = 0
        C_IOTA = 128
        C_TRI = 256
        C_FLAG = 384
        C_B31 = 392
        C_CORR = 400
        C_PSC = 464
        C_PW2 = 480
        ident = cst[:, C_ID:C_ID + 128]
        iota = cst[:, C_IOTA:C_IOTA + 128]
        tri = cst[:, C_TRI:C_TRI + 128]
        bT = sb("bT", [128, 2, 8, 128])
        b_bT = Buf("bT")
        kb.dma("sp", bT[:], biasT, writes=[b_bT])
        ones_b = sb("ones_b", [128, 128], BF16)
        b_ones = Buf("ones")
        kb.op("pool", lambda e: e.memset(ones_b[:], 1.0), writes=[b_ones])

        evac_rr = [0]

        def evac(out, in_, reads, writes, scale=None):
            evac_rr[0] ^= 1
            if scale is not None:
                return kb.op("act", lambda e: e.activation(out=out, in_=in_, func=AF.Copy, scale=scale),
                             reads=reads, writes=writes)
            if evac_rr[0]:
                return kb.op("act", lambda e: e.activation(out=out, in_=in_, func=AF.Copy), reads=reads, writes=writes)
            return kb.op("dve", lambda e: e.tensor_copy(out=out, in_=in_), reads=reads, writes=writes)

        pb_rr = [0]

        def next_bank(choices):
            pb_rr[0] += 1
            return choices[pb_rr[0] % len(choices)]

        def dump(ap_list):
            stg = dbg_stg
            b_stg = Buf("dbgstg")
            col = 0
            for ap, bl, n in ap_list:
                kb.op("dve", lambda e, ap=ap, n=n: e.tensor_copy(out=stg[:, 0:n], in_=ap),
                      reads=bl, writes=[b_stg])
                t = kb.dma("sp", dbg_out[:, col:col + n], stg[:, 0:n], reads=[b_stg])
                kb.wait("sp", t)
                col += n

        xg_t = [None, None]
        b_xg = bufs(2, "xg")
        xg_rr = [0]

        def load_xg(s, tg):
            i = xg_rr[0]
            xg_rr[0] ^= 1
            for q4 in range(4):
                kb.dma("pool", xg_t[i][:, 4 * q4:4 * q4 + 4, :], xT[s, :, 4 * q4:4 * q4 + 4, tg * 512:(tg + 1) * 512],
                       writes=[b_xg[i]])
            return xg_t[i], b_xg[i]

        def proj_fm(wt, wb, xg, bx, out_ap, out_bufs, scale=None):
            bk = next_bank([0, 1, 2, 7])
            for kc in range(16):
                kb.op("pe", lambda e, kc=kc: e.matmul(ps[:, bk, :], lhsT=wt(kc), rhs=xg[:, kc, :],
                                                      start=(kc == 0), stop=(kc == 15)),
                      reads=[wb, bx], writes=[PB[bk]], sig=(kc == 15))
            return evac(out_ap, ps[:, bk, :], [PB[bk]], out_bufs, scale=scale)

        phA = ExitStack()
        es.enter_context(phA)
        xg_t[0] = phA.enter_context(nc.sbuf_tensor("xgA0", [128, 16, 512], BF16))
        xg_t[1] = phA.enter_context(nc.sbuf_tensor("xgA1", [128, 16, 512], BF16))
        qiT = phA.enter_context(nc.sbuf_tensor("qiT", [128, 8, TOK], BF16))
        b_qiT = bufs(4, "qiT")
        kiT = phA.enter_context(nc.sbuf_tensor("kiT", [128, 2 * TOK], BF16))
        b_kiT = bufs(8, "kiT")
        widx = phA.enter_context(nc.sbuf_tensor("widx", [128, 16, 16], F32))
        b_widx = bufs(16, "widx")
        with ExitStack() as sc:
            wqi = sc.enter_context(nc.sbuf_tensor("wqi", [128, 8, 16, 128], BF16))
            b_wqi = Buf("wqi")
            wki = sc.enter_context(nc.sbuf_tensor("wki", [128, 16, 128], BF16))
            b_wki = Buf("wki")
            wwi = sc.enter_context(nc.sbuf_tensor("wwi", [128, 16, 16], BF16))
            b_wwi = Buf("wwi")
            kb.dma("pool", wki[:], w_fm[32], writes=[b_wki])
            for c in range(8):
                kb.dma("pool", wqi[:, c], w_fm[24 + c], writes=[b_wqi])
            kb.dma("pool", wwi[:], w_wi, writes=[b_wwi])
            for s in range(2):
                for tg in range(4):
                    xg, bx = load_xg(s, tg)
                    kg = s * 4 + tg
                    proj_fm(lambda kc: wki[:, kc, :], b_wki, xg, bx, kiT[:, kg * 512:(kg + 1) * 512], [b_kiT[kg]])
                    if stop_after == "A1":
                        dump([(kiT[:, 0:512], b_kiT[0:1], 512), (xg[:, 0, :], [bx], 512)])
                        return nc
                    if s == 1:
                        for c in range(8):
                            proj_fm(lambda kc, c=c: wqi[:, c, kc, :], b_wqi, xg, bx,
                                    qiT[:, c, tg * 512:(tg + 1) * 512], [b_qiT[tg]])
                        for tt in range(4):
                            bk = next_bank([0, 1, 2, 7])
                            for kc in range(16):
                                kb.op("pe", lambda e, kc=kc, tt=tt: e.matmul(
                                    ps[:, bk, 0:16], lhsT=xg[:, kc, tt * 128:(tt + 1) * 128], rhs=wwi[:, kc, :],
                                    start=(kc == 0), stop=(kc == 15)),
                                    reads=[b_wwi, bx], writes=[PB[bk]], sig=(kc == 15))
                            evac(widx[:, tg * 4 + tt, :], ps[:, bk, 0:16], [PB[bk]], [b_widx[tg * 4 + tt]])
        if stop_after == "A":
            dump([(kiT[:, 0:2048], b_kiT[0:4], 2048), (kiT[:, 2048:4096], b_kiT[4:8], 2048),
                  (qiT[:, 0, 0:2048], b_qiT, 2048), (widx[:, :, :].rearrange("p a b -> p (a b)"), b_widx, 256)])
            return nc

        kb.barrier()
        b_maskd = bufs(4, "maskd")
        with ExitStack() as sc:
            sbB = lambda name, shape, dt=F32: sc.enter_context(nc.sbuf_tensor(name, list(shape), dt))
            score2 = [sbB("score%d" % p_, [128, 4096]) for p_ in range(2)]
            b_score2 = [bufs(8, "score%d_" % p_) for p_ in range(2)]
            maskf = sbB("maskf", [128, 4096])
            b_maskf = Buf("maskf")
            junkA = sbB("junkA", [128, 4096], BF16)
            b_junkA = Buf("junkA")
            Rt = [sbB("R%d" % i_, [128, 512]) for i_ in range(3)]
            b_R = bufs(3, "R")
            acc = sbB("accB", [128, 512])
            b_acc = Buf("acc")
            mx2 = [sbB("mxall%d" % p_, [128, 16]) for p_ in range(2)]
            b_mx2 = bufs(2, "mx")
            sm2 = [sbB("smallB%d" % p_, [128, 64]) for p_ in range(2)]
            b_sm2 = bufs(2, "small")
            b_nm2 = bufs(2, "negmid")
            b_cs2 = bufs(2, "cs")
            mT = sbB("maskTg", [128, 32, 512], BF16)
            b_mT = Buf("maskTg")
            r_rr = [0]

            def make_units(i):
                g = i // 4
                p_ = i % 2
                score, b_score, mxall, b_mx = score2[p_], b_score2[p_], mx2[p_], b_mx2[p_]
                NK = (17 + i) * 128
                nkt = (NK + 511) // 512
                units = []
                for kt in range(nkt):
                    wk = min(512, NK - kt * 512)
                    direct = kt >= 4
                    for hn, h in enumerate([0, 2, 4, 6, 8, 10, 12, 14, 1, 3, 5, 7, 9, 11, 13, 15]):
                        def unit(kt=kt, wk=wk, direct=direct, hn=hn, h=h):
                            cp, r0 = h // 2, 64 * (h % 2)
                            bk = next_bank([0, 1, 2])
                            kb.op("pe", lambda e: e.matmul(
                                ps[:, bk, 0:wk], lhsT=qiT[r0:r0 + 64, cp, i * 128:(i + 1) * 128],
                                rhs=kiT[r0:r0 + 64, kt * 512:kt * 512 + wk], start=True, stop=True),
                                reads=[b_qiT[g], b_kiT[kt]], writes=[PB[bk]])
                            ri = r_rr[0] % 3
                            r_rr[0] += 1
                            kb.op("act", lambda e: e.activation(
                                out=Rt[ri][:, 0:wk], in_=ps[:, bk, 0:wk], func=AF.Relu),
                                reads=[PB[bk]], writes=[b_R[ri]])
                            wcol = widx[:, i, h:h + 1]
                            if hn == 0:
                                kb.op("dve", lambda e: e.tensor_scalar(
                                    out=acc[:, 0:wk], in0=Rt[ri][:, 0:wk], scalar1=wcol, scalar2=None, op0=ALU.mult),
                                    reads=[b_R[ri], b_widx[i]], writes=[b_acc])
                            elif hn == 15 and direct:
                                kb.op("dve", lambda e: e.scalar_tensor_tensor(
                                    out=score[:, kt * 512:kt * 512 + wk], in0=Rt[ri][:, 0:wk], scalar=wcol,
                                    in1=acc[:, 0:wk], op0=ALU.mult, op1=ALU.add),
                                    reads=[b_R[ri], b_widx[i], b_acc], writes=[b_score[kt]])
                            else:
                                kb.op("dve", lambda e: e.scalar_tensor_tensor(
                                    out=acc[:, 0:wk], in0=Rt[ri][:, 0:wk], scalar=wcol,
                                    in1=acc[:, 0:wk], op0=ALU.mult, op1=ALU.add),
                                    reads=[b_R[ri], b_widx[i]], writes=[b_acc])
                            if hn == 15:
                                src = score[:, kt * 512:kt * 512 + wk] if direct else acc[:, 0:wk]
                                bsrc = b_score[kt] if direct else b_acc
                                kb.op("dve", lambda e: e.tensor_reduce(
                                    out=mxall[:, kt:kt + 1], in_=src, axis=AX.X, op=ALU.max),
                                    reads=[bsrc], writes=[b_mx])
                                kb.op("dve", lambda e: e.tensor_reduce(
                                    out=mxall[:, 8 + kt:9 + kt], in_=src, axis=AX.X, op=ALU.min),
                                    reads=[bsrc], writes=[b_mx])
                                if not direct:
                                    kb.op("dve", lambda e: e.tensor_scalar(
                                        out=score[:, kt * 512:kt * 512 + wk], in0=acc[:, 0:wk],
                                        scalar1=cst[:, C_FLAG:C_FLAG + 1], scalar2=cst[:, C_FLAG + 1:C_FLAG + 2],
                                        op0=ALU.mult, op1=ALU.add),
                                        reads=[b_acc, b_cst], writes=[b_score[kt]])
                        units.append(unit)
                return units

            def post_score(i):
                p_ = i % 2
                score, b_score, mxall, b_mx, sm, b_sm = score2[p_], b_score2[p_], mx2[p_], b_mx2[p_], sm2[p_], b_sm2[p_]
                NK = (17 + i) * 128
                nkt = (NK + 511) // 512
                negmid, tmpc, Mp, negD = sm[:, 0:1], sm[:, 2:3], sm[:, 3:4], sm[:, 8:8 + NIT + 1]
                kd = (NK - 128) // 512
                kb.op("dve", lambda e: e.tensor_tensor(
                    out=score[:, NK - 128:NK], in0=score[:, NK - 128:NK], in1=tri, op=ALU.add),
                    reads=[b_cst], writes=[b_score[kd]])
                kb.op("dve", lambda e: e.tensor_reduce(out=Mp, in_=mxall[:, 0:nkt], axis=AX.X, op=ALU.max),
                      reads=[b_mx], writes=[b_sm])
                kb.op("dve", lambda e: e.tensor_reduce(out=tmpc, in_=mxall[:, 8:8 + nkt], axis=AX.X, op=ALU.min),
                      reads=[b_mx], writes=[b_sm])
                kb.op("dve", lambda e: e.scalar_tensor_tensor(out=Mp, in0=tmpc, scalar=-1.0, in1=Mp, op0=ALU.mult,
                                                              op1=ALU.max), writes=[b_sm])
                kb.op("dve", lambda e: e.tensor_scalar(out=Mp, in0=Mp, scalar1=-1.001, scalar2=-1e-20,
                                                       op0=ALU.mult, op1=ALU.add), writes=[b_sm])
                kb.op("dve", lambda e: e.tensor_scalar(out=negD, in0=cst[:, C_PW2:C_PW2 + NIT + 1], scalar1=Mp,
                                                       scalar2=None, op0=ALU.mult), reads=[b_cst], writes=[b_sm])
                kb.op("dve", lambda e: e.memset(negmid, 0.0), writes=[b_nm2[p_]])

            def make_steps(i):
                p_ = i % 2
                score, b_score, sm, b_sm = score2[p_], b_score2[p_], sm2[p_], b_sm2[p_]
                NK = (17 + i) * 128
                nkt = (NK + 511) // 512
                negmid, cs, tmpc, negD = sm[:, 0:1], sm[:, 1:2], sm[:, 2:3], sm[:, 8:8 + NIT + 1]
                thr = 2.0 * TOPK - NK - 0.5
                steps = []
                for k in range(NIT):
                    def act_fn():
                        kb.op("act", lambda e: e.activation(
                            out=junkA[:, 0:NK], in_=score[:, 0:NK], func=AF.Sign, bias=negmid, scale=1.0,
                            accum_out=cs),
                            reads=b_score[0:nkt] + [b_nm2[p_]], writes=[b_junkA, b_cs2[p_]])

                    def dve_fn(k=k):
                        kb.op("dve", lambda e: e.tensor_scalar(out=tmpc, in0=cs, scalar1=thr, scalar2=0.5,
                                                               op0=ALU.is_ge, op1=ALU.subtract),
                              reads=[b_cs2[p_]], writes=[b_sm])
                        kb.op("dve", lambda e: e.scalar_tensor_tensor(
                            out=negmid, in0=tmpc, scalar=negD[:, k:k + 1], in1=negmid, op0=ALU.mult, op1=ALU.add),
                            reads=[b_sm], writes=[b_nm2[p_]])
                    steps.append((act_fn, dve_fn))
                return steps

            def finalize(i):
                g, j = i // 4, i % 4
                p_ = i % 2
                score, b_score, sm, b_sm = score2[p_], b_score2[p_], sm2[p_], b_sm2[p_]
                NK = (17 + i) * 128
                nkt = (NK + 511) // 512
                negmid, tau, negD = sm[:, 0:1], sm[:, 4:5], sm[:, 8:8 + NIT + 1]
                if j == 0:
                    kb.op("pool", lambda e: e.memset(mT[:], 0.0), writes=[b_mT])
                kb.op("dve", lambda e: e.tensor_tensor(out=tau, in0=negD[:, NIT:NIT + 1], in1=negmid, op=ALU.subtract),
                      reads=[b_nm2[p_]], writes=[b_sm])
                kb.op("dve", lambda e: e.tensor_scalar(
                    out=maskf[:, 0:NK], in0=score[:, 0:NK], scalar1=tau, scalar2=None, op0=ALU.is_ge),
                    reads=b_score[0:nkt] + [b_sm], writes=[b_maskf])
                nch = 17 + i
                for c0 in range(0, nch, 4):
                    n4 = min(4, nch - c0)
                    bk = next_bank([3, 4])
                    for cc in range(n4):
                        c = c0 + cc
                        kb.op("pe", lambda e, c=c, cc=cc: e.transpose(
                            ps[:, bk, cc * 128:(cc + 1) * 128], maskf[:, c * 128:(c + 1) * 128], ident),
                            reads=[b_maskf, b_cst], writes=[PB[bk]], sig=(cc == n4 - 1))
                    kb.op("act", lambda e, c0=c0, n4=n4: e.activation(
                        out=mT[:, c0:c0 + n4, j * 128:(j + 1) * 128],
                        in_=ps[:, bk, 0:n4 * 128].rearrange("p (c q) -> p c q", c=n4), func=AF.Copy),
                        reads=[PB[bk]], writes=[b_mT])
                if j == 3:
                    kb.dma("sp", maskT_d[g], mT[:], reads=[b_mT], writes=[b_maskd[g]])

            prev = None
            for i in range(16):
                units = make_units(i)
                steps = make_steps(prev) if prev is not None else []
                nU = len(units)
                spacing = max(4, nU // (NIT + 1))
                ka = 0
                kd_ = 0
                for u, unit in enumerate(units):
                    unit()
                    if ka < len(steps) and u == ka * spacing + 1:
                        steps[ka][0]()
                        ka += 1
                    if kd_ < len(steps) and kd_ < ka and u == kd_ * spacing + 1 + spacing // 2:
                        steps[kd_][1]()
                        kd_ += 1
                while kd_ < len(steps):
                    if ka == kd_:
                        steps[ka][0]()
                        ka += 1
                    steps[kd_][1]()
                    kd_ += 1
                post_score(i)
                if prev is not None:
                    finalize(prev)
                prev = i
            for st_ in make_steps(prev):
                st_[0]()
                st_[1]()
            finalize(prev)
        phA.close()
        pers = ExitStack()
        es.enter_context(pers)
        psb = lambda name, shape, dt=F32: pers.enter_context(nc.sbuf_tensor(name, list(shape), dt))
        poolT = psb("poolT", [128, 8, TOK], BF16)
        b_poolT = bufs(4, "poolT")
        attnT = psb("attnT", [128, 8, TOK], BF16)
        b_attnT = [bufs(4, "attnT%d" % h) for h in range(8)]
        scCD = ExitStack()
        es.enter_context(scCD)
        xg_t[0] = scCD.enter_context(nc.sbuf_tensor("xgC0", [128, 16, 512], BF16))
        xg_t[1] = scCD.enter_context(nc.sbuf_tensor("xgC1", [128, 16, 512], BF16))
        b_xg[0], b_xg[1] = Buf("xgc0"), Buf("xgc1")

        kb.barrier()
        with ExitStack() as sc:
            sbC = lambda name, shape, dt=F32: sc.enter_context(nc.sbuf_tensor(name, list(shape), dt))
            wpl = sbC("wpl", [128, 8, 16, 128], BF16)
            b_wpl = Buf("wpl")
            for c in range(8):
                kb.dma("pool", wpl[:, c], w_fm[c], writes=[b_wpl])
            pw = sbC("pw", [128, 4, 2, 256], BF16)
            b_pw = Buf("pw")
            kb.dma("pool", pw[:], pool_w, writes=[b_pw])
            xh = sbC("xh", [128, 16, 16], BF16)
            b_xh = Buf("xh")
            for q4 in range(4):
                kb.dma("pool", xh[:, 4 * q4:4 * q4 + 4, :], xT[0, :, 4 * q4:4 * q4 + 4, TOK - 16:TOK], writes=[b_xh])
            hal = sbC("hal", [128, 8, 16])
            b_hal = bufs(8, "hal")
            vb = [sbC("vb%d" % i, [128, 528]) for i in range(2)]
            b_vb = bufs(2, "vb")
            sa = sbC("sa", [128, 528])
            sbb = sbC("sbb", [128, 528])
            b_sa, b_sb = Buf("sa"), Buf("sb")
            t16 = sbC("t16", [128, 16])
            b_t16 = Buf("t16")
            plb = [sbC("plb%d" % i, [128, 512], BF16) for i in range(2)]
            b_plb = bufs(2, "plb")
            for cp in range(8):
                bk = next_bank([0, 1, 2, 7])
                for kc in range(16):
                    kb.op("pe", lambda e, kc=kc, cp=cp, bk=bk: e.matmul(
                        ps[:, bk, 0:16], lhsT=wpl[:, cp, kc, :], rhs=xh[:, kc, :], start=(kc == 0), stop=(kc == 15)),
                        reads=[b_wpl, b_xh], writes=[PB[bk]], sig=(kc == 15))
                evac(hal[:, cp, :], ps[:, bk, 0:16], [PB[bk]], [b_hal[cp]])
            vrr = 0
            for tg in range(4):
                xg, bx = load_xg(1, tg)
                for gq in range(4):
                    wwin = (2, 4, 8, 16)[gq]
                    for cc in range(2):
                        cp = 2 * gq + cc
                        vi = vrr % 2
                        vrr += 1
                        V = vb[vi]
                        bV = b_vb[vi]
                        kb.op("dve", lambda e, V=V, cp=cp: e.tensor_copy(out=V[:, 0:16], in_=hal[:, cp, :]),
                              reads=[b_hal[cp]], writes=[bV])
                        proj_fm(lambda kc, cp=cp: wpl[:, cp, kc, :], b_wpl, xg, bx, V[:, 16:528], [bV])
                        kb.op("act", lambda e, V=V, cp=cp: e.activation(out=hal[:, cp, :], in_=V[:, 512:528], func=AF.Copy),
                              reads=[bV], writes=[b_hal[cp]])
                        kb.op("dve", lambda e, V=V: e.tensor_tensor(out=sa[:, 1:528], in0=V[:, 1:528], in1=V[:, 0:527],
                                                                     op=ALU.add), reads=[bV], writes=[b_sa])
                        Sfin, bS = sa, b_sa
                        if gq >= 1:
                            kb.op("dve", lambda e: e.tensor_tensor(out=sbb[:, 3:528], in0=sa[:, 3:528], in1=sa[:, 1:526],
                                                                   op=ALU.add), reads=[b_sa], writes=[b_sb])
                            Sfin, bS = sbb, b_sb
                        if gq >= 2:
                            kb.op("dve", lambda e: e.tensor_tensor(out=sa[:, 7:528], in0=sbb[:, 7:528], in1=sbb[:, 3:524],
                                                                   op=ALU.add), reads=[b_sb], writes=[b_sa])
                            Sfin, bS = sa, b_sa
                        if gq >= 3:
                            kb.op("dve", lambda e: e.tensor_tensor(out=sbb[:, 15:528], in0=sa[:, 15:528], in1=sa[:, 7:520],
                                                                   op=ALU.add), reads=[b_sa], writes=[b_sb])
                            Sfin, bS = sbb, b_sb
                        kb.op("dve", lambda e, Sfin=Sfin, V=V, cc=cc, wwin=wwin: e.scalar_tensor_tensor(
                            out=plb[cc][:, :], in0=Sfin[:, 16:528], scalar=1.0 / wwin, in1=V[:, 16:528],
                            op0=ALU.mult, op1=ALU.subtract), reads=[bS, bV], writes=[b_plb[cc]])
                        if tg == 0:
                            kb.op("dve", lambda e, Sfin=Sfin, gq=gq: e.tensor_tensor(
                                out=t16[:, :], in0=Sfin[:, 16:32], in1=cst[:, C_CORR + 16 * gq:C_CORR + 16 * gq + 16],
                                op=ALU.mult), reads=[bS, b_cst], writes=[b_t16])
                            kb.op("dve", lambda e, V=V, cc=cc, wwin=wwin: e.scalar_tensor_tensor(
                                out=plb[cc][:, 0:16], in0=t16[:, :], scalar=1.0 / wwin, in1=V[:, 16:32],
                                op0=ALU.mult, op1=ALU.subtract), reads=[b_t16, bV], writes=[b_plb[cc]])
                    for dc in range(2):
                        bk = next_bank([0, 1, 2, 7])
                        for cc in range(2):
                            kb.op("pe", lambda e, cc=cc, dc=dc, gq=gq, bk=bk: e.matmul(
                                ps[:, bk, :], lhsT=pw[:, gq, cc, dc * 128:(dc + 1) * 128], rhs=plb[cc][:, :],
                                start=(cc == 0), stop=(cc == 1)),
                                reads=[b_pw, b_plb[cc]], writes=[PB[bk]], sig=(cc == 1))
                        oc = 2 * gq + dc
                        kb.op("act", lambda e, oc=oc, bk=bk, tg=tg: e.activation(
                            out=poolT[:, oc, tg * 512:(tg + 1) * 512], in_=ps[:, bk, :], func=AF.Copy,
                            scale=cst[:, C_PSC + oc:C_PSC + oc + 1]),
                            reads=[PB[bk], b_cst], writes=[b_poolT[tg]])
        if stop_after == "C":
            dump([(poolT[:, 0, 0:1024], b_poolT, 1024), (poolT[:, 7, 0:1024], b_poolT, 1024),
                  (poolT[:, 3, 1024:2048], b_poolT, 1024)])
            return nc

        kb.barrier()
        with ExitStack() as sc:
            sbD = lambda name, shape, dt=F32: sc.enter_context(nc.sbuf_tensor(name, list(shape), dt))
            kT2 = sbD("kT2", [128, 2, 2 * TOK], BF16)
            b_kT2 = bufs(8, "kT2")
            v2 = sbD("v2", [128, 32, 256], BF16)
            b_v2 = bufs(8, "v2")
            qT2 = sbD("qT2", [128, 2, TOK], BF16)
            b_qT2 = bufs(4, "qT2")
            SB_S = [0, 1, 2]
            for hg in range(4):
                kb.barrier()
                scP = ExitStack()
                sbP = lambda name, shape, dt=F32: scP.enter_context(nc.sbuf_tensor(name + "_g%d" % hg, list(shape), dt))
                wq2 = sbP("wq2", [128, 2, 16, 128], BF16)
                wk2 = sbP("wk2", [128, 2, 16, 128], BF16)
                wv2 = sbP("wv2", [128, 16, 256], BF16)
                b_wq2, b_wk2, b_wv2 = Buf("wq2"), Buf("wk2"), Buf("wv2")
                for hl in range(2):
                    kb.dma("pool", wq2[:, hl], w_fm[8 + 2 * hg + hl], writes=[b_wq2])
                    kb.dma("pool", wk2[:, hl], w_fm[16 + 2 * hg + hl], writes=[b_wk2])
                kb.dma("pool", wv2[:], w_v[hg], writes=[b_wv2])
                for s in range(2):
                    for tg in range(4):
                        xg, bx = load_xg(s, tg)
                        kg = s * 4 + tg
                        for hl in range(2):
                            proj_fm(lambda kc, hl=hl: wk2[:, hl, kc, :], b_wk2, xg, bx,
                                    kT2[:, hl, kg * 512:(kg + 1) * 512], [b_kT2[kg]])
                            if s == 1:
                                proj_fm(lambda kc, hl=hl: wq2[:, hl, kc, :], b_wq2, xg, bx,
                                        qT2[:, hl, tg * 512:(tg + 1) * 512], [b_qT2[tg]])
                        for tt in range(4):
                            bk = 7
                            for kc in range(16):
                                kb.op("pe", lambda e, kc=kc, tt=tt, xg=xg: e.matmul(
                                    ps[:, bk, 0:256], lhsT=xg[:, kc, tt * 128:(tt + 1) * 128], rhs=wv2[:, kc, :],
                                    start=(kc == 0), stop=(kc == 15)),
                                    reads=[b_wv2, bx], writes=[PB[bk]], sig=(kc == 15))
                            evac(v2[:, kg * 4 + tt, :], ps[:, bk, 0:256], [PB[bk]], [b_v2[kg]])
                if stop_after == "D0":
                    dump([(kT2[:, 0, 0:1024], b_kT2[0:2], 1024), (qT2[:, 1, 0:1024], b_qT2[0:2], 1024),
                          (v2[:, 0:4, :].rearrange("p a b -> p (a b)"), b_v2[0:1], 1024)])
                    return nc
                kb.barrier()
                scP.close()
                scT = ExitStack()
                sbT = lambda name, shape, dt=F32: scT.enter_context(nc.sbuf_tensor(name + "_g%d" % hg, list(shape), dt))
                mk = sbT("mk", [128, 32, 512], BF16)
                b_mk = Buf("mk")
                Et = [sbT("E%d" % i_, [128, 512], BF16) for i_ in range(3)]
                b_E = bufs(3, "E")
                Pt = [sbT("P%d" % i_, [128, 512], BF16) for i_ in range(3)]
                b_P = bufs(3, "P")
                tmpn = [sbT("tmpn%d" % i_, [128, 128]) for i_ in range(2)]
                b_tmpn = bufs(2, "tmpn")
                rden = sbT("rden", [128, 512])
                b_rden = Buf("rden")
                for g in range(4):
                    kb.dma("sp", mk[:], maskT_d[g], reads=[b_maskd[g]], writes=[b_mk])
                    nch = 20 + 4 * g
                    for hl in range(2):
                        h = 2 * hg + hl
                        bO = 3 + 2 * (hl % 2)
                        bD = bO + 1

                        def issue_S(c, hl=hl, g=g):
                            bk = SB_S[c % 3]
                            kb.op("pe", lambda e: e.matmul(
                                ps[:, bk, :], lhsT=kT2[:, hl, c * 128:(c + 1) * 128],
                                rhs=qT2[:, hl, g * 512:(g + 1) * 512], start=True, stop=True),
                                reads=[b_kT2[c // 4], b_qT2[g]], writes=[PB[bk]])
                        issue_S(0)
                        if nch > 1:
                            issue_S(1)
                        for c in range(nch):
                            if c + 2 < nch:
                                issue_S(c + 2)
                            bk = SB_S[c % 3]
                            ei = c % 3
                            tokE = kb.op("act", lambda e, bk=bk, ei=ei, h=h: e.activation(
                                out=Et[ei][:, :], in_=ps[:, bk, :], func=AF.Exp, scale=ATT_SCALE,
                                bias=cst[:, C_B31 + h:C_B31 + h + 1]),
                                reads=[PB[bk], b_cst], writes=[b_E[ei]])
                            cn = c - (15 + 4 * g)
                            if 0 <= cn <= 4:
                                for j in range(4):
                                    dl = 1 + j - cn
                                    if dl in (0, 1):
                                        ti = (j + cn) % 2
                                        kb.op("dve", lambda e, bk=bk, j=j, dl=dl, h=h, ti=ti: e.scalar_tensor_tensor(
                                            out=tmpn[ti][:, :], in0=ps[:, bk, j * 128:(j + 1) * 128], scalar=ATT_SCALE,
                                            in1=bT[:, dl, h, :], op0=ALU.mult, op1=ALU.add),
                                            reads=[PB[bk], b_bT], writes=[b_tmpn[ti]], deps=[tokE])
                                        kb.op("act", lambda e, ei=ei, j=j, ti=ti: e.activation(
                                            out=Et[ei][:, j * 128:(j + 1) * 128], in_=tmpn[ti][:, :], func=AF.Exp),
                                            reads=[b_tmpn[ti]], writes=[b_E[ei]])
                            kb.op("dve", lambda e, ei=ei, c=c: e.tensor_tensor(
                                out=Pt[ei][:, :], in0=Et[ei][:, :], in1=mk[:, c, :], op=ALU.mult),
                                reads=[b_E[ei], b_mk], writes=[b_P[ei]])
                            kb.op("pe", lambda e, ei=ei, c=c, hl=hl, bO=bO, nch=nch: e.matmul(
                                ps[:, bO, :], lhsT=v2[:, c, hl * 128:(hl + 1) * 128], rhs=Pt[ei][:, :],
                                start=(c == 0), stop=(c == nch - 1)),
                                reads=[b_v2[c // 4], b_P[ei]], writes=[PB[bO]], sig=(c == nch - 1))
                            kb.op("pe", lambda e, ei=ei, c=c, bD=bD, nch=nch: e.matmul(
                                ps[:, bD, :], lhsT=ones_b[:, :], rhs=Pt[ei][:, :],
                                start=(c == 0), stop=(c == nch - 1)),
                                reads=[b_ones, b_P[ei]], writes=[PB[bD]], sig=(c == nch - 1))
                        kb.op("dve", lambda e, bD=bD: e.reciprocal(out=rden[:, :], in_=ps[:, bD, :]),
                              reads=[PB[bD]], writes=[b_rden])
                        kb.op("dve", lambda e, bO=bO, h=h, g=g: e.tensor_tensor(
                            out=attnT[:, h, g * 512:(g + 1) * 512], in0=ps[:, bO, :], in1=rden[:, :], op=ALU.mult),
                            reads=[PB[bO], b_rden], writes=[b_attnT[h][g]])
                        if stop_after == "D1":
                            dump([(attnT[:, 0, 0:512], [b_attnT[0][0]], 512), (rden[:, :], [b_rden], 512)])
                            return nc
                scT.close()
        if stop_after == "D":
            dump([(attnT[:, 0, 0:1024], b_attnT[0], 1024), (attnT[:, 7, 0:1024], b_attnT[7], 1024),
                  (attnT[:, 3, 1024:2048], b_attnT[3], 1024)])
            return nc

        scCD.close()
        kb.barrier()
        b_hT_d = bufs(4, "hT_d")
        b_htok = bufs(16, "htok")

        def layer_norm(e_sb, src, bsrc, gam, bet, b_gb, outt, bout, xc, b_xc, junkf, b_jf, st, b_st):
            kb.op("dve", lambda e: e.tensor_scalar(out=junkf[:, :], in0=src, scalar1=1.0, scalar2=None, op0=ALU.mult,
                                                   op1=ALU.add, accum_out=st[:, 0:1]), reads=[bsrc], writes=[b_jf, b_st])
            kb.op("dve", lambda e: e.tensor_scalar(out=st[:, 1:2], in0=st[:, 0:1], scalar1=1.0 / D, scalar2=None,
                                                   op0=ALU.mult), writes=[b_st])
            kb.op("dve", lambda e: e.tensor_scalar(out=xc[:, :], in0=src, scalar1=st[:, 1:2], scalar2=None,
                                                   op0=ALU.subtract), reads=[bsrc, b_st], writes=[b_xc])
            kb.op("dve", lambda e: e.tensor_tensor(out=junkf[:, :], in0=xc[:, :], in1=xc[:, :], op=ALU.mult),
                  reads=[b_xc], writes=[b_jf])
            kb.op("dve", lambda e: e.tensor_scalar(out=junkf[:, :], in0=junkf[:, :], scalar1=1.0, scalar2=None,
                                                   op0=ALU.mult, op1=ALU.add, accum_out=st[:, 2:3]),
                  writes=[b_jf, b_st])
            kb.op("dve", lambda e: e.tensor_scalar(out=st[:, 3:4], in0=st[:, 2:3], scalar1=1.0 / D, scalar2=LN_EPS,
                                                   op0=ALU.mult, op1=ALU.add), writes=[b_st])
            kb.op("act", lambda e: e.activation(out=st[:, 4:5], in_=st[:, 3:4], func=AF.Sqrt), writes=[b_st])
            kb.op("dve", lambda e: e.reciprocal(out=st[:, 5:6], in_=st[:, 4:5]), writes=[b_st])
            kb.op("dve", lambda e: e.scalar_tensor_tensor(out=xc[:, :], in0=xc[:, :], scalar=st[:, 5:6], in1=gam,
                                                          op0=ALU.mult, op1=ALU.mult),
                  reads=[b_st, b_gb], writes=[b_xc])
            return kb.op("dve", lambda e: e.tensor_tensor(out=outt, in0=xc[:, :], in1=bet, op=ALU.add),
                         reads=[b_xc, b_gb], writes=[bout])

        with ExitStack() as sc:
            sbE = lambda name, shape, dt=F32: sc.enter_context(nc.sbuf_tensor(name, list(shape), dt))
            wo = sbE("wo", [128, 16, D], BF16)
            b_wo = Buf("wo")
            for c in range(4):
                kb.dma("pool", wo[:, 4 * c:4 * c + 4, :], w_out[:, 4 * c:4 * c + 4, :], writes=[b_wo])
            gb = sbE("gb1", [128, 2, D])
            b_gb = Buf("gb1")
            kb.dma("sp", gb[:, 0, :], lnp[0], writes=[b_gb])
            kb.dma("sp", gb[:, 1, :], lnp[1], writes=[b_gb])
            xt = [sbE("xt0", [128, D])] * 2
            b_xt = [Buf("xt")] * 2
            hpre2 = [sbE("hpre0", [128, D])] * 2
            b_hpre2 = [Buf("hpre")] * 2
            xc = sbE("xc", [128, D])
            b_xc = Buf("xc")
            junkf = sbE("junkf", [128, D])
            b_jf = Buf("junkf")
            hh = [sbE("hh0", [128, D])] * 2
            b_hh = [Buf("hh")] * 2
            st = sbE("st", [128, 8])
            b_st = Buf("st")
            hTs = sbE("hTs", [128, 16, 128], BF16)
            b_hTs = Buf("hTs")
            kb.dma("sp", xt[0][:, :], x_tok[0:128, :], writes=[b_xt[0]])
            for tt in range(16):
                tg = tt // 4
                xi = tt % 2
                hpre, b_hpre = hpre2[tt % 2], b_hpre2[tt % 2]
                for dt_ in range(4):
                    bk = next_bank([0, 1, 2, 7])
                    for kc in range(16):
                        if kc < 8:
                            lh = poolT[:, kc, tt * 128:(tt + 1) * 128]
                            rb = b_poolT[tg]
                        else:
                            lh = attnT[:, kc - 8, tt * 128:(tt + 1) * 128]
                            rb = b_attnT[kc - 8][tg]
                        kb.op("pe", lambda e, lh=lh, kc=kc, dt_=dt_, bk=bk, wo=wo: e.matmul(
                            ps[:, bk, :], lhsT=lh, rhs=wo[:, kc, dt_ * 512:(dt_ + 1) * 512],
                            start=(kc == 0), stop=(kc == 15)),
                            reads=[rb, b_wo], writes=[PB[bk]], sig=(kc == 15))
                    kb.op("dve", lambda e, xi=xi, dt_=dt_, bk=bk: e.scalar_tensor_tensor(
                        out=hpre[:, dt_ * 512:(dt_ + 1) * 512], in0=xt[xi][:, dt_ * 512:(dt_ + 1) * 512], scalar=ALPHA,
                        in1=ps[:, bk, :], op0=ALU.mult, op1=ALU.add),
                        reads=[b_xt[xi], PB[bk]], writes=[b_hpre])
                hi = tt % 2
                if tt + 1 < 16:
                    kb.dma("sp", xt[(tt + 1) % 2][:, :], x_tok[(tt + 1) * 128:(tt + 2) * 128, :],
                           writes=[b_xt[(tt + 1) % 2]])
                layer_norm(None, hpre[:, :], b_hpre, gb[:, 0, :], gb[:, 1, :], b_gb, hh[hi][:, :], b_hh[hi],
                           xc, b_xc, junkf, b_jf, st, b_st)
                kb.dma("sp", h_tok[tt * 128:(tt + 1) * 128, :], hh[hi][:, :], reads=[b_hh[hi]], writes=[b_htok[tt]])
                for c0 in range(0, 16, 4):
                    bk = next_bank([3, 4, 5, 6])
                    for cc in range(4):
                        kb.op("pe", lambda e, c0=c0, cc=cc, bk=bk, hi=hi: e.transpose(
                            ps[:, bk, cc * 128:(cc + 1) * 128], hh[hi][:, (c0 + cc) * 128:(c0 + cc + 1) * 128], ident),
                            reads=[b_hh[hi], b_cst], writes=[PB[bk]], sig=(cc == 3))
                    evac(hTs[:, c0:c0 + 4, :],
                         ps[:, bk, :].rearrange("p (c q) -> p c q", c=4), [PB[bk]], [b_hTs])
                for q4 in range(4):
                    kb.dma("sp", hT_d[:, 4 * q4:4 * q4 + 4, tt * 128:(tt + 1) * 128], hTs[:, 4 * q4:4 * q4 + 4, :],
                           reads=[b_hTs], writes=[b_hT_d[tg]])
        pers.close()
        if stop_after == "E":
            t1 = kb.dma("sp", dbg_out[:, 0:2048], h_tok[0:128, :], reads=[b_htok[0]])
            t2 = kb.dma("sp", dbg_out[:, 2048:4096], h_tok[1920:2048, :], reads=[b_htok[15]])
            kb.wait("sp", t1)
            kb.wait("sp", t2)
            return nc

        kb.barrier()
        b_W1 = bufs(32, "W1")
        with ExitStack() as sc:
            sbF = lambda name, shape, dt=F32: sc.enter_context(nc.sbuf_tensor(name, list(shape), dt))
            wqs = sbF("wqs", [128, 16, D], BF16)
            b_wqs = Buf("wqs")
            for c in range(4):
                kb.dma("pool", wqs[:, 4 * c:4 * c + 4, :], wq[:, 4 * c:4 * c + 4, :], writes=[b_wqs])
            skT = sbF("skT", [128, 2, 128], BF16)
            b_skT = Buf("skT")
            kb.dma("pool", skT[:], subkT, writes=[b_skT])
            hTg = [sbF("hTg0", [128, 16, 512], BF16)] * 2
            b_hTg = [Buf("hTg")] * 2
            qpT = sbF("qpT", [128, 16, 512], BF16)
            b_qpT = Buf("qpT")
            s_sb = sbF("s_sb", [128, 16, 128])
            s2_sb = sbF("s2_sb", [128, 16, 128])
            b_s, b_s2 = Buf("s"), Buf("s2")
            v16 = sbF("v16", [128, 16, 16])
            i16 = sbF("i16", [128, 16, 16], U32)
            i16f = sbF("i16f", [128, 16, 16])
            b_v16, b_i16, b_i16f = Buf("v16"), Buf("i16"), Buf("i16f")
            cand = sbF("cand", [128, 8, 256])
            cand2 = sbF("cand2", [128, 8, 256])
            b_cand, b_cand2 = Buf("cand"), Buf("cand2")
            best = sbF("best", [128, 8, 16])
            bidx = sbF("bidx", [128, 8, 16], U32)
            aiu = sbF("aiu", [128, 128], U32)
            biu = sbF("biu", [128, 128], U32)
            af = sbF("af", [128, 128])
            bf = sbF("bf", [128, 128])
            b_best, b_bidx, b_bidf, b_af, b_bf = Buf("best"), Buf("bidx"), Buf("bidf"), Buf("af"), Buf("bf")
            eb = sbF("eb", [128, 8, 16])
            zz = sbF("zz", [128, 16])
            b_eb, b_zz = Buf("eb"), Buf("zz")
            oh = sbF("oh", [128, 128, 16])
            b_oh = Buf("oh")
            IG = sbF("IG", [128, 3, 128])
            b_IG = Buf("IG")
            IGT = sbF("IGT", [128, 3, 128])
            b_IGT = Buf("IGT")
            eq2 = [sbF("eq0", [128, 32, 128], BF16)] * 2
            b_eq2 = [Buf("eq")] * 2
            At2 = [sbF("At0", [128, 32, 128], BF16)] * 2
            Bt2 = [sbF("Bt0", [128, 32, 128], BF16)] * 2
            b_At2, b_Bt2 = [Buf("At")] * 2, [Buf("Bt")] * 2
            ab_rr = [0]
            stg = [sbF("stg0", [128, 128, 64], BF16)] * 2
            b_stg = [Buf("stg")] * 2
            for tg in range(4):
                hx = hTg[tg % 2]
                bhx = b_hTg[tg % 2]
                for q4 in range(4):
                    kb.dma("sp", hx[:, 4 * q4:4 * q4 + 4, :], hT_d[:, 4 * q4:4 * q4 + 4, tg * 512:(tg + 1) * 512],
                           reads=[b_hT_d[tg]], writes=[bhx])
                for n in range(16):
                    proj_fm(lambda kc, n=n: wqs[:, kc, n * 128:(n + 1) * 128], b_wqs, hx, bhx, qpT[:, n, :], [b_qpT])
                for tt in range(4):
                    T = 4 * tg + tt
                    for n4 in range(4):
                        bk = next_bank([3, 4, 5, 6])
                        for nn in range(4):
                            n = 4 * n4 + nn
                            kb.op("pe", lambda e, n=n, nn=nn, bk=bk, tt=tt: e.matmul(
                                ps[:, bk, nn * 128:(nn + 1) * 128], lhsT=qpT[:, n, tt * 128:(tt + 1) * 128],
                                rhs=skT[:, n % 2, :], start=True, stop=True),
                                reads=[b_qpT, b_skT], writes=[PB[bk]], sig=(nn == 3))
                        evac(s_sb[:, 4 * n4:4 * n4 + 4, :], ps[:, bk, :].rearrange("p (c q) -> p c q", c=4),
                             [PB[bk]], [b_s])
                    bv, bi, bs2 = bufs(16, "v16n"), bufs(16, "i16n"), bufs(16, "s2n")
                    for n in range(16):
                        kb.op("dve", lambda e, n=n: e.max(out=v16[:, n, 0:8], in_=s_sb[:, n, :]),
                              reads=[b_s], writes=[bv[n]], deps=[b_v16.w] + list(b_v16.r.values()))
                    for n in range(16):
                        kb.op("dve", lambda e, n=n: e.max_index(out=i16[:, n, 0:8], in_max=v16[:, n, 0:8],
                                                               in_values=s_sb[:, n, :]),
                              reads=[b_s, bv[n]], writes=[bi[n]], deps=[b_i16.w] + list(b_i16.r.values()))
                    for n in range(16):
                        kb.op("dve", lambda e, n=n: e.match_replace(out=s2_sb[:, n, :], in_to_replace=v16[:, n, 0:8],
                                                                   in_values=s_sb[:, n, :], imm_value=NEG),
                              reads=[b_s, bv[n]], writes=[bs2[n]])
                    for n in range(16):
                        kb.op("dve", lambda e, n=n: e.max(out=v16[:, n, 8:16], in_=s2_sb[:, n, :]),
                              reads=[bs2[n]], writes=[bv[n]])
                    for n in range(16):
                        kb.op("dve", lambda e, n=n: e.max_index(out=i16[:, n, 8:16], in_max=v16[:, n, 8:16],
                                                               in_values=s2_sb[:, n, :]),
                              reads=[bs2[n], bv[n]], writes=[bi[n]])
                    kb.op("dve", lambda e: e.tensor_copy(out=i16f[:], in_=i16[:]), reads=bi, writes=[b_i16f, b_i16])
                    kb.op("dve", lambda e: e.tensor_tensor(
                        out=cand[:].rearrange("p h (a b) -> p h a b", a=16),
                        in0=sap(v16, [[32, 8], [1, 16], [0, 16]]),
                        in1=sap(v16, [[32, 8], [0, 16], [1, 16]], off=16), op=ALU.add),
                        reads=bv, writes=[b_cand, b_v16])
                    bb, bx_, bc2 = bufs(8, "besth"), bufs(8, "bidxh"), bufs(8, "cand2h")
                    for h in range(8):
                        kb.op("dve", lambda e, h=h: e.max(out=best[:, h, 0:8], in_=cand[:, h, :]),
                              reads=[b_cand], writes=[bb[h]], deps=[b_best.w] + list(b_best.r.values()))
                    for h in range(8):
                        kb.op("dve", lambda e, h=h: e.max_index(out=bidx[:, h, 0:8], in_max=best[:, h, 0:8],
                                                               in_values=cand[:, h, :]),
                              reads=[b_cand, bb[h]], writes=[bx_[h]], deps=[b_bidx.w] + list(b_bidx.r.values()))
                    for h in range(8):
                        kb.op("dve", lambda e, h=h: e.match_replace(out=cand2[:, h, :], in_to_replace=best[:, h, 0:8],
                                                                   in_values=cand[:, h, :], imm_value=NEG),
                              reads=[b_cand, bb[h]], writes=[bc2[h]])
                    for h in range(8):
                        kb.op("dve", lambda e, h=h: e.max(out=best[:, h, 8:16], in_=cand2[:, h, :]),
                              reads=[bc2[h]], writes=[bb[h]])
                    for h in range(8):
                        kb.op("dve", lambda e, h=h: e.max_index(out=bidx[:, h, 8:16], in_max=best[:, h, 8:16],
                                                               in_values=cand2[:, h, :]),
                              reads=[bc2[h], bb[h]], writes=[bx_[h]])
                    kb.op("dve", lambda e: e.tensor_copy(out=zz[:, 0:1], in_=best[:, 0, 0:1]),
                          reads=bb + bx_, writes=[b_best, b_bidx, b_zz])
                    kb.op("dve", lambda e: e.tensor_tensor(out=eb[:], in0=best[:], in1=sap(best, [[16, 8], [0, 16]]),
                                                           op=ALU.subtract), reads=[b_best], writes=[b_eb])
                    kb.op("act", lambda e: e.activation(out=eb[:], in_=eb[:], func=AF.Exp), writes=[b_eb])
                    kb.op("dve", lambda e: e.tensor_reduce(out=zz[:, 0:8], in_=eb[:], axis=AX.X, op=ALU.add),
                          reads=[b_eb], writes=[b_zz])
                    kb.op("dve", lambda e: e.reciprocal(out=zz[:, 8:16], in_=zz[:, 0:8]), writes=[b_zz])
                    kb.op("dve", lambda e: e.tensor_tensor(
                        out=IG[:, 2, :].rearrange("p (h r) -> p h r", h=8), in0=eb[:],
                        in1=sap(zz, [[1, 8], [0, 16]], off=8), op=ALU.mult),
                        reads=[b_eb, b_zz], writes=[b_IG])
                    kb.op("dve", lambda e: e.tensor_single_scalar(out=aiu[:], in_=bidx[:].rearrange("p h r -> p (h r)"),
                                                                  scalar=4, op=ALU.logical_shift_right),
                          reads=[b_bidx], writes=[b_bidf])
                    kb.op("dve", lambda e: e.tensor_single_scalar(out=biu[:], in_=bidx[:].rearrange("p h r -> p (h r)"),
                                                                  scalar=15, op=ALU.bitwise_and),
                          reads=[b_bidx], writes=[b_bidf])
                    kb.op("dve", lambda e: e.tensor_copy(out=af[:], in_=aiu[:]), reads=[b_bidf], writes=[b_af])
                    kb.op("dve", lambda e: e.tensor_copy(out=bf[:], in_=biu[:]), reads=[b_bidf], writes=[b_bf])
                    for which, sel, boff in ((0, af, 0), (1, bf, 16)):
                        bsel = b_af if which == 0 else b_bf
                        kb.op("dve", lambda e, sel=sel: e.tensor_tensor(
                            out=oh[:], in0=sap(cst, [[0, 128], [1, 16]], off=C_IOTA),
                            in1=sap(sel, [[1, 128], [0, 16]]), op=ALU.is_equal),
                            reads=[bsel, b_cst], writes=[b_oh])
                        kb.op("dve", lambda e, boff=boff: e.tensor_tensor(
                            out=oh[:].rearrange("p (h r) a -> p h r a", h=8),
                            in0=oh[:].rearrange("p (h r) a -> p h r a", h=8),
                            in1=sap(i16f, [[32, 8], [0, 16], [1, 16]], off=boff), op=ALU.mult),
                            reads=[b_i16f], writes=[b_oh])
                        kb.op("dve", lambda e, which=which: e.tensor_reduce(out=IG[:, which, :], in_=oh[:], axis=AX.X,
                                                                          op=ALU.add),
                              reads=[b_oh], writes=[b_IG])
                    bk = next_bank([3, 4, 5, 6])
                    for w3 in range(3):
                        kb.op("pe", lambda e, w3=w3, bk=bk: e.transpose(ps[:, bk, w3 * 128:(w3 + 1) * 128],
                                                                        IG[:, w3, :], ident),
                              reads=[b_IG, b_cst], writes=[PB[bk]], sig=(w3 == 2))
                    kb.op("act", lambda e, bk=bk: e.activation(out=IGT[:], in_=ps[:, bk, 0:384].rearrange(
                        "p (c q) -> p c q", c=3), func=AF.Copy), reads=[PB[bk]], writes=[b_IGT])
                    for sbk in range(2):
                        si = (2 * T + sbk) % 2
                        for s32 in range(2):
                            t0 = sbk * 64 + s32 * 32
                            ai_ = ab_rr[0] % 2
                            ab_rr[0] += 1
                            eq, b_eq = eq2[ai_], b_eq2[ai_]
                            At, b_At = At2[ai_], b_At2[ai_]
                            Bt, b_Bt = Bt2[ai_], b_Bt2[ai_]
                            kb.op("dve", lambda e, t0=t0: e.tensor_tensor(
                                out=eq[:], in0=sap(cst, [[0, 32], [1, 128]], off=C_IOTA),
                                in1=sap(IGT, [[1, 32], [0, 128]], off=0 * 128 + t0), op=ALU.is_equal),
                                reads=[b_IGT, b_cst], writes=[b_eq])
                            kb.op("dve", lambda e, t0=t0: e.tensor_tensor(
                                out=At[:], in0=eq[:], in1=sap(IGT, [[1, 32], [0, 128]], off=2 * 128 + t0), op=ALU.mult),
                                reads=[b_eq, b_IGT], writes=[b_At])
                            kb.op("dve", lambda e, t0=t0: e.tensor_tensor(
                                out=Bt[:], in0=sap(cst, [[0, 32], [1, 128]], off=C_IOTA),
                                in1=sap(IGT, [[1, 32], [0, 128]], off=1 * 128 + t0), op=ALU.is_equal),
                                reads=[b_IGT, b_cst], writes=[b_Bt])
                            for t4 in range(8):
                                bk = next_bank([0, 1, 2, 7])
                                for q4 in range(4):
                                    tl = 4 * t4 + q4
                                    kb.op("pe", lambda e, tl=tl, q4=q4, bk=bk: e.matmul(
                                        ps[:, bk, q4 * 128:(q4 + 1) * 128], lhsT=At[:, tl, :], rhs=Bt[:, tl, :],
                                        start=True, stop=True),
                                        reads=[b_At, b_Bt], writes=[PB[bk]], sig=(q4 == 3))
                                kb.op("act", lambda e, t4=t4, bk=bk, si=si, s32=s32: e.activation(
                                    out=sap(stg[si], [[1, 4], [64, 128]], off=s32 * 32 + 4 * t4),
                                    in_=ps[:, bk, :].rearrange("p (t j) -> p t j", t=4), func=AF.Copy),
                                    reads=[PB[bk]], writes=[b_stg[si]])
                        kb.dma("sp", W1[2 * T + sbk], stg[si][:], reads=[b_stg[si]], writes=[b_W1[2 * T + sbk]])
                    if stop_after == "F2" and T == 0:
                        wchk = sbF("wchk", [128, 1024], BF16)
                        b_wchk = Buf("wchk")
                        kb.dma("sp", wchk[:], W1[0, :, 0:16, :].rearrange("i j t -> i (j t)"), reads=[b_W1[0]], writes=[b_wchk])
                        dump([(wchk[:, :], [b_wchk], 1024), (IG[:, 0, :], [b_IG], 128), (IG[:, 1, :], [b_IG], 128),
                              (IG[:, 2, :], [b_IG], 128)])
                        return nc
                    if stop_after == "F" and T == 0:
                        dump([(IG[:, 0, :], [b_IG], 128), (IG[:, 1, :], [b_IG], 128), (IG[:, 2, :], [b_IG], 128),
                              (s_sb[:, 0, :], [b_s], 128), (s_sb[:, 1, :], [b_s], 128)])
                        return nc

        kb.barrier()
        if stop_after == "Fend":
            kb.barrier()
            return nc
        NJ = 4
        NR = 0 if stop_after == "G4" else (2 if stop_after in ("G2", "G3") else 128 // NJ)
        for half in range(2):
            kb.barrier()
            with ExitStack() as sc:
                sbG = lambda name, shape, dt=F32: sc.enter_context(nc.sbuf_tensor(name + "_h%d" % half, list(shape), dt))
                hTh = sbG("hTh", [128, 16, 1024], BF16)
                b_hTh = Buf("hTh")
                for tg2 in range(2):
                    tg = 2 * half + tg2
                    for q4 in range(4):
                        kb.dma("sp", hTh[:, 4 * q4:4 * q4 + 4, tg2 * 512:(tg2 + 1) * 512],
                               hT_d[:, 4 * q4:4 * q4 + 4, tg * 512:(tg + 1) * 512],
                               reads=[b_hT_d[tg]], writes=[b_hTh])
                accG = sbG("accG", [128, 8, D])
                b_accG = [bufs(2, "accG%d" % t) for t in range(8)]
                with ExitStack() as sc2:
                    sbH = lambda name, shape, dt=F32: sc2.enter_context(nc.sbuf_tensor(name + "_h%d" % half, list(shape), dt))
                    Wr = [sbH("Wr%d" % i, [128, 16, NJ, 64], BF16) for i in range(2)]
                    b_Wr = bufs(2, "Wr")
                    vr = [sbH("vr%d" % i, [128, NJ, D], BF16) for i in range(2)]
                    b_vr = bufs(2, "vr")
                    uj = [sbH("uj%d" % i, [128, 16, 128], BF16) for i in range(2)]
                    b_uj = bufs(2, "uj")
                    Gr = [sbH("Gr%d" % i, [128, NJ, 1024], BF16) for i in range(2)]
                    b_Gr = bufs(2, "Gr")
                    ga = [sbH("ga%d" % i, [128, 512], BF16) for i in range(2)]
                    b_ga = bufs(2, "ga")
                    urr = [0]
                    grr = [0]

                    def phaseA(r):
                        ri = r % 2
                        j0 = r * NJ
                        for q4 in range(4):
                            b0_ = 16 * half + 4 * q4
                            kb.dma("sp", Wr[ri][:, 4 * q4:4 * q4 + 4], W1[b0_:b0_ + 4, :, j0:j0 + NJ, :].rearrange(
                                "b i j t -> i b j t"), reads=b_W1[b0_:b0_ + 4], writes=[b_Wr[ri]])
                        for jj in range(NJ):
                            ui = urr[0] % 2
                            urr[0] += 1
                            kb.dma("pool", uj[ui][:], uT[j0 + jj], writes=[b_uj[ui]])
                            for tg2 in range(2):
                                bk = next_bank([0, 1])
                                for kc in range(16):
                                    kb.op("pe", lambda e, kc=kc, ui=ui, tg2=tg2, bk=bk: e.matmul(
                                        ps[:, bk, :], lhsT=uj[ui][:, kc, :], rhs=hTh[:, kc, tg2 * 512:(tg2 + 1) * 512],
                                        start=(kc == 0), stop=(kc == 15)),
                                        reads=[b_uj[ui], b_hTh], writes=[PB[bk]], sig=(kc == 15))
                                gi = grr[0] % 2
                                grr[0] += 1
                                kb.op("act", lambda e, gi=gi, bk=bk: e.activation(out=ga[gi][:, :], in_=ps[:, bk, :],
                                                                                 func=AF.Gelu),
                                      reads=[PB[bk]], writes=[b_ga[gi]])
                                kb.op("dve", lambda e, gi=gi, ri=ri, jj=jj, tg2=tg2: e.tensor_tensor(
                                    out=Gr[ri][:, jj, tg2 * 512:(tg2 + 1) * 512].rearrange("p (b t) -> p b t", b=8),
                                    in0=ga[gi][:, :].rearrange("p (b t) -> p b t", b=8),
                                    in1=Wr[ri][:, tg2 * 8:(tg2 + 1) * 8, jj, :], op=ALU.mult),
                                    reads=[b_ga[gi], b_Wr[ri]], writes=[b_Gr[ri]])
                        kb.dma("pool", vr[ri][:], vL[j0:j0 + NJ].rearrange("j i d -> i j d"), writes=[b_vr[ri]])

                    def phaseB(r):
                        ri = r % 2
                        for tt in range(8):
                            for dh in range(2):
                                b0 = 2 + 2 * ((tt * 2 + dh) % 3)
                                for jj in range(NJ):
                                    for dq in range(2):
                                        kb.op("pe", lambda e, jj=jj, dq=dq, tt=tt, dh=dh, b0=b0: e.matmul(
                                            ps[:, b0 + dq, :], lhsT=Gr[ri][:, jj, tt * 128:(tt + 1) * 128],
                                            rhs=vr[ri][:, jj, dh * 1024 + dq * 512:dh * 1024 + (dq + 1) * 512],
                                            start=(jj == 0), stop=(jj == NJ - 1)),
                                            reads=[b_Gr[ri], b_vr[ri]], writes=[PB[b0 + dq]],
                                            sig=(jj == NJ - 1 and dq == 1))
                                pin = ps[:, b0:b0 + 2, :].rearrange("p a b -> p (a b)")
                                aout = accG[:, tt, dh * 1024:(dh + 1) * 1024]
                                if r == 0:
                                    kb.op("dve", lambda e, pin=pin, aout=aout: e.tensor_copy(out=aout, in_=pin),
                                          reads=[PB[b0], PB[b0 + 1]], writes=[b_accG[tt][dh]])
                                else:
                                    kb.op("dve", lambda e, pin=pin, aout=aout: e.tensor_tensor(
                                        out=aout, in0=aout, in1=pin, op=ALU.add),
                                        reads=[PB[b0], PB[b0 + 1]], writes=[b_accG[tt][dh]])

                    if stop_after == "G1":
                        phaseA(0)
                        phaseB(0)
                        dump([(accG[:, 0, 0:1024], b_accG[0], 1024), (accG[:, 7, 1024:2048], b_accG[7], 1024),
                              (Gr[0][:, 0, :], [b_Gr[0]], 1024), (Gr[0][:, 3, :], [b_Gr[0]], 1024)])
                        return nc
                    if NR > 0:
                        phaseA(0)
                    for r in range(NR):
                        if r + 1 < NR:
                            phaseA(r + 1)
                        phaseB(r)
                kb.barrier()
                if stop_after == "G3":
                    dump([(accG[:, 0, 0:1024], b_accG[0], 1024), (accG[:, 7, 1024:2048], b_accG[7], 1024)])
                    return nc
                with ExitStack() as sc3:
                    sbL = lambda name, shape, dt=F32: sc3.enter_context(nc.sbuf_tensor(name + "_h%d" % half, list(shape), dt))
                    gb2 = sbL("gb2", [128, 2, D])
                    b_gb2 = Buf("gb2")
                    kb.dma("sp", gb2[:, 0, :], lnp[2], writes=[b_gb2])
                    kb.dma("sp", gb2[:, 1, :], lnp[3], writes=[b_gb2])
                    hres = [sbL("hres%d" % i, [128, D]) for i in range(2)]
                    b_hres = bufs(2, "hres")
                    xc2 = sbL("xc2", [128, D])
                    b_xc2 = Buf("xc2")
                    jf2 = sbL("jf2", [128, D])
                    b_jf2 = Buf("jf2")
                    yo = [sbL("yo%d" % i, [128, D]) for i in range(2)]
                    b_yo = bufs(2, "yo")
                    st2 = sbL("st2", [128, 8])
                    b_st2 = Buf("st2")
                    outs = []
                    kb.dma("sp", hres[0][:, :], h_tok[(8 * half) * 128:(8 * half + 1) * 128, :],
                           reads=[b_htok[8 * half]], writes=[b_hres[0]])
                    for tt in range(8):
                        T = 8 * half + tt
                        hi = tt % 2
                        if tt + 1 < 8:
                            kb.dma("sp", hres[1 - hi][:, :], h_tok[(T + 1) * 128:(T + 2) * 128, :],
                                   reads=[b_htok[T + 1]], writes=[b_hres[1 - hi]])
                        kb.op("dve", lambda e, hi=hi, tt=tt: e.scalar_tensor_tensor(
                            out=hres[hi][:, :], in0=hres[hi][:, :], scalar=ALPHA, in1=accG[:, tt, :],
                            op0=ALU.mult, op1=ALU.add),
                            reads=b_accG[tt], writes=[b_hres[hi]])
                        layer_norm(None, hres[hi][:, :], b_hres[hi], gb2[:, 0, :], gb2[:, 1, :], b_gb2,
                                   yo[hi][:, :], b_yo[hi], xc2, b_xc2, jf2, b_jf2, st2, b_st2)
                        outs.append(kb.dma("sp", y[T * 128:(T + 1) * 128, :], yo[hi][:, :], reads=[b_yo[hi]]))
                    for t in outs:
                        kb.wait("sp", t)
                    if stop_after in ("G2", "G4"):
                        dump([(accG[:, 0, 0:1024], b_accG[0], 1024), (accG[:, 7, 1024:2048], b_accG[7], 1024),
                              (yo[1][:, 0:1024], [b_yo[1]], 1024), (hres[1][:, 0:1024], [b_hres[1]], 1024)])
                        return nc
        print("instructions:", kb.nins, "sbuf remaining:", nc.sbuf_bytes_remaining)
    return nc


def _t5_bucket(dist):
    dist = np.asarray(dist)
    d = np.maximum(dist, 1).astype(np.float32)
    large = 16 + (np.log(d / np.float32(16)) / np.float32(np.log(128 / 16)) * np.float32(16)).astype(np.int32)
    large = np.minimum(large, 31)
    return np.where(dist < 16, dist, large)


def _prep_shared(w_in, pool_w, pool_scale, rel_bias, w_out, ln1_g, ln1_b, peer_wq, peer_subkeys, peer_u, peer_v,
                 ln2_g, ln2_b):
    f = np.float32
    w = w_in[0]
    cols = []
    for c in range(8):
        cols.append(w[:, c * 128:(c + 1) * 128])
    for c in range(8):
        cols.append(w[:, 1024 + c * 128:1024 + (c + 1) * 128])
    for c in range(8):
        cols.append(w[:, 2048 + c * 128:2048 + (c + 1) * 128])
    for c in range(8):
        cols.append(w[:, 4096 + c * 128:4096 + (c + 1) * 128])
    ki = w[:, 5120:5184]
    cols.append(np.concatenate([ki, ki], axis=1))
    w_fm = np.stack([c.reshape(16, 128, 128).transpose(1, 0, 2) for c in cols]).astype(f)
    wv = w[:, 3072:4096]
    w_v = np.stack([wv[:, hg * 256:(hg + 1) * 256].reshape(16, 128, 256).transpose(1, 0, 2) for hg in range(4)])
    w_wi = w[:, 5184:5200].reshape(16, 128, 16).transpose(1, 0, 2)
    pw = pool_w[0].reshape(4, 2, 128, 256).transpose(2, 0, 1, 3)
    kk = np.arange(128)[:, None]
    qq = np.arange(128)[None, :]
    bt = np.zeros((128, 2, 8, 128), f)
    for dl in range(2):
        bkt = _t5_bucket(np.maximum(dl * 128 + qq - kk, 0))
        bt[:, dl, :, :] = rel_bias[bkt].transpose(0, 2, 1)
    wo = w_out[0].reshape(16, 128, D).transpose(1, 0, 2)
    lnp = np.stack([np.broadcast_to(a[0][None, :], (128, D)) for a in (ln1_g, ln1_b, ln2_g, ln2_b)])
    wqh = peer_wq[0].reshape(16, 128, D).transpose(1, 0, 2)
    skT = peer_subkeys[0].transpose(2, 0, 1)
    u = peer_u[0].reshape(128, 128, 16, 128)
    uT = u.transpose(1, 3, 2, 0)
    vv = peer_v[0].reshape(128, 128, D).transpose(1, 0, 2)
    c = lambda a: np.ascontiguousarray(a, dtype=f)
    return dict(w_fm=c(w_fm), w_v=c(w_v), w_wi=c(w_wi), pool_w=c(pw), biasT=c(bt), w_out=c(wo), lnp=c(lnp),
                wq=c(wqh), subkT=c(skT), uT=c(uT), vL=c(vv))


def _consts(hf, pool_scale, rel_bias):
    f = np.float32
    cst = np.zeros((128, 1024), f)
    cst[:, 0:128] = np.eye(128, dtype=f)
    cst[:, 128:256] = np.arange(128, dtype=f)[None, :]
    qq = np.arange(128)[:, None]
    kk = np.arange(128)[None, :]
    cst[:, 256:384] = np.where(kk <= qq, 0.0, NEG).astype(f)
    valid = 1.0 if hf == 1 else 0.0
    cst[:, 384] = valid
    cst[:, 385] = (valid - 1.0) * 1.0e30
    cst[:, 392:400] = rel_bias[31][None, :]
    for gq, wwin in enumerate((2, 4, 8, 16)):
        pos = np.arange(16)
        if hf == 0:
            corr = wwin / np.minimum(pos + 1, wwin).astype(f)
        else:
            corr = np.ones(16, f)
        cst[:, 400 + 16 * gq:400 + 16 * gq + 16] = corr[None, :]
    cst[:, 464:472] = pool_scale[0].reshape(8, 128).T
    cst[:, 480:512] = (2.0 ** -np.arange(32, dtype=np.float64)).astype(f)[None, :]
    return cst


def _core_inputs(x, shared, pool_scale, rel_bias):
    in_maps = []
    for c in range(8):
        b, hf = c // 2, c % 2
        own = x[b, hf * TOK:(hf + 1) * TOK]
        prev = x[b, 0:TOK] if hf == 1 else np.zeros_like(own)
        xT = np.stack([prev.T.reshape(16, 128, TOK).transpose(1, 0, 2), own.T.reshape(16, 128, TOK).transpose(1, 0, 2)])
        m = dict(shared)
        m["xT"] = np.ascontiguousarray(xT, dtype=np.float32)
        m["x_tok"] = np.ascontiguousarray(own, dtype=np.float32)
        m["consts"] = _consts(hf, pool_scale, rel_bias)
        in_maps.append(m)
    return in_maps


def kernel(x, w_in, pool_w, pool_scale, rel_bias, w_out, ln1_g, ln1_b, peer_wq, peer_subkeys, peer_u, peer_v,
           ln2_g, ln2_b):
    args = [np.asarray(a, dtype=np.float32) for a in (x, w_in, pool_w, pool_scale, rel_bias, w_out, ln1_g, ln1_b,
                                                      peer_wq, peer_subkeys, peer_u, peer_v, ln2_g, ln2_b)]
    (x, w_in, pool_w, pool_scale, rel_bias, w_out, ln1_g, ln1_b, peer_wq, peer_subkeys, peer_u, peer_v,
     ln2_g, ln2_b) = args
    shared = _prep_shared(w_in, pool_w, pool_scale, rel_bias, w_out, ln1_g, ln1_b, peer_wq, peer_subkeys,
                          peer_u, peer_v, ln2_g, ln2_b)
    in_maps = _core_inputs(x, shared, pool_scale, rel_bias)
    nc = build_nc()
    res = run_bass_kernel_spmd(nc, in_maps, core_ids=list(range(8)))
    out = np.zeros((4, S, D), np.float32)
    for c in range(8):
        b, hf = c // 2, c % 2
        out[b, hf * TOK:(hf + 1) * TOK] = res.results[c]["y"]
    return out
```

```python
import numpy as np
from contextlib import ExitStack
import concourse.bass as bass
import concourse.mybir as mybir
from concourse.bass_utils import run_bass_kernel_spmd

F32 = mybir.dt.float32
BF16 = mybir.dt.bfloat16
U32 = mybir.dt.uint32
ALU = mybir.AluOpType
AF = mybir.ActivationFunctionType
AX = mybir.AxisListType

D = 2048
S = 4096
TOK = 2048
NEG = -1.0e30
ALPHA = 2.0 ** 0.25
LN_EPS = 1e-5
NIT = 16
TOPK = 256
ATT_SCALE = 128.0 ** -0.5
NSLOT = 6


class Buf:
    __slots__ = ("w", "r", "name")

    def __init__(self, name=""):
        self.w = None
        self.r = {}
        self.name = name


class KB:
    def __init__(self, nc, es):
        self.nc = nc
        self.engs = {"pe": nc.tensor, "dve": nc.vector, "act": nc.scalar, "pool": nc.gpsimd, "sp": nc.sync}
        self.psem = {e: es.enter_context(nc.semaphore("prog_" + e)) for e in ["pe", "dve", "act", "pool"]}
        self.cnt = {e: 0 for e in self.psem}
        self.seen = {e: {} for e in self.engs}
        self.pending = {e: [] for e in self.engs}
        self.dslots = {q: [[es.enter_context(nc.semaphore("dq_%s_%d" % (q, i))), 0, "dq_%s_%d" % (q, i)]
                           for i in range(NSLOT)] for q in ["sp", "pool"]}
        self.dnext = {q: 0 for q in self.dslots}
        self.nins = 0

    def wait(self, e, tok):
        if tok is None:
            return
        sem, val, key = tok
        if self.seen[e].get(key, 0) >= val:
            return
        self.engs[e].wait_ge(sem, val)
        self.seen[e][key] = val

    def _deps(self, e, reads, writes, deps):
        for b in reads:
            self.wait(e, b.w)
        for b in writes:
            self.wait(e, b.w)
            for t in b.r.values():
                self.wait(e, t)
        for t in deps:
            self.wait(e, t)

    def op(self, e, fn, reads=(), writes=(), deps=(), sig=True):
        self._deps(e, reads, writes, deps)
        ins = fn(self.engs[e])
        self.nins += 1
        if not sig:
            self.pending[e].append((list(reads), list(writes)))
            return None
        self.cnt[e] += 1
        ins.then_inc(self.psem[e], 1)
        key = "prog_" + e
        tok = (self.psem[e], self.cnt[e], key)
        allr = list(reads)
        allw = list(writes)
        for (r, w) in self.pending[e]:
            allr += r
            allw += w
        self.pending[e] = []
        for b in allw:
            b.w = tok
            b.r = {}
        for b in allr:
            if b not in allw:
                b.r[key] = tok
        return tok

    def barrier(self):
        toks = []
        for e in self.psem:
            if self.cnt[e] > 0:
                toks.append((self.psem[e], self.cnt[e], "prog_" + e))
        for q in self.dslots:
            for sem, cnt, key in self.dslots[q]:
                if cnt > 0:
                    toks.append((sem, cnt, key))
        for e in self.engs:
            for t in toks:
                self.wait(e, t)

    def dma(self, q, out, in_, reads=(), writes=(), deps=()):
        self._deps(q, reads, writes, deps)
        slot = self.dslots[q][self.dnext[q]]
        self.dnext[q] = (self.dnext[q] + 1) % NSLOT
        sem, cnt, key = slot
        if cnt > 0:
            self.wait(q, (sem, cnt, key))
        self.engs[q].dma_start(out=out, in_=in_).then_inc(sem, 16)
        self.nins += 1
        slot[1] = cnt + 16
        tok = (sem, cnt + 16, key)
        for b in writes:
            b.w = tok
            b.r = {}
        for b in reads:
            b.r[key] = tok
        return tok


def sap(t, dims, off=0, parts=128, p0=0):
    fs = 1
    for s_ in t.shape[1:]:
        fs *= int(s_)
    return bass.AP(t, p0 * fs + off, [[fs, parts]] + [[int(a), int(b)] for a, b in dims])


def bufs(n, name=""):
    return [Buf("%s%d" % (name, i)) for i in range(n)]


def build_nc(stop_after=None, small_peer=False):
    nc = bass.Bass("TRN2", target_bir_lowering=False)
    dbg = {}

    def din(name, shape, dt=F32):
        return nc.dram_tensor(name, list(shape), dt, kind="ExternalInput").ap()

    xT = din("xT", [2, 4, 128, 16, 512])
    x_tok = din("x_tok", [TOK, D])
    w_fm = din("w_fm", [33, 128, 16, 128])
    w_v = din("w_v", [4, 128, 16, 256])
    w_wi = din("w_wi", [128, 16, 16])
    pool_w = din("pool_w", [128, 4, 2, 256])
    consts = din("consts", [128, 1024])
    biasT = din("biasT", [128, 2, 8, 128])
    w_out = din("w_out", [128, 16, D])
    lnp = din("lnp", [4, 128, D])
    wq = din("wq", [128, 16, D])
    subkT = din("subkT", [128, 2, 128])
    uT = din("uT", [8 if small_peer else 128, 128, 16, 128])
    vL = din("vL", [8 if small_peer else 128, 128, D])
    y = nc.dram_tensor("y", [TOK, D], F32, kind="ExternalOutput").ap()
    maskT_d = nc.dram_tensor("maskT_d", [4, 128, 32, 512], BF16).ap()
    hT_d = nc.dram_tensor("hT_d", [128, 16, TOK], BF16).ap()
    h_tok = nc.dram_tensor("h_tok", [TOK, D], F32).ap()
    W1 = nc.dram_tensor("W1", [32, 128, 128, 64], BF16).ap()
    if stop_after is not None:
        dbg_out = nc.dram_tensor("dbg", [128, 8192], F32, kind="ExternalOutput").ap()

    with ExitStack() as es:
        kb = KB(nc, es)
        sb = lambda name, shape, dt=F32: es.enter_context(nc.sbuf_tensor(name, list(shape), dt))
        ps = es.enter_context(nc.psum_tensor("ps", [128, 8, 512], F32))
        PB = bufs(8, "psb")

        dbg_stg = sb("dbg_stg", [128, 1024]) if stop_after is not None else None
        cst = sb("cst", [128, 1024])
        b_cst = Buf("cst")
        kb.dma("sp", cst[:], consts, writes=[b_cst])
        C_ID = 0
        C_IOTA = 128
        C_TRI = 256
        C_FLAG = 384
        C_B31 = 392
        C_CORR = 400
        C_PSC = 464
        C_PW2 = 480
        ident = cst[:, C_ID:C_ID + 128]
        iota = cst[:, C_IOTA:C_IOTA + 128]
        tri = cst[:, C_TRI:C_TRI + 128]
        bT = sb("bT", [128, 2, 8, 128])
        b_bT = Buf("bT")
        kb.dma("sp", bT[:], biasT, writes=[b_bT])
        ones_b = sb("ones_b", [128, 128], BF16)
        b_ones = Buf("ones")
        kb.op("pool", lambda e: e.memset(ones_b[:], 1.0), writes=[b_ones])

        evac_rr = [0]

        def evac(out, in_, reads, writes, scale=None):
            evac_rr[0] ^= 1
            if scale is not None:
                return kb.op("act", lambda e: e.activation(out=out, in_=in_, func=AF.Copy, scale=scale),
                             reads=reads, writes=writes)
            if evac_rr[0]:
                return kb.op("act", lambda e: e.activation(out=out, in_=in_, func=AF.Copy), reads=reads, writes=writes)
            return kb.op("dve", lambda e: e.tensor_copy(out=out, in_=in_), reads=reads, writes=writes)

        pb_rr = [0]

        def next_bank(choices):
            pb_rr[0] += 1
            return choices[pb_rr[0] % len(choices)]

        def dump(ap_list):
            stg = dbg_stg
            b_stg = Buf("dbgstg")
            col = 0
            for ap, bl, n in ap_list:
                kb.op("dve", lambda e, ap=ap, n=n: e.tensor_copy(out=stg[:, 0:n], in_=ap),
                      reads=bl, writes=[b_stg])
                t = kb.dma("sp", dbg_out[:, col:col + n], stg[:, 0:n], reads=[b_stg])
                kb.wait("sp", t)
                col += n

        xg_t = [None, None]
        b_xg = bufs(2, "xg")
        xg_rr = [0]

        def load_xg(s, tg):
            i = xg_rr[0]
            xg_rr[0] ^= 1
            kb.dma("pool", xg_t[i][:], xT[s, tg], writes=[b_xg[i]])
            return xg_t[i], b_xg[i]

        def proj_fm(wt, wb, xg, bx, out_ap, out_bufs, scale=None):
            bk = next_bank([0, 1, 2, 7])
            for kc in range(16):
                kb.op("pe", lambda e, kc=kc: e.matmul(ps[:, bk, :], lhsT=wt(kc), rhs=xg[:, kc, :],
                                                      start=(kc == 0), stop=(kc == 15)),
                      reads=[wb, bx], writes=[PB[bk]], sig=(kc == 15))
            return evac(out_ap, ps[:, bk, :], [PB[bk]], out_bufs, scale=scale)

        phA = ExitStack()
        es.enter_context(phA)
        xg_t[0] = phA.enter_context(nc.sbuf_tensor("xgA0", [128, 16, 512], BF16))
        xg_t[1] = phA.enter_context(nc.sbuf_tensor("xgA1", [128, 16, 512], BF16))
        qiT = phA.enter_context(nc.sbuf_tensor("qiT", [128, 8, TOK], BF16))
        b_qiT = bufs(4, "qiT")
        kiT = phA.enter_context(nc.sbuf_tensor("kiT", [128, 2 * TOK], BF16))
        b_kiT = bufs(8, "kiT")
        widx = phA.enter_context(nc.sbuf_tensor("widx", [128, 16, 16], F32))
        b_widx = bufs(16, "widx")
        with ExitStack() as sc:
            wqi = sc.enter_context(nc.sbuf_tensor("wqi", [128, 8, 16, 128], BF16))
            b_wqi = Buf("wqi")
            wki = sc.enter_context(nc.sbuf_tensor("wki", [128, 16, 128], BF16))
            b_wki = Buf("wki")
            wwi = sc.enter_context(nc.sbuf_tensor("wwi", [128, 16, 16], BF16))
            b_wwi = Buf("wwi")
            kb.dma("pool", wki[:], w_fm[32], writes=[b_wki])
            for c in range(8):
                kb.dma("pool", wqi[:, c], w_fm[24 + c], writes=[b_wqi])
            kb.dma("pool", wwi[:], w_wi, writes=[b_wwi])
            for s in range(2):
                for tg in range(4):
                    xg, bx = load_xg(s, tg)
                    kg = s * 4 + tg
                    proj_fm(lambda kc: wki[:, kc, :], b_wki, xg, bx, kiT[:, kg * 512:(kg + 1) * 512], [b_kiT[kg]])
                    if stop_after == "A1":
                        dump([(kiT[:, 0:512], b_kiT[0:1], 512), (xg[:, 0, :], [bx], 512)])
                        return nc
                    if s == 1:
                        for c in range(8):
                            proj_fm(lambda kc, c=c: wqi[:, c, kc, :], b_wqi, xg, bx,
                                    qiT[:, c, tg * 512:(tg + 1) * 512], [b_qiT[tg]])
                        for tt in range(4):
                            bk = next_bank([0, 1, 2, 7])
                            for kc in range(16):
                                kb.op("pe", lambda e, kc=kc, tt=tt: e.matmul(
                                    ps[:, bk, 0:16], lhsT=xg[:, kc, tt * 128:(tt + 1) * 128], rhs=wwi[:, kc, :],
                                    start=(kc == 0), stop=(kc == 15)),
                                    reads=[b_wwi, bx], writes=[PB[bk]], sig=(kc == 15))
                            evac(widx[:, tg * 4 + tt, :], ps[:, bk, 0:16], [PB[bk]], [b_widx[tg * 4 + tt]])
        if stop_after == "A":
            dump([(kiT[:, 0:2048], b_kiT[0:4], 2048), (kiT[:, 2048:4096], b_kiT[4:8], 2048),
                  (qiT[:, 0, 0:2048], b_qiT, 2048), (widx[:, :, :].rearrange("p a b -> p (a b)"), b_widx, 256)])
            return nc

        kb.barrier()
        b_maskd = bufs(4, "maskd")
        with ExitStack() as sc:
            sbB = lambda name, shape, dt=F32: sc.enter_context(nc.sbuf_tensor(name, list(shape), dt))
            score2 = [sbB("score%d" % p_, [128, 4096]) for p_ in range(2)]
            b_score2 = [bufs(8, "score%d_" % p_) for p_ in range(2)]
            maskf = sbB("maskf", [128, 4096])
            b_maskf = Buf("maskf")
            junkA = sbB("junkA", [128, 4096], BF16)
            b_junkA = Buf("junkA")
            Rt = [sbB("R%d" % i_, [128, 512]) for i_ in range(3)]
            b_R = bufs(3, "R")
            acc = sbB("accB", [128, 512])
            b_acc = Buf("acc")
            mx2 = [sbB("mxall%d" % p_, [128, 16]) for p_ in range(2)]
            b_mx2 = bufs(2, "mx")
            sm2 = [sbB("smallB%d" % p_, [128, 64]) for p_ in range(2)]
            b_sm2 = bufs(2, "small")
            b_nm2 = bufs(2, "negmid")
            b_cs2 = bufs(2, "cs")
            mT = sbB("maskTg", [128, 32, 512], BF16)
            b_mT = Buf("maskTg")
            r_rr = [0]

            def make_units(i):
                g = i // 4
                p_ = i % 2
                score, b_score, mxall, b_mx = score2[p_], b_score2[p_], mx2[p_], b_mx2[p_]
                NK = (17 + i) * 128
                nkt = (NK + 511) // 512
                units = []
                for kt in range(nkt):
                    wk = min(512, NK - kt * 512)
                    direct = kt >= 4
                    for hn, h in enumerate([0, 2, 4, 6, 8, 10, 12, 14, 1, 3, 5, 7, 9, 11, 13, 15]):
                        def unit(kt=kt, wk=wk, direct=direct, hn=hn, h=h):
                            cp, r0 = h // 2, 64 * (h % 2)
                            bk = next_bank([0, 1, 2])
                            kb.op("pe", lambda e: e.matmul(
                                ps[:, bk, 0:wk], lhsT=qiT[r0:r0 + 64, cp, i * 128:(i + 1) * 128],
                                rhs=kiT[r0:r0 + 64, kt * 512:kt * 512 + wk], start=True, stop=True),
                                reads=[b_qiT[g], b_kiT[kt]], writes=[PB[bk]])
                            ri = r_rr[0] % 3
                            r_rr[0] += 1
                            kb.op("act", lambda e: e.activation(
                                out=Rt[ri][:, 0:wk], in_=ps[:, bk, 0:wk], func=AF.Relu),
                                reads=[PB[bk]], writes=[b_R[ri]])
                            wcol = widx[:, i, h:h + 1]
                            if hn == 0:
                                kb.op("dve", lambda e: e.tensor_scalar(
                                    out=acc[:, 0:wk], in0=Rt[ri][:, 0:wk], scalar1=wcol, scalar2=None, op0=ALU.mult),
                                    reads=[b_R[ri], b_widx[i]], writes=[b_acc])
                            elif hn == 15 and direct:
                                kb.op("dve", lambda e: e.scalar_tensor_tensor(
                                    out=score[:, kt * 512:kt * 512 + wk], in0=Rt[ri][:, 0:wk], scalar=wcol,
                                    in1=acc[:, 0:wk], op0=ALU.mult, op1=ALU.add),
                                    reads=[b_R[ri], b_widx[i], b_acc], writes=[b_score[kt]])
                            else:
                                kb.op("dve", lambda e: e.scalar_tensor_tensor(
                                    out=acc[:, 0:wk], in0=Rt[ri][:, 0:wk], scalar=wcol,
                                    in1=acc[:, 0:wk], op0=ALU.mult, op1=ALU.add),
                                    reads=[b_R[ri], b_widx[i]], writes=[b_acc])
                            if hn == 15:
                                src = score[:, kt * 512:kt * 512 + wk] if direct else acc[:, 0:wk]
                                bsrc = b_score[kt] if direct else b_acc
                                kb.op("dve", lambda e: e.tensor_reduce(
                                    out=mxall[:, kt:kt + 1], in_=src, axis=AX.X, op=ALU.max),
                                    reads=[bsrc], writes=[b_mx])
                                kb.op("dve", lambda e: e.tensor_reduce(
                                    out=mxall[:, 8 + kt:9 + kt], in_=src, axis=AX.X, op=ALU.min),
                                    reads=[bsrc], writes=[b_mx])
                                if not direct:
                                    kb.op("dve", lambda e: e.tensor_scalar(
                                        out=score[:, kt * 512:kt * 512 + wk], in0=acc[:, 0:wk],
                                        scalar1=cst[:, C_FLAG:C_FLAG + 1], scalar2=cst[:, C_FLAG + 1:C_FLAG + 2],
                                        op0=ALU.mult, op1=ALU.add),
                                        reads=[b_acc, b_cst], writes=[b_score[kt]])
                        units.append(unit)
                return units

            def post_score(i):
                p_ = i % 2
                score, b_score, mxall, b_mx, sm, b_sm = score2[p_], b_score2[p_], mx2[p_], b_mx2[p_], sm2[p_], b_sm2[p_]
                NK = (17 + i) * 128
                nkt = (NK + 511) // 512
                negmid, tmpc, Mp, negD = sm[:, 0:1], sm[:, 2:3], sm[:, 3:4], sm[:, 8:8 + NIT + 1]
                kd = (NK - 128) // 512
                kb.op("dve", lambda e: e.tensor_tensor(
                    out=score[:, NK - 128:NK], in0=score[:, NK - 128:NK], in1=tri, op=ALU.add),
                    reads=[b_cst], writes=[b_score[kd]])
                kb.op("dve", lambda e: e.tensor_reduce(out=Mp, in_=mxall[:, 0:nkt], axis=AX.X, op=ALU.max),
                      reads=[b_mx], writes=[b_sm])
                kb.op("dve", lambda e: e.tensor_reduce(out=tmpc, in_=mxall[:, 8:8 + nkt], axis=AX.X, op=ALU.min),
                      reads=[b_mx], writes=[b_sm])
                kb.op("dve", lambda e: e.scalar_tensor_tensor(out=Mp, in0=tmpc, scalar=-1.0, in1=Mp, op0=ALU.mult,
                                                              op1=ALU.max), writes=[b_sm])
                kb.op("dve", lambda e: e.tensor_scalar(out=Mp, in0=Mp, scalar1=-1.001, scalar2=-1e-20,
                                                       op0=ALU.mult, op1=ALU.add), writes=[b_sm])
                kb.op("dve", lambda e: e.tensor_scalar(out=negD, in0=cst[:, C_PW2:C_PW2 + NIT + 1], scalar1=Mp,
                                                       scalar2=None, op0=ALU.mult), reads=[b_cst], writes=[b_sm])
                kb.op("dve", lambda e: e.memset(negmid, 0.0), writes=[b_nm2[p_]])

            def make_steps(i):
                p_ = i % 2
                score, b_score, sm, b_sm = score2[p_], b_score2[p_], sm2[p_], b_sm2[p_]
                NK = (17 + i) * 128
                nkt = (NK + 511) // 512
                negmid, cs, tmpc, negD = sm[:, 0:1], sm[:, 1:2], sm[:, 2:3], sm[:, 8:8 + NIT + 1]
                thr = 2.0 * TOPK - NK - 0.5
                steps = []
                for k in range(NIT):
                    def act_fn():
                        kb.op("act", lambda e: e.activation(
                            out=junkA[:, 0:NK], in_=score[:, 0:NK], func=AF.Sign, bias=negmid, scale=1.0,
                            accum_out=cs),
                            reads=b_score[0:nkt] + [b_nm2[p_]], writes=[b_junkA, b_cs2[p_]])

                    def dve_fn(k=k):
                        kb.op("dve", lambda e: e.tensor_scalar(out=tmpc, in0=cs, scalar1=thr, scalar2=0.5,
                                                               op0=ALU.is_ge, op1=ALU.subtract),
                              reads=[b_cs2[p_]], writes=[b_sm])
                        kb.op("dve", lambda e: e.scalar_tensor_tensor(
                            out=negmid, in0=tmpc, scalar=negD[:, k:k + 1], in1=negmid, op0=ALU.mult, op1=ALU.add),
                            reads=[b_sm], writes=[b_nm2[p_]])
                    steps.append((act_fn, dve_fn))
                return steps

            def finalize(i):
                g, j = i // 4, i % 4
                p_ = i % 2
                score, b_score, sm, b_sm = score2[p_], b_score2[p_], sm2[p_], b_sm2[p_]
                NK = (17 + i) * 128
                nkt = (NK + 511) // 512
                negmid, tau, negD = sm[:, 0:1], sm[:, 4:5], sm[:, 8:8 + NIT + 1]
                if j == 0:
                    kb.op("pool", lambda e: e.memset(mT[:], 0.0), writes=[b_mT])
                kb.op("dve", lambda e: e.tensor_tensor(out=tau, in0=negD[:, NIT:NIT + 1], in1=negmid, op=ALU.subtract),
                      reads=[b_nm2[p_]], writes=[b_sm])
                kb.op("dve", lambda e: e.tensor_scalar(
                    out=maskf[:, 0:NK], in0=score[:, 0:NK], scalar1=tau, scalar2=None, op0=ALU.is_ge),
                    reads=b_score[0:nkt] + [b_sm], writes=[b_maskf])
                nch = 17 + i
                for c0 in range(0, nch, 4):
                    n4 = min(4, nch - c0)
                    bk = next_bank([3, 4])
                    for cc in range(n4):
                        c = c0 + cc
                        kb.op("pe", lambda e, c=c, cc=cc: e.transpose(
                            ps[:, bk, cc * 128:(cc + 1) * 128], maskf[:, c * 128:(c + 1) * 128], ident),
                            reads=[b_maskf, b_cst], writes=[PB[bk]], sig=(cc == n4 - 1))
                    kb.op("act", lambda e, c0=c0, n4=n4: e.activation(
                        out=mT[:, c0:c0 + n4, j * 128:(j + 1) * 128],
                        in_=ps[:, bk, 0:n4 * 128].rearrange("p (c q) -> p c q", c=n4), func=AF.Copy),
                        reads=[PB[bk]], writes=[b_mT])
                if j == 3:
                    kb.dma("sp", maskT_d[g], mT[:], reads=[b_mT], writes=[b_maskd[g]])

            prev = None
            for i in range(16):
                units = make_units(i)
                steps = make_steps(prev) if prev is not None else []
                nU = len(units)
                spacing = max(4, nU // (NIT + 1))
                ka = 0
                kd_ = 0
                for u, unit in enumerate(units):
                    unit()
                    if ka < len(steps) and u == ka * spacing + 1:
                        steps[ka][0]()
                        ka += 1
                    if kd_ < len(steps) and kd_ < ka and u == kd_ * spacing + 1 + spacing // 2:
                        steps[kd_][1]()
                        kd_ += 1
                while kd_ < len(steps):
                    if ka == kd_:
                        steps[ka][0]()
                        ka += 1
                    steps[kd_][1]()
                    kd_ += 1
                post_score(i)
                if prev is not None:
                    finalize(prev)
                prev = i
            for st_ in make_steps(prev):
                st_[0]()
                st_[1]()
            finalize(prev)
        phA.close()
        pers = ExitStack()
        es.enter_context(pers)
        psb = lambda name, shape, dt=F32: pers.enter_context(nc.sbuf_tensor(name, list(shape), dt))
        poolT = psb("poolT", [128, 8, TOK], BF16)
        b_poolT = bufs(4, "poolT")
        attnT = psb("attnT", [128, 8, TOK], BF16)
        b_attnT = [bufs(4, "attnT%d" % h) for h in range(8)]
        scCD = ExitStack()
        es.enter_context(scCD)
        xg_t[0] = scCD.enter_context(nc.sbuf_tensor("xgC0", [128, 16, 512], BF16))
        xg_t[1] = scCD.enter_context(nc.sbuf_tensor("xgC1", [128, 16, 512], BF16))
        b_xg[0], b_xg[1] = Buf("xgc0"), Buf("xgc1")

        kb.barrier()
        with ExitStack() as sc:
            sbC = lambda name, shape, dt=F32: sc.enter_context(nc.sbuf_tensor(name, list(shape), dt))
            wpl = sbC("wpl", [128, 8, 16, 128], BF16)
            b_wpl = Buf("wpl")
            for c in range(8):
                kb.dma("pool", wpl[:, c], w_fm[c], writes=[b_wpl])
            pw = sbC("pw", [128, 4, 2, 256], BF16)
            b_pw = Buf("pw")
            kb.dma("pool", pw[:], pool_w, writes=[b_pw])
            xh = sbC("xh", [128, 16, 16], BF16)
            b_xh = Buf("xh")
            for q4 in range(4):
                kb.dma("pool", xh[:, 4 * q4:4 * q4 + 4, :], xT[0, 3, :, 4 * q4:4 * q4 + 4, 496:512], writes=[b_xh])
            hal = sbC("hal", [128, 8, 16])
            b_hal = bufs(8, "hal")
            vb = [sbC("vb%d" % i, [128, 528]) for i in range(2)]
            b_vb = bufs(2, "vb")
            sa = sbC("sa", [128, 528])
            sbb = sbC("sbb", [128, 528])
            b_sa, b_sb = Buf("sa"), Buf("sb")
            t16 = sbC("t16", [128, 16])
            b_t16 = Buf("t16")
            plb = [sbC("plb%d" % i, [128, 512], BF16) for i in range(2)]
            b_plb = bufs(2, "plb")
            for cp in range(8):
                bk = next_bank([0, 1, 2, 7])
                for kc in range(16):
                    kb.op("pe", lambda e, kc=kc, cp=cp, bk=bk: e.matmul(
                        ps[:, bk, 0:16], lhsT=wpl[:, cp, kc, :], rhs=xh[:, kc, :], start=(kc == 0), stop=(kc == 15)),
                        reads=[b_wpl, b_xh], writes=[PB[bk]], sig=(kc == 15))
                evac(hal[:, cp, :], ps[:, bk, 0:16], [PB[bk]], [b_hal[cp]])
            vrr = 0
            for tg in range(4):
                xg, bx = load_xg(1, tg)
                for gq in range(4):
                    wwin = (2, 4, 8, 16)[gq]
                    for cc in range(2):
                        cp = 2 * gq + cc
                        vi = vrr % 2
                        vrr += 1
                        V = vb[vi]
                        bV = b_vb[vi]
                        kb.op("dve", lambda e, V=V, cp=cp: e.tensor_copy(out=V[:, 0:16], in_=hal[:, cp, :]),
                              reads=[b_hal[cp]], writes=[bV])
                        proj_fm(lambda kc, cp=cp: wpl[:, cp, kc, :], b_wpl, xg, bx, V[:, 16:528], [bV])
                        kb.op("act", lambda e, V=V, cp=cp: e.activation(out=hal[:, cp, :], in_=V[:, 512:528], func=AF.Copy),
                              reads=[bV], writes=[b_hal[cp]])
                        kb.op("dve", lambda e, V=V: e.tensor_tensor(out=sa[:, 1:528], in0=V[:, 1:528], in1=V[:, 0:527],
                                                                     op=ALU.add), reads=[bV], writes=[b_sa])
                        Sfin, bS = sa, b_sa
                        if gq >= 1:
                            kb.op("dve", lambda e: e.tensor_tensor(out=sbb[:, 3:528], in0=sa[:, 3:528], in1=sa[:, 1:526],
                                                                   op=ALU.add), reads=[b_sa], writes=[b_sb])
                            Sfin, bS = sbb, b_sb
                        if gq >= 2:
                            kb.op("dve", lambda e: e.tensor_tensor(out=sa[:, 7:528], in0=sbb[:, 7:528], in1=sbb[:, 3:524],
                                                                   op=ALU.add), reads=[b_sb], writes=[b_sa])
                            Sfin, bS = sa, b_sa
                        if gq >= 3:
                            kb.op("dve", lambda e: e.tensor_tensor(out=sbb[:, 15:528], in0=sa[:, 15:528], in1=sa[:, 7:520],
                                                                   op=ALU.add), reads=[b_sa], writes=[b_sb])
                            Sfin, bS = sbb, b_sb
                        kb.op("dve", lambda e, Sfin=Sfin, V=V, cc=cc, wwin=wwin: e.scalar_tensor_tensor(
                            out=plb[cc][:, :], in0=Sfin[:, 16:528], scalar=1.0 / wwin, in1=V[:, 16:528],
                            op0=ALU.mult, op1=ALU.subtract), reads=[bS, bV], writes=[b_plb[cc]])
                        if tg == 0:
                            kb.op("dve", lambda e, Sfin=Sfin, gq=gq: e.tensor_tensor(
                                out=t16[:, :], in0=Sfin[:, 16:32], in1=cst[:, C_CORR + 16 * gq:C_CORR + 16 * gq + 16],
                                op=ALU.mult), reads=[bS, b_cst], writes=[b_t16])
                            kb.op("dve", lambda e, V=V, cc=cc, wwin=wwin: e.scalar_tensor_tensor(
                                out=plb[cc][:, 0:16], in0=t16[:, :], scalar=1.0 / wwin, in1=V[:, 16:32],
                                op0=ALU.mult, op1=ALU.subtract), reads=[b_t16, bV], writes=[b_plb[cc]])
                    for dc in range(2):
                        bk = next_bank([0, 1, 2, 7])
                        for cc in range(2):
                            kb.op("pe", lambda e, cc=cc, dc=dc, gq=gq, bk=bk: e.matmul(
                                ps[:, bk, :], lhsT=pw[:, gq, cc, dc * 128:(dc + 1) * 128], rhs=plb[cc][:, :],
                                start=(cc == 0), stop=(cc == 1)),
                                reads=[b_pw, b_plb[cc]], writes=[PB[bk]], sig=(cc == 1))
                        oc = 2 * gq + dc
                        kb.op("act", lambda e, oc=oc, bk=bk, tg=tg: e.activation(
                            out=poolT[:, oc, tg * 512:(tg + 1) * 512], in_=ps[:, bk, :], func=AF.Copy,
                            scale=cst[:, C_PSC + oc:C_PSC + oc + 1]),
                            reads=[PB[bk], b_cst], writes=[b_poolT[tg]])
        if stop_after == "C":
            dump([(poolT[:, 0, 0:1024], b_poolT, 1024), (poolT[:, 7, 0:1024], b_poolT, 1024),
                  (poolT[:, 3, 1024:2048], b_poolT, 1024)])
            return nc

        kb.barrier()
        with ExitStack() as sc:
            sbD = lambda name, shape, dt=F32: sc.enter_context(nc.sbuf_tensor(name, list(shape), dt))
            kT2 = sbD("kT2", [128, 2, 2 * TOK], BF16)
            b_kT2 = bufs(8, "kT2")
            v2 = sbD("v2", [128, 32, 256], BF16)
            b_v2 = bufs(8, "v2")
            qT2 = sbD("qT2", [128, 2, TOK], BF16)
            b_qT2 = bufs(4, "qT2")
            SB_S = [0, 1, 2]
            for hg in range(4):
                kb.barrier()
                scP = ExitStack()
                sbP = lambda name, shape, dt=F32: scP.enter_context(nc.sbuf_tensor(name + "_g%d" % hg, list(shape), dt))
                wq2 = sbP("wq2", [128, 2, 16, 128], BF16)
                wk2 = sbP("wk2", [128, 2, 16, 128], BF16)
                wv2 = sbP("wv2", [128, 16, 256], BF16)
                b_wq2, b_wk2, b_wv2 = Buf("wq2"), Buf("wk2"), Buf("wv2")
                for hl in range(2):
                    kb.dma("pool", wq2[:, hl], w_fm[8 + 2 * hg + hl], writes=[b_wq2])
                    kb.dma("pool", wk2[:, hl], w_fm[16 + 2 * hg + hl], writes=[b_wk2])
                kb.dma("pool", wv2[:], w_v[hg], writes=[b_wv2])
                for s in range(2):
                    for tg in range(4):
                        xg, bx = load_xg(s, tg)
                        kg = s * 4 + tg
                        for hl in range(2):
                            proj_fm(lambda kc, hl=hl: wk2[:, hl, kc, :], b_wk2, xg, bx,
                                    kT2[:, hl, kg * 512:(kg + 1) * 512], [b_kT2[kg]])
                            if s == 1:
                                proj_fm(lambda kc, hl=hl: wq2[:, hl, kc, :], b_wq2, xg, bx,
                                        qT2[:, hl, tg * 512:(tg + 1) * 512], [b_qT2[tg]])
                        for tt in range(4):
                            bk = 7
                            for kc in range(16):
                                kb.op("pe", lambda e, kc=kc, tt=tt, xg=xg: e.matmul(
                                    ps[:, bk, 0:256], lhsT=xg[:, kc, tt * 128:(tt + 1) * 128], rhs=wv2[:, kc, :],
                                    start=(kc == 0), stop=(kc == 15)),
                                    reads=[b_wv2, bx], writes=[PB[bk]], sig=(kc == 15))
                            evac(v2[:, kg * 4 + tt, :], ps[:, bk, 0:256], [PB[bk]], [b_v2[kg]])
                if stop_after == "D0":
                    dump([(kT2[:, 0, 0:1024], b_kT2[0:2], 1024), (qT2[:, 1, 0:1024], b_qT2[0:2], 1024),
                          (v2[:, 0:4, :].rearrange("p a b -> p (a b)"), b_v2[0:1], 1024)])
                    return nc
                kb.barrier()
                scP.close()
                scT = ExitStack()
                sbT = lambda name, shape, dt=F32: scT.enter_context(nc.sbuf_tensor(name + "_g%d" % hg, list(shape), dt))
                mk = sbT("mk", [128, 32, 512], BF16)
                b_mk = Buf("mk")
                Et = [sbT("E%d" % i_, [128, 512], BF16) for i_ in range(3)]
                b_E = bufs(3, "E")
                Pt = [sbT("P%d" % i_, [128, 512], BF16) for i_ in range(3)]
                b_P = bufs(3, "P")
                tmpn = [sbT("tmpn%d" % i_, [128, 128]) for i_ in range(2)]
                b_tmpn = bufs(2, "tmpn")
                rden = sbT("rden", [128, 512])
                b_rden = Buf("rden")
                for g in range(4):
                    kb.dma("sp", mk[:], maskT_d[g], reads=[b_maskd[g]], writes=[b_mk])
                    nch = 20 + 4 * g
                    for hl in range(2):
                        h = 2 * hg + hl
                        bO = 3 + 2 * (hl % 2)
                        bD = bO + 1

                        def issue_S(c, hl=hl, g=g):
                            bk = SB_S[c % 3]
                            kb.op("pe", lambda e: e.matmul(
                                ps[:, bk, :], lhsT=kT2[:, hl, c * 128:(c + 1) * 128],
                                rhs=qT2[:, hl, g * 512:(g + 1) * 512], start=True, stop=True),
                                reads=[b_kT2[c // 4], b_qT2[g]], writes=[PB[bk]])
                        issue_S(0)
                        if nch > 1:
                            issue_S(1)
                        for c in range(nch):
                            if c + 2 < nch:
                                issue_S(c + 2)
                            bk = SB_S[c % 3]
                            ei = c % 3
                            tokE = kb.op("act", lambda e, bk=bk, ei=ei, h=h: e.activation(
                                out=Et[ei][:, :], in_=ps[:, bk, :], func=AF.Exp, scale=ATT_SCALE,
                                bias=cst[:, C_B31 + h:C_B31 + h + 1]),
                                reads=[PB[bk], b_cst], writes=[b_E[ei]])
                            cn = c - (15 + 4 * g)
                            if 0 <= cn <= 4:
                                for j in range(4):
                                    dl = 1 + j - cn
                                    if dl in (0, 1):
                                        ti = (j + cn) % 2
                                        kb.op("dve", lambda e, bk=bk, j=j, dl=dl, h=h, ti=ti: e.scalar_tensor_tensor(
                                            out=tmpn[ti][:, :], in0=ps[:, bk, j * 128:(j + 1) * 128], scalar=ATT_SCALE,
                                            in1=bT[:, dl, h, :], op0=ALU.mult, op1=ALU.add),
                                            reads=[PB[bk], b_bT], writes=[b_tmpn[ti]], deps=[tokE])
                                        kb.op("act", lambda e, ei=ei, j=j, ti=ti: e.activation(
                                            out=Et[ei][:, j * 128:(j + 1) * 128], in_=tmpn[ti][:, :], func=AF.Exp),
                                            reads=[b_tmpn[ti]], writes=[b_E[ei]])
                            kb.op("dve", lambda e, ei=ei, c=c: e.tensor_tensor(
                                out=Pt[ei][:, :], in0=Et[ei][:, :], in1=mk[:, c, :], op=ALU.mult),
                                reads=[b_E[ei], b_mk], writes=[b_P[ei]])
                            kb.op("pe", lambda e, ei=ei, c=c, hl=hl, bO=bO, nch=nch: e.matmul(
                                ps[:, bO, :], lhsT=v2[:, c, hl * 128:(hl + 1) * 128], rhs=Pt[ei][:, :],
                                start=(c == 0), stop=(c == nch - 1)),
                                reads=[b_v2[c // 4], b_P[ei]], writes=[PB[bO]], sig=(c == nch - 1))
                            kb.op("pe", lambda e, ei=ei, c=c, bD=bD, nch=nch: e.matmul(
                                ps[:, bD, :], lhsT=ones_b[:, :], rhs=Pt[ei][:, :],
                                start=(c == 0), stop=(c == nch - 1)),
                                reads=[b_ones, b_P[ei]], writes=[PB[bD]], sig=(c == nch - 1))
                        kb.op("dve", lambda e, bD=bD: e.reciprocal(out=rden[:, :], in_=ps[:, bD, :]),
                              reads=[PB[bD]], writes=[b_rden])
                        kb.op("dve", lambda e, bO=bO, h=h, g=g: e.tensor_tensor(
                            out=attnT[:, h, g * 512:(g + 1) * 512], in0=ps[:, bO, :], in1=rden[:, :], op=ALU.mult),
                            reads=[PB[bO], b_rden], writes=[b_attnT[h][g]])
                        if stop_after == "D1":
                            dump([(attnT[:, 0, 0:512], [b_attnT[0][0]], 512), (rden[:, :], [b_rden], 512)])
                            return nc
                scT.close()
        if stop_after == "D":
            dump([(attnT[:, 0, 0:1024], b_attnT[0], 1024), (attnT[:, 7, 0:1024], b_attnT[7], 1024),
                  (attnT[:, 3, 1024:2048], b_attnT[3], 1024)])
            return nc

        scCD.close()
        kb.barrier()
        b_hT_d = bufs(4, "hT_d")
        b_htok = bufs(16, "htok")

        def layer_norm(e_sb, src, bsrc, gam, bet, b_gb, outt, bout, xc, b_xc, junkf, b_jf, st, b_st):
            kb.op("dve", lambda e: e.tensor_scalar(out=junkf[:, :], in0=src, scalar1=1.0, scalar2=None, op0=ALU.mult,
                                                   op1=ALU.add, accum_out=st[:, 0:1]), reads=[bsrc], writes=[b_jf, b_st])
            kb.op("dve", lambda e: e.tensor_scalar(out=st[:, 1:2], in0=st[:, 0:1], scalar1=1.0 / D, scalar2=None,
                                                   op0=ALU.mult), writes=[b_st])
            kb.op("dve", lambda e: e.tensor_scalar(out=xc[:, :], in0=src, scalar1=st[:, 1:2], scalar2=None,
                                                   op0=ALU.subtract), reads=[bsrc, b_st], writes=[b_xc])
            kb.op("dve", lambda e: e.tensor_tensor(out=junkf[:, :], in0=xc[:, :], in1=xc[:, :], op=ALU.mult),
                  reads=[b_xc], writes=[b_jf])
            kb.op("dve", lambda e: e.tensor_scalar(out=junkf[:, :], in0=junkf[:, :], scalar1=1.0, scalar2=None,
                                                   op0=ALU.mult, op1=ALU.add, accum_out=st[:, 2:3]),
                  writes=[b_jf, b_st])
            kb.op("dve", lambda e: e.tensor_scalar(out=st[:, 3:4], in0=st[:, 2:3], scalar1=1.0 / D, scalar2=LN_EPS,
                                                   op0=ALU.mult, op1=ALU.add), writes=[b_st])
            kb.op("act", lambda e: e.activation(out=st[:, 4:5], in_=st[:, 3:4], func=AF.Sqrt), writes=[b_st])
            kb.op("dve", lambda e: e.reciprocal(out=st[:, 5:6], in_=st[:, 4:5]), writes=[b_st])
            kb.op("dve", lambda e: e.scalar_tensor_tensor(out=xc[:, :], in0=xc[:, :], scalar=st[:, 5:6], in1=gam,
                                                          op0=ALU.mult, op1=ALU.mult),
                  reads=[b_st, b_gb], writes=[b_xc])
            return kb.op("dve", lambda e: e.tensor_tensor(out=outt, in0=xc[:, :], in1=bet, op=ALU.add),
                         reads=[b_xc, b_gb], writes=[bout])

        with ExitStack() as sc:
            sbE = lambda name, shape, dt=F32: sc.enter_context(nc.sbuf_tensor(name, list(shape), dt))
            wo = sbE("wo", [128, 16, D], BF16)
            b_wo = Buf("wo")
            for c in range(4):
                kb.dma("pool", wo[:, 4 * c:4 * c + 4, :], w_out[:, 4 * c:4 * c + 4, :], writes=[b_wo])
            gb = sbE("gb1", [128, 2, D])
            b_gb = Buf("gb1")
            kb.dma("sp", gb[:, 0, :], lnp[0], writes=[b_gb])
            kb.dma("sp", gb[:, 1, :], lnp[1], writes=[b_gb])
            xt = [sbE("xt0", [128, D])] * 2
            b_xt = [Buf("xt")] * 2
            hpre2 = [sbE("hpre0", [128, D])] * 2
            b_hpre2 = [Buf("hpre")] * 2
            xc = sbE("xc", [128, D])
            b_xc = Buf("xc")
            junkf = sbE("junkf", [128, D])
            b_jf = Buf("junkf")
            hh = [sbE("hh0", [128, D])] * 2
            b_hh = [Buf("hh")] * 2
            st = sbE("st", [128, 8])
            b_st = Buf("st")
            hTs = sbE("hTs", [128, 16, 128], BF16)
            b_hTs = Buf("hTs")
            kb.dma("sp", xt[0][:, :], x_tok[0:128, :], writes=[b_xt[0]])
            for tt in range(16):
                tg = tt // 4
                xi = tt % 2
                hpre, b_hpre = hpre2[tt % 2], b_hpre2[tt % 2]
                for dt_ in range(4):
                    bk = next_bank([0, 1, 2, 7])
                    for kc in range(16):
                        if kc < 8:
                            lh = poolT[:, kc, tt * 128:(tt + 1) * 128]
                            rb = b_poolT[tg]
                        else:
                            lh = attnT[:, kc - 8, tt * 128:(tt + 1) * 128]
                            rb = b_attnT[kc - 8][tg]
                        kb.op("pe", lambda e, lh=lh, kc=kc, dt_=dt_, bk=bk, wo=wo: e.matmul(
                            ps[:, bk, :], lhsT=lh, rhs=wo[:, kc, dt_ * 512:(dt_ + 1) * 512],
                            start=(kc == 0), stop=(kc == 15)),
                            reads=[rb, b_wo], writes=[PB[bk]], sig=(kc == 15))
                    kb.op("dve", lambda e, xi=xi, dt_=dt_, bk=bk: e.scalar_tensor_tensor(
                        out=hpre[:, dt_ * 512:(dt_ + 1) * 512], in0=xt[xi][:, dt_ * 512:(dt_ + 1) * 512], scalar=ALPHA,
                        in1=ps[:, bk, :], op0=ALU.mult, op1=ALU.add),
                        reads=[b_xt[xi], PB[bk]], writes=[b_hpre])
                hi = tt % 2
                if tt + 1 < 16:
                    kb.dma("sp", xt[(tt + 1) % 2][:, :], x_tok[(tt + 1) * 128:(tt + 2) * 128, :],
                           writes=[b_xt[(tt + 1) % 2]])
                layer_norm(None, hpre[:, :], b_hpre, gb[:, 0, :], gb[:, 1, :], b_gb, hh[hi][:, :], b_hh[hi],
                           xc, b_xc, junkf, b_jf, st, b_st)
                kb.dma("sp", h_tok[tt * 128:(tt + 1) * 128, :], hh[hi][:, :], reads=[b_hh[hi]], writes=[b_htok[tt]])
                for c0 in range(0, 16, 4):
                    bk = next_bank([3, 4, 5, 6])
                    for cc in range(4):
                        kb.op("pe", lambda e, c0=c0, cc=cc, bk=bk, hi=hi: e.transpose(
                            ps[:, bk, cc * 128:(cc + 1) * 128], hh[hi][:, (c0 + cc) * 128:(c0 + cc + 1) * 128], ident),
                            reads=[b_hh[hi], b_cst], writes=[PB[bk]], sig=(cc == 3))
                    evac(hTs[:, c0:c0 + 4, :],
                         ps[:, bk, :].rearrange("p (c q) -> p c q", c=4), [PB[bk]], [b_hTs])
                for q4 in range(4):
                    kb.dma("sp", hT_d[:, 4 * q4:4 * q4 + 4, tt * 128:(tt + 1) * 128], hTs[:, 4 * q4:4 * q4 + 4, :],
                           reads=[b_hTs], writes=[b_hT_d[tg]])
        pers.close()
        if stop_after == "E":
            t1 = kb.dma("sp", dbg_out[:, 0:2048], h_tok[0:128, :], reads=[b_htok[0]])
            t2 = kb.dma("sp", dbg_out[:, 2048:4096], h_tok[1920:2048, :], reads=[b_htok[15]])
            kb.wait("sp", t1)
            kb.wait("sp", t2)
            return nc

        kb.barrier()
        b_W1 = bufs(32, "W1")
        with ExitStack() as sc:
            sbF = lambda name, shape, dt=F32: sc.enter_context(nc.sbuf_tensor(name, list(shape), dt))
            wqs = sbF("wqs", [128, 16, D], BF16)
            b_wqs = Buf("wqs")
            for c in range(4):
                kb.dma("pool", wqs[:, 4 * c:4 * c + 4, :], wq[:, 4 * c:4 * c + 4, :], writes=[b_wqs])
            skT = sbF("skT", [128, 2, 128], BF16)
            b_skT = Buf("skT")
            kb.dma("pool", skT[:], subkT, writes=[b_skT])
            hTg = [sbF("hTg0", [128, 16, 512], BF16)] * 2
            b_hTg = [Buf("hTg")] * 2
            qpT = sbF("qpT", [128, 16, 512], BF16)
            b_qpT = Buf("qpT")
            s_sb = sbF("s_sb", [128, 16, 128])
            s2_sb = sbF("s2_sb", [128, 16, 128])
            b_s, b_s2 = Buf("s"), Buf("s2")
            v16 = sbF("v16", [128, 16, 16])
            i16 = sbF("i16", [128, 16, 16], U32)
            i16f = sbF("i16f", [128, 16, 16])
            b_v16, b_i16, b_i16f = Buf("v16"), Buf("i16"), Buf("i16f")
            cand = sbF("cand", [128, 8, 256])
            cand2 = sbF("cand2", [128, 8, 256])
            b_cand, b_cand2 = Buf("cand"), Buf("cand2")
            best = sbF("best", [128, 8, 16])
            bidx = sbF("bidx", [128, 8, 16], U32)
            aiu = sbF("aiu", [128, 128], U32)
            biu = sbF("biu", [128, 128], U32)
            af = sbF("af", [128, 128])
            bf = sbF("bf", [128, 128])
            b_best, b_bidx, b_bidf, b_af, b_bf = Buf("best"), Buf("bidx"), Buf("bidf"), Buf("af"), Buf("bf")
            eb = sbF("eb", [128, 8, 16])
            zz = sbF("zz", [128, 16])
            b_eb, b_zz = Buf("eb"), Buf("zz")
            oh = sbF("oh", [128, 128, 16])
            b_oh = Buf("oh")
            IG = sbF("IG", [128, 3, 128])
            b_IG = Buf("IG")
            IGT = sbF("IGT", [128, 3, 128])
            b_IGT = Buf("IGT")
            eq2 = [sbF("eq0", [128, 32, 128], BF16)] * 2
            b_eq2 = [Buf("eq")] * 2
            At2 = [sbF("At0", [128, 32, 128], BF16)] * 2
            Bt2 = [sbF("Bt0", [128, 32, 128], BF16)] * 2
            b_At2, b_Bt2 = [Buf("At")] * 2, [Buf("Bt")] * 2
            ab_rr = [0]
            stg = [sbF("stg0", [128, 128, 64], BF16)] * 2
            b_stg = [Buf("stg")] * 2
            for tg in range(4):
                hx = hTg[tg % 2]
                bhx = b_hTg[tg % 2]
                for q4 in range(4):
                    kb.dma("sp", hx[:, 4 * q4:4 * q4 + 4, :], hT_d[:, 4 * q4:4 * q4 + 4, tg * 512:(tg + 1) * 512],
                           reads=[b_hT_d[tg]], writes=[bhx])
                for n in range(16):
                    proj_fm(lambda kc, n=n: wqs[:, kc, n * 128:(n + 1) * 128], b_wqs, hx, bhx, qpT[:, n, :], [b_qpT])
                for tt in range(4):
                    T = 4 * tg + tt
                    for n4 in range(4):
                        bk = next_bank([3, 4, 5, 6])
                        for nn in range(4):
                            n = 4 * n4 + nn
                            kb.op("pe", lambda e, n=n, nn=nn, bk=bk, tt=tt: e.matmul(
                                ps[:, bk, nn * 128:(nn + 1) * 128], lhsT=qpT[:, n, tt * 128:(tt + 1) * 128],
                                rhs=skT[:, n % 2, :], start=True, stop=True),
                                reads=[b_qpT, b_skT], writes=[PB[bk]], sig=(nn == 3))
                        evac(s_sb[:, 4 * n4:4 * n4 + 4, :], ps[:, bk, :].rearrange("p (c q) -> p c q", c=4),
                             [PB[bk]], [b_s])
                    bv, bi, bs2 = bufs(16, "v16n"), bufs(16, "i16n"), bufs(16, "s2n")
                    for n in range(16):
                        kb.op("dve", lambda e, n=n: e.max(out=v16[:, n, 0:8], in_=s_sb[:, n, :]),
                              reads=[b_s], writes=[bv[n]], deps=[b_v16.w] + list(b_v16.r.values()))
                    for n in range(16):
                        kb.op("dve", lambda e, n=n: e.max_index(out=i16[:, n, 0:8], in_max=v16[:, n, 0:8],
                                                               in_values=s_sb[:, n, :]),
                              reads=[b_s, bv[n]], writes=[bi[n]], deps=[b_i16.w] + list(b_i16.r.values()))
                    for n in range(16):
                        kb.op("dve", lambda e, n=n: e.match_replace(out=s2_sb[:, n, :], in_to_replace=v16[:, n, 0:8],
                                                                   in_values=s_sb[:, n, :], imm_value=NEG),
                              reads=[b_s, bv[n]], writes=[bs2[n]])
                    for n in range(16):
                        kb.op("dve", lambda e, n=n: e.max(out=v16[:, n, 8:16], in_=s2_sb[:, n, :]),
                              reads=[bs2[n]], writes=[bv[n]])
                    for n in range(16):
                        kb.op("dve", lambda e, n=n: e.max_index(out=i16[:, n, 8:16], in_max=v16[:, n, 8:16],
                                                               in_values=s2_sb[:, n, :]),
                              reads=[bs2[n], bv[n]], writes=[bi[n]])
                    kb.op("dve", lambda e: e.tensor_copy(out=i16f[:], in_=i16[:]), reads=bi, writes=[b_i16f, b_i16])
                    kb.op("dve", lambda e: e.tensor_tensor(
                        out=cand[:].rearrange("p h (a b) -> p h a b", a=16),
                        in0=sap(v16, [[32, 8], [1, 16], [0, 16]]),
                        in1=sap(v16, [[32, 8], [0, 16], [1, 16]], off=16), op=ALU.add),
                        reads=bv, writes=[b_cand, b_v16])
                    bb, bx_, bc2 = bufs(8, "besth"), bufs(8, "bidxh"), bufs(8, "cand2h")
                    for h in range(8):
                        kb.op("dve", lambda e, h=h: e.max(out=best[:, h, 0:8], in_=cand[:, h, :]),
                              reads=[b_cand], writes=[bb[h]], deps=[b_best.w] + list(b_best.r.values()))
                    for h in range(8):
                        kb.op("dve", lambda e, h=h: e.max_index(out=bidx[:, h, 0:8], in_max=best[:, h, 0:8],
                                                               in_values=cand[:, h, :]),
                              reads=[b_cand, bb[h]], writes=[bx_[h]], deps=[b_bidx.w] + list(b_bidx.r.values()))
                    for h in range(8):
                        kb.op("dve", lambda e, h=h: e.match_replace(out=cand2[:, h, :], in_to_replace=best[:, h, 0:8],
                                                                   in_values=cand[:, h, :], imm_value=NEG),
                              reads=[b_cand, bb[h]], writes=[bc2[h]])
                    for h in range(8):
                        kb.op("dve", lambda e, h=h: e.max(out=best[:, h, 8:16], in_=cand2[:, h, :]),
                              reads=[bc2[h]], writes=[bb[h]])
                    for h in range(8):
                        kb.op("dve", lambda e, h=h: e.max_index(out=bidx[:, h, 8:16], in_max=best[:, h, 8:16],
                                                               in_values=cand2[:, h, :]),
                              reads=[bc2[h], bb[h]], writes=[bx_[h]])
                    kb.op("dve", lambda e: e.tensor_copy(out=zz[:, 0:1], in_=best[:, 0, 0:1]),
                          reads=bb + bx_, writes=[b_best, b_bidx, b_zz])
                    kb.op("dve", lambda e: e.tensor_tensor(out=eb[:], in0=best[:], in1=sap(best, [[16, 8], [0, 16]]),
                                                           op=ALU.subtract), reads=[b_best], writes=[b_eb])
                    kb.op("act", lambda e: e.activation(out=eb[:], in_=eb[:], func=AF.Exp), writes=[b_eb])
                    kb.op("dve", lambda e: e.tensor_reduce(out=zz[:, 0:8], in_=eb[:], axis=AX.X, op=ALU.add),
                          reads=[b_eb], writes=[b_zz])
                    kb.op("dve", lambda e: e.reciprocal(out=zz[:, 8:16], in_=zz[:, 0:8]), writes=[b_zz])
                    kb.op("dve", lambda e: e.tensor_tensor(
                        out=IG[:, 2, :].rearrange("p (h r) -> p h r", h=8), in0=eb[:],
                        in1=sap(zz, [[1, 8], [0, 16]], off=8), op=ALU.mult),
                        reads=[b_eb, b_zz], writes=[b_IG])
                    kb.op("dve", lambda e: e.tensor_single_scalar(out=aiu[:], in_=bidx[:].rearrange("p h r -> p (h r)"),
                                                                  scalar=4, op=ALU.logical_shift_right),
                          reads=[b_bidx], writes=[b_bidf])
                    kb.op("dve", lambda e: e.tensor_single_scalar(out=biu[:], in_=bidx[:].rearrange("p h r -> p (h r)"),
                                                                  scalar=15, op=ALU.bitwise_and),
                          reads=[b_bidx], writes=[b_bidf])
                    kb.op("dve", lambda e: e.tensor_copy(out=af[:], in_=aiu[:]), reads=[b_bidf], writes=[b_af])
                    kb.op("dve", lambda e: e.tensor_copy(out=bf[:], in_=biu[:]), reads=[b_bidf], writes=[b_bf])
                    for which, sel, boff in ((0, af, 0), (1, bf, 16)):
                        bsel = b_af if which == 0 else b_bf
                        kb.op("dve", lambda e, sel=sel: e.tensor_tensor(
                            out=oh[:], in0=sap(cst, [[0, 128], [1, 16]], off=C_IOTA),
                            in1=sap(sel, [[1, 128], [0, 16]]), op=ALU.is_equal),
                            reads=[bsel, b_cst], writes=[b_oh])
                        kb.op("dve", lambda e, boff=boff: e.tensor_tensor(
                            out=oh[:].rearrange("p (h r) a -> p h r a", h=8),
                            in0=oh[:].rearrange("p (h r) a -> p h r a", h=8),
                            in1=sap(i16f, [[32, 8], [0, 16], [1, 16]], off=boff), op=ALU.mult),
                            reads=[b_i16f], writes=[b_oh])
                        kb.op("dve", lambda e, which=which: e.tensor_reduce(out=IG[:, which, :], in_=oh[:], axis=AX.X,
                                                                          op=ALU.add),
                              reads=[b_oh], writes=[b_IG])
                    bk = next_bank([3, 4, 5, 6])
                    for w3 in range(3):
                        kb.op("pe", lambda e, w3=w3, bk=bk: e.transpose(ps[:, bk, w3 * 128:(w3 + 1) * 128],
                                                                        IG[:, w3, :], ident),
                              reads=[b_IG, b_cst], writes=[PB[bk]], sig=(w3 == 2))
                    kb.op("act", lambda e, bk=bk: e.activation(out=IGT[:], in_=ps[:, bk, 0:384].rearrange(
                        "p (c q) -> p c q", c=3), func=AF.Copy), reads=[PB[bk]], writes=[b_IGT])
                    for sbk in range(2):
                        si = (2 * T + sbk) % 2
                        for s32 in range(2):
                            t0 = sbk * 64 + s32 * 32
                            ai_ = ab_rr[0] % 2
                            ab_rr[0] += 1
                            eq, b_eq = eq2[ai_], b_eq2[ai_]
                            At, b_At = At2[ai_], b_At2[ai_]
                            Bt, b_Bt = Bt2[ai_], b_Bt2[ai_]
                            kb.op("dve", lambda e, t0=t0: e.tensor_tensor(
                                out=eq[:], in0=sap(cst, [[0, 32], [1, 128]], off=C_IOTA),
                                in1=sap(IGT, [[1, 32], [0, 128]], off=0 * 128 + t0), op=ALU.is_equal),
                                reads=[b_IGT, b_cst], writes=[b_eq])
                            kb.op("dve", lambda e, t0=t0: e.tensor_tensor(
                                out=At[:], in0=eq[:], in1=sap(IGT, [[1, 32], [0, 128]], off=2 * 128 + t0), op=ALU.mult),
                                reads=[b_eq, b_IGT], writes=[b_At])
                            kb.op("dve", lambda e, t0=t0: e.tensor_tensor(
                                out=Bt[:], in0=sap(cst, [[0, 32], [1, 128]], off=C_IOTA),
                                in1=sap(IGT, [[1, 32], [0, 128]], off=1 * 128 + t0), op=ALU.is_equal),
                                reads=[b_IGT, b_cst], writes=[b_Bt])
                            for t4 in range(8):
                                bk = next_bank([0, 1, 2, 7])
                                for q4 in range(4):
                                    tl = 4 * t4 + q4
                                    kb.op("pe", lambda e, tl=tl, q4=q4, bk=bk: e.matmul(
                                        ps[:, bk, q4 * 128:(q4 + 1) * 128], lhsT=At[:, tl, :], rhs=Bt[:, tl, :],
                                        start=True, stop=True),
                                        reads=[b_At, b_Bt], writes=[PB[bk]], sig=(q4 == 3))
                                kb.op("act", lambda e, t4=t4, bk=bk, si=si, s32=s32: e.activation(
                                    out=sap(stg[si], [[1, 4], [64, 128]], off=s32 * 32 + 4 * t4),
                                    in_=ps[:, bk, :].rearrange("p (t j) -> p t j", t=4), func=AF.Copy),
                                    reads=[PB[bk]], writes=[b_stg[si]])
                        kb.dma("sp", W1[2 * T + sbk], stg[si][:], reads=[b_stg[si]], writes=[b_W1[2 * T + sbk]])
                    if stop_after == "F2" and T == 0:
                        wchk = sbF("wchk", [128, 1024], BF16)
                        b_wchk = Buf("wchk")
                        kb.dma("sp", wchk[:], W1[0, :, 0:16, :].rearrange("i j t -> i (j t)"), reads=[b_W1[0]], writes=[b_wchk])
                        dump([(wchk[:, :], [b_wchk], 1024), (IG[:, 0, :], [b_IG], 128), (IG[:, 1, :], [b_IG], 128),
                              (IG[:, 2, :], [b_IG], 128)])
                        return nc
                    if stop_after == "F" and T == 0:
                        dump([(IG[:, 0, :], [b_IG], 128), (IG[:, 1, :], [b_IG], 128), (IG[:, 2, :], [b_IG], 128),
                              (s_sb[:, 0, :], [b_s], 128), (s_sb[:, 1, :], [b_s], 128)])
                        return nc

        kb.barrier()
        if stop_after == "Fend":
            kb.barrier()
            return nc
        NJ = 4
        NR = 0 if stop_after == "G4" else (2 if stop_after in ("G2", "G3") else 128 // NJ)
        for half in range(2):
            kb.barrier()
            with ExitStack() as sc:
                sbG = lambda name, shape, dt=F32: sc.enter_context(nc.sbuf_tensor(name + "_h%d" % half, list(shape), dt))
                hTh = sbG("hTh", [128, 16, 1024], BF16)
                b_hTh = Buf("hTh")
                for tg2 in range(2):
                    tg = 2 * half + tg2
                    for q4 in range(4):
                        kb.dma("sp", hTh[:, 4 * q4:4 * q4 + 4, tg2 * 512:(tg2 + 1) * 512],
                               hT_d[:, 4 * q4:4 * q4 + 4, tg * 512:(tg + 1) * 512],
                               reads=[b_hT_d[tg]], writes=[b_hTh])
                accG = sbG("accG", [128, 8, D])
                b_accG = [bufs(2, "accG%d" % t) for t in range(8)]
                with ExitStack() as sc2:
                    sbH = lambda name, shape, dt=F32: sc2.enter_context(nc.sbuf_tensor(name + "_h%d" % half, list(shape), dt))
                    Wr = [sbH("Wr%d" % i, [128, 16, NJ, 64], BF16) for i in range(2)]
                    b_Wr = bufs(2, "Wr")
                    vr = [sbH("vr%d" % i, [128, NJ, D], BF16) for i in range(2)]
                    b_vr = bufs(2, "vr")
                    uj = [sbH("uj%d" % i, [128, 16, 128], BF16) for i in range(2)]
                    b_uj = bufs(2, "uj")
                    Gr = [sbH("Gr%d" % i, [128, NJ, 1024], BF16) for i in range(2)]
                    b_Gr = bufs(2, "Gr")
                    ga = [sbH("ga%d" % i, [128, 512], BF16) for i in range(2)]
                    b_ga = bufs(2, "ga")
                    urr = [0]
                    grr = [0]

                    def phaseA(r):
                        ri = r % 2
                        j0 = r * NJ
                        for q4 in range(4):
                            b0_ = 16 * half + 4 * q4
                            kb.dma("sp", Wr[ri][:, 4 * q4:4 * q4 + 4], W1[b0_:b0_ + 4, :, j0:j0 + NJ, :].rearrange(
                                "b i j t -> i b j t"), reads=b_W1[b0_:b0_ + 4], writes=[b_Wr[ri]])
                        for jj in range(NJ):
                            ui = urr[0] % 2
                            urr[0] += 1
                            kb.dma("pool", uj[ui][:], uT[j0 + jj], writes=[b_uj[ui]])
                            for tg2 in range(2):
                                bk = next_bank([0, 1])
                                for kc in range(16):
                                    kb.op("pe", lambda e, kc=kc, ui=ui, tg2=tg2, bk=bk: e.matmul(
                                        ps[:, bk, :], lhsT=uj[ui][:, kc, :], rhs=hTh[:, kc, tg2 * 512:(tg2 + 1) * 512],
                                        start=(kc == 0), stop=(kc == 15)),
                                        reads=[b_uj[ui], b_hTh], writes=[PB[bk]], sig=(kc == 15))
                                gi = grr[0] % 2
                                grr[0] += 1
                                kb.op("act", lambda e, gi=gi, bk=bk: e.activation(out=ga[gi][:, :], in_=ps[:, bk, :],
                                                                                 func=AF.Gelu),
                                      reads=[PB[bk]], writes=[b_ga[gi]])
                                kb.op("dve", lambda e, gi=gi, ri=ri, jj=jj, tg2=tg2: e.tensor_tensor(
                                    out=Gr[ri][:, jj, tg2 * 512:(tg2 + 1) * 512].rearrange("p (b t) -> p b t", b=8),
                                    in0=ga[gi][:, :].rearrange("p (b t) -> p b t", b=8),
                                    in1=Wr[ri][:, tg2 * 8:(tg2 + 1) * 8, jj, :], op=ALU.mult),
                                    reads=[b_ga[gi], b_Wr[ri]], writes=[b_Gr[ri]])
                        kb.dma("pool", vr[ri][:], vL[j0:j0 + NJ].rearrange("j i d -> i j d"), writes=[b_vr[ri]])

                    def phaseB(r):
                        ri = r % 2
                        for tt in range(8):
                            for dh in range(2):
                                b0 = 2 + 2 * ((tt * 2 + dh) % 3)
                                for jj in range(NJ):
                                    for dq in range(2):
                                        kb.op("pe", lambda e, jj=jj, dq=dq, tt=tt, dh=dh, b0=b0: e.matmul(
                                            ps[:, b0 + dq, :], lhsT=Gr[ri][:, jj, tt * 128:(tt + 1) * 128],
                                            rhs=vr[ri][:, jj, dh * 1024 + dq * 512:dh * 1024 + (dq + 1) * 512],
                                            start=(jj == 0), stop=(jj == NJ - 1)),
                                            reads=[b_Gr[ri], b_vr[ri]], writes=[PB[b0 + dq]],
                                            sig=(jj == NJ - 1 and dq == 1))
                                pin = ps[:, b0:b0 + 2, :].rearrange("p a b -> p (a b)")
                                aout = accG[:, tt, dh * 1024:(dh + 1) * 1024]
                                if r == 0:
                                    kb.op("dve", lambda e, pin=pin, aout=aout: e.tensor_copy(out=aout, in_=pin),
                                          reads=[PB[b0], PB[b0 + 1]], writes=[b_accG[tt][dh]])
                                else:
                                    kb.op("dve", lambda e, pin=pin, aout=aout: e.tensor_tensor(
                                        out=aout, in0=aout, in1=pin, op=ALU.add),
                                        reads=[PB[b0], PB[b0 + 1]], writes=[b_accG[tt][dh]])

                    if stop_after == "G1":
                        phaseA(0)
                        phaseB(0)
                        dump([(accG[:, 0, 0:1024], b_accG[0], 1024), (accG[:, 7, 1024:2048], b_accG[7], 1024),
                              (Gr[0][:, 0, :], [b_Gr[0]], 1024), (Gr[0][:, 3, :], [b_Gr[0]], 1024)])
                        return nc
                    if NR > 0:
                        phaseA(0)
                    for r in range(NR):
                        if r + 1 < NR:
                            phaseA(r + 1)
                        phaseB(r)
                kb.barrier()
                if stop_after == "G3":
                    dump([(accG[:, 0, 0:1024], b_accG[0], 1024), (accG[:, 7, 1024:2048], b_accG[7], 1024)])
                    return nc
                with ExitStack() as sc3:
                    sbL = lambda name, shape, dt=F32: sc3.enter_context(nc.sbuf_tensor(name + "_h%d" % half, list(shape), dt))
                    gb2 = sbL("gb2", [128, 2, D])
                    b_gb2 = Buf("gb2")
                    kb.dma("sp", gb2[:, 0, :], lnp[2], writes=[b_gb2])
                    kb.dma("sp", gb2[:, 1, :], lnp[3], writes=[b_gb2])
                    hres = [sbL("hres%d" % i, [128, D]) for i in range(2)]
                    b_hres = bufs(2, "hres")
                    xc2 = sbL("xc2", [128, D])
                    b_xc2 = Buf("xc2")
                    jf2 = sbL("jf2", [128, D])
                    b_jf2 = Buf("jf2")
                    yo = [sbL("yo%d" % i, [128, D]) for i in range(2)]
                    b_yo = bufs(2, "yo")
                    st2 = sbL("st2", [128, 8])
                    b_st2 = Buf("st2")
                    outs = []
                    kb.dma("sp", hres[0][:, :], h_tok[(8 * half) * 128:(8 * half + 1) * 128, :],
                           reads=[b_htok[8 * half]], writes=[b_hres[0]])
                    for tt in range(8):
                        T = 8 * half + tt
                        hi = tt % 2
                        if tt + 1 < 8:
                            kb.dma("sp", hres[1 - hi][:, :], h_tok[(T + 1) * 128:(T + 2) * 128, :],
                                   reads=[b_htok[T + 1]], writes=[b_hres[1 - hi]])
                        kb.op("dve", lambda e, hi=hi, tt=tt: e.scalar_tensor_tensor(
                            out=hres[hi][:, :], in0=hres[hi][:, :], scalar=ALPHA, in1=accG[:, tt, :],
                            op0=ALU.mult, op1=ALU.add),
                            reads=b_accG[tt], writes=[b_hres[hi]])
                        layer_norm(None, hres[hi][:, :], b_hres[hi], gb2[:, 0, :], gb2[:, 1, :], b_gb2,
                                   yo[hi][:, :], b_yo[hi], xc2, b_xc2, jf2, b_jf2, st2, b_st2)
                        outs.append(kb.dma("sp", y[T * 128:(T + 1) * 128, :], yo[hi][:, :], reads=[b_yo[hi]]))
                    for t in outs:
                        kb.wait("sp", t)
                    if stop_after in ("G2", "G4"):
                        dump([(accG[:, 0, 0:1024], b_accG[0], 1024), (accG[:, 7, 1024:2048], b_accG[7], 1024),
                              (yo[1][:, 0:1024], [b_yo[1]], 1024), (hres[1][:, 0:1024], [b_hres[1]], 1024)])
                        return nc
        print("instructions:", kb.nins, "sbuf remaining:", nc.sbuf_bytes_remaining)
    return nc


def _t5_bucket(dist):
    dist = np.asarray(dist)
    d = np.maximum(dist, 1).astype(np.float32)
    large = 16 + (np.log(d / np.float32(16)) / np.float32(np.log(128 / 16)) * np.float32(16)).astype(np.int32)
    large = np.minimum(large, 31)
    return np.where(dist < 16, dist, large)


def _prep_shared(w_in, pool_w, pool_scale, rel_bias, w_out, ln1_g, ln1_b, peer_wq, peer_subkeys, peer_u, peer_v,
                 ln2_g, ln2_b):
    f = np.float32
    w = w_in[0]
    cols = []
    for c in range(8):
        cols.append(w[:, c * 128:(c + 1) * 128])
    for c in range(8):
        cols.append(w[:, 1024 + c * 128:1024 + (c + 1) * 128])
    for c in range(8):
        cols.append(w[:, 2048 + c * 128:2048 + (c + 1) * 128])
    for c in range(8):
        cols.append(w[:, 4096 + c * 128:4096 + (c + 1) * 128])
    ki = w[:, 5120:5184]
    cols.append(np.concatenate([ki, ki], axis=1))
    w_fm = np.stack([c.reshape(16, 128, 128).transpose(1, 0, 2) for c in cols]).astype(f)
    wv = w[:, 3072:4096]
    w_v = np.stack([wv[:, hg * 256:(hg + 1) * 256].reshape(16, 128, 256).transpose(1, 0, 2) for hg in range(4)])
    w_wi = w[:, 5184:5200].reshape(16, 128, 16).transpose(1, 0, 2)
    pw = pool_w[0].reshape(4, 2, 128, 256).transpose(2, 0, 1, 3)
    kk = np.arange(128)[:, None]
    qq = np.arange(128)[None, :]
    bt = np.zeros((128, 2, 8, 128), f)
    for dl in range(2):
        bkt = _t5_bucket(np.maximum(dl * 128 + qq - kk, 0))
        bt[:, dl, :, :] = rel_bias[bkt].transpose(0, 2, 1)
    wo = w_out[0].reshape(16, 128, D).transpose(1, 0, 2)
    lnp = np.stack([np.broadcast_to(a[0][None, :], (128, D)) for a in (ln1_g, ln1_b, ln2_g, ln2_b)])
    wqh = peer_wq[0].reshape(16, 128, D).transpose(1, 0, 2)
    skT = peer_subkeys[0].transpose(2, 0, 1)
    u = peer_u[0].reshape(128, 128, 16, 128)
    uT = u.transpose(1, 3, 2, 0)
    vv = peer_v[0].reshape(128, 128, D).transpose(1, 0, 2)
    c = lambda a: np.ascontiguousarray(a, dtype=f)
    return dict(w_fm=c(w_fm), w_v=c(w_v), w_wi=c(w_wi), pool_w=c(pw), biasT=c(bt), w_out=c(wo), lnp=c(lnp),
                wq=c(wqh), subkT=c(skT), uT=c(uT), vL=c(vv))


def _consts(hf, pool_scale, rel_bias):
    f = np.float32
    cst = np.zeros((128, 1024), f)
    cst[:, 0:128] = np.eye(128, dtype=f)
    cst[:, 128:256] = np.arange(128, dtype=f)[None, :]
    qq = np.arange(128)[:, None]
    kk = np.arange(128)[None, :]
    cst[:, 256:384] = np.where(kk <= qq, 0.0, NEG).astype(f)
    valid = 1.0 if hf == 1 else 0.0
    cst[:, 384] = valid
    cst[:, 385] = (valid - 1.0) * 1.0e30
    cst[:, 392:400] = rel_bias[31][None, :]
    for gq, wwin in enumerate((2, 4, 8, 16)):
        pos = np.arange(16)
        if hf == 0:
            corr = wwin / np.minimum(pos + 1, wwin).astype(f)
        else:
            corr = np.ones(16, f)
        cst[:, 400 + 16 * gq:400 + 16 * gq + 16] = corr[None, :]
    cst[:, 464:472] = pool_scale[0].reshape(8, 128).T
    cst[:, 480:512] = (2.0 ** -np.arange(32, dtype=np.float64)).astype(f)[None, :]
    return cst


def _core_inputs(x, shared, pool_scale, rel_bias):
    in_maps = []
    for c in range(8):
        b, hf = c // 2, c % 2
        own = x[b, hf * TOK:(hf + 1) * TOK]
        prev = x[b, 0:TOK] if hf == 1 else np.zeros_like(own)
        xT = np.stack([hx_.T.reshape(16, 128, 4, 512).transpose(2, 1, 0, 3) for hx_ in (prev, own)])
        m = dict(shared)
        m["xT"] = np.ascontiguousarray(xT, dtype=np.float32)
        m["x_tok"] = np.ascontiguousarray(own, dtype=np.float32)
        m["consts"] = _consts(hf, pool_scale, rel_bias)
        in_maps.append(m)
    return in_maps


def kernel(x, w_in, pool_w, pool_scale, rel_bias, w_out, ln1_g, ln1_b, peer_wq, peer_subkeys, peer_u, peer_v,
           ln2_g, ln2_b):
    args = [np.asarray(a, dtype=np.float32) for a in (x, w_in, pool_w, pool_scale, rel_bias, w_out, ln1_g, ln1_b,
                                                      peer_wq, peer_subkeys, peer_u, peer_v, ln2_g, ln2_b)]
    (x, w_in, pool_w, pool_scale, rel_bias, w_out, ln1_g, ln1_b, peer_wq, peer_subkeys, peer_u, peer_v,
     ln2_g, ln2_b) = args
    shared = _prep_shared(w_in, pool_w, pool_scale, rel_bias, w_out, ln1_g, ln1_b, peer_wq, peer_subkeys,
                          peer_u, peer_v, ln2_g, ln2_b)
    in_maps = _core_inputs(x, shared, pool_scale, rel_bias)
    nc = build_nc()
    res = run_bass_kernel_spmd(nc, in_maps, core_ids=list(range(8)))
    out = np.zeros((4, S, D), np.float32)
    for c in range(8):
        b, hf = c // 2, c % 2
        out[b, hf * TOK:(hf + 1) * TOK] = res.results[c]["y"]
    return out
```
